# Optimizing a Trainium2 kernel written in Bass

```python
import math
import jax, jax.numpy as jnp
from jax import lax
import numpy as np

D_MODEL = 1024
BATCH = 8
SEQ = 4096
DEPTH = 4

GRID_W = 64
CTX_LEN = 256
N_MIXERS = 3
N_MOD = 6
NORM_EPS = 1e-6
FFN_HIDDEN = ((8 * D_MODEL + 3 * 256 - 1) // (3 * 256)) * 256
S5_GROUP_CH = 16
S5_GROUPS = D_MODEL // S5_GROUP_CH
S5_STATE = 64
SSD_D_INNER = 2 * D_MODEL
SSD_HEADDIM = 64
SSD_HEADS = SSD_D_INNER // SSD_HEADDIM
SSD_GROUPS = 8
SSD_HPG = SSD_HEADS // SSD_GROUPS
SSD_STATE = 128
SSD_CONV = 5
SSD_CHUNK = 128
SSD_BC_DIM = 2 * 2 * SSD_GROUPS * SSD_STATE
SSD_CONV_DIM = SSD_D_INNER + SSD_BC_DIM
SSD_IN_DIM = SSD_D_INNER + SSD_CONV_DIM + 2 * SSD_HEADS
NA_HEADS = 16
NA_HEAD_DIM = D_MODEL // NA_HEADS
NA_ROWS = 8
NA_COLS = 16
N_S5 = (DEPTH + N_MIXERS - 1) // N_MIXERS
N_SSD = (DEPTH + N_MIXERS - 2) // N_MIXERS
N_NA = (DEPTH + N_MIXERS - 3) // N_MIXERS

kernel_name = "hybrid_s5_ssd_natten_prefix_dit"

F32 = jnp.float32


def rms_norm(x, g):
    x32 = x.astype(F32)
    y = x32 * lax.rsqrt(jnp.mean(x32 * x32, axis=-1, keepdims=True) + NORM_EPS)
    return y.astype(x.dtype) * g


def modulate(h, shift, scale):
    return h * (1.0 + scale) + shift


def swiglu(h, w1, w3, w2):
    return (jax.nn.silu(h @ w1) * (h @ w3)) @ w2


def _linear_combine(e_i, e_j):
    a_i, b_i = e_i
    a_j, b_j = e_j
    return a_j * a_i, a_j * b_i + b_j


def s5_direction(u_ctx, u_lat, lam_re, lam_im, log_step, b_re, b_im, c_re, c_im):
    lam = lax.complex(lam_re.astype(F32), lam_im.astype(F32))
    lam_bar = jnp.exp(lam * jnp.exp(log_step.astype(F32))[:, None])
    b_bar = ((lam_bar - 1.0) / lam)[..., None] * lax.complex(b_re.astype(F32), b_im.astype(F32))
    c_mat = lax.complex(c_re.astype(F32), c_im.astype(F32))

    def run(u, h0):
        bu = jnp.einsum('lgh,gnh->lgn', u.astype(F32), b_bar)
        bu = bu.at[0].add(lam_bar * h0)
        a = jnp.broadcast_to(lam_bar, bu.shape)
        _, h = lax.associative_scan(_linear_combine, (a, bu), axis=0)
        return jnp.einsum('lgn,ghn->lgh', h, c_mat).real, h[-1]

    y_c, h_c = run(u_ctx, jnp.zeros(lam_bar.shape, lam_bar.dtype))
    y_l, _ = run(u_lat, h_c)
    return y_c, y_l


def s5_mixer(h_ctx, h_lat, lam_re, lam_im, log_step, b_re, b_im, c_re, c_im, d_skip, w_glu, b_glu):
    bsz, L, _ = h_lat.shape
    u_c = h_ctx.reshape(bsz, h_ctx.shape[1], S5_GROUPS, S5_GROUP_CH)
    u_l = h_lat.reshape(bsz, L, S5_GROUPS, S5_GROUP_CH)
    p_fwd = (lam_re[0], lam_im[0], log_step[0], b_re[0], b_im[0], c_re[0], c_im[0])
    p_bwd = (lam_re[1], lam_im[1], log_step[1], b_re[1], b_im[1], c_re[1], c_im[1])

    def per_sample(u):
        uc, ul = u
        fc, fl = s5_direction(uc, ul, *p_fwd)
        bc, bl = s5_direction(uc[::-1], ul[::-1], *p_bwd)
        return fc + bc[::-1], fl + bl[::-1]

    y_c, y_l = lax.map(per_sample, (u_c, u_l))

    def out(y, h):
        y = y.reshape(h.shape).astype(h.dtype) + d_skip * h
        g = jax.nn.gelu(y)
        return g * jax.nn.sigmoid(g @ w_glu + b_glu)

    return out(y_c, h_ctx), out(y_l, h_lat)


def _dwconv_centred(x, w, b):
    k = w.shape[0]
    y = lax.conv_general_dilated(x, w[:, None, :].astype(x.dtype), window_strides=(1,),
                                 padding=[(k // 2, k // 2)], dimension_numbers=('NWC', 'WIO', 'NWC'),
                                 feature_group_count=x.shape[-1])
    return y + b


def _ssd_chunked(xs, dt, a, b_in, c_in, h0):
    bsz, L = xs.shape[0], xs.shape[1]
    nc = L // SSD_CHUNK
    xdt = (xs.astype(F32) * dt[..., None]).reshape(bsz, nc, SSD_CHUNK, SSD_GROUPS, SSD_HPG, SSD_HEADDIM)
    a_dt = (dt * a).reshape(bsz, nc, SSD_CHUNK, SSD_GROUPS, SSD_HPG)
    a_cs = jnp.moveaxis(jnp.cumsum(a_dt, axis=2), 2, -1)
    bq = b_in.reshape(bsz, nc, SSD_CHUNK, SSD_GROUPS, SSD_STATE)
    cq = c_in.reshape(bsz, nc, SSD_CHUNK, SSD_GROUPS, SSD_STATE)
    mask = np.tril(np.ones((SSD_CHUNK, SSD_CHUNK), dtype=bool))
    seg = a_cs[..., :, None] - a_cs[..., None, :]
    decay = jnp.where(mask, jnp.exp(jnp.where(mask, seg, 0.0)), 0.0)
    cb = jnp.einsum('bclgn,bcsgn->bcgls', cq, bq)
    y_diag = jnp.einsum('bcgls,bcgels,bcsgep->bclgep', cb, decay, xdt)
    decay_to_end = jnp.exp(a_cs[..., -1:] - a_cs)
    chunk_states = jnp.einsum('bclgn,bcgel,bclgep->bcgepn', bq, decay_to_end, xdt)
    chunk_decay = jnp.exp(a_cs[..., -1])

    def chunk_step(h, inp):
        dec, st = inp
        return dec[..., None, None] * h + st, h

    h_last, h_starts = lax.scan(chunk_step, h0, (jnp.moveaxis(chunk_decay, 1, 0), jnp.moveaxis(chunk_states, 1, 0)))
    h_starts = jnp.moveaxis(h_starts, 0, 1)
    y_off = jnp.einsum('bclgn,bcgepn,bcgel->bclgep', cq, h_starts, jnp.exp(a_cs))
    return (y_diag + y_off).reshape(bsz, L, SSD_HEADS, SSD_HEADDIM), h_last


def _ssd_project(h, w_in, conv_w, conv_b, dt_bias):
    bsz, L, _ = h.shape
    zxbcdt = h @ w_in
    z = zxbcdt[..., :SSD_D_INNER]
    xbc = jax.nn.silu(_dwconv_centred(zxbcdt[..., SSD_D_INNER:SSD_D_INNER + SSD_CONV_DIM], conv_w, conv_b))
    dt_raw = zxbcdt[..., SSD_D_INNER + SSD_CONV_DIM:].reshape(bsz, L, 2, SSD_HEADS)
    xs = xbc[..., :SSD_D_INNER].reshape(bsz, L, SSD_HEADS, SSD_HEADDIM)
    bc = xbc[..., SSD_D_INNER:].reshape(bsz, L, 2, 2, SSD_GROUPS, SSD_STATE)
    dt = jax.nn.softplus(dt_raw.astype(F32) + dt_bias.astype(F32))
    return z, xs, bc, dt


def ssd_mixer(h_ctx, h_lat, w_in, conv_w, conv_b, dt_bias, a_log, d_skip, norm_w, w_out):
    bsz = h_lat.shape[0]
    zc, xc, bcc, dtc = _ssd_project(h_ctx, w_in, conv_w, conv_b, dt_bias)
    zl, xl, bcl, dtl = _ssd_project(h_lat, w_in, conv_w, conv_b, dt_bias)
    a = -jnp.exp(a_log.astype(F32))
    h0 = jnp.zeros((bsz, SSD_GROUPS, SSD_HPG, SSD_HEADDIM, SSD_STATE), F32)
    flip = lambda t: jnp.flip(t, axis=1)

    def scan_dir(d, xc_, bcc_, dtc_, xl_, bcl_, dtl_):
        y_c, h_c = _ssd_chunked(xc_, dtc_[:, :, d], a[d], bcc_[:, :, d, 0], bcc_[:, :, d, 1], h0)
        y_l, _ = _ssd_chunked(xl_, dtl_[:, :, d], a[d], bcl_[:, :, d, 0], bcl_[:, :, d, 1], h_c)
        return y_c, y_l

    yf_c, yf_l = scan_dir(0, xc, bcc, dtc, xl, bcl, dtl)
    yb_c, yb_l = scan_dir(1, flip(xc), flip(bcc), flip(dtc), flip(xl), flip(bcl), flip(dtl))

    def out(y_f, y_b, xs, z, h):
        y = y_f + flip(y_b) + d_skip[:, None] * xs
        y = y.reshape(h.shape[0], h.shape[1], SSD_D_INNER).astype(h.dtype) * jax.nn.silu(z)
        return rms_norm(y, norm_w) @ w_out

    return out(yf_c, yb_c, xc, zc, h_ctx), out(yf_l, yb_l, xl, zl, h_lat)


def na_mixer(h_ctx, h_lat, w_qkv, w_o, rpb):
    bsz, L, _ = h_lat.shape
    rows = L // GRID_W
    kr = min(NA_ROWS, rows)
    scale = 1.0 / math.sqrt(NA_HEAD_DIM)
    q, k, v = jnp.split((h_lat @ w_qkv).reshape(bsz, rows, GRID_W, 3, NA_HEADS, NA_HEAD_DIM), 3, axis=3)
    q, k, v = q[:, :, :, 0], k[:, :, :, 0], v[:, :, :, 0]
    qc, kc, vc = jnp.split(h_ctx @ w_qkv, 3, axis=-1)
    qc, kc, vc = [t.reshape(bsz, t.shape[1], NA_HEADS, NA_HEAD_DIM) for t in (qc, kc, vc)]

    s_cc = jnp.einsum('bqhd,bkhd->bhqk', qc, kc).astype(F32) * scale
    y_ctx = jnp.einsum('bhqk,bkhd->bqhd', jax.nn.softmax(s_cc, axis=-1).astype(vc.dtype), vc)
    y_ctx = y_ctx.reshape(bsz, -1, D_MODEL) @ w_o

    col_start = np.clip(np.arange(GRID_W) - NA_COLS // 2, 0, GRID_W - NA_COLS)
    col_idx = col_start[:, None] + np.arange(NA_COLS)[None, :]
    col_off = col_idx - np.arange(GRID_W)[:, None] + (NA_COLS - 1)

    def one_row(inp):
        q_row, r = inp
        start = jnp.clip(r - kr // 2, 0, rows - kr)
        k_win = lax.dynamic_slice_in_dim(k, start, kr, axis=1)[:, :, col_idx]
        v_win = lax.dynamic_slice_in_dim(v, start, kr, axis=1)[:, :, col_idx]
        row_off = start + jnp.arange(kr) - r + (NA_ROWS - 1)
        bias = rpb[:, row_off[:, None, None], col_off[None, :, :]]
        bias = jnp.transpose(bias, (0, 2, 1, 3))
        s_loc = jnp.einsum('bqhd,biqjhd->bhqij', q_row, k_win) * scale + bias[None]
        s_ctx = jnp.einsum('bqhd,bkhd->bhqk', q_row, kc) * scale
        s = jnp.concatenate([s_loc.reshape(bsz, NA_HEADS, GRID_W, kr * NA_COLS), s_ctx], axis=-1).astype(F32)
        p = jax.nn.softmax(s, axis=-1).astype(v.dtype)
        p_loc = p[..., :kr * NA_COLS].reshape(bsz, NA_HEADS, GRID_W, kr, NA_COLS)
        p_ctx = p[..., kr * NA_COLS:]
        return (jnp.einsum('bhqij,biqjhd->bqhd', p_loc, v_win)
                + jnp.einsum('bhqk,bkhd->bqhd', p_ctx, vc))

    y = lax.map(one_row, (jnp.moveaxis(q, 1, 0), jnp.arange(rows)))
    y_lat = jnp.moveaxis(y, 0, 1).reshape(bsz, L, D_MODEL) @ w_o
    return y_ctx, y_lat


def setup_inputs(seed: int = 0) -> dict:
    key = jax.random.key(seed)
    k = jax.random.split(key, 33)

    def nrm(kk, shape, scale=1.0):
        return scale * jax.random.normal(kk, shape, F32)

    lam_im = jnp.pi * jnp.arange(S5_STATE, dtype=F32)
    dt_init = jnp.exp(jax.random.uniform(k[24], (N_SSD, 2, SSD_HEADS), F32, math.log(1e-3), math.log(1e-1)))
    return {
        "x": nrm(k[0], (BATCH, SEQ, D_MODEL)),
        "c": nrm(k[1], (BATCH, D_MODEL)),
        "ctx": nrm(k[2], (BATCH, CTX_LEN, D_MODEL)),
        "c_ctx": nrm(k[3], (D_MODEL,)),
        "ada_w": nrm(k[4], (DEPTH, D_MODEL, N_MOD * D_MODEL), 0.5 * D_MODEL ** -0.5),
        "ada_b": nrm(k[5], (DEPTH, N_MOD * D_MODEL), 0.01),
        "norm_mix": 1.0 + nrm(k[6], (DEPTH, D_MODEL), 0.01),
        "norm_ffn": 1.0 + nrm(k[7], (DEPTH, D_MODEL), 0.01),
        "norm_final": 1.0 + nrm(k[8], (D_MODEL,), 0.01),
        "ffn_w1": nrm(k[9], (DEPTH, D_MODEL, FFN_HIDDEN), D_MODEL ** -0.5),
        "ffn_w3": nrm(k[10], (DEPTH, D_MODEL, FFN_HIDDEN), D_MODEL ** -0.5),
        "ffn_w2": nrm(k[11], (DEPTH, FFN_HIDDEN, D_MODEL), FFN_HIDDEN ** -0.5),
        "s5_lam_re": -0.5 + nrm(k[12], (N_S5, 2, S5_GROUPS, S5_STATE), 0.01),
        "s5_lam_im": lam_im + nrm(k[13], (N_S5, 2, S5_GROUPS, S5_STATE), 0.01),
        "s5_log_step": jax.random.uniform(k[14], (N_S5, 2, S5_GROUPS), F32, math.log(1e-3), math.log(1e-1)),
        "s5_b_re": nrm(k[15], (N_S5, 2, S5_GROUPS, S5_STATE, S5_GROUP_CH), (2 * S5_GROUP_CH) ** -0.5),
        "s5_b_im": nrm(k[16], (N_S5, 2, S5_GROUPS, S5_STATE, S5_GROUP_CH), (2 * S5_GROUP_CH) ** -0.5),
        "s5_c_re": nrm(k[17], (N_S5, 2, S5_GROUPS, S5_GROUP_CH, S5_STATE), (S5_STATE) ** -0.5),
        "s5_c_im": nrm(k[18], (N_S5, 2, S5_GROUPS, S5_GROUP_CH, S5_STATE), (S5_STATE) ** -0.5),
        "s5_d": nrm(k[19], (N_S5, D_MODEL), 1.0),
        "s5_w_glu": nrm(k[20], (N_S5, D_MODEL, D_MODEL), D_MODEL ** -0.5),
        "s5_b_glu": nrm(k[21], (N_S5, D_MODEL), 0.01),
        "ssd_w_in": nrm(k[22], (N_SSD, D_MODEL, SSD_IN_DIM), D_MODEL ** -0.5),
        "ssd_conv_w": nrm(k[23], (N_SSD, SSD_CONV, SSD_CONV_DIM), SSD_CONV ** -0.5),
        "ssd_conv_b": nrm(k[25], (N_SSD, SSD_CONV_DIM), 0.01),
        "ssd_dt_bias": dt_init + jnp.log(-jnp.expm1(-dt_init)),
        "ssd_a_log": jnp.log(jax.random.uniform(k[26], (N_SSD, 2, SSD_HEADS), F32, 1.0, 16.0)),
        "ssd_d": 1.0 + nrm(k[27], (N_SSD, SSD_HEADS), 0.01),
        "ssd_norm": 1.0 + nrm(k[28], (N_SSD, SSD_D_INNER), 0.01),
        "ssd_w_out": nrm(k[29], (N_SSD, SSD_D_INNER, D_MODEL), SSD_D_INNER ** -0.5),
        "na_w_qkv": nrm(k[30], (N_NA, D_MODEL, 3 * D_MODEL), D_MODEL ** -0.5),
        "na_w_o": nrm(k[31], (N_NA, D_MODEL, D_MODEL), D_MODEL ** -0.5),
        "na_rpb": nrm(k[32], (N_NA, NA_HEADS, 2 * NA_ROWS - 1, 2 * NA_COLS - 1), 0.02),
    }


def reference(x, c, ctx, c_ctx, ada_w, ada_b, norm_mix, norm_ffn, norm_final, ffn_w1, ffn_w3, ffn_w2,
              s5_lam_re, s5_lam_im, s5_log_step, s5_b_re, s5_b_im, s5_c_re, s5_c_im, s5_d, s5_w_glu, s5_b_glu,
              ssd_w_in, ssd_conv_w, ssd_conv_b, ssd_dt_bias, ssd_a_log, ssd_d, ssd_norm, ssd_w_out,
              na_w_qkv, na_w_o, na_rpb):
    sc = jax.nn.silu(c)
    scc = jax.nn.silu(c_ctx)
    xc = ctx
    for i in range(DEPTH):
        kind, j = i % N_MIXERS, i // N_MIXERS
        m_l = jnp.split(sc @ ada_w[i] + ada_b[i], N_MOD, axis=-1)
        m_c = jnp.split(scc @ ada_w[i] + ada_b[i], N_MOD, axis=-1)
        h_l = modulate(rms_norm(x, norm_mix[i]), m_l[0][:, None], m_l[1][:, None])
        h_c = modulate(rms_norm(xc, norm_mix[i]), m_c[0], m_c[1])
        if kind == 0:
            y_c, y_l = s5_mixer(h_c, h_l, s5_lam_re[j], s5_lam_im[j], s5_log_step[j], s5_b_re[j], s5_b_im[j],
                                s5_c_re[j], s5_c_im[j], s5_d[j], s5_w_glu[j], s5_b_glu[j])
        elif kind == 1:
            y_c, y_l = ssd_mixer(h_c, h_l, ssd_w_in[j], ssd_conv_w[j], ssd_conv_b[j], ssd_dt_bias[j],
                                 ssd_a_log[j], ssd_d[j], ssd_norm[j], ssd_w_out[j])
        else:
            y_c, y_l = na_mixer(h_c, h_l, na_w_qkv[j], na_w_o[j], na_rpb[j])
        x = x + m_l[2][:, None] * y_l
        h_l = modulate(rms_norm(x, norm_ffn[i]), m_l[3][:, None], m_l[4][:, None])
        x = x + m_l[5][:, None] * swiglu(h_l, ffn_w1[i], ffn_w3[i], ffn_w2[i])
        if i < DEPTH - 1:
            xc = xc + m_c[2] * y_c
            h_c = modulate(rms_norm(xc, norm_ffn[i]), m_c[3], m_c[4])
            xc = xc + m_c[5] * swiglu(h_c, ffn_w1[i], ffn_w3[i], ffn_w2[i])
    return rms_norm(x, norm_final)
```

```python
import numpy as np
from contextlib import ExitStack
import concourse.bass as bass
import concourse.mybir as mybir
from concourse.bass_utils import run_bass_kernel_spmd

F32 = mybir.dt.float32
BF16 = mybir.dt.bfloat16
I32 = mybir.dt.int32
AF = mybir.ActivationFunctionType
ALU = mybir.AluOpType
AX = mybir.AxisListType

ENGS = ("sync", "gpsimd", "scalar", "vector", "tensor")
NDMA_SEM = 24
SEM_EPOCH = 12000

D = 1024
LC = 256
LL = 4096
T = LC + LL
NFT = 8
FH = 2816
NHT = 22
DEPTH = 4
EPS = 1e-6
ARENA_F32 = 46 * 1024


class Buf:
    __slots__ = ("name", "w", "r", "psum")

    def __init__(self, name, psum=False):
        self.name = name
        self.w = None
        self.r = []
        self.psum = psum


class Op:
    __slots__ = ("eng", "fn", "idx", "deps", "need_inc", "val", "is_dma", "semi", "is_barrier")

    def __init__(self, eng, fn, is_dma):
        self.eng = eng
        self.fn = fn
        self.is_dma = is_dma
        self.deps = []
        self.need_inc = False
        self.val = 0
        self.semi = -1
        self.idx = -1
        self.is_barrier = False


class Prog:
    def __init__(self, nc):
        self.nc = nc
        self.ops = {e: [] for e in ENGS}
        self.nreal = {e: 0 for e in ENGS}
        self.es = ExitStack()
        self.engsem = {e: self.es.enter_context(nc.semaphore("s_" + e)) for e in ENGS}
        self.dmasem = [self.es.enter_context(nc.semaphore("d%d" % i)) for i in range(NDMA_SEM)]
        self.dma_last = [None] * NDMA_SEM
        self.dma_cnt = [0] * NDMA_SEM
        self.dma_rr = 0

    def emit(self, eng, fn, reads=(), writes=(), dma=False):
        op = Op(eng, fn, dma)
        op.idx = self.nreal[eng]
        self.nreal[eng] += 1
        deps = []
        for b in reads:
            if b.w is not None:
                deps.append(b.w)
            if b.psum:
                deps.extend(b.r)
        for b in writes:
            if b.w is not None:
                deps.append(b.w)
            deps.extend(b.r)
        if dma:
            k = self.dma_rr
            self.dma_rr = (k + 1) % NDMA_SEM
            if self.dma_last[k] is not None:
                deps.append(self.dma_last[k])
            self.dma_last[k] = op
            self.dma_cnt[k] += 16
            op.semi = k
            op.val = self.dma_cnt[k]
        seen = set()
        for d in deps:
            if d is op or id(d) in seen:
                continue
            seen.add(id(d))
            op.deps.append(d)
        for b in reads:
            if b.psum:
                b.w = op
                b.r = []
            else:
                b.r.append(op)
        for b in writes:
            b.w = op
            b.r = []
        self.ops[eng].append(op)
        return op

    def fence(self, eng, deps):
        op = Op(eng, None, False)
        op.idx = self.nreal[eng]
        op.deps = list(deps)
        self.ops[eng].append(op)
        return op

    def barrier(self):
        lasts = []
        for e in ENGS:
            for o in reversed(self.ops[e]):
                if o.fn is not None:
                    lasts.append(o)
                    break
        for o in self.dma_last:
            if o is not None:
                lasts.append(o)
        for e in ENGS:
            self.fence(e, lasts).is_barrier = True

    def _needs_wait(self, op, d):
        if d.is_dma:
            return True
        if d.eng != op.eng:
            return True
        if op.eng == "tensor":
            return False
        return (op.idx - d.idx) <= 2

    def build(self):
        nc = self.nc
        for e in ENGS:
            for op in self.ops[e]:
                for d in op.deps:
                    if not d.is_dma and self._needs_wait(op, d):
                        d.need_inc = True
        self.epoch_sems = {e: [self.engsem[e]] for e in ENGS}
        for e in ENGS:
            c = 0
            ep = 0
            for op in self.ops[e]:
                if op.fn is None:
                    if op.is_barrier and c > SEM_EPOCH:
                        ep += 1
                        c = 0
                        self.epoch_sems[e].append(self.es.enter_context(nc.semaphore("s_%s_%d" % (e, ep))))
                    continue
                if op.is_dma:
                    continue
                op.semi = ep
                if op.need_inc:
                    c += 1
                    op.val = c
        self.counts = {e: 0 for e in ENGS}

        def mk_body(e):
            def body(eng):
                waited = {}
                for op in self.ops[e]:
                    for d in op.deps:
                        if not self._needs_wait(op, d):
                            continue
                        if d.is_dma:
                            key = ("d", d.semi)
                            sem = self.dmasem[d.semi]
                        else:
                            key = ("e", d.eng, d.semi)
                            sem = self.epoch_sems[d.eng][d.semi]
                        if waited.get(key, 0) >= d.val:
                            continue
                        waited[key] = d.val
                        eng.wait_ge(sem, d.val)
                    if op.fn is None:
                        continue
                    inst = op.fn(eng)
                    self.counts[e] += 1
                    if op.is_dma:
                        inst.then_inc(self.dmasem[op.semi], 16)
                    elif op.need_inc:
                        inst.then_inc(self.epoch_sems[e][op.semi], 1)
            return body

        with nc.Block() as block:
            for e in ENGS:
                if self.ops[e]:
                    getattr(block, e)(mk_body(e))
        self.es.close()


class Tl:
    __slots__ = ("ap", "buf")

    def __init__(self, ap, buf):
        self.ap = ap
        self.buf = buf


class KB:
    def __init__(self, nc, cfg):
        self.nc = nc
        self.cfg = cfg
        self.p = Prog(nc)
        self.es = ExitStack()
        self.arena = self.es.enter_context(nc.sbuf_tensor("arena", [128, ARENA_F32], F32))
        self.arena_bf = self.arena.bitcast(BF16)
        self.psum = [self.es.enter_context(nc.psum_tensor("ps%d" % i, [128, 512], F32)) for i in range(8)]
        self.PS = [Tl(self.psum[i][:], Buf("ps%d" % i, psum=True)) for i in range(8)]
        self.PSB = [self.psum[i].bitcast(BF16) for i in range(8)]
        self.top = 0
        self.ptr = 0
        self.nb = 0
        self.dram = {}
        self.out_ops = []

    def _al(self, n_f32, persistent):
        n_f32 = (n_f32 + 7) // 8 * 8
        if persistent:
            assert self.ptr == self.top, "persistent alloc only between phases"
            off = self.top
            self.top += n_f32
            self.ptr = self.top
        else:
            off = self.ptr
            self.ptr += n_f32
        assert self.ptr <= ARENA_F32, "arena overflow %d" % self.ptr
        return off

    def f32(self, n, name=None, persistent=False, shape=None):
        off = self._al(n, persistent)
        ap = self.arena[:, off:off + n]
        if shape is not None:
            ap = self._reshape(ap, shape)
        self.nb += 1
        return Tl(ap, Buf(name or "t%d" % self.nb))

    def bf16(self, n, name=None, persistent=False, shape=None):
        off = self._al((n + 1) // 2, persistent)
        ap = self.arena_bf[:, 2 * off:2 * off + n]
        if shape is not None:
            ap = self._reshape(ap, shape)
        self.nb += 1
        return Tl(ap, Buf(name or "t%d" % self.nb))

    def i32(self, n, name=None):
        off = self._al(n, False)
        ap = self.arena.bitcast(I32)[:, off:off + n]
        self.nb += 1
        return Tl(ap, Buf(name or "t%d" % self.nb))

    @staticmethod
    def _reshape(ap, shape):
        if len(shape) == 2:
            return ap.rearrange("p (a b) -> p a b", b=shape[1])
        if len(shape) == 3:
            return ap.rearrange("p (a b c) -> p a b c", b=shape[1], c=shape[2])
        raise ValueError

    def new_phase(self):
        self.p.barrier()
        self.ptr = self.top

    def dram_t(self, name, shape, dt, kind="Internal"):
        t = self.nc.dram_tensor(name, shape, dt, kind=kind)
        tl = Tl(t.ap(), Buf(name))
        self.dram[name] = tl
        return tl

    def E(self, eng, fn, r=(), w=()):
        return self.p.emit(eng, fn, [t.buf for t in r], [t.buf for t in w])

    def DMA(self, eng, out_ap, in_ap, r=(), w=(), slow=False):
        if slow:
            fn = lambda e: e.dma_start(out=out_ap, in_=in_ap, allow_slow_non_contiguous=True)
        else:
            fn = lambda e: e.dma_start(out=out_ap, in_=in_ap)
        return self.p.emit(eng, fn, [t.buf for t in r], [t.buf for t in w], dma=True)

    def mm(self, ps_ap, lhsT, rhs, start, stop, r=(), w=()):
        return self.E("tensor", lambda e: e.matmul(ps_ap, lhsT=lhsT, rhs=rhs, start=start, stop=stop), r, w)

    def tr(self, ps_ap, in_ap, ident_ap, r=(), w=()):
        return self.E("tensor", lambda e: e.transpose(out=ps_ap, in_=in_ap, identity=ident_ap), r, w)

    def act(self, out, in_, func, r=(), w=(), scale=None, bias=None, accum=None):
        kw = {}
        if scale is not None:
            kw["scale"] = scale
        if bias is not None:
            kw["bias"] = bias
        if accum is not None:
            kw["accum_out"] = accum
        return self.E("scalar", lambda e: e.activation(out=out, in_=in_, func=func, **kw), r, w)

    def tt(self, eng, out, in0, in1, op, r=(), w=()):
        return self.E(eng, lambda e: e.tensor_tensor(out=out, in0=in0, in1=in1, op=op), r, w)

    def ts(self, eng, out, in0, s1, op0, s2=None, op1=None, r=(), w=(), accum=None):
        if op1 is None:
            return self.E(eng, lambda e: e.tensor_scalar(out=out, in0=in0, scalar1=s1, scalar2=None, op0=op0), r, w)
        if accum is not None:
            return self.E(eng, lambda e: e.tensor_scalar(out=out, in0=in0, scalar1=s1, scalar2=s2, op0=op0, op1=op1, accum_out=accum), r, w)
        return self.E(eng, lambda e: e.tensor_scalar(out=out, in0=in0, scalar1=s1, scalar2=s2, op0=op0, op1=op1), r, w)

    def stt(self, out, in0, scalar, in1, op0, op1, r=(), w=()):
        return self.E("vector", lambda e: e.scalar_tensor_tensor(out=out, in0=in0, scalar=scalar, in1=in1, op0=op0, op1=op1), r, w)

    def cp(self, eng, out, in_, r=(), w=()):
        if eng == "scalar":
            return self.act(out, in_, AF.Copy, r, w)
        return self.E(eng, lambda e: e.tensor_copy(out=out, in_=in_), r, w)

    def memset(self, eng, ap, val, w=()):
        return self.E(eng, lambda e: e.memset(ap, val), (), w)

    def consts(self):
        self.ident = self.f32(128, "ident", True)
        self.identb = self.bf16(128, "identb", True)
        self.ones = self.f32(128, "ones", True)
        self.epsc = self.f32(8, "epsc", True)
        self.memset("gpsimd", self.ident.ap, 0.0, [self.ident])
        idap = self.ident.ap
        self.E("gpsimd", lambda e: e.affine_select(out=idap, in_=idap, pattern=[[-1, 128]], compare_op=ALU.not_equal,
                                                   fill=1.0, base=0, channel_multiplier=1), [self.ident], [self.ident])
        self.cp("vector", self.identb.ap, self.ident.ap, [self.ident], [self.identb])
        self.memset("vector", self.ones.ap, 1.0, [self.ones])
        self.memset("vector", self.epsc.ap[:, 0:1], EPS, [self.epsc])
        self.memset("vector", self.epsc.ap[:, 1:2], 0.0, [self.epsc])
        self.memset("vector", self.epsc.ap[:, 2:3], 1.0, [self.epsc])

    @staticmethod
    def chunks(w_lat=512):
        ch = [(0, LC, True)]
        for s in range(LC, T, w_lat):
            ch.append((s, w_lat, False))
        return ch

    def prologue_transpose(self, x_in, ctx_in, XT):
        self.new_phase()
        xin = [self.f32(4 * D, "xin%d" % i, shape=(4, D)) for i in range(2)]
        stage = [self.f32(NFT * 512, "stg%d" % i, shape=(NFT, 512)) for i in range(2)]
        for ci, (s, w, isctx) in enumerate(self.chunks()):
            xi = xin[ci % 2]
            st = stage[ci % 2]
            ntt = w // 128
            if isctx:
                src = ctx_in.ap.rearrange("(tt p) f -> p tt f", p=128)
            else:
                src = x_in.ap[s - LC:s - LC + w, :].rearrange("(tt p) f -> p tt f", p=128)
            self.DMA("sync", xi.ap[:, 0:ntt, :], src, r=[x_in], w=[xi])
            for ft in range(NFT):
                ps = self.PS[ft]
                for tt in range(ntt):
                    self.tr(ps.ap[:, tt * 128:(tt + 1) * 128], xi.ap[:, tt, ft * 128:(ft + 1) * 128], self.ident.ap,
                            r=[xi, self.ident], w=[ps])
                self.cp("scalar" if ft % 2 else "vector", st.ap[:, ft, 0:w], ps.ap[:, 0:w], r=[ps], w=[st])
            dst = XT.ap[:, s:s + w].rearrange("(ft p) t -> p ft t", p=128)
            self.DMA("sync", dst, st.ap[:, :, 0:w], r=[st], w=[XT])

    def adaln(self, c_in, cctx_in, ada_w, ada_b, norm_mix, norm_ffn, norm_final, layers):
        self.mod = {}
        for l in layers:
            self.mod[l] = self.f32(96, "mod%d" % l, True, shape=(6, 8, 2))
        self.nw = self.f32(9 * 8, "nw", True, shape=(9, 8))
        self.AB = {}
        for l in layers:
            self.AB[l] = self.f32(4 * 16, "AB%d" % l, True, shape=(4, 8, 2))
        self.new_phase()
        sT = self.f32(16, "sT", shape=(8, 2))
        craw = self.f32(16, "craw", shape=(8, 2))
        self.DMA("sync", craw.ap[:, :, 0], c_in.ap.rearrange("(kt p) -> p kt", p=128), r=[c_in], w=[craw], slow=True)
        self.DMA("sync", craw.ap[:, :, 1], cctx_in.ap.rearrange("(kt p) -> p kt", p=128), r=[cctx_in], w=[craw], slow=True)
        self.act(sT.ap, craw.ap, AF.Silu, r=[craw], w=[sT])
        for k, nwt in enumerate([norm_mix, norm_ffn]):
            self.DMA("sync", self.nw.ap[:, 4 * k:4 * k + 4, :], nwt.ap.rearrange("l (ft p) -> p l ft", p=128), r=[nwt], w=[self.nw], slow=True)
        self.DMA("sync", self.nw.ap[:, 8, :], norm_final.ap.rearrange("(ft p) -> p ft", p=128), r=[norm_final], w=[self.nw], slow=True)
        wbuf = [self.f32(8 * 512, "adaw%d" % i, shape=(8, 512)) for i in range(3)]
        bbuf = [self.f32(512, "adab%d" % i) for i in range(3)]
        onesrow = self.ones.ap[0:1, 0:2]
        it = 0
        for l in layers:
            for cj in range(12):
                wb = wbuf[it % 3]
                bb = bbuf[it % 3]
                ps = self.PS[it % 4]
                it += 1
                self.DMA("sync", wb.ap, ada_w.ap[l, :, cj * 512:(cj + 1) * 512].rearrange("(kt p) n -> p kt n", p=128), r=[ada_w], w=[wb])
                self.DMA("sync", bb.ap[0:1, :], ada_b.ap[l:l + 1, cj * 512:(cj + 1) * 512], r=[ada_b], w=[bb])
                for jj in range(4):
                    j = cj * 4 + jj
                    o = ps.ap[:, jj * 2:jj * 2 + 2]
                    for kt in range(8):
                        self.mm(o, wb.ap[:, kt, jj * 128:(jj + 1) * 128], sT.ap[:, kt, :], kt == 0, False, r=[wb, sT], w=[ps])
                    self.mm(o, bb.ap[0:1, jj * 128:(jj + 1) * 128], onesrow, False, True, r=[bb, self.ones], w=[ps])
                m = cj * 4 // 8
                ft0 = (cj * 4) % 8
                self.cp("vector", self.mod[l].ap[:, m, ft0:ft0 + 4, :], ps.ap[:, 0:8].rearrange("p (a b) -> p a b", b=2), r=[ps], w=[self.mod[l]])
        for l in layers:
            for k, (mi, nwi) in enumerate([(1, l), (4, 4 + l)]):
                nwb = self.nw.ap[:, nwi, :].unsqueeze(2).to_broadcast([128, 8, 2])
                self.stt(self.AB[l].ap[:, k, :, :], self.mod[l].ap[:, mi, :, :], 1.0, nwb, ALU.add, ALU.mult, r=[self.mod[l], self.nw], w=[self.AB[l]])

    def norm_phase(self, XT, HT, A_sel, B_sel, deps_r):
        self.new_phase()
        xin = [self.f32(NFT * 512, "nx%d" % i, shape=(NFT, 512)) for i in range(2)]
        sq = [self.f32(NFT * 512, "nsq%d" % i, shape=(NFT, 512)) for i in range(2)]
        hb = [self.bf16(NFT * 512, "nh%d" % i, shape=(NFT, 512)) for i in range(2)]
        rt = [self.f32(512, "nrt%d" % i) for i in range(2)]
        for ci, (s, w, isctx) in enumerate(self.chunks()):
            xi, sqi, hbi, rti = xin[ci % 2], sq[ci % 2], hb[ci % 2], rt[ci % 2]
            ps = self.PS[ci % 2]
            self.DMA("sync", xi.ap[:, :, 0:w], XT.ap[:, s:s + w].rearrange("(ft p) t -> p ft t", p=128), r=[XT], w=[xi])
            self.act(sqi.ap[:, :, 0:w], xi.ap[:, :, 0:w], AF.Square, r=[xi], w=[sqi])
            for ft in range(NFT):
                self.mm(ps.ap[:, 0:w], self.ones.ap, sqi.ap[:, ft, 0:w], ft == 0, ft == NFT - 1, r=[self.ones, sqi], w=[ps])
            self.act(rti.ap[:, 0:w], ps.ap[:, 0:w], AF.Sqrt, r=[ps, self.epsc], w=[rti], scale=1.0 / D, bias=self.epsc.ap[:, 0:1])
            self.E("vector", lambda e, o=rti.ap[:, 0:w]: e.reciprocal(out=o, in_=o), r=[rti], w=[rti])
            for ft in range(NFT):
                a = A_sel(ft, isctx)
                b = B_sel(ft, isctx)
                self.stt(sqi.ap[:, ft, 0:w], xi.ap[:, ft, 0:w], a, rti.ap[:, 0:w], ALU.mult, ALU.mult, r=[xi, rti] + deps_r, w=[sqi])
                self.act(hbi.ap[:, ft, 0:w], sqi.ap[:, ft, 0:w], AF.Identity, r=[sqi] + deps_r, w=[hbi], bias=b)
            self.DMA("sync", HT.ap[:, s:s + w].rearrange("(ft p) t -> p ft t", p=128), hbi.ap[:, :, 0:w], r=[hbi], w=[HT])

    def norm_layer(self, XT, HT, l, which):
        AB = self.AB[l]
        mod = self.mod[l]
        k = 0 if which == 0 else 1
        smi = 0 if which == 0 else 3
        A_sel = lambda ft, isctx: AB.ap[:, k, ft, (1 if isctx else 0):(1 if isctx else 0) + 1]
        B_sel = lambda ft, isctx: mod.ap[:, smi, ft, (1 if isctx else 0):(1 if isctx else 0) + 1]
        self.norm_phase(XT, HT, A_sel, B_sel, [AB, mod])

    def final_phase(self, XT, out_t):
        self.new_phase()
        xin = [self.f32(NFT * 512, "fx%d" % i, shape=(NFT, 512)) for i in range(2)]
        sq = [self.f32(NFT * 512, "fsq%d" % i, shape=(NFT, 512)) for i in range(2)]
        rt = [self.f32(512, "frt%d" % i) for i in range(2)]
        ob = [self.f32(4 * D, "fo%d" % i, shape=(4, D)) for i in range(2)]
        ci = 0
        for (s, w, isctx) in self.chunks():
            if isctx:
                continue
            xi, sqi, rti, obi = xin[ci % 2], sq[ci % 2], rt[ci % 2], ob[ci % 2]
            ps = self.PS[ci % 2]
            ci += 1
            self.DMA("sync", xi.ap, XT.ap[:, s:s + w].rearrange("(ft p) t -> p ft t", p=128), r=[XT], w=[xi])
            self.act(sqi.ap, xi.ap, AF.Square, r=[xi], w=[sqi])
            for ft in range(NFT):
                self.mm(ps.ap, self.ones.ap, sqi.ap[:, ft, :], ft == 0, ft == NFT - 1, r=[self.ones, sqi], w=[ps])
            self.act(rti.ap, ps.ap, AF.Sqrt, r=[ps, self.epsc], w=[rti], scale=1.0 / D, bias=self.epsc.ap[:, 0:1])
            self.E("vector", lambda e, o=rti.ap: e.reciprocal(out=o, in_=o), r=[rti], w=[rti])
            for ft in range(NFT):
                self.stt(sqi.ap[:, ft, :], xi.ap[:, ft, :], self.nw.ap[:, 8, ft:ft + 1], rti.ap, ALU.mult, ALU.mult, r=[xi, rti, self.nw], w=[sqi])
            for tt in range(4):
                for half in range(2):
                    pso = self.PS[2 + (tt * 2 + half) % 6]
                    for q in range(4):
                        ft = half * 4 + q
                        self.tr(pso.ap[:, q * 128:(q + 1) * 128], sqi.ap[:, ft, tt * 128:(tt + 1) * 128], self.ident.ap, r=[sqi, self.ident], w=[pso])
                    self.cp("scalar" if half else "vector", obi.ap[:, tt, half * 512:(half + 1) * 512], pso.ap, r=[pso], w=[obi])
            dst = out_t.ap[s - LC:s - LC + w, :].rearrange("(tt p) f -> p tt f", p=128)
            self.out_ops.append(self.DMA("sync", dst, obi.ap, r=[obi], w=[out_t]))

    def load_w_bf16(self, dst, src_ap, src_tl, nkt, col0=None, col1=None):
        for kt in range(nkt):
            s = src_ap[kt * 128:(kt + 1) * 128, :] if col0 is None else src_ap[kt * 128:(kt + 1) * 128, col0:col1]
            self.DMA("gpsimd", dst.ap[:, kt, :], s, r=[src_tl], w=[dst])

    def ffn_phase(self, XT, HT, w1, w3, w2, l):
        self.new_phase()
        W = 256
        w1s = self.bf16(8 * FH, "w1s", shape=(8, FH))
        w3s = self.bf16(8 * FH, "w3s", shape=(8, FH))
        w2s = self.bf16(NHT * D, "w2s", shape=(NHT, D))
        self.load_w_bf16(w1s, w1.ap[l], w1, 8)
        self.load_w_bf16(w3s, w3.ap[l], w3, 8)
        self.load_w_bf16(w2s, w2.ap[l], w2, NHT)
        hb = [self.bf16(NFT * W, "fh%d" % i, shape=(NFT, W)) for i in range(2)]
        xb = [self.f32(NFT * W, "fxx%d" % i, shape=(NFT, W)) for i in range(2)]
        gb = [self.bf16(NHT * W, "fg%d" % i, shape=(NHT, W)) for i in range(1)]
        sl = [self.bf16(W, "fs%d" % i) for i in range(2)]
        mod = self.mod[l]
        ntile = T // W
        for ti in range(ntile):
            s = ti * W
            isctx = s < LC
            cs = 1 if isctx else 0
            hbi, xbi, gbi = hb[ti % 2], xb[ti % 2], gb[0]
            self.DMA("sync", hbi.ap, HT.ap[:, s:s + W].rearrange("(ft p) t -> p ft t", p=128), r=[HT], w=[hbi])
            self.DMA("sync", xbi.ap, XT.ap[:, s:s + W].rearrange("(ft p) t -> p ft t", p=128), r=[XT], w=[xbi])
            for j in range(NHT):
                pa = self.PS[(j % 2) * 2]
                pb = self.PS[(j % 2) * 2 + 1]
                for kt in range(8):
                    self.mm(pa.ap[:, 0:W], w1s.ap[:, kt, j * 128:(j + 1) * 128], hbi.ap[:, kt, :], kt == 0, kt == 7, r=[w1s, hbi], w=[pa])
                for kt in range(8):
                    self.mm(pb.ap[:, 0:W], w3s.ap[:, kt, j * 128:(j + 1) * 128], hbi.ap[:, kt, :], kt == 0, kt == 7, r=[w3s, hbi], w=[pb])
                sli = sl[j % 2]
                self.act(sli.ap, pa.ap[:, 0:W], AF.Silu, r=[pa], w=[sli])
                self.tt("vector", gbi.ap[:, j, :], pb.ap[:, 0:W], sli.ap, ALU.mult, r=[pb, sli], w=[gbi])
            for fo in range(NFT):
                po = self.PS[4 + fo % 4]
                for j in range(NHT):
                    self.mm(po.ap[:, 0:W], w2s.ap[:, j, fo * 128:(fo + 1) * 128], gbi.ap[:, j, :], j == 0, j == NHT - 1, r=[w2s, gbi], w=[po])
                self.stt(xbi.ap[:, fo, :], po.ap[:, 0:W], mod.ap[:, 5, fo, cs:cs + 1], xbi.ap[:, fo, :], ALU.mult, ALU.add, r=[po, mod, xbi], w=[xbi])
            self.DMA("sync", XT.ap[:, s:s + W].rearrange("(ft p) t -> p ft t", p=128), xbi.ap, r=[xbi], w=[XT])

    def rev_ap(self, ap2d, start, n):
        pstride = ap2d.ap[0][0]
        return bass.AP(ap2d.tensor, ap2d.offset + start + n - 1, [[pstride, 128], [-1, n]])

    def sincos_turns(self, eng, turns, n, osin, ocos, tmp):
        ti, tf, fr, s2, s4 = tmp["ti"], tmp["tf"], tmp["fr"], tmp["s2"], tmp["s4"]
        sl = lambda t: t.ap[:, 0:n]
        self.cp("vector", sl(ti), sl(turns), r=[turns], w=[ti])
        self.cp("vector", sl(tf), sl(ti), r=[ti], w=[tf])
        self.tt(eng, sl(fr), sl(turns), sl(tf), ALU.subtract, r=[turns, tf], w=[fr])
        self.act(sl(s2), sl(fr), AF.Sin, r=[fr], w=[s2], scale=float(np.pi))
        self.act(sl(s4), sl(fr), AF.Sin, r=[fr], w=[s4], scale=float(np.pi / 2))
        self.tt(eng, sl(s4), sl(s4), sl(s4), ALU.mult, r=[s4], w=[s4])
        self.ts(eng, sl(s4), sl(s4), -4.0, ALU.mult, 2.0, ALU.add, r=[s4], w=[s4])
        self.tt(eng, sl(osin), sl(s2), sl(s4), ALU.mult, r=[s2, s4], w=[osin])
        self.tt(eng, sl(s2), sl(s2), sl(s2), ALU.mult, r=[s2], w=[s2])
        self.ts(eng, sl(ocos), sl(s2), -2.0, ALU.mult, 1.0, ALU.add, r=[s2], w=[ocos])

    def build_mask8(self):
        self.mask8 = self.f32(8, "mask8", True)
        m = self.mask8.ap
        self.memset("gpsimd", m, 1.0, [self.mask8])
        self.E("gpsimd", lambda e: e.affine_select(out=m, in_=m, pattern=[[-16, 8]], compare_op=ALU.is_ge, fill=0.0, base=0, channel_multiplier=1), [self.mask8], [self.mask8])
        self.E("gpsimd", lambda e: e.affine_select(out=m, in_=m, pattern=[[16, 8]], compare_op=ALU.is_ge, fill=0.0, base=15, channel_multiplier=-1), [self.mask8], [self.mask8])

    def s5_phase(self, HT, GT, P, j):
        self.new_phase()
        V, G = "vector", "gpsimd"
        def sc_tile(nm):
            return self.f32(64, nm)
        lr, li, ls = sc_tile("lr"), sc_tile("li"), sc_tile("ls")
        for d in range(2):
            self.DMA("sync", lr.ap[:, d * 32:(d + 1) * 32], P["lam_re"].ap[j, d].rearrange("(p two) n -> (two n) p", two=2), r=[P["lam_re"]], w=[lr], slow=True)
            self.DMA("sync", li.ap[:, d * 32:(d + 1) * 32], P["lam_im"].ap[j, d].rearrange("(p two) n -> (two n) p", two=2), r=[P["lam_im"]], w=[li], slow=True)
        lsrow = self.f32(128, "lsrow")
        self.DMA("sync", lsrow.ap[0:1, :], P["log_step"].ap[j:j + 1].rearrange("o d g -> o (d g)"), r=[P["log_step"]], w=[lsrow])
        psb = self.PS[0]
        self.mm(psb.ap[:, 0:128], self.ones.ap[0:1, :], lsrow.ap[0:1, :], True, True, r=[self.ones, lsrow], w=[psb])
        for d in range(2):
            src = psb.ap[:, d * 64:(d + 1) * 64].rearrange("q (p two) -> q p two", two=2)
            self.cp(V, ls.ap[0:64, d * 32:(d + 1) * 32], src[0:64, :, 0], r=[psb], w=[ls])
            self.cp(V, ls.ap[64:128, d * 32:(d + 1) * 32], src[64:128, :, 1], r=[psb], w=[ls])
        step, zr, zi, rr, tq = sc_tile("step"), sc_tile("zr"), sc_tile("zi"), sc_tile("rr"), sc_tile("tq")
        tmp = {"ti": self.i32(512, "ti"), "tf": self.f32(512, "tf"), "fr": self.f32(512, "fr"), "s2": self.f32(512, "s2"), "s4": self.f32(512, "s4")}
        sphi, cphi, frac = sc_tile("sphi"), sc_tile("cphi"), sc_tile("frac")
        self.act(step.ap, ls.ap, AF.Exp, r=[ls], w=[step])
        self.tt(V, zr.ap, lr.ap, step.ap, ALU.mult, r=[lr, step], w=[zr])
        self.tt(V, zi.ap, li.ap, step.ap, ALU.mult, r=[li, step], w=[zi])
        self.act(rr.ap, zr.ap, AF.Exp, r=[zr], w=[rr])
        self.ts(V, tq.ap, zi.ap, float(1.0 / (2 * np.pi)), ALU.mult, r=[zi], w=[tq])
        self.sincos_turns(V, tq, 64, sphi, cphi, tmp)
        self.cp(V, frac.ap, tmp["fr"].ap[:, 0:64], r=[tmp["fr"]], w=[frac])
        carry = {}
        for Q in (256, 512):
            tQ, sQ, cQ = sc_tile("tQ%d" % Q), sc_tile("sQ%d" % Q), sc_tile("cQ%d" % Q)
            self.ts(V, tQ.ap, frac.ap, float(Q), ALU.mult, r=[frac], w=[tQ])
            self.sincos_turns(V, tQ, 64, sQ, cQ, tmp)
            carry[Q] = (sQ, cQ)
        ar, ai, den, u, cr, ci, t1s, t2s = [sc_tile(n) for n in ("ar", "ai", "den", "u", "cr", "ci", "t1s", "t2s")]
        self.tt(V, ar.ap, rr.ap, cphi.ap, ALU.mult, r=[rr, cphi], w=[ar])
        self.tt(V, ai.ap, rr.ap, sphi.ap, ALU.mult, r=[rr, sphi], w=[ai])
        self.tt(V, t1s.ap, lr.ap, lr.ap, ALU.mult, r=[lr], w=[t1s])
        self.tt(V, t2s.ap, li.ap, li.ap, ALU.mult, r=[li], w=[t2s])
        self.tt(V, den.ap, t1s.ap, t2s.ap, ALU.add, r=[t1s, t2s], w=[den])
        self.E(V, lambda e: e.reciprocal(out=den.ap, in_=den.ap), r=[den], w=[den])
        self.ts(V, u.ap, ar.ap, -1.0, ALU.add, r=[ar], w=[u])
        self.tt(V, t1s.ap, u.ap, lr.ap, ALU.mult, r=[u, lr], w=[t1s])
        self.tt(V, t2s.ap, ai.ap, li.ap, ALU.mult, r=[ai, li], w=[t2s])
        self.tt(V, t1s.ap, t1s.ap, t2s.ap, ALU.add, r=[t1s, t2s], w=[t1s])
        self.tt(V, cr.ap, t1s.ap, den.ap, ALU.mult, r=[t1s, den], w=[cr])
        self.tt(V, t1s.ap, ai.ap, lr.ap, ALU.mult, r=[ai, lr], w=[t1s])
        self.tt(V, t2s.ap, u.ap, li.ap, ALU.mult, r=[u, li], w=[t2s])
        self.tt(V, t1s.ap, t1s.ap, t2s.ap, ALU.subtract, r=[t1s, t2s], w=[t1s])
        self.tt(V, ci.ap, t1s.ap, den.ap, ALU.mult, r=[t1s, den], w=[ci])
        braw = {}
        for nm in ("b_re", "b_im"):
            braw[nm] = self.f32(2 * 32 * 16, "braw_" + nm, shape=(2, 32, 16))
            for d in range(2):
                self.DMA("sync", braw[nm].ap[:, d, :, :], P[nm].ap[j, d].rearrange("(p two) n h -> (two n) p h", two=2), r=[P[nm]], w=[braw[nm]], slow=True)
        dsk = self.f32(8, "dsk")
        self.DMA("sync", dsk.ap, P["d"].ap[j].rearrange("(ft p) -> p ft", p=128), r=[P["d"]], w=[dsk], slow=True)
        M1 = {}
        for k in range(4):
            for nm in ("b_re", "b_im"):
                M1[(k, nm)] = self.f32(128, "M1_%d%s" % (k, nm))
                self.memset(G, M1[(k, nm)].ap, 0.0, [M1[(k, nm)]])
        Jrow = self.f32(512, "Jrow")
        self.E(G, lambda e: e.iota(Jrow.ap, pattern=[[1, 512]], base=0, channel_multiplier=0, allow_small_or_imprecise_dtypes=True), (), [Jrow])
        WTS = [self.bf16(8 * 5 * 128, "wts%d" % i, shape=(8, 5, 128)) for i in range(2)]
        craw = [[self.f32(64, "craw%d_%d" % (i, q)) for q in range(2)] for i in range(2)]
        Spair = [self.f32(128, "Spair%d" % i) for i in range(2)]
        U = [self.bf16(T, "U%d" % i) for i in range(2)]
        Yacc = self.f32(T, "Yacc")
        gt = self.bf16(T, "gt")
        TAB = [{n: self.f32(512, "%s%d" % (n, i)) for n in ("COS", "SIN", "wr", "wi", "ta", "tb")} for i in range(2)]
        WK = [{n: self.f32(512, "%s%d" % (n, i)) for n in ("t1", "t2", "t3", "t4", "bre", "bim", "gre", "gim")} for i in range(2)]
        MK = [{n: self.bf16(512, "%s%d" % (n, i)) for n in ("m1", "m2", "m3", "m4")} for i in range(2)]
        init = [self.f32(4, "init%d" % i) for i in range(2)]
        fwd_chunks = [(0, LC)] + [(s, 512) for s in range(LC, T, 512)]
        bwd_chunks = [(0, LC)] + [(T - 512 * (i + 1), 512) for i in range(8)]
        it = 0
        tg = 0
        for ft in range(NFT):
            Ui = U[ft % 2]
            self.DMA("sync", Ui.ap, HT.ap[ft * 128:(ft + 1) * 128, :], r=[HT], w=[Ui])
            W = WTS[ft % 2]
            for d in range(2):
                cr_ = craw[d]
                self.DMA("sync", cr_[0].ap, P["c_re"].ap[j, d, ft * 8:(ft + 1) * 8].rearrange("g h n -> (g h) n"), r=[P["c_re"]], w=[cr_[0]])
                self.DMA("sync", cr_[1].ap, P["c_im"].ap[j, d, ft * 8:(ft + 1) * 8].rearrange("g h n -> (g h) n"), r=[P["c_im"]], w=[cr_[1]])
                for k in range(4):
                    p_ = ft * 4 + k
                    wi_ = d * 4 + k
                    c1, c2 = 32 * k, 32 * k + 16
                    for bi, nm in enumerate(("b_re", "b_im")):
                        m1t = M1[(k, nm)]
                        self.cp(G, m1t.ap[0:64, c1:c1 + 16], braw[nm].ap[0:64, d, p_, :], r=[braw[nm]], w=[m1t])
                        self.cp(G, m1t.ap[64:128, c2:c2 + 16], braw[nm].ap[64:128, d, p_, :], r=[braw[nm]], w=[m1t])
                        ps = self.PS[6 + (tg % 2)]
                        tg += 1
                        self.tr(ps.ap[:, 0:128], m1t.ap, self.ident.ap, r=[m1t, self.ident], w=[ps])
                        self.cp("scalar", W.ap[:, wi_, bi, :], ps.ap[:, 0:128], r=[ps], w=[W])
                    for q in range(2):
                        sp = Spair[q]
                        self.ts(G, sp.ap[:, 0:64], cr_[q].ap, self.mask8.ap[:, 2 * k:2 * k + 1], ALU.mult, 0.0, ALU.add, r=[cr_[q], self.mask8], w=[sp])
                        self.ts(G, sp.ap[:, 64:128], cr_[q].ap, self.mask8.ap[:, 2 * k + 1:2 * k + 2], ALU.mult, 0.0, ALU.add, r=[cr_[q], self.mask8], w=[sp])
                        ps = self.PS[6 + (tg % 2)]
                        tg += 1
                        self.tr(ps.ap[:, 0:128], sp.ap, self.ident.ap, r=[sp, self.ident], w=[ps])
                        if q == 0:
                            self.cp("scalar", W.ap[:, wi_, 2, :], ps.ap[:, 0:128], r=[ps], w=[W])
                            self.act(W.ap[:, wi_, 3, :], ps.ap[:, 0:128], AF.Copy, r=[ps], w=[W], scale=-1.0)
                        else:
                            self.act(W.ap[:, wi_, 4, :], ps.ap[:, 0:128], AF.Copy, r=[ps], w=[W], scale=-1.0)
            first = True
            for d in range(2):
                chunks = fwd_chunks if d == 0 else bwd_chunks
                for k in range(4):
                    p_ = ft * 4 + k
                    wi_ = d * 4 + k
                    col = d * 32 + p_
                    tab = TAB[it % 2]
                    it += 1
                    self.ts(G, tab["ta"].ap, Jrow.ap, frac.ap[:, col:col + 1], ALU.mult, 0.0, ALU.add, r=[Jrow, frac], w=[tab["ta"]])
                    self.sincos_turns(G, tab["ta"], 512, tab["SIN"], tab["COS"], tmp)
                    crc, cic = cr.ap[:, col:col + 1], ci.ap[:, col:col + 1]
                    self.ts(G, tab["ta"].ap, tab["COS"].ap, crc, ALU.mult, 0.0, ALU.add, r=[tab["COS"], cr], w=[tab["ta"]])
                    self.ts(G, tab["tb"].ap, tab["SIN"].ap, cic, ALU.mult, 0.0, ALU.add, r=[tab["SIN"], ci], w=[tab["tb"]])
                    self.tt(G, tab["wr"].ap, tab["ta"].ap, tab["tb"].ap, ALU.add, r=[tab["ta"], tab["tb"]], w=[tab["wr"]])
                    self.ts(G, tab["ta"].ap, tab["COS"].ap, cic, ALU.mult, 0.0, ALU.add, r=[tab["COS"], ci], w=[tab["ta"]])
                    self.ts(G, tab["tb"].ap, tab["SIN"].ap, crc, ALU.mult, 0.0, ALU.add, r=[tab["SIN"], cr], w=[tab["tb"]])
                    self.tt(G, tab["wi"].ap, tab["ta"].ap, tab["tb"].ap, ALU.subtract, r=[tab["ta"], tab["tb"]], w=[tab["wi"]])
                    rcol = rr.ap[:, col:col + 1]
                    for cidx, (s, n) in enumerate(chunks):
                        wk = WK[cidx % 2]
                        mk = MK[cidx % 2]
                        ini = init[cidx % 2]
                        pre, pim = self.PS[(cidx % 2) * 2], self.PS[(cidx % 2) * 2 + 1]
                        py = self.PS[4 + cidx % 2]
                        urhs = Ui.ap[:, s:s + n] if d == 0 else self.rev_ap(Ui.ap, s, n)
                        self.mm(pre.ap[:, 0:n], W.ap[:, wi_, 0, :], urhs, True, True, r=[W, Ui], w=[pre])
                        self.mm(pim.ap[:, 0:n], W.ap[:, wi_, 1, :], urhs, True, True, r=[W, Ui], w=[pim])
                        c_ = lambda t: t.ap[:, 0:n]
                        self.tt(V, c_(wk["t1"]), pre.ap[:, 0:n], c_(tab["wr"]), ALU.mult, r=[pre, tab["wr"]], w=[wk["t1"]])
                        self.tt(V, c_(wk["t2"]), pim.ap[:, 0:n], c_(tab["wi"]), ALU.mult, r=[pim, tab["wi"]], w=[wk["t2"]])
                        self.tt(V, c_(wk["t3"]), pim.ap[:, 0:n], c_(tab["wr"]), ALU.mult, r=[pim, tab["wr"]], w=[wk["t3"]])
                        self.tt(V, c_(wk["t4"]), pre.ap[:, 0:n], c_(tab["wi"]), ALU.mult, r=[pre, tab["wi"]], w=[wk["t4"]])
                        self.tt(G, c_(wk["bre"]), c_(wk["t1"]), c_(wk["t2"]), ALU.subtract, r=[wk["t1"], wk["t2"]], w=[wk["bre"]])
                        self.tt(G, c_(wk["bim"]), c_(wk["t3"]), c_(wk["t4"]), ALU.add, r=[wk["t3"], wk["t4"]], w=[wk["bim"]])
                        rb = rcol.to_broadcast([128, n])
                        if cidx == 0:
                            i_re, i_im, ir = 0.0, 0.0, []
                        else:
                            pini = init[(cidx - 1) % 2]
                            i_re, i_im, ir = pini.ap[:, 0:1], pini.ap[:, 1:2], [pini]
                        self.E(V, lambda e, o=c_(wk["gre"]), d1=c_(wk["bre"]), i0=i_re, rb=rb: e.tensor_tensor_scan(out=o, data0=rb, data1=d1, initial=i0, op0=ALU.mult, op1=ALU.add),
                               r=[wk["bre"], rr] + ir, w=[wk["gre"]])
                        self.E(V, lambda e, o=c_(wk["gim"]), d1=c_(wk["bim"]), i0=i_im, rb=rb: e.tensor_tensor_scan(out=o, data0=rb, data1=d1, initial=i0, op0=ALU.mult, op1=ALU.add),
                               r=[wk["bim"], rr] + ir, w=[wk["gim"]])
                        if cidx < len(chunks) - 1:
                            sQ, cQ = carry[n]
                            sq_, cq_ = sQ.ap[:, col:col + 1], cQ.ap[:, col:col + 1]
                            gre_l, gim_l = wk["gre"].ap[:, n - 1:n], wk["gim"].ap[:, n - 1:n]
                            self.ts(V, ini.ap[:, 2:3], gim_l, sq_, ALU.mult, r=[wk["gim"], sQ], w=[ini])
                            self.ts(V, ini.ap[:, 3:4], gim_l, cq_, ALU.mult, r=[wk["gim"], cQ], w=[ini])
                            self.stt(ini.ap[:, 0:1], gre_l, cq_, ini.ap[:, 2:3], ALU.mult, ALU.subtract, r=[wk["gre"], cQ, ini], w=[ini])
                            self.stt(ini.ap[:, 1:2], gre_l, sq_, ini.ap[:, 3:4], ALU.mult, ALU.add, r=[wk["gre"], sQ, ini], w=[ini])
                        self.tt(G, c_(mk["m1"]), c_(wk["gre"]), c_(tab["COS"]), ALU.mult, r=[wk["gre"], tab["COS"]], w=[mk["m1"]])
                        self.tt(G, c_(mk["m2"]), c_(wk["gim"]), c_(tab["SIN"]), ALU.mult, r=[wk["gim"], tab["SIN"]], w=[mk["m2"]])
                        self.tt(G, c_(mk["m3"]), c_(wk["gre"]), c_(tab["SIN"]), ALU.mult, r=[wk["gre"], tab["SIN"]], w=[mk["m3"]])
                        self.tt(G, c_(mk["m4"]), c_(wk["gim"]), c_(tab["COS"]), ALU.mult, r=[wk["gim"], tab["COS"]], w=[mk["m4"]])
                        for mi, (mn, wsel) in enumerate((("m1", 2), ("m2", 3), ("m3", 4), ("m4", 4))):
                            mr = mk[mn].ap[:, 0:n] if d == 0 else self.rev_ap(mk[mn].ap, 0, n)
                            self.mm(py.ap[:, 0:n], W.ap[:, wi_, wsel, :], mr, mi == 0, mi == 3, r=[W, mk[mn]], w=[py])
                        if first:
                            self.cp("scalar", Yacc.ap[:, s:s + n], py.ap[:, 0:n], r=[py], w=[Yacc])
                        else:
                            self.tt(V, Yacc.ap[:, s:s + n], py.ap[:, 0:n], Yacc.ap[:, s:s + n], ALU.add, r=[py, Yacc], w=[Yacc])
                    first = False
            self.stt(Yacc.ap, Ui.ap, dsk.ap[:, ft:ft + 1], Yacc.ap, ALU.mult, ALU.add, r=[Ui, dsk, Yacc], w=[Yacc])
            self.act(gt.ap, Yacc.ap, AF.Gelu_apprx_tanh, r=[Yacc], w=[gt])
            self.DMA("sync", GT.ap[ft * 128:(ft + 1) * 128, :], gt.ap, r=[gt], w=[GT])

    def proj_phase(self, XT, IN, Wd, w_ap, nkt, l, glu_bias=None):
        self.new_phase()
        Wt = 512
        ws = self.bf16(nkt * D, "pw", shape=(nkt, D))
        self.load_w_bf16(ws, w_ap, Wd, nkt)
        bg = None
        if glu_bias is not None:
            bg = self.f32(8, "bglu")
            self.DMA("sync", bg.ap, glu_bias[1].rearrange("(ft p) -> p ft", p=128), r=[glu_bias[0]], w=[bg], slow=True)
        ib = [self.bf16(nkt * Wt, "pi%d" % i, shape=(nkt, Wt)) for i in range(2)]
        xb = [self.f32(NFT * Wt, "px%d" % i, shape=(NFT, Wt)) for i in range(2)]
        sg = [self.f32(Wt, "psg%d" % i) for i in range(2)]
        mod = self.mod[l]
        for ci, (s, w, isctx) in enumerate(self.chunks()):
            cs = 1 if isctx else 0
            ibi, xbi = ib[ci % 2], xb[ci % 2]
            self.DMA("sync", ibi.ap[:, :, 0:w], IN.ap[:, s:s + w].rearrange("(kt p) t -> p kt t", p=128), r=[IN], w=[ibi])
            self.DMA("sync", xbi.ap[:, :, 0:w], XT.ap[:, s:s + w].rearrange("(ft p) t -> p ft t", p=128), r=[XT], w=[xbi])
            for fo in range(NFT):
                po = self.PS[fo % 4]
                for kt in range(nkt):
                    self.mm(po.ap[:, 0:w], ws.ap[:, kt, fo * 128:(fo + 1) * 128], ibi.ap[:, kt, 0:w], kt == 0, kt == nkt - 1, r=[ws, ibi], w=[po])
                gate = mod.ap[:, 2, fo, cs:cs + 1]
                if glu_bias is not None:
                    sgi = sg[fo % 2]
                    self.act(sgi.ap[:, 0:w], po.ap[:, 0:w], AF.Sigmoid, r=[po, bg], w=[sgi], bias=bg.ap[:, fo:fo + 1])
                    self.tt("gpsimd", sgi.ap[:, 0:w], sgi.ap[:, 0:w], ibi.ap[:, fo, 0:w], ALU.mult, r=[sgi, ibi], w=[sgi])
                    self.stt(xbi.ap[:, fo, 0:w], sgi.ap[:, 0:w], gate, xbi.ap[:, fo, 0:w], ALU.mult, ALU.add, r=[sgi, mod, xbi], w=[xbi])
                else:
                    self.stt(xbi.ap[:, fo, 0:w], po.ap[:, 0:w], gate, xbi.ap[:, fo, 0:w], ALU.mult, ALU.add, r=[po, mod, xbi], w=[xbi])
            self.DMA("sync", XT.ap[:, s:s + w].rearrange("(ft p) t -> p ft t", p=128), xbi.ap[:, :, 0:w], r=[xbi], w=[XT])


    def row_bcast(self, dst, row_ap, src_tl, n, rowtmp, func=None, scale=None):
        self.DMA("sync", rowtmp.ap[0:1, 0:n], row_ap, r=[src_tl], w=[rowtmp])
        for c0 in range(0, n, 512):
            w = min(512, n - c0)
            ps = self.PS[7]
            self.mm(ps.ap[:, 0:w], self.ones.ap[0:1, :], rowtmp.ap[0:1, c0:c0 + w], True, True, r=[self.ones, rowtmp], w=[ps])
            if func is None:
                self.cp("vector", dst.ap[:, c0:c0 + w], ps.ap[:, 0:w], r=[ps], w=[dst])
            else:
                self.act(dst.ap[:, c0:c0 + w], ps.ap[:, 0:w], func, r=[ps], w=[dst], scale=scale)

    def tri_mask(self, nm, base, cm, step, op):
        t = self.f32(128, nm)
        self.memset("gpsimd", t.ap, 1.0, [t])
        self.E("gpsimd", lambda e: e.affine_select(out=t.ap, in_=t.ap, pattern=[[step, 128]], compare_op=op, fill=0.0, base=base, channel_multiplier=cm), [t], [t])
        return t

    def ssd_phase_a(self, HT, P, SZ, XS, BC, DTR):
        self.new_phase()
        w_in = P["w_in"]
        Hres = self.bf16(8 * T, "Hres", shape=(8, T))
        for kt in range(8):
            self.DMA("sync", Hres.ap[:, kt, :], HT.ap[kt * 128:(kt + 1) * 128, :], r=[HT], w=[Hres])
        wz = self.bf16(8 * 2048, "wz", shape=(8, 2048))
        self.load_w_bf16(wz, w_in.ap[0], w_in, 8, 0, 2048)
        szb = [self.bf16(2048, "szb%d" % i) for i in range(2)]
        for tt in range(T // 128):
            sb = szb[tt % 2]
            for zc in range(4):
                ps = self.PS[zc]
                for kt in range(8):
                    self.mm(ps.ap, Hres.ap[:, kt, tt * 128:(tt + 1) * 128], wz.ap[:, kt, zc * 512:(zc + 1) * 512], kt == 0, kt == 7, r=[Hres, wz], w=[ps])
                self.act(sb.ap[:, zc * 512:(zc + 1) * 512], ps.ap, AF.Silu, r=[ps], w=[sb])
            self.DMA("sync", SZ.ap[tt * 128:(tt + 1) * 128, :], sb.ap, r=[sb], w=[SZ])
        self.new_phase()
        Hres = self.bf16(8 * T, "Hres2", shape=(8, T))
        for kt in range(8):
            self.DMA("sync", Hres.ap[:, kt, :], HT.ap[kt * 128:(kt + 1) * 128, :], r=[HT], w=[Hres])
        cw = self.f32(5 * 48, "cw", shape=(5, 48))
        cb = self.f32(48, "cb")
        for k in range(5):
            self.DMA("sync", cw.ap[:, k, :], P["conv_w"].ap[0, k].rearrange("(ot p) -> p ot", p=128), r=[P["conv_w"]], w=[cw], slow=True)
        self.DMA("sync", cb.ap, P["conv_b"].ap[0].rearrange("(ot p) -> p ot", p=128), r=[P["conv_b"]], w=[cb], slow=True)
        wch = [self.bf16(8 * 512, "wch%d" % i, shape=(8, 512)) for i in range(2)]
        xp = [self.bf16(T, "xp%d" % i) for i in range(2)]
        ob = [self.bf16(T, "ob%d" % i) for i in range(2)]
        dg = [self.bf16(5 * 128, "dg%d" % i, shape=(5, 128)) for i in range(2)]
        chunks = self.chunks()
        pi = 0
        for wc in range(12):
            wb = wch[wc % 2]
            self.load_w_bf16(wb, w_in.ap[0], w_in, 8, 2048 + wc * 512, 2048 + (wc + 1) * 512)
            for q in range(4):
                ot = wc * 4 + q
                xpi, obi, dgi = xp[ot % 2], ob[ot % 2], dg[ot % 2]
                for k in range(5):
                    self.ts("gpsimd", dgi.ap[:, k, :], self.identb.ap, cw.ap[:, k, ot:ot + 1], ALU.mult, 0.0, ALU.add, r=[self.identb, cw], w=[dgi])
                for ci, (s, w, isctx) in enumerate(chunks):
                    ps = self.PS[pi % 4]
                    pi += 1
                    for kt in range(8):
                        self.mm(ps.ap[:, 0:w], wb.ap[:, kt, q * 128:(q + 1) * 128], Hres.ap[:, kt, s:s + w], kt == 0, kt == 7, r=[wb, Hres], w=[ps])
                    self.cp("vector" if ci % 2 else "scalar", xpi.ap[:, s:s + w], ps.ap[:, 0:w], r=[ps], w=[xpi])
                for ci, (s, w, isctx) in enumerate(chunks):
                    q0, q1 = (0, LC) if isctx else (LC, T)
                    ps = self.PS[4 + ci % 4]
                    for ki, k in enumerate((2, 0, 1, 3, 4)):
                        o = k - 2
                        i0 = max(0, q0 - s - o)
                        i1 = min(w, q1 - s - o)
                        self.mm(ps.ap[:, i0:i1], dgi.ap[:, k, :], xpi.ap[:, s + i0 + o:s + i1 + o], ki == 0, ki == 4, r=[dgi, xpi], w=[ps])
                    self.act(obi.ap[:, s:s + w], ps.ap[:, 0:w], AF.Silu, r=[ps, cb], w=[obi], bias=cb.ap[:, ot:ot + 1])
                dst = XS.ap[ot * 128:(ot + 1) * 128, :] if ot < 16 else BC.ap[(ot - 16) * 128:(ot - 15) * 128, :]
                self.DMA("sync", dst, obi.ap, r=[obi], w=[XS if ot < 16 else BC])
        wdt = self.bf16(8 * 64, "wdt", shape=(8, 64))
        self.load_w_bf16(wdt, w_in.ap[0], w_in, 8, 8192, 8256)
        dtb = self.f32(T, "dtb")
        for ci, (s, w, isctx) in enumerate(chunks):
            ps = self.PS[ci % 4]
            for kt in range(8):
                self.mm(ps.ap[0:64, 0:w], wdt.ap[:, kt, :], Hres.ap[:, kt, s:s + w], kt == 0, kt == 7, r=[wdt, Hres], w=[ps])
            self.cp("vector", dtb.ap[0:64, s:s + w], ps.ap[0:64, 0:w], r=[ps], w=[dtb])
        self.DMA("sync", DTR.ap, dtb.ap[0:64, :], r=[dtb], w=[DTR])

    def ssd_phase_b(self, P, XS, BC, DTR, YF, YB):
        self.new_phase()
        V, G = "vector", "gpsimd"
        GT_ = self.tri_mask("mGT", 0, 1, -1, ALU.is_gt)
        LE_ = self.tri_mask("mLE", 0, -1, 1, ALU.is_ge)
        LT_ = self.tri_mask("mLT", 0, -1, 1, ALU.is_gt)
        GE_ = self.tri_mask("mGE", 0, 1, -1, ALU.is_ge)
        NEGf, NEGb = self.f32(128, "NEGf"), self.f32(128, "NEGb")
        self.ts(G, NEGf.ap, GT_.ap, -30000.0, ALU.mult, 0.0, ALU.add, r=[GT_], w=[NEGf])
        self.ts(G, NEGb.ap, LT_.ap, -30000.0, ALU.mult, 0.0, ALU.add, r=[LT_], w=[NEGb])
        rowtmp = self.f32(64, "rowtmp")
        Arow = [self.f32(32, "Arow%d" % d) for d in range(2)]
        Brow = [self.f32(32, "Brow%d" % d) for d in range(2)]
        Drow = self.f32(32, "Drow")
        for d in range(2):
            self.row_bcast(Arow[d], P["a_log"].ap[0, d:d + 1, :], P["a_log"], 32, rowtmp, func=AF.Exp)
            self.ts(V, Arow[d].ap, Arow[d].ap, -1.0, ALU.mult, r=[Arow[d]], w=[Arow[d]])
            self.row_bcast(Brow[d], P["dt_bias"].ap[0, d:d + 1, :], P["dt_bias"], 32, rowtmp)
        self.row_bcast(Drow, P["d"].ap[0:1, :], P["d"], 32, rowtmp)
        H = self.f32(2048, "Hst", shape=(32, 64))
        Hb = self.bf16(2048, "Hb")
        xsT = [self.bf16(16 * 128, "xsT%d" % i, shape=(16, 128)) for i in range(2)]
        BTt = [self.bf16(8 * 128, "BT%d" % i, shape=(8, 128)) for i in range(2)]
        CTt = [self.bf16(8 * 128, "CT%d" % i, shape=(8, 128)) for i in range(2)]
        dtr = [self.f32(128, "dtr%d" % i) for i in range(2)]
        ybuf = [self.bf16(2048, "ybuf%d" % i) for i in range(2)]
        xdt = self.bf16(2048, "xdt", shape=(32, 64))
        xdte = self.bf16(2048, "xdte", shape=(32, 64))
        dskt = self.f32(2048, "dskt", shape=(32, 64))
        Btok = self.bf16(1024, "Btok", shape=(8, 128))
        dtt = self.f32(32, "dtt")
        adt = self.f32(32, "adt")
        dex = self.f32(96, "dex")
        dtd = self.f32(32, "dtd")
        Xh = [self.f32(128, "Xh%d" % i) for i in range(4)]
        Ld = [self.bf16(128, "Ld%d" % i) for i in range(4)]
        Mt = [self.bf16(128, "Mt%d" % i) for i in range(4)]
        ytmp = [self.f32(256, "ytmp%d" % i, shape=(4, 64)) for i in range(2)]
        htmp = [self.f32(256, "htmp%d" % i, shape=(4, 64)) for i in range(2)]
        nchunk = T // 128
        it = 0
        for d in range(2):
            if d == 0:
                order = list(range(nchunk))
                mX, mE, mTE, NEG = GT_, LE_, GT_, NEGf
                mSeg = LE_
            else:
                order = [1, 0] + list(range(nchunk - 1, 1, -1))
                mX, mE, mTE, NEG = LT_, GE_, LT_, NEGb
                mSeg = GE_
            YO = YF if d == 0 else YB
            self.memset(V, H.ap, 0.0, [H])
            self.memset(V, Hb.ap, 0.0, [Hb])
            for c in order:
                s0 = c * 128
                xi, bi, cti, dri, yb = xsT[it % 2], BTt[it % 2], CTt[it % 2], dtr[it % 2], ybuf[it % 2]
                it += 1
                self.DMA("sync", xi.ap, XS.ap[:, s0:s0 + 128].rearrange("(t p) c -> p t c", p=128), r=[XS], w=[xi])
                self.DMA("sync", bi.ap, BC.ap[(d * 2) * 1024:(d * 2 + 1) * 1024, s0:s0 + 128].rearrange("(t p) c -> p t c", p=128), r=[BC], w=[bi])
                self.DMA("sync", cti.ap, BC.ap[(d * 2 + 1) * 1024:(d * 2 + 2) * 1024, s0:s0 + 128].rearrange("(t p) c -> p t c", p=128), r=[BC], w=[cti])
                self.DMA("sync", dri.ap[0:32, :], DTR.ap[d * 32:(d + 1) * 32, s0:s0 + 128], r=[DTR], w=[dri])
                psm = self.PS[6]
                self.tr(psm.ap[:, 0:32], dri.ap[0:32, :], self.ident.ap[0:32, 0:32], r=[dri, self.ident], w=[psm])
                self.tt(V, dtt.ap, psm.ap[:, 0:32], Brow[d].ap, ALU.add, r=[psm, Brow[d]], w=[dtt])
                self.act(dtt.ap, dtt.ap, AF.Exp, r=[dtt], w=[dtt])
                self.act(dtt.ap, dtt.ap, AF.Ln, r=[dtt, self.epsc], w=[dtt], bias=self.epsc.ap[:, 2:3])
                self.tt(V, adt.ap, dtt.ap, Arow[d].ap, ALU.mult, r=[dtt, Arow[d]], w=[adt])
                self.mm(psm.ap[:, 32:64], self.ones.ap, adt.ap, True, True, r=[self.ones, adt], w=[psm])
                self.mm(psm.ap[:, 64:96], mTE.ap, adt.ap, True, True, r=[mTE, adt], w=[psm])
                self.mm(psm.ap[:, 96:128], mE.ap, adt.ap, True, True, r=[mE, adt], w=[psm])
                self.act(dex.ap, psm.ap[:, 32:128], AF.Exp, r=[psm], w=[dex])
                self.tt(V, dtd.ap, dtt.ap, dex.ap[:, 32:64], ALU.mult, r=[dtt, dex], w=[dtd])
                for half in range(2):
                    pst = self.PS[4 + half]
                    psb16 = self.PSB[4 + half]
                    for q in range(8):
                        t_ = half * 8 + q
                        self.tr(psb16[:, q * 128:(q + 1) * 128], xi.ap[:, t_, :], self.identb.ap, r=[xi, self.identb], w=[pst])
                    src = psb16[:, 0:1024].rearrange("p (h c) -> p h c", c=64)
                    hs = slice(half * 16, (half + 1) * 16)
                    bc = lambda col: col.unsqueeze(2).to_broadcast([128, 16, 64])
                    self.tt(V, xdt.ap[:, hs, :], src, bc(dtt.ap[:, hs]), ALU.mult, r=[pst, dtt], w=[xdt])
                    self.tt(V, xdte.ap[:, hs, :], src, bc(dtd.ap[:, hs]), ALU.mult, r=[pst, dtd], w=[xdte])
                    if d == 0:
                        self.tt(V, dskt.ap[:, hs, :], src, bc(Drow.ap[:, hs]), ALU.mult, r=[pst, Drow], w=[dskt])
                pst = self.PS[7]
                psb16 = self.PSB[7]
                for g in range(8):
                    self.tr(psb16[:, g * 128:(g + 1) * 128], bi.ap[:, g, :], self.identb.ap, r=[bi, self.identb], w=[pst])
                self.cp("scalar", Btok.ap, psb16[:, 0:1024].rearrange("p (g c) -> p g c", c=128), r=[pst], w=[Btok])
                for g in range(8):
                    pcb = self.PS[g % 2]
                    self.mm(pcb.ap[:, 0:128], bi.ap[:, g, :], cti.ap[:, g, :], True, True, r=[bi, cti], w=[pcb])
                    py = self.PS[2 + g % 2]
                    for hh in range(4):
                        h = g * 4 + hh
                        xh, ld, mt = Xh[hh], Ld[hh], Mt[hh]
                        self.ts(G, xh.ap, mX.ap, adt.ap[:, h:h + 1], ALU.mult, 0.0, ALU.add, r=[mX, adt], w=[xh])
                    for hh in range(4):
                        h = g * 4 + hh
                        xh, ld, mt = Xh[hh], Ld[hh], Mt[hh]
                        pss = self.PS[4 + hh % 2]
                        sgap = pss.ap[:, 0:128]
                        self.mm(sgap, xh.ap, mSeg.ap, True, False, r=[xh, mSeg], w=[pss])
                        self.mm(sgap, self.ident.ap, NEG.ap, False, True, r=[self.ident, NEG], w=[pss])
                        self.act(ld.ap, sgap, AF.Exp, r=[pss], w=[ld])
                        self.tt(V, mt.ap, pcb.ap[:, 0:128], ld.ap, ALU.mult, r=[pcb, ld], w=[mt])
                        self.mm(py.ap[:, hh * 64:(hh + 1) * 64], mt.ap, xdt.ap[:, h, :], True, True, r=[mt, xdt], w=[py])
                        self.mm(py.ap[:, 256 + hh * 64:256 + (hh + 1) * 64], cti.ap[:, g, :], Hb.ap[:, h * 64:(h + 1) * 64], True, True, r=[cti, Hb], w=[py])
                    gs = slice(g * 4, (g + 1) * 4)
                    yt = ytmp[g % 2]
                    Eb = dex.ap[:, 64 + g * 4:64 + (g + 1) * 4].unsqueeze(2).to_broadcast([128, 4, 64])
                    self.tt(V, yt.ap, py.ap[:, 256:512].rearrange("p (h c) -> p h c", c=64), Eb, ALU.mult, r=[py, dex], w=[yt])
                    if d == 0:
                        self.tt(G, yt.ap, yt.ap, dskt.ap[:, gs, :], ALU.add, r=[yt, dskt], w=[yt])
                    self.tt(V, yb.ap[:, g * 256:(g + 1) * 256].rearrange("p (h c) -> p h c", c=64), py.ap[:, 0:256].rearrange("p (h c) -> p h c", c=64), yt.ap, ALU.add, r=[py, yt], w=[yb])
                    pst2 = self.PS[7]
                    self.mm(pst2.ap[:, 0:256], Btok.ap[:, g, :], xdte.ap[:, gs, :].rearrange("p h c -> p (h c)"), True, True, r=[Btok, xdte], w=[pst2])
                    ht = htmp[g % 2]
                    Db = dex.ap[:, g * 4:(g + 1) * 4].unsqueeze(2).to_broadcast([128, 4, 64])
                    self.tt(G, ht.ap, H.ap[:, gs, :], Db, ALU.mult, r=[H, dex], w=[ht])
                    self.tt(V, H.ap[:, gs, :], pst2.ap[:, 0:256].rearrange("p (h c) -> p h c", c=64), ht.ap, ALU.add, r=[pst2, ht], w=[H])
                    self.cp("scalar", Hb.ap[:, g * 256:(g + 1) * 256], H.ap[:, gs, :].rearrange("p h c -> p (h c)"), r=[H], w=[Hb])
                self.DMA("sync", YO.ap[s0:s0 + 128, :], yb.ap, r=[yb], w=[YO])

    def ssd_phase_c(self, XT, P, SZ, YF, YB, l):
        self.new_phase()
        V, G = "vector", "gpsimd"
        wo = self.bf16(16 * D, "wo", shape=(16, D))
        self.load_w_bf16(wo, P["w_out"].ap[0], P["w_out"], 16)
        nrow = self.f32(2048, "nrow")
        rowtmp = self.f32(2048, "rowtmp2")
        self.row_bcast(nrow, P["norm"].ap[0:1, :], P["norm"], 2048, rowtmp)
        yf = [self.bf16(2048, "cyf%d" % i) for i in range(2)]
        ybk = [self.bf16(2048, "cyb%d" % i) for i in range(2)]
        sz = [self.bf16(2048, "csz%d" % i) for i in range(2)]
        yy = [self.f32(2048, "cyy%d" % i) for i in range(2)]
        sqj = self.f32(2048, "csq")
        gnb = [self.bf16(2048, "cgn%d" % i) for i in range(2)]
        ss = [self.f32(8, "css%d" % i) for i in range(2)]
        gnT = [self.bf16(16 * 512, "gnT%d" % i, shape=(16, 512)) for i in range(2)]
        xb = [self.f32(NFT * 512, "cx%d" % i, shape=(NFT, 512)) for i in range(2)]
        mod = self.mod[l]
        ti = 0
        for ci, (s, w, isctx) in enumerate(self.chunks()):
            cs = 1 if isctx else 0
            gT, xbi = gnT[ci % 2], xb[ci % 2]
            self.DMA("sync", xbi.ap[:, :, 0:w], XT.ap[:, s:s + w].rearrange("(ft p) t -> p ft t", p=128), r=[XT], w=[xbi])
            for tt in range(w // 128):
                t0 = s + tt * 128
                a, b, z_, y_, g_, s_ = yf[ti % 2], ybk[ti % 2], sz[ti % 2], yy[ti % 2], gnb[ti % 2], ss[ti % 2]
                ti += 1
                self.DMA("sync", a.ap, YF.ap[t0:t0 + 128, :], r=[YF], w=[a])
                self.DMA("sync", b.ap, YB.ap[t0:t0 + 128, :], r=[YB], w=[b])
                self.DMA("sync", z_.ap, SZ.ap[t0:t0 + 128, :], r=[SZ], w=[z_])
                self.tt(G, y_.ap, a.ap, b.ap, ALU.add, r=[a, b], w=[y_])
                self.tt(V, y_.ap, y_.ap, z_.ap, ALU.mult, r=[y_, z_], w=[y_])
                self.act(sqj.ap, y_.ap, AF.Square, r=[y_], w=[sqj, s_], accum=s_.ap[:, 0:1])
                self.act(s_.ap[:, 1:2], s_.ap[:, 0:1], AF.Sqrt, r=[s_, self.epsc], w=[s_], scale=1.0 / 2048, bias=self.epsc.ap[:, 0:1])
                self.E(V, lambda e, o=s_.ap[:, 2:3], i=s_.ap[:, 1:2]: e.reciprocal(out=o, in_=i), r=[s_], w=[s_])
                self.stt(g_.ap, y_.ap, s_.ap[:, 2:3], nrow.ap, ALU.mult, ALU.mult, r=[y_, s_, nrow], w=[g_])
                for half in range(2):
                    pst = self.PS[4 + (ti * 2 + half) % 4]
                    psb16 = self.PSB[4 + (ti * 2 + half) % 4]
                    for q in range(8):
                        kt = half * 8 + q
                        self.tr(psb16[:, q * 128:(q + 1) * 128], g_.ap[:, kt * 128:(kt + 1) * 128], self.identb.ap, r=[g_, self.identb], w=[pst])
                    self.cp("scalar" if half else "vector", gT.ap[:, half * 8:(half + 1) * 8, tt * 128:(tt + 1) * 128],
                            psb16[:, 0:1024].rearrange("p (k c) -> p k c", c=128), r=[pst], w=[gT])
            for fo in range(NFT):
                po = self.PS[fo % 4]
                for kt in range(16):
                    self.mm(po.ap[:, 0:w], wo.ap[:, kt, fo * 128:(fo + 1) * 128], gT.ap[:, kt, 0:w], kt == 0, kt == 15, r=[wo, gT], w=[po])
                self.stt(xbi.ap[:, fo, 0:w], po.ap[:, 0:w], mod.ap[:, 2, fo, cs:cs + 1], xbi.ap[:, fo, 0:w], ALU.mult, ALU.add, r=[po, mod, xbi], w=[xbi])
            self.DMA("sync", XT.ap[:, s:s + w].rearrange("(ft p) t -> p ft t", p=128), xbi.ap[:, :, 0:w], r=[xbi], w=[XT])


    def na_bias_table(self, rpb, BTD):
        self.new_phase()
        neg = self.f32(7680, "negt")
        self.memset("vector", neg.ap, -30000.0, [neg])
        self.DMA("sync", BTD.ap.rearrange("h w r c -> (h w r c)").rearrange("(p n) -> p n", p=128), neg.ap, r=[neg], w=[BTD])
        HS = 64 * 15 * 64
        for h in range(16):
            so = h * 465
            do = h * HS
            dst = bass.AP(BTD.ap.tensor, do + 8 * 960, [[64, 15], [961, 49], [1, 16]])
            src = bass.AP(rpb.ap.tensor, so + 7, [[31, 15], [0, 49], [1, 16]])
            self.DMA("sync", dst, src, r=[rpb], w=[BTD], slow=True)
            dst = bass.AP(BTD.ap.tensor, do, [[64, 15], [960, 8], [1, 16]])
            src = bass.AP(rpb.ap.tensor, so + 15, [[31, 15], [-1, 8], [1, 16]])
            self.DMA("sync", dst, src, r=[rpb], w=[BTD], slow=True)
            dst = bass.AP(BTD.ap.tensor, do + 57 * 960 + 48, [[64, 15], [960, 7], [1, 16]])
            src = bass.AP(rpb.ap.tensor, so + 6, [[31, 15], [-1, 7], [1, 16]])
            self.DMA("sync", dst, src, r=[rpb], w=[BTD], slow=True)

    def na_phase_a(self, HT, wqkv, QT, KT, VT):
        self.new_phase()
        Hres = self.bf16(8 * T, "Hres", shape=(8, T))
        for kt in range(8):
            self.DMA("sync", Hres.ap[:, kt, :], HT.ap[kt * 128:(kt + 1) * 128, :], r=[HT], w=[Hres])
        wv = self.bf16(8 * 1024, "wv", shape=(8, 1024))
        self.load_w_bf16(wv, wqkv.ap[0], wqkv, 8, 2048, 3072)
        vb = [self.bf16(1024, "vb%d" % i) for i in range(2)]
        for tt in range(T // 128):
            v_ = vb[tt % 2]
            for vc in range(2):
                ps = self.PS[vc + 2 * (tt % 2)]
                for kt in range(8):
                    self.mm(ps.ap, Hres.ap[:, kt, tt * 128:(tt + 1) * 128], wv.ap[:, kt, vc * 512:(vc + 1) * 512], kt == 0, kt == 7, r=[Hres, wv], w=[ps])
                self.cp("scalar" if vc else "vector", v_.ap[:, vc * 512:(vc + 1) * 512], ps.ap, r=[ps], w=[v_])
            self.DMA("sync", VT.ap[tt * 128:(tt + 1) * 128, :], v_.ap, r=[v_], w=[VT])
        wch = [self.bf16(8 * 512, "wq%d" % i, shape=(8, 512)) for i in range(2)]
        ob = [self.bf16(T, "qo%d" % i) for i in range(2)]
        chunks = self.chunks()
        pi = 0
        for wc in range(4):
            wb = wch[wc % 2]
            self.load_w_bf16(wb, wqkv.ap[0], wqkv, 8, wc * 512, (wc + 1) * 512)
            for q in range(4):
                ot = wc * 4 + q
                obi = ob[ot % 2]
                for ci, (s, w, isctx) in enumerate(chunks):
                    ps = self.PS[4 + pi % 4]
                    pi += 1
                    for kt in range(8):
                        self.mm(ps.ap[:, 0:w], wb.ap[:, kt, q * 128:(q + 1) * 128], Hres.ap[:, kt, s:s + w], kt == 0, kt == 7, r=[wb, Hres], w=[ps])
                    if ot < 8:
                        self.act(obi.ap[:, s:s + w], ps.ap[:, 0:w], AF.Copy, r=[ps], w=[obi], scale=0.125)
                    else:
                        self.cp("vector", obi.ap[:, s:s + w], ps.ap[:, 0:w], r=[ps], w=[obi])
                dst = QT.ap[ot * 128:(ot + 1) * 128, :] if ot < 8 else KT.ap[(ot - 8) * 128:(ot - 7) * 128, :]
                self.DMA("sync", dst, obi.ap, r=[obi], w=[QT if ot < 8 else KT])

    def na_phase_b(self, QT, KT, VT, BTD, YT):
        self.new_phase()
        V, G = "vector", "gpsimd"
        Qp = self.bf16(T, "Qp")
        Kp = self.bf16(T, "Kp")
        Ve = self.bf16(34 * 128, "Ve", shape=(34, 128))
        Vo = self.bf16(33 * 128, "Vo", shape=(33, 128))
        BTt = self.f32(960, "BTt")
        YTp = self.bf16(T, "YTp")
        Qbd = [self.bf16(128, "Qbd%d" % i) for i in range(2)]
        for q_ in Qbd:
            self.memset(G, q_.ap, 0.0, [q_])
        sc = [self.f32(768, "sc%d" % i) for i in range(2)]
        pe = [self.bf16(768, "pe%d" % i) for i in range(2)]
        pn = [self.bf16(768, "pn%d" % i) for i in range(2)]
        pT = [self.bf16(768, "pT%d" % i, shape=(6, 128)) for i in range(2)]
        st = [self.f32(8, "st%d" % i) for i in range(2)]
        blocks = [("c", i) for i in range(4)] + [("l", r) for r in range(64)]
        it = 0
        for hp in range(8):
            self.DMA("sync", Qp.ap, QT.ap[hp * 128:(hp + 1) * 128, :], r=[QT], w=[Qp])
            self.DMA("sync", Kp.ap, KT.ap[hp * 128:(hp + 1) * 128, :], r=[KT], w=[Kp])
            self.DMA("sync", Ve.ap, VT.ap[:, hp * 128:(hp + 1) * 128].rearrange("(tt p) c -> p tt c", p=128), r=[VT], w=[Ve])
            self.DMA("sync", Vo.ap, VT.ap[64:64 + 33 * 128, hp * 128:(hp + 1) * 128].rearrange("(tt p) c -> p tt c", p=128), r=[VT], w=[Vo])
            self.DMA("sync", BTt.ap, BTD.ap[2 * hp:2 * hp + 2].rearrange("two w r c -> (two w) (r c)"), r=[BTD], w=[BTt])
            for (kind, idx) in blocks:
                qb, sci, pei, pni, pTi, sti = Qbd[it % 2], sc[it % 2], pe[it % 2], pn[it % 2], pT[it % 2], st[it % 2]
                ps_l, ps_c = self.PS[(it % 2) * 2], self.PS[(it % 2) * 2 + 1]
                pst, pstb = self.PS[4 + it % 2], self.PSB[4 + it % 2]
                pso = self.PS[6 + it % 2]
                it += 1
                if kind == "c":
                    qpos = idx * 64
                    nk = 256
                else:
                    r = idx
                    qpos = LC + r * 64
                    start = min(max(r - 4, 0), 56)
                    ro0 = start - r + 7
                    kpos = LC + start * 64
                    nk = 768
                self.cp(G, qb.ap[0:64, 0:64], Qp.ap[0:64, qpos:qpos + 64], r=[Qp], w=[qb])
                self.cp(G, qb.ap[64:128, 64:128], Qp.ap[64:128, qpos:qpos + 64], r=[Qp], w=[qb])
                if kind == "l":
                    self.mm(ps_l.ap, qb.ap, Kp.ap[:, kpos:kpos + 512], True, True, r=[qb, Kp], w=[ps_l])
                    self.mm(ps_c.ap[:, 0:256], qb.ap, Kp.ap[:, 0:256], True, True, r=[qb, Kp], w=[ps_c])
                    self.tt(V, sci.ap[:, 0:512], ps_l.ap, BTt.ap[:, ro0 * 64:ro0 * 64 + 512], ALU.add, r=[ps_l, BTt], w=[sci])
                    self.cp("scalar", sci.ap[:, 512:768], ps_c.ap[:, 0:256], r=[ps_c], w=[sci])
                else:
                    self.mm(ps_c.ap[:, 0:256], qb.ap, Kp.ap[:, 0:256], True, True, r=[qb, Kp], w=[ps_c])
                    self.cp("scalar", sci.ap[:, 0:256], ps_c.ap[:, 0:256], r=[ps_c], w=[sci])
                self.E(V, lambda e, o=sti.ap[:, 0:1], i=sci.ap[:, 0:nk]: e.reduce_max(out=o, in_=i, axis=AX.X), r=[sci], w=[sti])
                self.ts(V, sti.ap[:, 1:2], sti.ap[:, 0:1], -1.0, ALU.mult, r=[sti], w=[sti])
                self.act(pei.ap[:, 0:nk], sci.ap[:, 0:nk], AF.Exp, r=[sci, sti], w=[pei, sti], bias=sti.ap[:, 1:2], accum=sti.ap[:, 2:3])
                self.E(V, lambda e, o=sti.ap[:, 3:4], i=sti.ap[:, 2:3]: e.reciprocal(out=o, in_=i), r=[sti], w=[sti])
                self.ts(G, pni.ap[:, 0:nk], pei.ap[:, 0:nk], sti.ap[:, 3:4], ALU.mult, 0.0, ALU.add, r=[pei, sti], w=[pni])
                nkt = nk // 128
                for kt in range(nkt):
                    self.tr(pstb[:, kt * 128:(kt + 1) * 128], pni.ap[:, kt * 128:(kt + 1) * 128], self.identb.ap, r=[pni, self.identb], w=[pst])
                self.cp("scalar", pTi.ap[:, 0:nkt, :], pstb[:, 0:nk].rearrange("p (k c) -> p k c", c=128), r=[pst], w=[pTi])
                for kt in range(nkt):
                    if kind == "c":
                        vt = Ve.ap[:, kt, :]
                    elif kt >= 4:
                        vt = Ve.ap[:, kt - 4, :]
                    else:
                        tok0 = kpos + kt * 128
                        vt = Ve.ap[:, tok0 // 128, :] if tok0 % 128 == 0 else Vo.ap[:, (tok0 - 64) // 128, :]
                    self.mm(pso.ap[:, 0:128], vt, pTi.ap[:, kt, :], kt == 0, kt == nkt - 1, r=[Ve, Vo, pTi], w=[pso])
                self.cp(V, YTp.ap[0:64, qpos:qpos + 64], pso.ap[0:64, 0:64], r=[pso], w=[YTp])
                self.cp("scalar", YTp.ap[64:128, qpos:qpos + 64], pso.ap[64:128, 64:128], r=[pso], w=[YTp])
            self.DMA("sync", YT.ap[hp * 128:(hp + 1) * 128, :], YTp.ap, r=[YTp], w=[YT])


def _build(cfg):
    nc = bass.Bass("TRN2", target_bir_lowering=False)
    kb = KB(nc, cfg)
    IN = lambda n, s: kb.dram_t(n, s, F32, kind="ExternalInput")
    x_in = IN("x", [LL, D])
    ctx_in = IN("ctx", [LC, D])
    c_in = IN("c", [D])
    cctx_in = IN("c_ctx", [D])
    ada_w = IN("ada_w", [DEPTH, D, 6 * D])
    ada_b = IN("ada_b", [DEPTH, 6 * D])
    norm_mix = IN("norm_mix", [DEPTH, D])
    norm_ffn = IN("norm_ffn", [DEPTH, D])
    norm_final = IN("norm_final", [D])
    w1 = IN("ffn_w1", [DEPTH, D, FH])
    w3 = IN("ffn_w3", [DEPTH, D, FH])
    w2 = IN("ffn_w2", [DEPTH, FH, D])
    S5 = {
        "lam_re": IN("s5_lam_re", [2, 2, 64, 64]), "lam_im": IN("s5_lam_im", [2, 2, 64, 64]),
        "log_step": IN("s5_log_step", [2, 2, 64]),
        "b_re": IN("s5_b_re", [2, 2, 64, 64, 16]), "b_im": IN("s5_b_im", [2, 2, 64, 64, 16]),
        "c_re": IN("s5_c_re", [2, 2, 64, 16, 64]), "c_im": IN("s5_c_im", [2, 2, 64, 16, 64]),
        "d": IN("s5_d", [2, D]), "w_glu": IN("s5_w_glu", [2, D, D]), "b_glu": IN("s5_b_glu", [2, D]),
    }
    SSD = {
        "w_in": IN("ssd_w_in", [1, D, 8256]), "conv_w": IN("ssd_conv_w", [1, 5, 6144]), "conv_b": IN("ssd_conv_b", [1, 6144]),
        "dt_bias": IN("ssd_dt_bias", [1, 2, 32]), "a_log": IN("ssd_a_log", [1, 2, 32]), "d": IN("ssd_d", [1, 32]),
        "norm": IN("ssd_norm", [1, 2048]), "w_out": IN("ssd_w_out", [1, 2048, D]),
    }
    NA = {"w_qkv": IN("na_w_qkv", [1, D, 3 * D]), "w_o": IN("na_w_o", [1, D, D]), "rpb": IN("na_rpb", [1, 16, 15, 31])}
    out_t = kb.dram_t("out", [LL, D], F32, kind="ExternalOutput")
    QT = kb.dram_t("QT", [D, T], BF16)
    KT = kb.dram_t("KT", [D, T], BF16)
    VT = kb.dram_t("VT", [T, D], BF16)
    YT = kb.dram_t("YT", [D, T], BF16)
    BTD = kb.dram_t("BTD", [16, 64, 15, 64], F32)
    SZ = kb.dram_t("SZ", [T, 2048], BF16)
    XS = kb.dram_t("XS", [2048, T], BF16)
    BC = kb.dram_t("BC", [4096, T], BF16)
    DTR = kb.dram_t("DTR", [64, T], F32)
    YF = kb.dram_t("YF", [T, 2048], BF16)
    YB = kb.dram_t("YB", [T, 2048], BF16)
    XT = kb.dram_t("XT", [D, T], F32)
    HT = kb.dram_t("HT", [D, T], BF16)
    GT = kb.dram_t("GT", [D, T], BF16)
    layers = cfg.get("layers", list(range(DEPTH)))
    kb.consts()
    kb.build_mask8()
    kb.adaln(c_in, cctx_in, ada_w, ada_b, norm_mix, norm_ffn, norm_final, layers)
    kb.prologue_transpose(x_in, ctx_in, XT)
    for l in layers:
        kind, j = l % 3, l // 3
        if cfg.get("mixer", True):
            kb.norm_layer(XT, HT, l, 0)
            if kind == 0:
                kb.s5_phase(HT, GT, S5, j)
                kb.proj_phase(XT, GT, S5["w_glu"], S5["w_glu"].ap[j], 8, l, glu_bias=(S5["b_glu"], S5["b_glu"].ap[j]))
            elif kind == 1:
                kb.ssd_phase_a(HT, SSD, SZ, XS, BC, DTR)
                kb.ssd_phase_b(SSD, XS, BC, DTR, YF, YB)
                kb.ssd_phase_c(XT, SSD, SZ, YF, YB, l)
            else:
                kb.na_bias_table(NA["rpb"], BTD)
                kb.na_phase_a(HT, NA["w_qkv"], QT, KT, VT)
                kb.na_phase_b(QT, KT, VT, BTD, YT)
                kb.proj_phase(XT, YT, NA["w_o"], NA["w_o"].ap[0], 8, l)
        if cfg.get("ffn", True):
            kb.norm_layer(XT, HT, l, 1)
            kb.ffn_phase(XT, HT, w1, w3, w2, l)
    kb.final_phase(XT, out_t)
    kb.p.fence("sync", kb.out_ops)
    kb.p.build()
    kb.es.close()
    return nc, kb


INPUT_NAMES = ["x", "ctx", "c", "c_ctx", "ada_w", "ada_b", "norm_mix", "norm_ffn", "norm_final", "ffn_w1", "ffn_w3", "ffn_w2",
               "s5_lam_re", "s5_lam_im", "s5_log_step", "s5_b_re", "s5_b_im", "s5_c_re", "s5_c_im", "s5_d", "s5_w_glu", "s5_b_glu",
               "ssd_w_in", "ssd_conv_w", "ssd_conv_b", "ssd_dt_bias", "ssd_a_log", "ssd_d", "ssd_norm", "ssd_w_out",
               "na_w_qkv", "na_w_o", "na_rpb"]


def kernel(**inputs):
    cfg = {}
    nc, kb = _build(cfg)
    n = 8
    in_maps = []
    for b in range(n):
        m = {}
        for k in INPUT_NAMES:
            v = np.ascontiguousarray(inputs[k], dtype=np.float32)
            if k in ("x", "ctx", "c"):
                v = np.ascontiguousarray(v[b])
            m[k] = v
        in_maps.append(m)
    res = run_bass_kernel_spmd(nc, in_maps, core_ids=list(range(n)))
    return np.stack([np.asarray(r["out"], dtype=np.float32) for r in res.results], axis=0)
```

```python
import numpy as np
from contextlib import ExitStack
import concourse.bass as bass
import concourse.mybir as mybir
from concourse.bass_utils import run_bass_kernel_spmd

F32 = mybir.dt.float32
BF16 = mybir.dt.bfloat16
I32 = mybir.dt.int32
AF = mybir.ActivationFunctionType
ALU = mybir.AluOpType
AX = mybir.AxisListType

ENGS = ("sync", "gpsimd", "scalar", "vector", "tensor")
NDMA_SEM = 24
SEM_EPOCH = 12000

D = 1024
LC = 256
LL = 4096
T = LC + LL
NFT = 8
FH = 2816
NHT = 22
DEPTH = 4
EPS = 1e-6
ARENA_F32 = 46 * 1024


class Buf:
    __slots__ = ("name", "w", "r", "psum")

    def __init__(self, name, psum=False):
        self.name = name
        self.w = None
        self.r = []
        self.psum = psum


class Op:
    __slots__ = ("eng", "fn", "idx", "deps", "need_inc", "val", "is_dma", "semi", "is_barrier")

    def __init__(self, eng, fn, is_dma):
        self.eng = eng
        self.fn = fn
        self.is_dma = is_dma
        self.deps = []
        self.need_inc = False
        self.val = 0
        self.semi = -1
        self.idx = -1
        self.is_barrier = False


class Prog:
    def __init__(self, nc):
        self.nc = nc
        self.ops = {e: [] for e in ENGS}
        self.nreal = {e: 0 for e in ENGS}
        self.es = ExitStack()
        self.engsem = {e: self.es.enter_context(nc.semaphore("s_" + e)) for e in ENGS}
        self.dmasem = [self.es.enter_context(nc.semaphore("d%d" % i)) for i in range(NDMA_SEM)]
        self.dma_last = [None] * NDMA_SEM
        self.dma_cnt = [0] * NDMA_SEM
        self.dma_rr = 0

    def emit(self, eng, fn, reads=(), writes=(), dma=False):
        op = Op(eng, fn, dma)
        op.idx = self.nreal[eng]
        self.nreal[eng] += 1
        deps = []
        for b in reads:
            if b.w is not None:
                deps.append(b.w)
            if b.psum:
                deps.extend(b.r)
        for b in writes:
            if b.w is not None:
                deps.append(b.w)
            deps.extend(b.r)
        if dma:
            k = self.dma_rr
            self.dma_rr = (k + 1) % NDMA_SEM
            if self.dma_last[k] is not None:
                deps.append(self.dma_last[k])
            self.dma_last[k] = op
            self.dma_cnt[k] += 16
            op.semi = k
            op.val = self.dma_cnt[k]
        seen = set()
        for d in deps:
            if d is op or id(d) in seen:
                continue
            seen.add(id(d))
            op.deps.append(d)
        for b in reads:
            if b.psum:
                b.w = op
                b.r = []
            else:
                b.r.append(op)
        for b in writes:
            b.w = op
            b.r = []
        self.ops[eng].append(op)
        return op

    def fence(self, eng, deps):
        op = Op(eng, None, False)
        op.idx = self.nreal[eng]
        op.deps = list(deps)
        self.ops[eng].append(op)
        return op

    def barrier(self):
        lasts = []
        for e in ENGS:
            for o in reversed(self.ops[e]):
                if o.fn is not None:
                    lasts.append(o)
                    break
        for o in self.dma_last:
            if o is not None:
                lasts.append(o)
        for e in ENGS:
            self.fence(e, lasts).is_barrier = True

    def _needs_wait(self, op, d):
        if d.is_dma:
            return True
        if d.eng != op.eng:
            return True
        if op.eng == "tensor":
            return False
        return (op.idx - d.idx) <= 2

    def build(self):
        nc = self.nc
        for e in ENGS:
            for op in self.ops[e]:
                for d in op.deps:
                    if not d.is_dma and self._needs_wait(op, d):
                        d.need_inc = True
        self.epoch_sems = {e: [self.engsem[e]] for e in ENGS}
        for e in ENGS:
            c = 0
            ep = 0
            for op in self.ops[e]:
                if op.fn is None:
                    if op.is_barrier and c > SEM_EPOCH:
                        ep += 1
                        c = 0
                        self.epoch_sems[e].append(self.es.enter_context(nc.semaphore("s_%s_%d" % (e, ep))))
                    continue
                if op.is_dma:
                    continue
                op.semi = ep
                if op.need_inc:
                    c += 1
                    op.val = c
        self.counts = {e: 0 for e in ENGS}

        def mk_body(e):
            def body(eng):
                waited = {}
                for op in self.ops[e]:
                    for d in op.deps:
                        if not self._needs_wait(op, d):
                            continue
                        if d.is_dma:
                            key = ("d", d.semi)
                            sem = self.dmasem[d.semi]
                        else:
                            key = ("e", d.eng, d.semi)
                            sem = self.epoch_sems[d.eng][d.semi]
                        if waited.get(key, 0) >= d.val:
                            continue
                        waited[key] = d.val
                        eng.wait_ge(sem, d.val)
                    if op.fn is None:
                        continue
                    inst = op.fn(eng)
                    self.counts[e] += 1
                    if op.is_dma:
                        inst.then_inc(self.dmasem[op.semi], 16)
                    elif op.need_inc:
                        inst.then_inc(self.epoch_sems[e][op.semi], 1)
            return body

        with nc.Block() as block:
            for e in ENGS:
                if self.ops[e]:
                    getattr(block, e)(mk_body(e))
        self.es.close()


class Tl:
    __slots__ = ("ap", "buf")

    def __init__(self, ap, buf):
        self.ap = ap
        self.buf = buf


class KB:
    def __init__(self, nc, cfg):
        self.nc = nc
        self.cfg = cfg
        self.p = Prog(nc)
        self.es = ExitStack()
        self.arena = self.es.enter_context(nc.sbuf_tensor("arena", [128, ARENA_F32], F32))
        self.arena_bf = self.arena.bitcast(BF16)
        self.psum = [self.es.enter_context(nc.psum_tensor("ps%d" % i, [128, 512], F32)) for i in range(8)]
        self.PS = [Tl(self.psum[i][:], Buf("ps%d" % i, psum=True)) for i in range(8)]
        self.PSB = [self.psum[i].bitcast(BF16) for i in range(8)]
        self.top = 0
        self.ptr = 0
        self.nb = 0
        self.dram = {}
        self.out_ops = []

    def _al(self, n_f32, persistent):
        n_f32 = (n_f32 + 7) // 8 * 8
        if persistent:
            assert self.ptr == self.top, "persistent alloc only between phases"
            off = self.top
            self.top += n_f32
            self.ptr = self.top
        else:
            off = self.ptr
            self.ptr += n_f32
        assert self.ptr <= ARENA_F32, "arena overflow %d" % self.ptr
        return off

    def f32(self, n, name=None, persistent=False, shape=None):
        off = self._al(n, persistent)
        ap = self.arena[:, off:off + n]
        if shape is not None:
            ap = self._reshape(ap, shape)
        self.nb += 1
        return Tl(ap, Buf(name or "t%d" % self.nb))

    def bf16(self, n, name=None, persistent=False, shape=None):
        off = self._al((n + 1) // 2, persistent)
        ap = self.arena_bf[:, 2 * off:2 * off + n]
        if shape is not None:
            ap = self._reshape(ap, shape)
        self.nb += 1
        return Tl(ap, Buf(name or "t%d" % self.nb))

    def i32(self, n, name=None):
        off = self._al(n, False)
        ap = self.arena.bitcast(I32)[:, off:off + n]
        self.nb += 1
        return Tl(ap, Buf(name or "t%d" % self.nb))

    @staticmethod
    def _reshape(ap, shape):
        if len(shape) == 2:
            return ap.rearrange("p (a b) -> p a b", b=shape[1])
        if len(shape) == 3:
            return ap.rearrange("p (a b c) -> p a b c", b=shape[1], c=shape[2])
        raise ValueError

    def new_phase(self):
        self.p.barrier()
        self.ptr = self.top

    def dram_t(self, name, shape, dt, kind="Internal"):
        t = self.nc.dram_tensor(name, shape, dt, kind=kind)
        tl = Tl(t.ap(), Buf(name))
        self.dram[name] = tl
        return tl

    def E(self, eng, fn, r=(), w=()):
        return self.p.emit(eng, fn, [t.buf for t in r], [t.buf for t in w])

    def DMA(self, eng, out_ap, in_ap, r=(), w=(), slow=False):
        if slow:
            fn = lambda e: e.dma_start(out=out_ap, in_=in_ap, allow_slow_non_contiguous=True)
        else:
            fn = lambda e: e.dma_start(out=out_ap, in_=in_ap)
        return self.p.emit(eng, fn, [t.buf for t in r], [t.buf for t in w], dma=True)

    def mm(self, ps_ap, lhsT, rhs, start, stop, r=(), w=()):
        return self.E("tensor", lambda e: e.matmul(ps_ap, lhsT=lhsT, rhs=rhs, start=start, stop=stop), r, w)

    def tr(self, ps_ap, in_ap, ident_ap, r=(), w=()):
        return self.E("tensor", lambda e: e.transpose(out=ps_ap, in_=in_ap, identity=ident_ap), r, w)

    def act(self, out, in_, func, r=(), w=(), scale=None, bias=None, accum=None):
        kw = {}
        if scale is not None:
            kw["scale"] = scale
        if bias is not None:
            kw["bias"] = bias
        if accum is not None:
            kw["accum_out"] = accum
        return self.E("scalar", lambda e: e.activation(out=out, in_=in_, func=func, **kw), r, w)

    def tt(self, eng, out, in0, in1, op, r=(), w=()):
        return self.E(eng, lambda e: e.tensor_tensor(out=out, in0=in0, in1=in1, op=op), r, w)

    def ts(self, eng, out, in0, s1, op0, s2=None, op1=None, r=(), w=(), accum=None):
        if op1 is None:
            return self.E(eng, lambda e: e.tensor_scalar(out=out, in0=in0, scalar1=s1, scalar2=None, op0=op0), r, w)
        if accum is not None:
            return self.E(eng, lambda e: e.tensor_scalar(out=out, in0=in0, scalar1=s1, scalar2=s2, op0=op0, op1=op1, accum_out=accum), r, w)
        return self.E(eng, lambda e: e.tensor_scalar(out=out, in0=in0, scalar1=s1, scalar2=s2, op0=op0, op1=op1), r, w)

    def stt(self, out, in0, scalar, in1, op0, op1, r=(), w=()):
        return self.E("vector", lambda e: e.scalar_tensor_tensor(out=out, in0=in0, scalar=scalar, in1=in1, op0=op0, op1=op1), r, w)

    def cp(self, eng, out, in_, r=(), w=()):
        if eng == "scalar":
            return self.act(out, in_, AF.Copy, r, w)
        return self.E(eng, lambda e: e.tensor_copy(out=out, in_=in_), r, w)

    def memset(self, eng, ap, val, w=()):
        return self.E(eng, lambda e: e.memset(ap, val), (), w)

    def consts(self):
        self.ident = self.f32(128, "ident", True)
        self.identb = self.bf16(128, "identb", True)
        self.ones = self.f32(128, "ones", True)
        self.epsc = self.f32(8, "epsc", True)
        self.memset("gpsimd", self.ident.ap, 0.0, [self.ident])
        idap = self.ident.ap
        self.E("gpsimd", lambda e: e.affine_select(out=idap, in_=idap, pattern=[[-1, 128]], compare_op=ALU.not_equal,
                                                   fill=1.0, base=0, channel_multiplier=1), [self.ident], [self.ident])
        self.cp("vector", self.identb.ap, self.ident.ap, [self.ident], [self.identb])
        self.memset("vector", self.ones.ap, 1.0, [self.ones])
        self.memset("vector", self.epsc.ap[:, 0:1], EPS, [self.epsc])
        self.memset("vector", self.epsc.ap[:, 1:2], 0.0, [self.epsc])
        self.memset("vector", self.epsc.ap[:, 2:3], 1.0, [self.epsc])

    @staticmethod
    def chunks(w_lat=512):
        ch = [(0, LC, True)]
        for s in range(LC, T, w_lat):
            ch.append((s, w_lat, False))
        return ch

    def prologue_transpose(self, x_in, ctx_in, XT):
        self.new_phase()
        xin = [self.f32(4 * D, "xin%d" % i, shape=(4, D)) for i in range(2)]
        stage = [self.f32(NFT * 512, "stg%d" % i, shape=(NFT, 512)) for i in range(2)]
        for ci, (s, w, isctx) in enumerate(self.chunks()):
            xi = xin[ci % 2]
            st = stage[ci % 2]
            ntt = w // 128
            if isctx:
                src = ctx_in.ap.rearrange("(tt p) f -> p tt f", p=128)
            else:
                src = x_in.ap[s - LC:s - LC + w, :].rearrange("(tt p) f -> p tt f", p=128)
            self.DMA("sync", xi.ap[:, 0:ntt, :], src, r=[x_in], w=[xi])
            for ft in range(NFT):
                ps = self.PS[ft]
                for tt in range(ntt):
                    self.tr(ps.ap[:, tt * 128:(tt + 1) * 128], xi.ap[:, tt, ft * 128:(ft + 1) * 128], self.ident.ap,
                            r=[xi, self.ident], w=[ps])
                self.cp("scalar" if ft % 2 else "vector", st.ap[:, ft, 0:w], ps.ap[:, 0:w], r=[ps], w=[st])
            dst = XT.ap[:, s:s + w].rearrange("(ft p) t -> p ft t", p=128)
            self.DMA("sync", dst, st.ap[:, :, 0:w], r=[st], w=[XT])

    def adaln(self, c_in, cctx_in, ada_w, ada_b, norm_mix, norm_ffn, norm_final, layers):
        self.mod = {}
        for l in layers:
            self.mod[l] = self.f32(96, "mod%d" % l, True, shape=(6, 8, 2))
        self.nw = self.f32(9 * 8, "nw", True, shape=(9, 8))
        self.AB = {}
        for l in layers:
            self.AB[l] = self.f32(4 * 16, "AB%d" % l, True, shape=(4, 8, 2))
        self.new_phase()
        sT = self.f32(16, "sT", shape=(8, 2))
        craw = self.f32(16, "craw", shape=(8, 2))
        self.DMA("sync", craw.ap[:, :, 0], c_in.ap.rearrange("(kt p) -> p kt", p=128), r=[c_in], w=[craw], slow=True)
        self.DMA("sync", craw.ap[:, :, 1], cctx_in.ap.rearrange("(kt p) -> p kt", p=128), r=[cctx_in], w=[craw], slow=True)
        self.act(sT.ap, craw.ap, AF.Silu, r=[craw], w=[sT])
        for k, nwt in enumerate([norm_mix, norm_ffn]):
            self.DMA("sync", self.nw.ap[:, 4 * k:4 * k + 4, :], nwt.ap.rearrange("l (ft p) -> p l ft", p=128), r=[nwt], w=[self.nw], slow=True)
        self.DMA("sync", self.nw.ap[:, 8, :], norm_final.ap.rearrange("(ft p) -> p ft", p=128), r=[norm_final], w=[self.nw], slow=True)
        wbuf = [self.f32(8 * 512, "adaw%d" % i, shape=(8, 512)) for i in range(3)]
        bbuf = [self.f32(512, "adab%d" % i) for i in range(3)]
        onesrow = self.ones.ap[0:1, 0:2]
        it = 0
        for l in layers:
            for cj in range(12):
                wb = wbuf[it % 3]
                bb = bbuf[it % 3]
                ps = self.PS[it % 4]
                it += 1
                self.DMA("sync", wb.ap, ada_w.ap[l, :, cj * 512:(cj + 1) * 512].rearrange("(kt p) n -> p kt n", p=128), r=[ada_w], w=[wb])
                self.DMA("sync", bb.ap[0:1, :], ada_b.ap[l:l + 1, cj * 512:(cj + 1) * 512], r=[ada_b], w=[bb])
                for jj in range(4):
                    j = cj * 4 + jj
                    o = ps.ap[:, jj * 2:jj * 2 + 2]
                    for kt in range(8):
                        self.mm(o, wb.ap[:, kt, jj * 128:(jj + 1) * 128], sT.ap[:, kt, :], kt == 0, False, r=[wb, sT], w=[ps])
                    self.mm(o, bb.ap[0:1, jj * 128:(jj + 1) * 128], onesrow, False, True, r=[bb, self.ones], w=[ps])
                m = cj * 4 // 8
                ft0 = (cj * 4) % 8
                self.cp("vector", self.mod[l].ap[:, m, ft0:ft0 + 4, :], ps.ap[:, 0:8].rearrange("p (a b) -> p a b", b=2), r=[ps], w=[self.mod[l]])
        for l in layers:
            for k, (mi, nwi) in enumerate([(1, l), (4, 4 + l)]):
                nwb = self.nw.ap[:, nwi, :].unsqueeze(2).to_broadcast([128, 8, 2])
                self.stt(self.AB[l].ap[:, k, :, :], self.mod[l].ap[:, mi, :, :], 1.0, nwb, ALU.add, ALU.mult, r=[self.mod[l], self.nw], w=[self.AB[l]])

    def norm_phase(self, XT, HT, A_sel, B_sel, deps_r):
        self.new_phase()
        xin = [self.f32(NFT * 512, "nx%d" % i, shape=(NFT, 512)) for i in range(2)]
        sq = [self.f32(NFT * 512, "nsq%d" % i, shape=(NFT, 512)) for i in range(2)]
        hb = [self.bf16(NFT * 512, "nh%d" % i, shape=(NFT, 512)) for i in range(2)]
        rt = [self.f32(512, "nrt%d" % i) for i in range(2)]
        for ci, (s, w, isctx) in enumerate(self.chunks()):
            xi, sqi, hbi, rti = xin[ci % 2], sq[ci % 2], hb[ci % 2], rt[ci % 2]
            ps = self.PS[ci % 2]
            self.DMA("sync", xi.ap[:, :, 0:w], XT.ap[:, s:s + w].rearrange("(ft p) t -> p ft t", p=128), r=[XT], w=[xi])
            self.act(sqi.ap[:, :, 0:w], xi.ap[:, :, 0:w], AF.Square, r=[xi], w=[sqi])
            for ft in range(NFT):
                self.mm(ps.ap[:, 0:w], self.ones.ap, sqi.ap[:, ft, 0:w], ft == 0, ft == NFT - 1, r=[self.ones, sqi], w=[ps])
            self.act(rti.ap[:, 0:w], ps.ap[:, 0:w], AF.Sqrt, r=[ps, self.epsc], w=[rti], scale=1.0 / D, bias=self.epsc.ap[:, 0:1])
            self.E("vector", lambda e, o=rti.ap[:, 0:w]: e.reciprocal(out=o, in_=o), r=[rti], w=[rti])
            for ft in range(NFT):
                a = A_sel(ft, isctx)
                b = B_sel(ft, isctx)
                self.stt(sqi.ap[:, ft, 0:w], xi.ap[:, ft, 0:w], a, rti.ap[:, 0:w], ALU.mult, ALU.mult, r=[xi, rti] + deps_r, w=[sqi])
                self.act(hbi.ap[:, ft, 0:w], sqi.ap[:, ft, 0:w], AF.Identity, r=[sqi] + deps_r, w=[hbi], bias=b)
            self.DMA("sync", HT.ap[:, s:s + w].rearrange("(ft p) t -> p ft t", p=128), hbi.ap[:, :, 0:w], r=[hbi], w=[HT])

    def norm_layer(self, XT, HT, l, which):
        AB = self.AB[l]
        mod = self.mod[l]
        k = 0 if which == 0 else 1
        smi = 0 if which == 0 else 3
        A_sel = lambda ft, isctx: AB.ap[:, k, ft, (1 if isctx else 0):(1 if isctx else 0) + 1]
        B_sel = lambda ft, isctx: mod.ap[:, smi, ft, (1 if isctx else 0):(1 if isctx else 0) + 1]
        self.norm_phase(XT, HT, A_sel, B_sel, [AB, mod])

    def final_phase(self, XT, out_t):
        self.new_phase()
        xin = [self.f32(NFT * 512, "fx%d" % i, shape=(NFT, 512)) for i in range(2)]
        sq = [self.f32(NFT * 512, "fsq%d" % i, shape=(NFT, 512)) for i in range(2)]
        rt = [self.f32(512, "frt%d" % i) for i in range(2)]
        ob = [self.f32(4 * D, "fo%d" % i, shape=(4, D)) for i in range(2)]
        ci = 0
        for (s, w, isctx) in self.chunks():
            if isctx:
                continue
            xi, sqi, rti, obi = xin[ci % 2], sq[ci % 2], rt[ci % 2], ob[ci % 2]
            ps = self.PS[ci % 2]
            ci += 1
            self.DMA("sync", xi.ap, XT.ap[:, s:s + w].rearrange("(ft p) t -> p ft t", p=128), r=[XT], w=[xi])
            self.act(sqi.ap, xi.ap, AF.Square, r=[xi], w=[sqi])
            for ft in range(NFT):
                self.mm(ps.ap, self.ones.ap, sqi.ap[:, ft, :], ft == 0, ft == NFT - 1, r=[self.ones, sqi], w=[ps])
            self.act(rti.ap, ps.ap, AF.Sqrt, r=[ps, self.epsc], w=[rti], scale=1.0 / D, bias=self.epsc.ap[:, 0:1])
            self.E("vector", lambda e, o=rti.ap: e.reciprocal(out=o, in_=o), r=[rti], w=[rti])
            for ft in range(NFT):
                self.stt(sqi.ap[:, ft, :], xi.ap[:, ft, :], self.nw.ap[:, 8, ft:ft + 1], rti.ap, ALU.mult, ALU.mult, r=[xi, rti, self.nw], w=[sqi])
            for tt in range(4):
                for half in range(2):
                    pso = self.PS[2 + (tt * 2 + half) % 6]
                    for q in range(4):
                        ft = half * 4 + q
                        self.tr(pso.ap[:, q * 128:(q + 1) * 128], sqi.ap[:, ft, tt * 128:(tt + 1) * 128], self.ident.ap, r=[sqi, self.ident], w=[pso])
                    self.cp("scalar" if half else "vector", obi.ap[:, tt, half * 512:(half + 1) * 512], pso.ap, r=[pso], w=[obi])
            dst = out_t.ap[s - LC:s - LC + w, :].rearrange("(tt p) f -> p tt f", p=128)
            self.out_ops.append(self.DMA("sync", dst, obi.ap, r=[obi], w=[out_t]))

    def load_w_bf16(self, dst, src_ap, src_tl, nkt, col0=None, col1=None):
        for kt in range(nkt):
            s = src_ap[kt * 128:(kt + 1) * 128, :] if col0 is None else src_ap[kt * 128:(kt + 1) * 128, col0:col1]
            self.DMA("gpsimd", dst.ap[:, kt, :], s, r=[src_tl], w=[dst])

    def ffn_phase(self, XT, HT, w1, w3, w2, l):
        self.new_phase()
        W = 256
        w1s = self.bf16(8 * FH, "w1s", shape=(8, FH))
        w3s = self.bf16(8 * FH, "w3s", shape=(8, FH))
        w2s = self.bf16(NHT * D, "w2s", shape=(NHT, D))
        self.load_w_bf16(w1s, w1.ap[l], w1, 8)
        self.load_w_bf16(w3s, w3.ap[l], w3, 8)
        self.load_w_bf16(w2s, w2.ap[l], w2, NHT)
        hb = [self.bf16(NFT * W, "fh%d" % i, shape=(NFT, W)) for i in range(2)]
        xb = [self.f32(NFT * W, "fxx%d" % i, shape=(NFT, W)) for i in range(2)]
        gb = [self.bf16(NHT * W, "fg%d" % i, shape=(NHT, W)) for i in range(1)]
        sl = [self.bf16(W, "fs%d" % i) for i in range(2)]
        mod = self.mod[l]
        ntile = T // W
        for ti in range(ntile):
            s = ti * W
            isctx = s < LC
            cs = 1 if isctx else 0
            hbi, xbi, gbi = hb[ti % 2], xb[ti % 2], gb[0]
            self.DMA("sync", hbi.ap, HT.ap[:, s:s + W].rearrange("(ft p) t -> p ft t", p=128), r=[HT], w=[hbi])
            self.DMA("sync", xbi.ap, XT.ap[:, s:s + W].rearrange("(ft p) t -> p ft t", p=128), r=[XT], w=[xbi])
            for j in range(NHT):
                pa = self.PS[(j % 2) * 2]
                pb = self.PS[(j % 2) * 2 + 1]
                for kt in range(8):
                    self.mm(pa.ap[:, 0:W], w1s.ap[:, kt, j * 128:(j + 1) * 128], hbi.ap[:, kt, :], kt == 0, kt == 7, r=[w1s, hbi], w=[pa])
                for kt in range(8):
                    self.mm(pb.ap[:, 0:W], w3s.ap[:, kt, j * 128:(j + 1) * 128], hbi.ap[:, kt, :], kt == 0, kt == 7, r=[w3s, hbi], w=[pb])
                sli = sl[j % 2]
                self.act(sli.ap, pa.ap[:, 0:W], AF.Silu, r=[pa], w=[sli])
                self.tt("vector", gbi.ap[:, j, :], pb.ap[:, 0:W], sli.ap, ALU.mult, r=[pb, sli], w=[gbi])
            for fo in range(NFT):
                po = self.PS[4 + fo % 4]
                for j in range(NHT):
                    self.mm(po.ap[:, 0:W], w2s.ap[:, j, fo * 128:(fo + 1) * 128], gbi.ap[:, j, :], j == 0, j == NHT - 1, r=[w2s, gbi], w=[po])
                self.stt(xbi.ap[:, fo, :], po.ap[:, 0:W], mod.ap[:, 5, fo, cs:cs + 1], xbi.ap[:, fo, :], ALU.mult, ALU.add, r=[po, mod, xbi], w=[xbi])
            self.DMA("sync", XT.ap[:, s:s + W].rearrange("(ft p) t -> p ft t", p=128), xbi.ap, r=[xbi], w=[XT])

    def rev_ap(self, ap2d, start, n):
        pstride = ap2d.ap[0][0]
        return bass.AP(ap2d.tensor, ap2d.offset + start + n - 1, [[pstride, 128], [-1, n]])

    def sincos_turns(self, eng, turns, n, osin, ocos, tmp, cast_eng="vector"):
        ti, tf, fr, s2, s4 = tmp["ti"], tmp["tf"], tmp["fr"], tmp["s2"], tmp["s4"]
        sl = lambda t: t.ap[:, 0:n]
        self.cp(cast_eng, sl(ti), sl(turns), r=[turns], w=[ti])
        self.cp(cast_eng, sl(tf), sl(ti), r=[ti], w=[tf])
        self.tt(eng, sl(fr), sl(turns), sl(tf), ALU.subtract, r=[turns, tf], w=[fr])
        self.act(sl(s2), sl(fr), AF.Sin, r=[fr], w=[s2], scale=float(np.pi))
        self.act(sl(s4), sl(fr), AF.Sin, r=[fr], w=[s4], scale=float(np.pi / 2))
        self.tt(eng, sl(s4), sl(s4), sl(s4), ALU.mult, r=[s4], w=[s4])
        self.ts(eng, sl(s4), sl(s4), -4.0, ALU.mult, 2.0, ALU.add, r=[s4], w=[s4])
        self.tt(eng, sl(osin), sl(s2), sl(s4), ALU.mult, r=[s2, s4], w=[osin])
        self.tt(eng, sl(s2), sl(s2), sl(s2), ALU.mult, r=[s2], w=[s2])
        self.ts(eng, sl(ocos), sl(s2), -2.0, ALU.mult, 1.0, ALU.add, r=[s2], w=[ocos])

    def build_mask8(self):
        self.mask8 = self.f32(8, "mask8", True)
        m = self.mask8.ap
        self.memset("gpsimd", m, 1.0, [self.mask8])
        self.E("gpsimd", lambda e: e.affine_select(out=m, in_=m, pattern=[[-16, 8]], compare_op=ALU.is_ge, fill=0.0, base=0, channel_multiplier=1), [self.mask8], [self.mask8])
        self.E("gpsimd", lambda e: e.affine_select(out=m, in_=m, pattern=[[16, 8]], compare_op=ALU.is_ge, fill=0.0, base=15, channel_multiplier=-1), [self.mask8], [self.mask8])

    def s5_phase(self, HT, GT, P, j):
        self.new_phase()
        V, G = "vector", "gpsimd"
        def sc_tile(nm):
            return self.f32(64, nm)
        lr, li, ls = sc_tile("lr"), sc_tile("li"), sc_tile("ls")
        for d in range(2):
            self.DMA("sync", lr.ap[:, d * 32:(d + 1) * 32], P["lam_re"].ap[j, d].rearrange("(p two) n -> (two n) p", two=2), r=[P["lam_re"]], w=[lr], slow=True)
            self.DMA("sync", li.ap[:, d * 32:(d + 1) * 32], P["lam_im"].ap[j, d].rearrange("(p two) n -> (two n) p", two=2), r=[P["lam_im"]], w=[li], slow=True)
        lsrow = self.f32(128, "lsrow")
        self.DMA("sync", lsrow.ap[0:1, :], P["log_step"].ap[j:j + 1].rearrange("o d g -> o (d g)"), r=[P["log_step"]], w=[lsrow])
        psb = self.PS[0]
        self.mm(psb.ap[:, 0:128], self.ones.ap[0:1, :], lsrow.ap[0:1, :], True, True, r=[self.ones, lsrow], w=[psb])
        for d in range(2):
            src = psb.ap[:, d * 64:(d + 1) * 64].rearrange("q (p two) -> q p two", two=2)
            self.cp(V, ls.ap[0:64, d * 32:(d + 1) * 32], src[0:64, :, 0], r=[psb], w=[ls])
            self.cp(V, ls.ap[64:128, d * 32:(d + 1) * 32], src[64:128, :, 1], r=[psb], w=[ls])
        step, zr, zi, rr, tq = sc_tile("step"), sc_tile("zr"), sc_tile("zi"), sc_tile("rr"), sc_tile("tq")
        tmp = {"ti": self.i32(512, "ti"), "tf": self.f32(512, "tf"), "fr": self.f32(512, "fr"), "s2": self.f32(512, "s2"), "s4": self.f32(512, "s4")}
        tmp2 = {"ti": self.i32(512, "ti2"), "tf": self.f32(512, "tf2"), "fr": self.f32(512, "fr2"), "s2": self.f32(512, "s22"), "s4": self.f32(512, "s42")}
        sphi, cphi, frac = sc_tile("sphi"), sc_tile("cphi"), sc_tile("frac")
        self.act(step.ap, ls.ap, AF.Exp, r=[ls], w=[step])
        self.tt(V, zr.ap, lr.ap, step.ap, ALU.mult, r=[lr, step], w=[zr])
        self.tt(V, zi.ap, li.ap, step.ap, ALU.mult, r=[li, step], w=[zi])
        self.act(rr.ap, zr.ap, AF.Exp, r=[zr], w=[rr])
        self.ts(V, tq.ap, zi.ap, float(1.0 / (2 * np.pi)), ALU.mult, r=[zi], w=[tq])
        self.sincos_turns(V, tq, 64, sphi, cphi, tmp)
        self.cp(V, frac.ap, tmp["fr"].ap[:, 0:64], r=[tmp["fr"]], w=[frac])
        carry = {}
        for Q in (256, 512):
            tQ, sQ, cQ = sc_tile("tQ%d" % Q), sc_tile("sQ%d" % Q), sc_tile("cQ%d" % Q)
            self.ts(V, tQ.ap, frac.ap, float(Q), ALU.mult, r=[frac], w=[tQ])
            self.sincos_turns(V, tQ, 64, sQ, cQ, tmp)
            carry[Q] = (sQ, cQ)
        ar, ai, den, u, cr, ci, t1s, t2s = [sc_tile(n) for n in ("ar", "ai", "den", "u", "cr", "ci", "t1s", "t2s")]
        self.tt(V, ar.ap, rr.ap, cphi.ap, ALU.mult, r=[rr, cphi], w=[ar])
        self.tt(V, ai.ap, rr.ap, sphi.ap, ALU.mult, r=[rr, sphi], w=[ai])
        self.tt(V, t1s.ap, lr.ap, lr.ap, ALU.mult, r=[lr], w=[t1s])
        self.tt(V, t2s.ap, li.ap, li.ap, ALU.mult, r=[li], w=[t2s])
        self.tt(V, den.ap, t1s.ap, t2s.ap, ALU.add, r=[t1s, t2s], w=[den])
        self.E(V, lambda e: e.reciprocal(out=den.ap, in_=den.ap), r=[den], w=[den])
        self.ts(V, u.ap, ar.ap, -1.0, ALU.add, r=[ar], w=[u])
        self.tt(V, t1s.ap, u.ap, lr.ap, ALU.mult, r=[u, lr], w=[t1s])
        self.tt(V, t2s.ap, ai.ap, li.ap, ALU.mult, r=[ai, li], w=[t2s])
        self.tt(V, t1s.ap, t1s.ap, t2s.ap, ALU.add, r=[t1s, t2s], w=[t1s])
        self.tt(V, cr.ap, t1s.ap, den.ap, ALU.mult, r=[t1s, den], w=[cr])
        self.tt(V, t1s.ap, ai.ap, lr.ap, ALU.mult, r=[ai, lr], w=[t1s])
        self.tt(V, t2s.ap, u.ap, li.ap, ALU.mult, r=[u, li], w=[t2s])
        self.tt(V, t1s.ap, t1s.ap, t2s.ap, ALU.subtract, r=[t1s, t2s], w=[t1s])
        self.tt(V, ci.ap, t1s.ap, den.ap, ALU.mult, r=[t1s, den], w=[ci])
        braw = {}
        for nm in ("b_re", "b_im"):
            braw[nm] = self.f32(2 * 32 * 16, "braw_" + nm, shape=(2, 32, 16))
            for d in range(2):
                self.DMA("sync", braw[nm].ap[:, d, :, :], P[nm].ap[j, d].rearrange("(p two) n h -> (two n) p h", two=2), r=[P[nm]], w=[braw[nm]], slow=True)
        dsk = self.f32(8, "dsk")
        self.DMA("sync", dsk.ap, P["d"].ap[j].rearrange("(ft p) -> p ft", p=128), r=[P["d"]], w=[dsk], slow=True)
        M1 = {}
        for k in range(4):
            for nm in ("b_re", "b_im"):
                M1[(k, nm)] = self.f32(128, "M1_%d%s" % (k, nm))
                self.memset(G, M1[(k, nm)].ap, 0.0, [M1[(k, nm)]])
        Jrow = self.f32(512, "Jrow")
        self.E(G, lambda e: e.iota(Jrow.ap, pattern=[[1, 512]], base=0, channel_multiplier=0, allow_small_or_imprecise_dtypes=True), (), [Jrow])
        WTS = [self.bf16(8 * 5 * 128, "wts%d" % i, shape=(8, 5, 128)) for i in range(2)]
        craw = [[self.f32(64, "craw%d_%d" % (i, q)) for q in range(2)] for i in range(2)]
        Spair = [self.f32(128, "Spair%d" % i) for i in range(2)]
        U = [self.bf16(T, "U%d" % i) for i in range(2)]
        Yacc = self.f32(T, "Yacc")
        gt = self.bf16(T, "gt")
        TAB = [{n: self.f32(512, "%s%d" % (n, i)) for n in ("COS", "SIN", "wr", "wi", "ta", "tb")} for i in range(2)]
        WK = [{n: self.f32(512, "%s%d" % (n, i)) for n in ("t1", "t2", "t3", "t4", "bre", "bim", "gre", "gim")} for i in range(2)]
        MK = [{n: self.bf16(512, "%s%d" % (n, i)) for n in ("m1", "m2", "m3", "m4")} for i in range(2)]
        init = [self.f32(4, "init%d" % i) for i in range(2)]
        fwd_chunks = [(0, LC)] + [(s, 512) for s in range(LC, T, 512)]
        bwd_chunks = [(0, LC)] + [(T - 512 * (i + 1), 512) for i in range(8)]
        tgc = [0]

        def emit_prep(ft):
            Ui = U[ft % 2]
            self.DMA("sync", Ui.ap, HT.ap[ft * 128:(ft + 1) * 128, :], r=[HT], w=[Ui])
            W = WTS[ft % 2]
            for d in range(2):
                cr_ = craw[d]
                self.DMA("sync", cr_[0].ap, P["c_re"].ap[j, d, ft * 8:(ft + 1) * 8].rearrange("g h n -> (g h) n"), r=[P["c_re"]], w=[cr_[0]])
                self.DMA("sync", cr_[1].ap, P["c_im"].ap[j, d, ft * 8:(ft + 1) * 8].rearrange("g h n -> (g h) n"), r=[P["c_im"]], w=[cr_[1]])
                for k in range(4):
                    p_ = ft * 4 + k
                    wi_ = d * 4 + k
                    c1, c2 = 32 * k, 32 * k + 16
                    for bi, nm in enumerate(("b_re", "b_im")):
                        m1t = M1[(k, nm)]
                        self.cp(G, m1t.ap[0:64, c1:c1 + 16], braw[nm].ap[0:64, d, p_, :], r=[braw[nm]], w=[m1t])
                        self.cp(G, m1t.ap[64:128, c2:c2 + 16], braw[nm].ap[64:128, d, p_, :], r=[braw[nm]], w=[m1t])
                        ps = self.PS[4 + (tgc[0] % 2)]
                        tgc[0] += 1
                        self.tr(ps.ap[:, 0:128], m1t.ap, self.ident.ap, r=[m1t, self.ident], w=[ps])
                        self.cp("scalar", W.ap[:, wi_, bi, :], ps.ap[:, 0:128], r=[ps], w=[W])
                    for q in range(2):
                        sp = Spair[q]
                        self.ts(G, sp.ap[:, 0:64], cr_[q].ap, self.mask8.ap[:, 2 * k:2 * k + 1], ALU.mult, 0.0, ALU.add, r=[cr_[q], self.mask8], w=[sp])
                        self.ts(G, sp.ap[:, 64:128], cr_[q].ap, self.mask8.ap[:, 2 * k + 1:2 * k + 2], ALU.mult, 0.0, ALU.add, r=[cr_[q], self.mask8], w=[sp])
                        ps = self.PS[4 + (tgc[0] % 2)]
                        tgc[0] += 1
                        self.tr(ps.ap[:, 0:128], sp.ap, self.ident.ap, r=[sp, self.ident], w=[ps])
                        if q == 0:
                            self.cp("scalar", W.ap[:, wi_, 2, :], ps.ap[:, 0:128], r=[ps], w=[W])
                            self.act(W.ap[:, wi_, 3, :], ps.ap[:, 0:128], AF.Copy, r=[ps], w=[W], scale=-1.0)
                        else:
                            self.act(W.ap[:, wi_, 4, :], ps.ap[:, 0:128], AF.Copy, r=[ps], w=[W], scale=-1.0)

        def emit_tables(tab, col):
            self.ts(G, tab["ta"].ap, Jrow.ap, frac.ap[:, col:col + 1], ALU.mult, 0.0, ALU.add, r=[Jrow, frac], w=[tab["ta"]])
            self.sincos_turns(G, tab["ta"], 512, tab["SIN"], tab["COS"], tmp2, cast_eng=G)
            crc, cic = cr.ap[:, col:col + 1], ci.ap[:, col:col + 1]
            self.ts(G, tab["ta"].ap, tab["COS"].ap, crc, ALU.mult, 0.0, ALU.add, r=[tab["COS"], cr], w=[tab["ta"]])
            self.ts(G, tab["tb"].ap, tab["SIN"].ap, cic, ALU.mult, 0.0, ALU.add, r=[tab["SIN"], ci], w=[tab["tb"]])
            self.tt(G, tab["wr"].ap, tab["ta"].ap, tab["tb"].ap, ALU.add, r=[tab["ta"], tab["tb"]], w=[tab["wr"]])
            self.ts(G, tab["ta"].ap, tab["COS"].ap, cic, ALU.mult, 0.0, ALU.add, r=[tab["COS"], ci], w=[tab["ta"]])
            self.ts(G, tab["tb"].ap, tab["SIN"].ap, crc, ALU.mult, 0.0, ALU.add, r=[tab["SIN"], cr], w=[tab["tb"]])
            self.tt(G, tab["wi"].ap, tab["ta"].ap, tab["tb"].ap, ALU.subtract, r=[tab["ta"], tab["tb"]], w=[tab["wi"]])

        items = []
        pd = 0
        for ft in range(NFT):
            for d in range(2):
                chunks = fwd_chunks if d == 0 else bwd_chunks
                for k in range(4):
                    for cidx, (s, n) in enumerate(chunks):
                        items.append(dict(ft=ft, d=d, k=k, cidx=cidx, s=s, n=n, pd=pd, last=(cidx == len(chunks) - 1),
                                          first_pd=(cidx == 0), first_y=(d == 0 and k == 0),
                                          ft_first=(d == 0 and k == 0 and cidx == 0), ft_last=(d == 1 and k == 3 and cidx == len(chunks) - 1)))
                    pd += 1

        def stage_a(i, it_):
            ft, d, k, s, n = it_["ft"], it_["d"], it_["k"], it_["s"], it_["n"]
            if it_["ft_first"]:
                emit_prep(ft)
            tab = TAB[it_["pd"] % 2]
            col = d * 32 + ft * 4 + k
            if it_["first_pd"] and it_["pd"] == 0:
                emit_tables(tab, col)
            if it_["cidx"] == 1 and it_["pd"] + 1 < 64:
                npd = it_["pd"] + 1
                nft, nd, nk = npd // 8, (npd // 4) % 2, npd % 4
                emit_tables(TAB[npd % 2], nd * 32 + nft * 4 + nk)
            Ui, W, wi_ = U[ft % 2], WTS[ft % 2], d * 4 + k
            wk = WK[i % 2]
            pre, pim = self.PS[(i % 2) * 2], self.PS[(i % 2) * 2 + 1]
            urhs = Ui.ap[:, s:s + n] if d == 0 else self.rev_ap(Ui.ap, s, n)
            self.mm(pre.ap[:, 0:n], W.ap[:, wi_, 0, :], urhs, True, True, r=[W, Ui], w=[pre])
            self.mm(pim.ap[:, 0:n], W.ap[:, wi_, 1, :], urhs, True, True, r=[W, Ui], w=[pim])
            c_ = lambda t: t.ap[:, 0:n]
            self.tt(V, c_(wk["t1"]), pre.ap[:, 0:n], c_(tab["wr"]), ALU.mult, r=[pre, tab["wr"]], w=[wk["t1"]])
            self.tt(V, c_(wk["t4"]), pre.ap[:, 0:n], c_(tab["wi"]), ALU.mult, r=[pre, tab["wi"]], w=[wk["t4"]])
            self.tt(V, c_(wk["t2"]), pim.ap[:, 0:n], c_(tab["wi"]), ALU.mult, r=[pim, tab["wi"]], w=[wk["t2"]])
            self.tt(V, c_(wk["t3"]), pim.ap[:, 0:n], c_(tab["wr"]), ALU.mult, r=[pim, tab["wr"]], w=[wk["t3"]])
            self.tt(G, c_(wk["bre"]), c_(wk["t1"]), c_(wk["t2"]), ALU.subtract, r=[wk["t1"], wk["t2"]], w=[wk["bre"]])
            self.tt(G, c_(wk["bim"]), c_(wk["t3"]), c_(wk["t4"]), ALU.add, r=[wk["t3"], wk["t4"]], w=[wk["bim"]])

        def stage_b(i, it_):
            ft, d, k, s, n, cidx = it_["ft"], it_["d"], it_["k"], it_["s"], it_["n"], it_["cidx"]
            tab = TAB[it_["pd"] % 2]
            col = d * 32 + ft * 4 + k
            Ui, W, wi_ = U[ft % 2], WTS[ft % 2], d * 4 + k
            wk, mk, ini = WK[i % 2], MK[i % 2], init[i % 2]
            py = self.PS[4 + i % 2]
            c_ = lambda t: t.ap[:, 0:n]
            rcol = rr.ap[:, col:col + 1]
            rb = rcol.to_broadcast([128, n])
            if cidx == 0:
                i_re, i_im, ir = 0.0, 0.0, []
            else:
                pini = init[(i - 1) % 2]
                i_re, i_im, ir = pini.ap[:, 0:1], pini.ap[:, 1:2], [pini]
            gre_t, gim_t = self.PS[6], self.PS[7]
            self.E(V, lambda e, o=gre_t.ap[:, 0:n], d1=c_(wk["bre"]), i0=i_re, rb=rb: e.tensor_tensor_scan(out=o, data0=rb, data1=d1, initial=i0, op0=ALU.mult, op1=ALU.add),
                   r=[wk["bre"], rr] + ir, w=[gre_t])
            self.E(V, lambda e, o=gim_t.ap[:, 0:n], d1=c_(wk["bim"]), i0=i_im, rb=rb: e.tensor_tensor_scan(out=o, data0=rb, data1=d1, initial=i0, op0=ALU.mult, op1=ALU.add),
                   r=[wk["bim"], rr] + ir, w=[gim_t])
            if not it_["last"]:
                sQ, cQ = carry[n]
                sq_, cq_ = sQ.ap[:, col:col + 1], cQ.ap[:, col:col + 1]
                gre_l, gim_l = gre_t.ap[:, n - 1:n], gim_t.ap[:, n - 1:n]
                self.ts(V, ini.ap[:, 2:3], gim_l, sq_, ALU.mult, r=[gim_t, sQ], w=[ini])
                self.ts(V, ini.ap[:, 3:4], gim_l, cq_, ALU.mult, r=[gim_t, cQ], w=[ini])
                self.stt(ini.ap[:, 0:1], gre_l, cq_, ini.ap[:, 2:3], ALU.mult, ALU.subtract, r=[gre_t, cQ, ini], w=[ini])
                self.stt(ini.ap[:, 1:2], gre_l, sq_, ini.ap[:, 3:4], ALU.mult, ALU.add, r=[gre_t, sQ, ini], w=[ini])
            self.tt(V, c_(mk["m1"]), gre_t.ap[:, 0:n], c_(tab["COS"]), ALU.mult, r=[gre_t, tab["COS"]], w=[mk["m1"]])
            self.tt(V, c_(mk["m3"]), gre_t.ap[:, 0:n], c_(tab["SIN"]), ALU.mult, r=[gre_t, tab["SIN"]], w=[mk["m3"]])
            self.tt(V, c_(mk["m2"]), gim_t.ap[:, 0:n], c_(tab["SIN"]), ALU.mult, r=[gim_t, tab["SIN"]], w=[mk["m2"]])
            self.tt(V, c_(mk["m4"]), gim_t.ap[:, 0:n], c_(tab["COS"]), ALU.mult, r=[gim_t, tab["COS"]], w=[mk["m4"]])
            for mi, (mn, wsel) in enumerate((("m1", 2), ("m2", 3), ("m3", 4), ("m4", 4))):
                mr = mk[mn].ap[:, 0:n] if d == 0 else self.rev_ap(mk[mn].ap, 0, n)
                self.mm(py.ap[:, 0:n], W.ap[:, wi_, wsel, :], mr, mi == 0, mi == 3, r=[W, mk[mn]], w=[py])
            if it_["first_y"]:
                self.cp("scalar", Yacc.ap[:, s:s + n], py.ap[:, 0:n], r=[py], w=[Yacc])
            else:
                self.tt(V, Yacc.ap[:, s:s + n], py.ap[:, 0:n], Yacc.ap[:, s:s + n], ALU.add, r=[py, Yacc], w=[Yacc])
            if it_["ft_last"]:
                self.stt(Yacc.ap, Ui.ap, dsk.ap[:, ft:ft + 1], Yacc.ap, ALU.mult, ALU.add, r=[Ui, dsk, Yacc], w=[Yacc])
                self.act(gt.ap, Yacc.ap, AF.Gelu_apprx_tanh, r=[Yacc], w=[gt])
                self.DMA("sync", GT.ap[ft * 128:(ft + 1) * 128, :], gt.ap, r=[gt], w=[GT])

        stage_a(0, items[0])
        for i in range(len(items)):
            if i + 1 < len(items):
                stage_a(i + 1, items[i + 1])
            stage_b(i, items[i])

    def proj_phase(self, XT, IN, Wd, w_ap, nkt, l, glu_bias=None):
        self.new_phase()
        Wt = 512
        ws = self.bf16(nkt * D, "pw", shape=(nkt, D))
        self.load_w_bf16(ws, w_ap, Wd, nkt)
        bg = None
        if glu_bias is not None:
            bg = self.f32(8, "bglu")
            self.DMA("sync", bg.ap, glu_bias[1].rearrange("(ft p) -> p ft", p=128), r=[glu_bias[0]], w=[bg], slow=True)
        ib = [self.bf16(nkt * Wt, "pi%d" % i, shape=(nkt, Wt)) for i in range(2)]
        xb = [self.f32(NFT * Wt, "px%d" % i, shape=(NFT, Wt)) for i in range(2)]
        sg = [self.f32(Wt, "psg%d" % i) for i in range(2)]
        mod = self.mod[l]
        for ci, (s, w, isctx) in enumerate(self.chunks()):
            cs = 1 if isctx else 0
            ibi, xbi = ib[ci % 2], xb[ci % 2]
            self.DMA("sync", ibi.ap[:, :, 0:w], IN.ap[:, s:s + w].rearrange("(kt p) t -> p kt t", p=128), r=[IN], w=[ibi])
            self.DMA("sync", xbi.ap[:, :, 0:w], XT.ap[:, s:s + w].rearrange("(ft p) t -> p ft t", p=128), r=[XT], w=[xbi])
            for fo in range(NFT):
                po = self.PS[fo % 4]
                for kt in range(nkt):
                    self.mm(po.ap[:, 0:w], ws.ap[:, kt, fo * 128:(fo + 1) * 128], ibi.ap[:, kt, 0:w], kt == 0, kt == nkt - 1, r=[ws, ibi], w=[po])
                gate = mod.ap[:, 2, fo, cs:cs + 1]
                if glu_bias is not None:
                    sgi = sg[fo % 2]
                    self.act(sgi.ap[:, 0:w], po.ap[:, 0:w], AF.Sigmoid, r=[po, bg], w=[sgi], bias=bg.ap[:, fo:fo + 1])
                    self.tt("gpsimd", sgi.ap[:, 0:w], sgi.ap[:, 0:w], ibi.ap[:, fo, 0:w], ALU.mult, r=[sgi, ibi], w=[sgi])
                    self.stt(xbi.ap[:, fo, 0:w], sgi.ap[:, 0:w], gate, xbi.ap[:, fo, 0:w], ALU.mult, ALU.add, r=[sgi, mod, xbi], w=[xbi])
                else:
                    self.stt(xbi.ap[:, fo, 0:w], po.ap[:, 0:w], gate, xbi.ap[:, fo, 0:w], ALU.mult, ALU.add, r=[po, mod, xbi], w=[xbi])
            self.DMA("sync", XT.ap[:, s:s + w].rearrange("(ft p) t -> p ft t", p=128), xbi.ap[:, :, 0:w], r=[xbi], w=[XT])


    def row_bcast(self, dst, row_ap, src_tl, n, rowtmp, func=None, scale=None):
        self.DMA("sync", rowtmp.ap[0:1, 0:n], row_ap, r=[src_tl], w=[rowtmp])
        for c0 in range(0, n, 512):
            w = min(512, n - c0)
            ps = self.PS[7]
            self.mm(ps.ap[:, 0:w], self.ones.ap[0:1, :], rowtmp.ap[0:1, c0:c0 + w], True, True, r=[self.ones, rowtmp], w=[ps])
            if func is None:
                self.cp("vector", dst.ap[:, c0:c0 + w], ps.ap[:, 0:w], r=[ps], w=[dst])
            else:
                self.act(dst.ap[:, c0:c0 + w], ps.ap[:, 0:w], func, r=[ps], w=[dst], scale=scale)

    def tri_mask(self, nm, base, cm, step, op):
        t = self.f32(128, nm)
        self.memset("gpsimd", t.ap, 1.0, [t])
        self.E("gpsimd", lambda e: e.affine_select(out=t.ap, in_=t.ap, pattern=[[step, 128]], compare_op=op, fill=0.0, base=base, channel_multiplier=cm), [t], [t])
        return t

    def ssd_phase_a(self, HT, P, SZ, XS, BC, DTR):
        self.new_phase()
        w_in = P["w_in"]
        Hres = self.bf16(8 * T, "Hres", shape=(8, T))
        for kt in range(8):
            self.DMA("sync", Hres.ap[:, kt, :], HT.ap[kt * 128:(kt + 1) * 128, :], r=[HT], w=[Hres])
        wz = self.bf16(8 * 2048, "wz", shape=(8, 2048))
        self.load_w_bf16(wz, w_in.ap[0], w_in, 8, 0, 2048)
        szb = [self.bf16(2048, "szb%d" % i) for i in range(2)]
        for tt in range(T // 128):
            sb = szb[tt % 2]
            for zc in range(4):
                ps = self.PS[zc]
                for kt in range(8):
                    self.mm(ps.ap, Hres.ap[:, kt, tt * 128:(tt + 1) * 128], wz.ap[:, kt, zc * 512:(zc + 1) * 512], kt == 0, kt == 7, r=[Hres, wz], w=[ps])
                self.act(sb.ap[:, zc * 512:(zc + 1) * 512], ps.ap, AF.Silu, r=[ps], w=[sb])
            self.DMA("sync", SZ.ap[tt * 128:(tt + 1) * 128, :], sb.ap, r=[sb], w=[SZ])
        self.new_phase()
        Hres = self.bf16(8 * T, "Hres2", shape=(8, T))
        for kt in range(8):
            self.DMA("sync", Hres.ap[:, kt, :], HT.ap[kt * 128:(kt + 1) * 128, :], r=[HT], w=[Hres])
        cw = self.f32(5 * 48, "cw", shape=(5, 48))
        cb = self.f32(48, "cb")
        for k in range(5):
            self.DMA("sync", cw.ap[:, k, :], P["conv_w"].ap[0, k].rearrange("(ot p) -> p ot", p=128), r=[P["conv_w"]], w=[cw], slow=True)
        self.DMA("sync", cb.ap, P["conv_b"].ap[0].rearrange("(ot p) -> p ot", p=128), r=[P["conv_b"]], w=[cb], slow=True)
        wch = [self.bf16(8 * 512, "wch%d" % i, shape=(8, 512)) for i in range(2)]
        xp = [self.bf16(T, "xp%d" % i) for i in range(2)]
        ob = [self.bf16(T, "ob%d" % i) for i in range(2)]
        dg = [self.bf16(5 * 128, "dg%d" % i, shape=(5, 128)) for i in range(2)]
        chunks = self.chunks()
        pi = 0
        for wc in range(12):
            wb = wch[wc % 2]
            self.load_w_bf16(wb, w_in.ap[0], w_in, 8, 2048 + wc * 512, 2048 + (wc + 1) * 512)
            for q in range(4):
                ot = wc * 4 + q
                xpi, obi, dgi = xp[ot % 2], ob[ot % 2], dg[ot % 2]
                for k in range(5):
                    self.ts("gpsimd", dgi.ap[:, k, :], self.identb.ap, cw.ap[:, k, ot:ot + 1], ALU.mult, 0.0, ALU.add, r=[self.identb, cw], w=[dgi])
                for ci, (s, w, isctx) in enumerate(chunks):
                    ps = self.PS[pi % 4]
                    pi += 1
                    for kt in range(8):
                        self.mm(ps.ap[:, 0:w], wb.ap[:, kt, q * 128:(q + 1) * 128], Hres.ap[:, kt, s:s + w], kt == 0, kt == 7, r=[wb, Hres], w=[ps])
                    self.cp("vector" if ci % 2 else "scalar", xpi.ap[:, s:s + w], ps.ap[:, 0:w], r=[ps], w=[xpi])
                for ci, (s, w, isctx) in enumerate(chunks):
                    q0, q1 = (0, LC) if isctx else (LC, T)
                    ps = self.PS[4 + ci % 4]
                    for ki, k in enumerate((2, 0, 1, 3, 4)):
                        o = k - 2
                        i0 = max(0, q0 - s - o)
                        i1 = min(w, q1 - s - o)
                        self.mm(ps.ap[:, i0:i1], dgi.ap[:, k, :], xpi.ap[:, s + i0 + o:s + i1 + o], ki == 0, ki == 4, r=[dgi, xpi], w=[ps])
                    self.act(obi.ap[:, s:s + w], ps.ap[:, 0:w], AF.Silu, r=[ps, cb], w=[obi], bias=cb.ap[:, ot:ot + 1])
                dst = XS.ap[ot * 128:(ot + 1) * 128, :] if ot < 16 else BC.ap[(ot - 16) * 128:(ot - 15) * 128, :]
                self.DMA("sync", dst, obi.ap, r=[obi], w=[XS if ot < 16 else BC])
        wdt = self.bf16(8 * 64, "wdt", shape=(8, 64))
        self.load_w_bf16(wdt, w_in.ap[0], w_in, 8, 8192, 8256)
        dtb = self.f32(T, "dtb")
        for ci, (s, w, isctx) in enumerate(chunks):
            ps = self.PS[ci % 4]
            for kt in range(8):
                self.mm(ps.ap[0:64, 0:w], wdt.ap[:, kt, :], Hres.ap[:, kt, s:s + w], kt == 0, kt == 7, r=[wdt, Hres], w=[ps])
            self.cp("vector", dtb.ap[0:64, s:s + w], ps.ap[0:64, 0:w], r=[ps], w=[dtb])
        self.DMA("sync", DTR.ap, dtb.ap[0:64, :], r=[dtb], w=[DTR])

    def ssd_phase_b(self, P, XS, BC, DTR, YF, YB):
        self.new_phase()
        V, G = "vector", "gpsimd"
        GT_ = self.tri_mask("mGT", 0, 1, -1, ALU.is_gt)
        LE_ = self.tri_mask("mLE", 0, -1, 1, ALU.is_ge)
        LT_ = self.tri_mask("mLT", 0, -1, 1, ALU.is_gt)
        GE_ = self.tri_mask("mGE", 0, 1, -1, ALU.is_ge)
        rowtmp = self.f32(64, "rowtmp")
        Arow = [self.f32(32, "Arow%d" % d) for d in range(2)]
        Brow = [self.f32(32, "Brow%d" % d) for d in range(2)]
        Drow = self.f32(32, "Drow")
        for d in range(2):
            self.row_bcast(Arow[d], P["a_log"].ap[0, d:d + 1, :], P["a_log"], 32, rowtmp, func=AF.Exp)
            self.ts(V, Arow[d].ap, Arow[d].ap, -1.0, ALU.mult, r=[Arow[d]], w=[Arow[d]])
            self.row_bcast(Brow[d], P["dt_bias"].ap[0, d:d + 1, :], P["dt_bias"], 32, rowtmp)
        self.row_bcast(Drow, P["d"].ap[0:1, :], P["d"], 32, rowtmp)
        H = self.f32(2048, "Hst", shape=(32, 64))
        Hb = self.bf16(2048, "Hb")
        NB = 2
        xsT = [self.bf16(16 * 128, "xsT%d" % i, shape=(16, 128)) for i in range(NB)]
        BTt = [self.bf16(8 * 128, "BT%d" % i, shape=(8, 128)) for i in range(NB)]
        CTt = [self.bf16(8 * 128, "CT%d" % i, shape=(8, 128)) for i in range(NB)]
        dtr = [self.f32(128, "dtr%d" % i) for i in range(NB)]
        ybuf = [self.bf16(2048, "ybuf%d" % i) for i in range(NB)]
        xdt = [self.bf16(2048, "xdt%d" % i, shape=(32, 64)) for i in range(NB)]
        xdte = [self.bf16(2048, "xdte%d" % i, shape=(32, 64)) for i in range(NB)]
        dskt = [self.f32(2048, "dskt%d" % i, shape=(32, 64)) for i in range(NB)]
        Btok = [self.bf16(1024, "Btok%d" % i, shape=(8, 128)) for i in range(NB)]
        dtt = [self.f32(32, "dtt%d" % i) for i in range(NB)]
        adt = [self.f32(32, "adt%d" % i) for i in range(NB)]
        dex = [self.f32(96, "dex%d" % i) for i in range(NB)]
        dtd = [self.f32(32, "dtd%d" % i) for i in range(NB)]
        Xh = [self.f32(512, "Xh%d" % i, shape=(4, 128)) for i in range(2)]
        Ld = [self.bf16(512, "Ld%d" % i, shape=(4, 128)) for i in range(2)]
        Mt = [self.bf16(512, "Mt%d" % i, shape=(4, 128)) for i in range(2)]
        CBm = [self.bf16(128, "CBm%d" % i) for i in range(2)]
        ytmp = [self.f32(256, "ytmp%d" % i, shape=(4, 64)) for i in range(2)]
        htmp = [self.f32(256, "htmp%d" % i, shape=(4, 64)) for i in range(2)]
        nchunk = T // 128
        work = []
        for d in range(2):
            order = list(range(nchunk)) if d == 0 else [1, 0] + list(range(nchunk - 1, 1, -1))
            for oi, c in enumerate(order):
                work.append((d, c, oi == 0))
        cfgd = {0: dict(mX=GT_, mE=LE_, mTE=GT_, mSeg=LE_, mCB=LE_), 1: dict(mX=LT_, mE=GE_, mTE=LT_, mSeg=GE_, mCB=GE_)}

        def prologue(wi):
            d, c, first = work[wi]
            cf = cfgd[d]
            b = wi % NB
            s0 = c * 128
            xi, bi, cti, dri = xsT[b], BTt[b], CTt[b], dtr[b]
            self.DMA("sync", xi.ap, XS.ap[:, s0:s0 + 128].rearrange("(t p) c -> p t c", p=128), r=[XS], w=[xi])
            self.DMA("sync", bi.ap, BC.ap[(d * 2) * 1024:(d * 2 + 1) * 1024, s0:s0 + 128].rearrange("(t p) c -> p t c", p=128), r=[BC], w=[bi])
            self.DMA("sync", cti.ap, BC.ap[(d * 2 + 1) * 1024:(d * 2 + 2) * 1024, s0:s0 + 128].rearrange("(t p) c -> p t c", p=128), r=[BC], w=[cti])
            self.DMA("sync", dri.ap[0:32, :], DTR.ap[d * 32:(d + 1) * 32, s0:s0 + 128], r=[DTR], w=[dri])
            psm = self.PS[6]
            dtt_, adt_, dex_, dtd_ = dtt[b], adt[b], dex[b], dtd[b]
            self.tr(psm.ap[:, 0:32], dri.ap[0:32, :], self.ident.ap[0:32, 0:32], r=[dri, self.ident], w=[psm])
            self.tt(V, dtt_.ap, psm.ap[:, 0:32], Brow[d].ap, ALU.add, r=[psm, Brow[d]], w=[dtt_])
            self.act(dtt_.ap, dtt_.ap, AF.Exp, r=[dtt_], w=[dtt_])
            self.act(dtt_.ap, dtt_.ap, AF.Ln, r=[dtt_, self.epsc], w=[dtt_], bias=self.epsc.ap[:, 2:3])
            self.tt(V, adt_.ap, dtt_.ap, Arow[d].ap, ALU.mult, r=[dtt_, Arow[d]], w=[adt_])
            self.mm(psm.ap[:, 32:64], self.ones.ap, adt_.ap, True, True, r=[self.ones, adt_], w=[psm])
            self.mm(psm.ap[:, 64:96], cf["mTE"].ap, adt_.ap, True, True, r=[cf["mTE"], adt_], w=[psm])
            self.mm(psm.ap[:, 96:128], cf["mE"].ap, adt_.ap, True, True, r=[cf["mE"], adt_], w=[psm])
            self.act(dex_.ap, psm.ap[:, 32:128], AF.Exp, r=[psm], w=[dex_])
            self.tt(V, dtd_.ap, dtt_.ap, dex_.ap[:, 32:64], ALU.mult, r=[dtt_, dex_], w=[dtd_])
            pst = self.PS[7]
            psb16 = self.PSB[7]
            for half in range(2):
                for q in range(8):
                    t_ = half * 8 + q
                    self.tr(psb16[:, q * 128:(q + 1) * 128], xi.ap[:, t_, :], self.identb.ap, r=[xi, self.identb], w=[pst])
                src = psb16[:, 0:1024].rearrange("p (h c) -> p h c", c=64)
                hs = slice(half * 16, (half + 1) * 16)
                bc = lambda col: col.unsqueeze(2).to_broadcast([128, 16, 64])
                self.tt(V, xdt[b].ap[:, hs, :], src, bc(dtt_.ap[:, hs]), ALU.mult, r=[pst, dtt_], w=[xdt[b]])
                self.tt(V, xdte[b].ap[:, hs, :], src, bc(dtd_.ap[:, hs]), ALU.mult, r=[pst, dtd_], w=[xdte[b]])
                if d == 0:
                    self.tt(V, dskt[b].ap[:, hs, :], src, bc(Drow.ap[:, hs]), ALU.mult, r=[pst, Drow], w=[dskt[b]])
            for g in range(8):
                self.tr(psb16[:, g * 128:(g + 1) * 128], bi.ap[:, g, :], self.identb.ap, r=[bi, self.identb], w=[pst])
            self.cp("scalar", Btok[b].ap, psb16[:, 0:1024].rearrange("p (g c) -> p g c", c=128), r=[pst], w=[Btok[b]])

        gi = [0]

        def s1(wi, g):
            d, c, first = work[wi]
            cf = cfgd[d]
            b = wi % NB
            k = (wi * 8 + g) % 2
            xh = Xh[k]
            for hh in range(4):
                h = g * 4 + hh
                self.ts(G, xh.ap[:, hh, :], cf["mX"].ap, adt[b].ap[:, h:h + 1], ALU.mult, 0.0, ALU.add, r=[cf["mX"], adt[b]], w=[xh])
            pcb = self.PS[k]
            self.mm(pcb.ap[:, 0:128], BTt[b].ap[:, g, :], CTt[b].ap[:, g, :], True, True, r=[BTt[b], CTt[b]], w=[pcb])
            psg = self.PS[4 + k]
            for hh in range(4):
                self.mm(psg.ap[:, hh * 128:(hh + 1) * 128], xh.ap[:, hh, :], cf["mSeg"].ap, True, True, r=[xh, cf["mSeg"]], w=[psg])

        def s2(wi, g):
            d, c, first = work[wi]
            cf = cfgd[d]
            k = (wi * 8 + g) % 2
            pcb, psg = self.PS[k], self.PS[4 + k]
            self.act(Ld[k].ap, psg.ap.rearrange("p (h c) -> p h c", c=128), AF.Exp, r=[psg], w=[Ld[k]])
            self.tt(V, CBm[k].ap, pcb.ap[:, 0:128], cf["mCB"].ap, ALU.mult, r=[pcb, cf["mCB"]], w=[CBm[k]])
            self.tt(V, Mt[k].ap, Ld[k].ap, CBm[k].ap.unsqueeze(1).to_broadcast([128, 4, 128]), ALU.mult, r=[Ld[k], CBm[k]], w=[Mt[k]])

        def s3(wi, g):
            d, c, first = work[wi]
            b = wi % NB
            k = (wi * 8 + g) % 2
            py = self.PS[2 + k]
            cti = CTt[b]
            for hh in range(4):
                h = g * 4 + hh
                self.mm(py.ap[:, hh * 64:(hh + 1) * 64], Mt[k].ap[:, hh, :], xdt[b].ap[:, h, :], True, True, r=[Mt[k], xdt[b]], w=[py])
                self.mm(py.ap[:, 256 + hh * 64:256 + (hh + 1) * 64], cti.ap[:, g, :], Hb.ap[:, h * 64:(h + 1) * 64], True, True, r=[cti, Hb], w=[py])
            gs = slice(g * 4, (g + 1) * 4)
            yt = ytmp[k]
            yb = ybuf[b]
            Eb = dex[b].ap[:, 64 + g * 4:64 + (g + 1) * 4].unsqueeze(2).to_broadcast([128, 4, 64])
            self.tt(V, yt.ap, py.ap[:, 256:512].rearrange("p (h c) -> p h c", c=64), Eb, ALU.mult, r=[py, dex[b]], w=[yt])
            if d == 0:
                self.tt(G, yt.ap, yt.ap, dskt[b].ap[:, gs, :], ALU.add, r=[yt, dskt[b]], w=[yt])
            self.tt(V, yb.ap[:, g * 256:(g + 1) * 256].rearrange("p (h c) -> p h c", c=64), py.ap[:, 0:256].rearrange("p (h c) -> p h c", c=64), yt.ap, ALU.add, r=[py, yt], w=[yb])
            pst2 = self.PS[6]
            self.mm(pst2.ap[:, 256:512], Btok[b].ap[:, g, :], xdte[b].ap[:, gs, :].rearrange("p h c -> p (h c)"), True, True, r=[Btok[b], xdte[b]], w=[pst2])
            ht = htmp[k]
            Db = dex[b].ap[:, g * 4:(g + 1) * 4].unsqueeze(2).to_broadcast([128, 4, 64])
            self.tt(G, ht.ap, H.ap[:, gs, :], Db, ALU.mult, r=[H, dex[b]], w=[ht])
            self.tt(V, H.ap[:, gs, :], pst2.ap[:, 256:512].rearrange("p (h c) -> p h c", c=64), ht.ap, ALU.add, r=[pst2, ht], w=[H])
            self.cp("scalar", Hb.ap[:, g * 256:(g + 1) * 256], H.ap[:, gs, :].rearrange("p h c -> p (h c)"), r=[H], w=[Hb])

        nw = len(work)
        prologue(0)
        s1(0, 0)
        for wi in range(nw):
            d, c, first = work[wi]
            if first:
                self.memset(V, H.ap, 0.0, [H])
                self.memset(V, Hb.ap, 0.0, [Hb])
            for g in range(8):
                if g == 4 and wi + 1 < nw:
                    prologue(wi + 1)
                if g + 1 < 8:
                    s1(wi, g + 1)
                elif wi + 1 < nw:
                    s1(wi + 1, 0)
                s2(wi, g)
                s3(wi, g)
            YO = YF if d == 0 else YB
            self.DMA("sync", YO.ap[c * 128:c * 128 + 128, :], ybuf[wi % NB].ap, r=[ybuf[wi % NB]], w=[YO])

    def ssd_phase_c(self, XT, P, SZ, YF, YB, l):
        self.new_phase()
        V, G = "vector", "gpsimd"
        wo = self.bf16(16 * D, "wo", shape=(16, D))
        self.load_w_bf16(wo, P["w_out"].ap[0], P["w_out"], 16)
        nrow = self.f32(2048, "nrow")
        rowtmp = self.f32(2048, "rowtmp2")
        self.row_bcast(nrow, P["norm"].ap[0:1, :], P["norm"], 2048, rowtmp)
        yf = [self.bf16(2048, "cyf%d" % i) for i in range(2)]
        ybk = [self.bf16(2048, "cyb%d" % i) for i in range(2)]
        sz = [self.bf16(2048, "csz%d" % i) for i in range(2)]
        yy = [self.f32(2048, "cyy%d" % i) for i in range(2)]
        sqj = self.f32(2048, "csq")
        gnb = [self.bf16(2048, "cgn%d" % i) for i in range(2)]
        ss = [self.f32(8, "css%d" % i) for i in range(2)]
        gnT = [self.bf16(16 * 512, "gnT%d" % i, shape=(16, 512)) for i in range(2)]
        xb = [self.f32(NFT * 512, "cx%d" % i, shape=(NFT, 512)) for i in range(2)]
        mod = self.mod[l]
        ti = 0
        for ci, (s, w, isctx) in enumerate(self.chunks()):
            cs = 1 if isctx else 0
            gT, xbi = gnT[ci % 2], xb[ci % 2]
            self.DMA("sync", xbi.ap[:, :, 0:w], XT.ap[:, s:s + w].rearrange("(ft p) t -> p ft t", p=128), r=[XT], w=[xbi])
            for tt in range(w // 128):
                t0 = s + tt * 128
                a, b, z_, y_, g_, s_ = yf[ti % 2], ybk[ti % 2], sz[ti % 2], yy[ti % 2], gnb[ti % 2], ss[ti % 2]
                ti += 1
                self.DMA("sync", a.ap, YF.ap[t0:t0 + 128, :], r=[YF], w=[a])
                self.DMA("sync", b.ap, YB.ap[t0:t0 + 128, :], r=[YB], w=[b])
                self.DMA("sync", z_.ap, SZ.ap[t0:t0 + 128, :], r=[SZ], w=[z_])
                self.tt(G, y_.ap, a.ap, b.ap, ALU.add, r=[a, b], w=[y_])
                self.tt(V, y_.ap, y_.ap, z_.ap, ALU.mult, r=[y_, z_], w=[y_])
                self.act(sqj.ap, y_.ap, AF.Square, r=[y_], w=[sqj, s_], accum=s_.ap[:, 0:1])
                self.act(s_.ap[:, 1:2], s_.ap[:, 0:1], AF.Sqrt, r=[s_, self.epsc], w=[s_], scale=1.0 / 2048, bias=self.epsc.ap[:, 0:1])
                self.E(V, lambda e, o=s_.ap[:, 2:3], i=s_.ap[:, 1:2]: e.reciprocal(out=o, in_=i), r=[s_], w=[s_])
                self.stt(g_.ap, y_.ap, s_.ap[:, 2:3], nrow.ap, ALU.mult, ALU.mult, r=[y_, s_, nrow], w=[g_])
                for half in range(2):
                    pst = self.PS[4 + (ti * 2 + half) % 4]
                    psb16 = self.PSB[4 + (ti * 2 + half) % 4]
                    for q in range(8):
                        kt = half * 8 + q
                        self.tr(psb16[:, q * 128:(q + 1) * 128], g_.ap[:, kt * 128:(kt + 1) * 128], self.identb.ap, r=[g_, self.identb], w=[pst])
                    self.cp("scalar" if half else "vector", gT.ap[:, half * 8:(half + 1) * 8, tt * 128:(tt + 1) * 128],
                            psb16[:, 0:1024].rearrange("p (k c) -> p k c", c=128), r=[pst], w=[gT])
            for fo in range(NFT):
                po = self.PS[fo % 4]
                for kt in range(16):
                    self.mm(po.ap[:, 0:w], wo.ap[:, kt, fo * 128:(fo + 1) * 128], gT.ap[:, kt, 0:w], kt == 0, kt == 15, r=[wo, gT], w=[po])
                self.stt(xbi.ap[:, fo, 0:w], po.ap[:, 0:w], mod.ap[:, 2, fo, cs:cs + 1], xbi.ap[:, fo, 0:w], ALU.mult, ALU.add, r=[po, mod, xbi], w=[xbi])
            self.DMA("sync", XT.ap[:, s:s + w].rearrange("(ft p) t -> p ft t", p=128), xbi.ap[:, :, 0:w], r=[xbi], w=[XT])


    def na_bias_table(self, rpb, BTD):
        self.new_phase()
        neg = self.f32(7680, "negt")
        self.memset("vector", neg.ap, -30000.0, [neg])
        self.DMA("sync", BTD.ap.rearrange("h w r c -> (h w r c)").rearrange("(p n) -> p n", p=128), neg.ap, r=[neg], w=[BTD])
        HS = 64 * 15 * 64
        for h in range(16):
            so = h * 465
            do = h * HS
            dst = bass.AP(BTD.ap.tensor, do + 8 * 960, [[64, 15], [961, 49], [1, 16]])
            src = bass.AP(rpb.ap.tensor, so + 7, [[31, 15], [0, 49], [1, 16]])
            self.DMA("sync", dst, src, r=[rpb], w=[BTD], slow=True)
            dst = bass.AP(BTD.ap.tensor, do, [[64, 15], [960, 8], [1, 16]])
            src = bass.AP(rpb.ap.tensor, so + 15, [[31, 15], [-1, 8], [1, 16]])
            self.DMA("sync", dst, src, r=[rpb], w=[BTD], slow=True)
            dst = bass.AP(BTD.ap.tensor, do + 57 * 960 + 48, [[64, 15], [960, 7], [1, 16]])
            src = bass.AP(rpb.ap.tensor, so + 6, [[31, 15], [-1, 7], [1, 16]])
            self.DMA("sync", dst, src, r=[rpb], w=[BTD], slow=True)

    def na_phase_a(self, HT, wqkv, QT, KT, VT):
        self.new_phase()
        Hres = self.bf16(8 * T, "Hres", shape=(8, T))
        for kt in range(8):
            self.DMA("sync", Hres.ap[:, kt, :], HT.ap[kt * 128:(kt + 1) * 128, :], r=[HT], w=[Hres])
        wv = self.bf16(8 * 1024, "wv", shape=(8, 1024))
        self.load_w_bf16(wv, wqkv.ap[0], wqkv, 8, 2048, 3072)
        vb = [self.bf16(1024, "vb%d" % i) for i in range(2)]
        for tt in range(T // 128):
            v_ = vb[tt % 2]
            for vc in range(2):
                ps = self.PS[vc + 2 * (tt % 2)]
                for kt in range(8):
                    self.mm(ps.ap, Hres.ap[:, kt, tt * 128:(tt + 1) * 128], wv.ap[:, kt, vc * 512:(vc + 1) * 512], kt == 0, kt == 7, r=[Hres, wv], w=[ps])
                self.cp("scalar" if vc else "vector", v_.ap[:, vc * 512:(vc + 1) * 512], ps.ap, r=[ps], w=[v_])
            self.DMA("sync", VT.ap[tt * 128:(tt + 1) * 128, :], v_.ap, r=[v_], w=[VT])
        wch = [self.bf16(8 * 512, "wq%d" % i, shape=(8, 512)) for i in range(2)]
        ob = [self.bf16(T, "qo%d" % i) for i in range(2)]
        chunks = self.chunks()
        pi = 0
        for wc in range(4):
            wb = wch[wc % 2]
            self.load_w_bf16(wb, wqkv.ap[0], wqkv, 8, wc * 512, (wc + 1) * 512)
            for q in range(4):
                ot = wc * 4 + q
                obi = ob[ot % 2]
                for ci, (s, w, isctx) in enumerate(chunks):
                    ps = self.PS[4 + pi % 4]
                    pi += 1
                    for kt in range(8):
                        self.mm(ps.ap[:, 0:w], wb.ap[:, kt, q * 128:(q + 1) * 128], Hres.ap[:, kt, s:s + w], kt == 0, kt == 7, r=[wb, Hres], w=[ps])
                    if ot < 8:
                        self.act(obi.ap[:, s:s + w], ps.ap[:, 0:w], AF.Copy, r=[ps], w=[obi], scale=0.125)
                    else:
                        self.cp("vector", obi.ap[:, s:s + w], ps.ap[:, 0:w], r=[ps], w=[obi])
                dst = QT.ap[ot * 128:(ot + 1) * 128, :] if ot < 8 else KT.ap[(ot - 8) * 128:(ot - 7) * 128, :]
                self.DMA("sync", dst, obi.ap, r=[obi], w=[QT if ot < 8 else KT])

    def na_phase_b(self, QT, KT, VT, BTD, YT):
        self.new_phase()
        V, G = "vector", "gpsimd"
        Qp = [self.bf16(T, "Qp%d" % i) for i in range(2)]
        Kp = [self.bf16(T, "Kp%d" % i) for i in range(2)]
        Ve = [self.bf16(34 * 128, "Ve%d" % i, shape=(34, 128)) for i in range(2)]
        Vo = [self.bf16(33 * 128, "Vo%d" % i, shape=(33, 128)) for i in range(2)]
        BTt = [self.f32(960, "BTt%d" % i) for i in range(2)]
        YTp = [self.bf16(T, "YTp%d" % i) for i in range(2)]
        Qbd = [self.bf16(128, "Qbd%d" % i) for i in range(2)]
        for q_ in Qbd:
            self.memset(G, q_.ap, 0.0, [q_])
        sc = [self.f32(768, "sc%d" % i) for i in range(2)]
        pe = [self.bf16(768, "pe%d" % i) for i in range(2)]
        pn = [self.bf16(768, "pn%d" % i) for i in range(2)]
        pT = [self.bf16(768, "pT%d" % i, shape=(6, 128)) for i in range(2)]
        st = [self.f32(8, "st%d" % i) for i in range(2)]
        blocks = []
        for hp in range(8):
            bl = [("c", i) for i in range(4)] + [("l", r) for r in range(64)]
            for bi, (kind, idx) in enumerate(bl):
                blocks.append((hp, kind, idx, bi == 0, bi == len(bl) - 1))

        def geom(kind, idx):
            if kind == "c":
                return idx * 64, 256, 0, 0
            r = idx
            start = min(max(r - 4, 0), 56)
            return LC + r * 64, 768, start - r + 7, LC + start * 64

        def load_pair(hp):
            h2 = hp % 2
            self.DMA("sync", Qp[h2].ap, QT.ap[hp * 128:(hp + 1) * 128, :], r=[QT], w=[Qp[h2]])
            self.DMA("sync", Kp[h2].ap, KT.ap[hp * 128:(hp + 1) * 128, :], r=[KT], w=[Kp[h2]])
            self.DMA("sync", Ve[h2].ap, VT.ap[:, hp * 128:(hp + 1) * 128].rearrange("(tt p) c -> p tt c", p=128), r=[VT], w=[Ve[h2]])
            self.DMA("sync", Vo[h2].ap, VT.ap[64:64 + 33 * 128, hp * 128:(hp + 1) * 128].rearrange("(tt p) c -> p tt c", p=128), r=[VT], w=[Vo[h2]])
            self.DMA("sync", BTt[h2].ap, BTD.ap[2 * hp:2 * hp + 2].rearrange("two w r c -> (two w) (r c)"), r=[BTD], w=[BTt[h2]])

        def s1(i):
            hp, kind, idx, pfirst, plast = blocks[i]
            h2 = hp % 2
            if pfirst:
                load_pair(hp)
            qpos, nk, ro0, kpos = geom(kind, idx)
            qb, sci, pei, sti = Qbd[i % 2], sc[i % 2], pe[i % 2], st[i % 2]
            ps_l, ps_c = self.PS[(i % 2) * 2], self.PS[(i % 2) * 2 + 1]
            self.cp(G, qb.ap[0:64, 0:64], Qp[h2].ap[0:64, qpos:qpos + 64], r=[Qp[h2]], w=[qb])
            self.cp(G, qb.ap[64:128, 64:128], Qp[h2].ap[64:128, qpos:qpos + 64], r=[Qp[h2]], w=[qb])
            if kind == "l":
                self.mm(ps_l.ap, qb.ap, Kp[h2].ap[:, kpos:kpos + 512], True, True, r=[qb, Kp[h2]], w=[ps_l])
                self.mm(ps_c.ap[:, 0:256], qb.ap, Kp[h2].ap[:, 0:256], True, True, r=[qb, Kp[h2]], w=[ps_c])
                self.tt(V, sci.ap[:, 0:512], ps_l.ap, BTt[h2].ap[:, ro0 * 64:ro0 * 64 + 512], ALU.add, r=[ps_l, BTt[h2]], w=[sci])
                self.cp("scalar", sci.ap[:, 512:768], ps_c.ap[:, 0:256], r=[ps_c], w=[sci])
            else:
                self.mm(ps_c.ap[:, 0:256], qb.ap, Kp[h2].ap[:, 0:256], True, True, r=[qb, Kp[h2]], w=[ps_c])
                self.cp("scalar", sci.ap[:, 0:256], ps_c.ap[:, 0:256], r=[ps_c], w=[sci])
            self.E(V, lambda e, o=sti.ap[:, 0:1], i_=sci.ap[:, 0:nk]: e.reduce_max(out=o, in_=i_, axis=AX.X), r=[sci], w=[sti])
            self.ts(V, sti.ap[:, 1:2], sti.ap[:, 0:1], -1.0, ALU.mult, r=[sti], w=[sti])
            self.act(pei.ap[:, 0:nk], sci.ap[:, 0:nk], AF.Exp, r=[sci, sti], w=[pei, sti], bias=sti.ap[:, 1:2], accum=sti.ap[:, 2:3])
            self.E(V, lambda e, o=sti.ap[:, 3:4], i_=sti.ap[:, 2:3]: e.reciprocal(out=o, in_=i_), r=[sti], w=[sti])

        def s2(i):
            hp, kind, idx, pfirst, plast = blocks[i]
            h2 = hp % 2
            qpos, nk, ro0, kpos = geom(kind, idx)
            pei, pni, pTi, sti = pe[i % 2], pn[i % 2], pT[i % 2], st[i % 2]
            pst, pstb = self.PS[4 + i % 2], self.PSB[4 + i % 2]
            pso = self.PS[6 + i % 2]
            self.ts(G, pni.ap[:, 0:nk], pei.ap[:, 0:nk], sti.ap[:, 3:4], ALU.mult, 0.0, ALU.add, r=[pei, sti], w=[pni])
            nkt = nk // 128
            for kt in range(nkt):
                self.tr(pstb[:, kt * 128:(kt + 1) * 128], pni.ap[:, kt * 128:(kt + 1) * 128], self.identb.ap, r=[pni, self.identb], w=[pst])
            self.cp("scalar", pTi.ap[:, 0:nkt, :], pstb[:, 0:nk].rearrange("p (k c) -> p k c", c=128), r=[pst], w=[pTi])
            for kt in range(nkt):
                if kind == "c":
                    vt = Ve[h2].ap[:, kt, :]
                elif kt >= 4:
                    vt = Ve[h2].ap[:, kt - 4, :]
                else:
                    tok0 = kpos + kt * 128
                    vt = Ve[h2].ap[:, tok0 // 128, :] if tok0 % 128 == 0 else Vo[h2].ap[:, (tok0 - 64) // 128, :]
                self.mm(pso.ap[:, 0:128], vt, pTi.ap[:, kt, :], kt == 0, kt == nkt - 1, r=[Ve[h2], Vo[h2], pTi], w=[pso])
            self.cp(V, YTp[h2].ap[0:64, qpos:qpos + 64], pso.ap[0:64, 0:64], r=[pso], w=[YTp[h2]])
            self.cp("scalar", YTp[h2].ap[64:128, qpos:qpos + 64], pso.ap[64:128, 64:128], r=[pso], w=[YTp[h2]])
            if plast:
                self.DMA("sync", YT.ap[hp * 128:(hp + 1) * 128, :], YTp[h2].ap, r=[YTp[h2]], w=[YT])

        nb_ = len(blocks)
        s1(0)
        for i in range(nb_):
            if i + 1 < nb_:
                s1(i + 1)
            s2(i)


def _build(cfg):
    nc = bass.Bass("TRN2", target_bir_lowering=False)
    kb = KB(nc, cfg)
    IN = lambda n, s: kb.dram_t(n, s, F32, kind="ExternalInput")
    x_in = IN("x", [LL, D])
    ctx_in = IN("ctx", [LC, D])
    c_in = IN("c", [D])
    cctx_in = IN("c_ctx", [D])
    ada_w = IN("ada_w", [DEPTH, D, 6 * D])
    ada_b = IN("ada_b", [DEPTH, 6 * D])
    norm_mix = IN("norm_mix", [DEPTH, D])
    norm_ffn = IN("norm_ffn", [DEPTH, D])
    norm_final = IN("norm_final", [D])
    w1 = IN("ffn_w1", [DEPTH, D, FH])
    w3 = IN("ffn_w3", [DEPTH, D, FH])
    w2 = IN("ffn_w2", [DEPTH, FH, D])
    S5 = {
        "lam_re": IN("s5_lam_re", [2, 2, 64, 64]), "lam_im": IN("s5_lam_im", [2, 2, 64, 64]),
        "log_step": IN("s5_log_step", [2, 2, 64]),
        "b_re": IN("s5_b_re", [2, 2, 64, 64, 16]), "b_im": IN("s5_b_im", [2, 2, 64, 64, 16]),
        "c_re": IN("s5_c_re", [2, 2, 64, 16, 64]), "c_im": IN("s5_c_im", [2, 2, 64, 16, 64]),
        "d": IN("s5_d", [2, D]), "w_glu": IN("s5_w_glu", [2, D, D]), "b_glu": IN("s5_b_glu", [2, D]),
    }
    SSD = {
        "w_in": IN("ssd_w_in", [1, D, 8256]), "conv_w": IN("ssd_conv_w", [1, 5, 6144]), "conv_b": IN("ssd_conv_b", [1, 6144]),
        "dt_bias": IN("ssd_dt_bias", [1, 2, 32]), "a_log": IN("ssd_a_log", [1, 2, 32]), "d": IN("ssd_d", [1, 32]),
        "norm": IN("ssd_norm", [1, 2048]), "w_out": IN("ssd_w_out", [1, 2048, D]),
    }
    NA = {"w_qkv": IN("na_w_qkv", [1, D, 3 * D]), "w_o": IN("na_w_o", [1, D, D]), "rpb": IN("na_rpb", [1, 16, 15, 31])}
    out_t = kb.dram_t("out", [LL, D], F32, kind="ExternalOutput")
    QT = kb.dram_t("QT", [D, T], BF16)
    KT = kb.dram_t("KT", [D, T], BF16)
    VT = kb.dram_t("VT", [T, D], BF16)
    YT = kb.dram_t("YT", [D, T], BF16)
    BTD = kb.dram_t("BTD", [16, 64, 15, 64], F32)
    SZ = kb.dram_t("SZ", [T, 2048], BF16)
    XS = kb.dram_t("XS", [2048, T], BF16)
    BC = kb.dram_t("BC", [4096, T], BF16)
    DTR = kb.dram_t("DTR", [64, T], F32)
    YF = kb.dram_t("YF", [T, 2048], BF16)
    YB = kb.dram_t("YB", [T, 2048], BF16)
    XT = kb.dram_t("XT", [D, T], F32)
    HT = kb.dram_t("HT", [D, T], BF16)
    GT = kb.dram_t("GT", [D, T], BF16)
    layers = cfg.get("layers", list(range(DEPTH)))
    kb.consts()
    kb.build_mask8()
    kb.adaln(c_in, cctx_in, ada_w, ada_b, norm_mix, norm_ffn, norm_final, layers)
    kb.prologue_transpose(x_in, ctx_in, XT)
    for l in layers:
        kind, j = l % 3, l // 3
        if cfg.get("mixer", True):
            kb.norm_layer(XT, HT, l, 0)
            if kind == 0:
                kb.s5_phase(HT, GT, S5, j)
                kb.proj_phase(XT, GT, S5["w_glu"], S5["w_glu"].ap[j], 8, l, glu_bias=(S5["b_glu"], S5["b_glu"].ap[j]))
            elif kind == 1:
                kb.ssd_phase_a(HT, SSD, SZ, XS, BC, DTR)
                kb.ssd_phase_b(SSD, XS, BC, DTR, YF, YB)
                kb.ssd_phase_c(XT, SSD, SZ, YF, YB, l)
            else:
                kb.na_bias_table(NA["rpb"], BTD)
                kb.na_phase_a(HT, NA["w_qkv"], QT, KT, VT)
                kb.na_phase_b(QT, KT, VT, BTD, YT)
                kb.proj_phase(XT, YT, NA["w_o"], NA["w_o"].ap[0], 8, l)
        if cfg.get("ffn", True):
            kb.norm_layer(XT, HT, l, 1)
            kb.ffn_phase(XT, HT, w1, w3, w2, l)
    kb.final_phase(XT, out_t)
    kb.p.fence("sync", kb.out_ops)
    kb.p.build()
    kb.es.close()
    return nc, kb


INPUT_NAMES = ["x", "ctx", "c", "c_ctx", "ada_w", "ada_b", "norm_mix", "norm_ffn", "norm_final", "ffn_w1", "ffn_w3", "ffn_w2",
               "s5_lam_re", "s5_lam_im", "s5_log_step", "s5_b_re", "s5_b_im", "s5_c_re", "s5_c_im", "s5_d", "s5_w_glu", "s5_b_glu",
               "ssd_w_in", "ssd_conv_w", "ssd_conv_b", "ssd_dt_bias", "ssd_a_log", "ssd_d", "ssd_norm", "ssd_w_out",
               "na_w_qkv", "na_w_o", "na_rpb"]


def kernel(**inputs):
    cfg = {}
    nc, kb = _build(cfg)
    n = 8
    in_maps = []
    for b in range(n):
        m = {}
        for k in INPUT_NAMES:
            v = np.ascontiguousarray(inputs[k], dtype=np.float32)
            if k in ("x", "ctx", "c"):
                v = np.ascontiguousarray(v[b])
            m[k] = v
        in_maps.append(m)
    res = run_bass_kernel_spmd(nc, in_maps, core_ids=list(range(n)))
    return np.stack([np.asarray(r["out"], dtype=np.float32) for r in res.results], axis=0)
```

```python
import numpy as np
from contextlib import ExitStack
import concourse.bass as bass
import concourse.mybir as mybir
from concourse.bass_utils import run_bass_kernel_spmd

F32 = mybir.dt.float32
BF16 = mybir.dt.bfloat16
I32 = mybir.dt.int32
AF = mybir.ActivationFunctionType
ALU = mybir.AluOpType
AX = mybir.AxisListType

ENGS = ("sync", "gpsimd", "scalar", "vector", "tensor")
NDMA_SEM = 24
SEM_EPOCH = 12000

D = 1024
LC = 256
LL = 4096
T = LC + LL
NFT = 8
FH = 2816
NHT = 22
DEPTH = 4
EPS = 1e-6
ARENA_F32 = 46 * 1024


class Buf:
    __slots__ = ("name", "w", "r", "psum")

    def __init__(self, name, psum=False):
        self.name = name
        self.w = None
        self.r = []
        self.psum = psum


class Op:
    __slots__ = ("eng", "fn", "idx", "deps", "need_inc", "val", "is_dma", "semi", "is_barrier")

    def __init__(self, eng, fn, is_dma):
        self.eng = eng
        self.fn = fn
        self.is_dma = is_dma
        self.deps = []
        self.need_inc = False
        self.val = 0
        self.semi = -1
        self.idx = -1
        self.is_barrier = False


class Prog:
    def __init__(self, nc):
        self.nc = nc
        self.ops = {e: [] for e in ENGS}
        self.nreal = {e: 0 for e in ENGS}
        self.es = ExitStack()
        self.engsem = {e: self.es.enter_context(nc.semaphore("s_" + e)) for e in ENGS}
        self.dmasem = [self.es.enter_context(nc.semaphore("d%d" % i)) for i in range(NDMA_SEM)]
        self.dma_last = [None] * NDMA_SEM
        self.dma_cnt = [0] * NDMA_SEM
        self.dma_rr = 0

    def emit(self, eng, fn, reads=(), writes=(), dma=False):
        op = Op(eng, fn, dma)
        op.idx = self.nreal[eng]
        self.nreal[eng] += 1
        deps = []
        for b in reads:
            if b.w is not None:
                deps.append(b.w)
            if b.psum:
                deps.extend(b.r)
        for b in writes:
            if b.w is not None:
                deps.append(b.w)
            deps.extend(b.r)
        if dma:
            k = self.dma_rr
            self.dma_rr = (k + 1) % NDMA_SEM
            if self.dma_last[k] is not None:
                deps.append(self.dma_last[k])
            self.dma_last[k] = op
            self.dma_cnt[k] += 16
            op.semi = k
            op.val = self.dma_cnt[k]
        seen = set()
        for d in deps:
            if d is op or id(d) in seen:
                continue
            seen.add(id(d))
            op.deps.append(d)
        for b in reads:
            if b.psum:
                b.w = op
                b.r = []
            else:
                b.r.append(op)
        for b in writes:
            b.w = op
            b.r = []
        self.ops[eng].append(op)
        return op

    def fence(self, eng, deps):
        op = Op(eng, None, False)
        op.idx = self.nreal[eng]
        op.deps = list(deps)
        self.ops[eng].append(op)
        return op

    def barrier(self):
        lasts = []
        for e in ENGS:
            for o in reversed(self.ops[e]):
                if o.fn is not None:
                    lasts.append(o)
                    break
        for o in self.dma_last:
            if o is not None:
                lasts.append(o)
        for e in ENGS:
            self.fence(e, lasts).is_barrier = True

    def _needs_wait(self, op, d):
        if d.is_dma:
            return True
        if d.eng != op.eng:
            return True
        if op.eng == "tensor":
            return False
        return (op.idx - d.idx) <= 2

    def build(self):
        nc = self.nc
        for e in ENGS:
            for op in self.ops[e]:
                for d in op.deps:
                    if not d.is_dma and self._needs_wait(op, d):
                        d.need_inc = True
        self.epoch_sems = {e: [self.engsem[e]] for e in ENGS}
        for e in ENGS:
            c = 0
            ep = 0
            for op in self.ops[e]:
                if op.fn is None:
                    if op.is_barrier and c > SEM_EPOCH:
                        ep += 1
                        c = 0
                        self.epoch_sems[e].append(self.es.enter_context(nc.semaphore("s_%s_%d" % (e, ep))))
                    continue
                if op.is_dma:
                    continue
                op.semi = ep
                if op.need_inc:
                    c += 1
                    op.val = c
        self.counts = {e: 0 for e in ENGS}

        def mk_body(e):
            def body(eng):
                waited = {}
                for op in self.ops[e]:
                    for d in op.deps:
                        if not self._needs_wait(op, d):
                            continue
                        if d.is_dma:
                            key = ("d", d.semi)
                            sem = self.dmasem[d.semi]
                        else:
                            key = ("e", d.eng, d.semi)
                            sem = self.epoch_sems[d.eng][d.semi]
                        if waited.get(key, 0) >= d.val:
                            continue
                        waited[key] = d.val
                        eng.wait_ge(sem, d.val)
                    if op.fn is None:
                        continue
                    inst = op.fn(eng)
                    self.counts[e] += 1
                    if op.is_dma:
                        inst.then_inc(self.dmasem[op.semi], 16)
                    elif op.need_inc:
                        inst.then_inc(self.epoch_sems[e][op.semi], 1)
            return body

        with nc.Block() as block:
            for e in ENGS:
                if self.ops[e]:
                    getattr(block, e)(mk_body(e))
        self.es.close()


class Tl:
    __slots__ = ("ap", "buf")

    def __init__(self, ap, buf):
        self.ap = ap
        self.buf = buf


class KB:
    def __init__(self, nc, cfg):
        self.nc = nc
        self.cfg = cfg
        self.p = Prog(nc)
        self.es = ExitStack()
        self.arena = self.es.enter_context(nc.sbuf_tensor("arena", [128, ARENA_F32], F32))
        self.arena_bf = self.arena.bitcast(BF16)
        self.psum = [self.es.enter_context(nc.psum_tensor("ps%d" % i, [128, 512], F32)) for i in range(8)]
        self.PS = [Tl(self.psum[i][:], Buf("ps%d" % i, psum=True)) for i in range(8)]
        self.PSB = [self.psum[i].bitcast(BF16) for i in range(8)]
        self.top = 0
        self.ptr = 0
        self.nb = 0
        self.dram = {}
        self.out_ops = []

    def _al(self, n_f32, persistent):
        n_f32 = (n_f32 + 7) // 8 * 8
        if persistent:
            assert self.ptr == self.top, "persistent alloc only between phases"
            off = self.top
            self.top += n_f32
            self.ptr = self.top
        else:
            off = self.ptr
            self.ptr += n_f32
        assert self.ptr <= ARENA_F32, "arena overflow %d" % self.ptr
        return off

    def f32(self, n, name=None, persistent=False, shape=None):
        off = self._al(n, persistent)
        ap = self.arena[:, off:off + n]
        if shape is not None:
            ap = self._reshape(ap, shape)
        self.nb += 1
        return Tl(ap, Buf(name or "t%d" % self.nb))

    def bf16(self, n, name=None, persistent=False, shape=None):
        off = self._al((n + 1) // 2, persistent)
        ap = self.arena_bf[:, 2 * off:2 * off + n]
        if shape is not None:
            ap = self._reshape(ap, shape)
        self.nb += 1
        return Tl(ap, Buf(name or "t%d" % self.nb))

    def i32(self, n, name=None):
        off = self._al(n, False)
        ap = self.arena.bitcast(I32)[:, off:off + n]
        self.nb += 1
        return Tl(ap, Buf(name or "t%d" % self.nb))

    @staticmethod
    def _reshape(ap, shape):
        if len(shape) == 2:
            return ap.rearrange("p (a b) -> p a b", b=shape[1])
        if len(shape) == 3:
            return ap.rearrange("p (a b c) -> p a b c", b=shape[1], c=shape[2])
        raise ValueError

    def new_phase(self):
        self.p.barrier()
        self.ptr = self.top

    def dram_t(self, name, shape, dt, kind="Internal"):
        t = self.nc.dram_tensor(name, shape, dt, kind=kind)
        tl = Tl(t.ap(), Buf(name))
        self.dram[name] = tl
        return tl

    def E(self, eng, fn, r=(), w=()):
        return self.p.emit(eng, fn, [t.buf for t in r], [t.buf for t in w])

    def DMA(self, eng, out_ap, in_ap, r=(), w=(), slow=False):
        if slow:
            fn = lambda e: e.dma_start(out=out_ap, in_=in_ap, allow_slow_non_contiguous=True)
        else:
            fn = lambda e: e.dma_start(out=out_ap, in_=in_ap)
        return self.p.emit(eng, fn, [t.buf for t in r], [t.buf for t in w], dma=True)

    def mm(self, ps_ap, lhsT, rhs, start, stop, r=(), w=()):
        return self.E("tensor", lambda e: e.matmul(ps_ap, lhsT=lhsT, rhs=rhs, start=start, stop=stop), r, w)

    def tr(self, ps_ap, in_ap, ident_ap, r=(), w=()):
        return self.E("tensor", lambda e: e.transpose(out=ps_ap, in_=in_ap, identity=ident_ap), r, w)

    def act(self, out, in_, func, r=(), w=(), scale=None, bias=None, accum=None):
        kw = {}
        if scale is not None:
            kw["scale"] = scale
        if bias is not None:
            kw["bias"] = bias
        if accum is not None:
            kw["accum_out"] = accum
        return self.E("scalar", lambda e: e.activation(out=out, in_=in_, func=func, **kw), r, w)

    def tt(self, eng, out, in0, in1, op, r=(), w=()):
        return self.E(eng, lambda e: e.tensor_tensor(out=out, in0=in0, in1=in1, op=op), r, w)

    def ts(self, eng, out, in0, s1, op0, s2=None, op1=None, r=(), w=(), accum=None):
        if op1 is None:
            return self.E(eng, lambda e: e.tensor_scalar(out=out, in0=in0, scalar1=s1, scalar2=None, op0=op0), r, w)
        if accum is not None:
            return self.E(eng, lambda e: e.tensor_scalar(out=out, in0=in0, scalar1=s1, scalar2=s2, op0=op0, op1=op1, accum_out=accum), r, w)
        return self.E(eng, lambda e: e.tensor_scalar(out=out, in0=in0, scalar1=s1, scalar2=s2, op0=op0, op1=op1), r, w)

    def stt(self, out, in0, scalar, in1, op0, op1, r=(), w=()):
        return self.E("vector", lambda e: e.scalar_tensor_tensor(out=out, in0=in0, scalar=scalar, in1=in1, op0=op0, op1=op1), r, w)

    def cp(self, eng, out, in_, r=(), w=()):
        if eng == "scalar":
            return self.act(out, in_, AF.Copy, r, w)
        return self.E(eng, lambda e: e.tensor_copy(out=out, in_=in_), r, w)

    def memset(self, eng, ap, val, w=()):
        return self.E(eng, lambda e: e.memset(ap, val), (), w)

    def consts(self):
        self.ident = self.f32(128, "ident", True)
        self.identb = self.bf16(128, "identb", True)
        self.ones = self.f32(128, "ones", True)
        self.epsc = self.f32(8, "epsc", True)
        self.memset("gpsimd", self.ident.ap, 0.0, [self.ident])
        idap = self.ident.ap
        self.E("gpsimd", lambda e: e.affine_select(out=idap, in_=idap, pattern=[[-1, 128]], compare_op=ALU.not_equal,
                                                   fill=1.0, base=0, channel_multiplier=1), [self.ident], [self.ident])
        self.cp("vector", self.identb.ap, self.ident.ap, [self.ident], [self.identb])
        self.memset("vector", self.ones.ap, 1.0, [self.ones])
        self.memset("vector", self.epsc.ap[:, 0:1], EPS, [self.epsc])
        self.memset("vector", self.epsc.ap[:, 1:2], 0.0, [self.epsc])
        self.memset("vector", self.epsc.ap[:, 2:3], 1.0, [self.epsc])

    @staticmethod
    def chunks(w_lat=512):
        ch = [(0, LC, True)]
        for s in range(LC, T, w_lat):
            ch.append((s, w_lat, False))
        return ch

    def prologue_transpose(self, x_in, ctx_in, XT):
        self.new_phase()
        xin = [self.f32(4 * D, "xin%d" % i, shape=(4, D)) for i in range(2)]
        stage = [self.f32(NFT * 512, "stg%d" % i, shape=(NFT, 512)) for i in range(2)]
        for ci, (s, w, isctx) in enumerate(self.chunks()):
            xi = xin[ci % 2]
            st = stage[ci % 2]
            ntt = w // 128
            if isctx:
                src = ctx_in.ap.rearrange("(tt p) f -> p tt f", p=128)
            else:
                src = x_in.ap[s - LC:s - LC + w, :].rearrange("(tt p) f -> p tt f", p=128)
            self.DMA("sync", xi.ap[:, 0:ntt, :], src, r=[x_in], w=[xi])
            for ft in range(NFT):
                ps = self.PS[ft]
                for tt in range(ntt):
                    self.tr(ps.ap[:, tt * 128:(tt + 1) * 128], xi.ap[:, tt, ft * 128:(ft + 1) * 128], self.ident.ap,
                            r=[xi, self.ident], w=[ps])
                self.cp("scalar" if ft % 2 else "vector", st.ap[:, ft, 0:w], ps.ap[:, 0:w], r=[ps], w=[st])
            dst = XT.ap[:, s:s + w].rearrange("(ft p) t -> p ft t", p=128)
            self.DMA("sync", dst, st.ap[:, :, 0:w], r=[st], w=[XT])

    def adaln(self, c_in, cctx_in, ada_w, ada_b, norm_mix, norm_ffn, norm_final, layers):
        self.mod = {}
        for l in layers:
            self.mod[l] = self.f32(96, "mod%d" % l, True, shape=(6, 8, 2))
        self.nw = self.f32(9 * 8, "nw", True, shape=(9, 8))
        self.AB = {}
        for l in layers:
            self.AB[l] = self.f32(4 * 16, "AB%d" % l, True, shape=(4, 8, 2))
        self.new_phase()
        sT = self.f32(16, "sT", shape=(8, 2))
        craw = self.f32(16, "craw", shape=(8, 2))
        self.DMA("sync", craw.ap[:, :, 0], c_in.ap.rearrange("(kt p) -> p kt", p=128), r=[c_in], w=[craw], slow=True)
        self.DMA("sync", craw.ap[:, :, 1], cctx_in.ap.rearrange("(kt p) -> p kt", p=128), r=[cctx_in], w=[craw], slow=True)
        self.act(sT.ap, craw.ap, AF.Silu, r=[craw], w=[sT])
        for k, nwt in enumerate([norm_mix, norm_ffn]):
            self.DMA("sync", self.nw.ap[:, 4 * k:4 * k + 4, :], nwt.ap.rearrange("l (ft p) -> p l ft", p=128), r=[nwt], w=[self.nw], slow=True)
        self.DMA("sync", self.nw.ap[:, 8, :], norm_final.ap.rearrange("(ft p) -> p ft", p=128), r=[norm_final], w=[self.nw], slow=True)
        wbuf = [self.f32(8 * 512, "adaw%d" % i, shape=(8, 512)) for i in range(3)]
        bbuf = [self.f32(512, "adab%d" % i) for i in range(3)]
        onesrow = self.ones.ap[0:1, 0:2]
        it = 0
        for l in layers:
            for cj in range(12):
                wb = wbuf[it % 3]
                bb = bbuf[it % 3]
                ps = self.PS[it % 4]
                it += 1
                self.DMA("sync", wb.ap, ada_w.ap[l, :, cj * 512:(cj + 1) * 512].rearrange("(kt p) n -> p kt n", p=128), r=[ada_w], w=[wb])
                self.DMA("sync", bb.ap[0:1, :], ada_b.ap[l:l + 1, cj * 512:(cj + 1) * 512], r=[ada_b], w=[bb])
                for jj in range(4):
                    j = cj * 4 + jj
                    o = ps.ap[:, jj * 2:jj * 2 + 2]
                    for kt in range(8):
                        self.mm(o, wb.ap[:, kt, jj * 128:(jj + 1) * 128], sT.ap[:, kt, :], kt == 0, False, r=[wb, sT], w=[ps])
                    self.mm(o, bb.ap[0:1, jj * 128:(jj + 1) * 128], onesrow, False, True, r=[bb, self.ones], w=[ps])
                m = cj * 4 // 8
                ft0 = (cj * 4) % 8
                self.cp("vector", self.mod[l].ap[:, m, ft0:ft0 + 4, :], ps.ap[:, 0:8].rearrange("p (a b) -> p a b", b=2), r=[ps], w=[self.mod[l]])
        for l in layers:
            for k, (mi, nwi) in enumerate([(1, l), (4, 4 + l)]):
                nwb = self.nw.ap[:, nwi, :].unsqueeze(2).to_broadcast([128, 8, 2])
                self.stt(self.AB[l].ap[:, k, :, :], self.mod[l].ap[:, mi, :, :], 1.0, nwb, ALU.add, ALU.mult, r=[self.mod[l], self.nw], w=[self.AB[l]])

    def norm_phase(self, XT, HT, A_sel, B_sel, deps_r):
        self.new_phase()
        xin = [self.f32(NFT * 512, "nx%d" % i, shape=(NFT, 512)) for i in range(2)]
        sq = [self.f32(NFT * 512, "nsq%d" % i, shape=(NFT, 512)) for i in range(2)]
        hb = [self.bf16(NFT * 512, "nh%d" % i, shape=(NFT, 512)) for i in range(2)]
        rt = [self.f32(512, "nrt%d" % i) for i in range(2)]
        for ci, (s, w, isctx) in enumerate(self.chunks()):
            xi, sqi, hbi, rti = xin[ci % 2], sq[ci % 2], hb[ci % 2], rt[ci % 2]
            ps = self.PS[ci % 2]
            self.DMA("sync", xi.ap[:, :, 0:w], XT.ap[:, s:s + w].rearrange("(ft p) t -> p ft t", p=128), r=[XT], w=[xi])
            self.act(sqi.ap[:, :, 0:w], xi.ap[:, :, 0:w], AF.Square, r=[xi], w=[sqi])
            for ft in range(NFT):
                self.mm(ps.ap[:, 0:w], self.ones.ap, sqi.ap[:, ft, 0:w], ft == 0, ft == NFT - 1, r=[self.ones, sqi], w=[ps])
            self.act(rti.ap[:, 0:w], ps.ap[:, 0:w], AF.Sqrt, r=[ps, self.epsc], w=[rti], scale=1.0 / D, bias=self.epsc.ap[:, 0:1])
            self.E("vector", lambda e, o=rti.ap[:, 0:w]: e.reciprocal(out=o, in_=o), r=[rti], w=[rti])
            for ft in range(NFT):
                a = A_sel(ft, isctx)
                b = B_sel(ft, isctx)
                self.stt(sqi.ap[:, ft, 0:w], xi.ap[:, ft, 0:w], a, rti.ap[:, 0:w], ALU.mult, ALU.mult, r=[xi, rti] + deps_r, w=[sqi])
                self.act(hbi.ap[:, ft, 0:w], sqi.ap[:, ft, 0:w], AF.Identity, r=[sqi] + deps_r, w=[hbi], bias=b)
            self.DMA("sync", HT.ap[:, s:s + w].rearrange("(ft p) t -> p ft t", p=128), hbi.ap[:, :, 0:w], r=[hbi], w=[HT])

    def norm_layer(self, XT, HT, l, which):
        AB = self.AB[l]
        mod = self.mod[l]
        k = 0 if which == 0 else 1
        smi = 0 if which == 0 else 3
        A_sel = lambda ft, isctx: AB.ap[:, k, ft, (1 if isctx else 0):(1 if isctx else 0) + 1]
        B_sel = lambda ft, isctx: mod.ap[:, smi, ft, (1 if isctx else 0):(1 if isctx else 0) + 1]
        self.norm_phase(XT, HT, A_sel, B_sel, [AB, mod])

    def final_phase(self, XT, out_t):
        self.new_phase()
        xin = [self.f32(NFT * 512, "fx%d" % i, shape=(NFT, 512)) for i in range(2)]
        sq = [self.f32(NFT * 512, "fsq%d" % i, shape=(NFT, 512)) for i in range(2)]
        rt = [self.f32(512, "frt%d" % i) for i in range(2)]
        ob = [self.f32(4 * D, "fo%d" % i, shape=(4, D)) for i in range(2)]
        ci = 0
        for (s, w, isctx) in self.chunks():
            if isctx:
                continue
            xi, sqi, rti, obi = xin[ci % 2], sq[ci % 2], rt[ci % 2], ob[ci % 2]
            ps = self.PS[ci % 2]
            ci += 1
            self.DMA("sync", xi.ap, XT.ap[:, s:s + w].rearrange("(ft p) t -> p ft t", p=128), r=[XT], w=[xi])
            self.act(sqi.ap, xi.ap, AF.Square, r=[xi], w=[sqi])
            for ft in range(NFT):
                self.mm(ps.ap, self.ones.ap, sqi.ap[:, ft, :], ft == 0, ft == NFT - 1, r=[self.ones, sqi], w=[ps])
            self.act(rti.ap, ps.ap, AF.Sqrt, r=[ps, self.epsc], w=[rti], scale=1.0 / D, bias=self.epsc.ap[:, 0:1])
            self.E("vector", lambda e, o=rti.ap: e.reciprocal(out=o, in_=o), r=[rti], w=[rti])
            for ft in range(NFT):
                self.stt(sqi.ap[:, ft, :], xi.ap[:, ft, :], self.nw.ap[:, 8, ft:ft + 1], rti.ap, ALU.mult, ALU.mult, r=[xi, rti, self.nw], w=[sqi])
            for tt in range(4):
                for half in range(2):
                    pso = self.PS[2 + (tt * 2 + half) % 6]
                    for q in range(4):
                        ft = half * 4 + q
                        self.tr(pso.ap[:, q * 128:(q + 1) * 128], sqi.ap[:, ft, tt * 128:(tt + 1) * 128], self.ident.ap, r=[sqi, self.ident], w=[pso])
                    self.cp("scalar" if half else "vector", obi.ap[:, tt, half * 512:(half + 1) * 512], pso.ap, r=[pso], w=[obi])
            dst = out_t.ap[s - LC:s - LC + w, :].rearrange("(tt p) f -> p tt f", p=128)
            self.out_ops.append(self.DMA("sync", dst, obi.ap, r=[obi], w=[out_t]))

    def load_w_bf16(self, dst, src_ap, src_tl, nkt, col0=None, col1=None):
        for kt in range(nkt):
            s = src_ap[kt * 128:(kt + 1) * 128, :] if col0 is None else src_ap[kt * 128:(kt + 1) * 128, col0:col1]
            self.DMA("gpsimd", dst.ap[:, kt, :], s, r=[src_tl], w=[dst])

    def ffn_phase(self, XT, HT, w1, w3, w2, l):
        self.new_phase()
        W = 256
        w1s = self.bf16(8 * FH, "w1s", shape=(8, FH))
        w3s = self.bf16(8 * FH, "w3s", shape=(8, FH))
        w2s = self.bf16(NHT * D, "w2s", shape=(NHT, D))
        self.load_w_bf16(w1s, w1.ap[l], w1, 8)
        self.load_w_bf16(w3s, w3.ap[l], w3, 8)
        self.load_w_bf16(w2s, w2.ap[l], w2, NHT)
        hb = [self.bf16(NFT * W, "fh%d" % i, shape=(NFT, W)) for i in range(2)]
        xb = [self.f32(NFT * W, "fxx%d" % i, shape=(NFT, W)) for i in range(2)]
        gb = [self.bf16(NHT * W, "fg%d" % i, shape=(NHT, W)) for i in range(1)]
        sl = [self.bf16(W, "fs%d" % i) for i in range(2)]
        mod = self.mod[l]
        ntile = T // W
        for ti in range(ntile):
            s = ti * W
            isctx = s < LC
            cs = 1 if isctx else 0
            hbi, xbi, gbi = hb[ti % 2], xb[ti % 2], gb[0]
            self.DMA("sync", hbi.ap, HT.ap[:, s:s + W].rearrange("(ft p) t -> p ft t", p=128), r=[HT], w=[hbi])
            self.DMA("sync", xbi.ap, XT.ap[:, s:s + W].rearrange("(ft p) t -> p ft t", p=128), r=[XT], w=[xbi])
            for j in range(NHT):
                pa = self.PS[(j % 2) * 2]
                pb = self.PS[(j % 2) * 2 + 1]
                for kt in range(8):
                    self.mm(pa.ap[:, 0:W], w1s.ap[:, kt, j * 128:(j + 1) * 128], hbi.ap[:, kt, :], kt == 0, kt == 7, r=[w1s, hbi], w=[pa])
                for kt in range(8):
                    self.mm(pb.ap[:, 0:W], w3s.ap[:, kt, j * 128:(j + 1) * 128], hbi.ap[:, kt, :], kt == 0, kt == 7, r=[w3s, hbi], w=[pb])
                sli = sl[j % 2]
                self.act(sli.ap, pa.ap[:, 0:W], AF.Silu, r=[pa], w=[sli])
                self.tt("vector", gbi.ap[:, j, :], pb.ap[:, 0:W], sli.ap, ALU.mult, r=[pb, sli], w=[gbi])
            for fo in range(NFT):
                po = self.PS[4 + fo % 4]
                for j in range(NHT):
                    self.mm(po.ap[:, 0:W], w2s.ap[:, j, fo * 128:(fo + 1) * 128], gbi.ap[:, j, :], j == 0, j == NHT - 1, r=[w2s, gbi], w=[po])
                self.stt(xbi.ap[:, fo, :], po.ap[:, 0:W], mod.ap[:, 5, fo, cs:cs + 1], xbi.ap[:, fo, :], ALU.mult, ALU.add, r=[po, mod, xbi], w=[xbi])
            self.DMA("sync", XT.ap[:, s:s + W].rearrange("(ft p) t -> p ft t", p=128), xbi.ap, r=[xbi], w=[XT])

    def rev_ap(self, ap2d, start, n):
        pstride = ap2d.ap[0][0]
        return bass.AP(ap2d.tensor, ap2d.offset + start + n - 1, [[pstride, 128], [-1, n]])

    def sincos_turns(self, eng, turns, n, osin, ocos, tmp, cast_eng="vector"):
        ti, tf, fr, s2, s4 = tmp["ti"], tmp["tf"], tmp["fr"], tmp["s2"], tmp["s4"]
        sl = lambda t: t.ap[:, 0:n]
        self.cp(cast_eng, sl(ti), sl(turns), r=[turns], w=[ti])
        self.cp(cast_eng, sl(tf), sl(ti), r=[ti], w=[tf])
        self.tt(eng, sl(fr), sl(turns), sl(tf), ALU.subtract, r=[turns, tf], w=[fr])
        self.act(sl(s2), sl(fr), AF.Sin, r=[fr], w=[s2], scale=float(np.pi))
        self.act(sl(s4), sl(fr), AF.Sin, r=[fr], w=[s4], scale=float(np.pi / 2))
        self.tt(eng, sl(s4), sl(s4), sl(s4), ALU.mult, r=[s4], w=[s4])
        self.ts(eng, sl(s4), sl(s4), -4.0, ALU.mult, 2.0, ALU.add, r=[s4], w=[s4])
        self.tt(eng, sl(osin), sl(s2), sl(s4), ALU.mult, r=[s2, s4], w=[osin])
        self.tt(eng, sl(s2), sl(s2), sl(s2), ALU.mult, r=[s2], w=[s2])
        self.ts(eng, sl(ocos), sl(s2), -2.0, ALU.mult, 1.0, ALU.add, r=[s2], w=[ocos])

    def build_mask8(self):
        self.mask8 = self.f32(8, "mask8", True)
        m = self.mask8.ap
        self.memset("gpsimd", m, 1.0, [self.mask8])
        self.E("gpsimd", lambda e: e.affine_select(out=m, in_=m, pattern=[[-16, 8]], compare_op=ALU.is_ge, fill=0.0, base=0, channel_multiplier=1), [self.mask8], [self.mask8])
        self.E("gpsimd", lambda e: e.affine_select(out=m, in_=m, pattern=[[16, 8]], compare_op=ALU.is_ge, fill=0.0, base=15, channel_multiplier=-1), [self.mask8], [self.mask8])

    def s5_phase(self, HT, GT, P, j):
        self.new_phase()
        V, G = "vector", "gpsimd"
        def sc_tile(nm):
            return self.f32(64, nm)
        lr, li, ls = sc_tile("lr"), sc_tile("li"), sc_tile("ls")
        for d in range(2):
            self.DMA("sync", lr.ap[:, d * 32:(d + 1) * 32], P["lam_re"].ap[j, d].rearrange("(p two) n -> (two n) p", two=2), r=[P["lam_re"]], w=[lr], slow=True)
            self.DMA("sync", li.ap[:, d * 32:(d + 1) * 32], P["lam_im"].ap[j, d].rearrange("(p two) n -> (two n) p", two=2), r=[P["lam_im"]], w=[li], slow=True)
        lsrow = self.f32(128, "lsrow")
        self.DMA("sync", lsrow.ap[0:1, :], P["log_step"].ap[j:j + 1].rearrange("o d g -> o (d g)"), r=[P["log_step"]], w=[lsrow])
        psb = self.PS[0]
        self.mm(psb.ap[:, 0:128], self.ones.ap[0:1, :], lsrow.ap[0:1, :], True, True, r=[self.ones, lsrow], w=[psb])
        for d in range(2):
            src = psb.ap[:, d * 64:(d + 1) * 64].rearrange("q (p two) -> q p two", two=2)
            self.cp(V, ls.ap[0:64, d * 32:(d + 1) * 32], src[0:64, :, 0], r=[psb], w=[ls])
            self.cp(V, ls.ap[64:128, d * 32:(d + 1) * 32], src[64:128, :, 1], r=[psb], w=[ls])
        step, zr, zi, rr, tq = sc_tile("step"), sc_tile("zr"), sc_tile("zi"), sc_tile("rr"), sc_tile("tq")
        tmp = {"ti": self.i32(512, "ti"), "tf": self.f32(512, "tf"), "fr": self.f32(512, "fr"), "s2": self.f32(512, "s2"), "s4": self.f32(512, "s4")}
        tmp2 = {"ti": self.i32(512, "ti2"), "tf": self.f32(512, "tf2"), "fr": self.f32(512, "fr2"), "s2": self.f32(512, "s22"), "s4": self.f32(512, "s42")}
        sphi, cphi, frac = sc_tile("sphi"), sc_tile("cphi"), sc_tile("frac")
        self.act(step.ap, ls.ap, AF.Exp, r=[ls], w=[step])
        self.tt(V, zr.ap, lr.ap, step.ap, ALU.mult, r=[lr, step], w=[zr])
        self.tt(V, zi.ap, li.ap, step.ap, ALU.mult, r=[li, step], w=[zi])
        self.act(rr.ap, zr.ap, AF.Exp, r=[zr], w=[rr])
        self.ts(V, tq.ap, zi.ap, float(1.0 / (2 * np.pi)), ALU.mult, r=[zi], w=[tq])
        self.sincos_turns(V, tq, 64, sphi, cphi, tmp)
        self.cp(V, frac.ap, tmp["fr"].ap[:, 0:64], r=[tmp["fr"]], w=[frac])
        carry = {}
        for Q in (256, 512):
            tQ, sQ, cQ = sc_tile("tQ%d" % Q), sc_tile("sQ%d" % Q), sc_tile("cQ%d" % Q)
            self.ts(V, tQ.ap, frac.ap, float(Q), ALU.mult, r=[frac], w=[tQ])
            self.sincos_turns(V, tQ, 64, sQ, cQ, tmp)
            carry[Q] = (sQ, cQ)
        ar, ai, den, u, cr, ci, t1s, t2s = [sc_tile(n) for n in ("ar", "ai", "den", "u", "cr", "ci", "t1s", "t2s")]
        self.tt(V, ar.ap, rr.ap, cphi.ap, ALU.mult, r=[rr, cphi], w=[ar])
        self.tt(V, ai.ap, rr.ap, sphi.ap, ALU.mult, r=[rr, sphi], w=[ai])
        self.tt(V, t1s.ap, lr.ap, lr.ap, ALU.mult, r=[lr], w=[t1s])
        self.tt(V, t2s.ap, li.ap, li.ap, ALU.mult, r=[li], w=[t2s])
        self.tt(V, den.ap, t1s.ap, t2s.ap, ALU.add, r=[t1s, t2s], w=[den])
        self.E(V, lambda e: e.reciprocal(out=den.ap, in_=den.ap), r=[den], w=[den])
        self.ts(V, u.ap, ar.ap, -1.0, ALU.add, r=[ar], w=[u])
        self.tt(V, t1s.ap, u.ap, lr.ap, ALU.mult, r=[u, lr], w=[t1s])
        self.tt(V, t2s.ap, ai.ap, li.ap, ALU.mult, r=[ai, li], w=[t2s])
        self.tt(V, t1s.ap, t1s.ap, t2s.ap, ALU.add, r=[t1s, t2s], w=[t1s])
        self.tt(V, cr.ap, t1s.ap, den.ap, ALU.mult, r=[t1s, den], w=[cr])
        self.tt(V, t1s.ap, ai.ap, lr.ap, ALU.mult, r=[ai, lr], w=[t1s])
        self.tt(V, t2s.ap, u.ap, li.ap, ALU.mult, r=[u, li], w=[t2s])
        self.tt(V, t1s.ap, t1s.ap, t2s.ap, ALU.subtract, r=[t1s, t2s], w=[t1s])
        self.tt(V, ci.ap, t1s.ap, den.ap, ALU.mult, r=[t1s, den], w=[ci])
        braw = {}
        for nm in ("b_re", "b_im"):
            braw[nm] = self.f32(2 * 32 * 16, "braw_" + nm, shape=(2, 32, 16))
            for d in range(2):
                self.DMA("sync", braw[nm].ap[:, d, :, :], P[nm].ap[j, d].rearrange("(p two) n h -> (two n) p h", two=2), r=[P[nm]], w=[braw[nm]], slow=True)
        dsk = self.f32(8, "dsk")
        self.DMA("sync", dsk.ap, P["d"].ap[j].rearrange("(ft p) -> p ft", p=128), r=[P["d"]], w=[dsk], slow=True)
        M1 = {}
        for k in range(4):
            for nm in ("b_re", "b_im"):
                M1[(k, nm)] = self.f32(128, "M1_%d%s" % (k, nm))
                self.memset(G, M1[(k, nm)].ap, 0.0, [M1[(k, nm)]])
        Jrow = self.f32(512, "Jrow")
        self.E(G, lambda e: e.iota(Jrow.ap, pattern=[[1, 512]], base=0, channel_multiplier=0, allow_small_or_imprecise_dtypes=True), (), [Jrow])
        WTS = [self.bf16(8 * 6 * 128, "wts%d" % i, shape=(8, 6, 128)) for i in range(2)]
        craw = [[self.f32(64, "craw%d_%d" % (i, q)) for q in range(2)] for i in range(2)]
        Spair = [self.f32(128, "Spair%d" % i) for i in range(2)]
        U = [self.bf16(T, "U%d" % i) for i in range(2)]
        Yacc = self.f32(T, "Yacc")
        gt = self.bf16(T, "gt")
        cmr, cpr, ncpr = sc_tile("cmr"), sc_tile("cpr"), sc_tile("ncpr")
        self.tt(V, cmr.ap, ci.ap, cr.ap, ALU.subtract, r=[ci, cr], w=[cmr])
        self.tt(V, cpr.ap, ci.ap, cr.ap, ALU.add, r=[ci, cr], w=[cpr])
        self.ts(V, ncpr.ap, cpr.ap, -1.0, ALU.mult, r=[cpr], w=[ncpr])
        lastc = [self.f32(2, "lastc%d" % i) for i in range(2)]
        nsQ = {}
        for Q in (256, 512):
            nsQ[Q] = sc_tile("nsQ%d" % Q)
            self.ts(V, nsQ[Q].ap, carry[Q][0].ap, -1.0, ALU.mult, r=[carry[Q][0]], w=[nsQ[Q]])
        TAB = [{n: self.f32(512, "%s%d" % (n, i)) for n in ("COS", "SIN", "wr", "bma", "apb", "ta", "tb")} for i in range(2)]
        WK = [{n: self.f32(512, "%s%d" % (n, i)) for n in ("k1", "k2", "k3", "pss", "bre", "bim")} for i in range(2)]
        MK = [{n: self.bf16(512, "%s%d" % (n, i)) for n in ("m1", "m2", "m3", "m4")} for i in range(2)]
        init = [self.f32(4, "init%d" % i) for i in range(2)]
        fwd_chunks = [(0, LC)] + [(s, 512) for s in range(LC, T, 512)]
        bwd_chunks = [(0, LC)] + [(T - 512 * (i + 1), 512) for i in range(8)]
        tgc = [0]

        def emit_prep(ft):
            Ui = U[ft % 2]
            self.DMA("sync", Ui.ap, HT.ap[ft * 128:(ft + 1) * 128, :], r=[HT], w=[Ui])
            W = WTS[ft % 2]
            for d in range(2):
                cr_ = craw[d]
                self.DMA("sync", cr_[0].ap, P["c_re"].ap[j, d, ft * 8:(ft + 1) * 8].rearrange("g h n -> (g h) n"), r=[P["c_re"]], w=[cr_[0]])
                self.DMA("sync", cr_[1].ap, P["c_im"].ap[j, d, ft * 8:(ft + 1) * 8].rearrange("g h n -> (g h) n"), r=[P["c_im"]], w=[cr_[1]])
                for k in range(4):
                    p_ = ft * 4 + k
                    wi_ = d * 4 + k
                    c1, c2 = 32 * k, 32 * k + 16
                    for bi, nm in enumerate(("b_re", "b_im")):
                        m1t = M1[(k, nm)]
                        self.cp(G, m1t.ap[0:64, c1:c1 + 16], braw[nm].ap[0:64, d, p_, :], r=[braw[nm]], w=[m1t])
                        self.cp(G, m1t.ap[64:128, c2:c2 + 16], braw[nm].ap[64:128, d, p_, :], r=[braw[nm]], w=[m1t])
                        ps = self.PS[5]
                        self.tr(ps.ap[:, bi * 128:(bi + 1) * 128], m1t.ap, self.ident.ap, r=[m1t, self.ident], w=[ps])
                        self.cp("scalar", W.ap[:, wi_, bi, :], ps.ap[:, bi * 128:(bi + 1) * 128], r=[ps], w=[W])
                    self.tt(G, W.ap[:, wi_, 5, :], W.ap[:, wi_, 0, :], W.ap[:, wi_, 1, :], ALU.add, r=[W], w=[W])
                    for q in range(2):
                        sp = Spair[q]
                        self.ts(G, sp.ap[:, 0:64], cr_[q].ap, self.mask8.ap[:, 2 * k:2 * k + 1], ALU.mult, 0.0, ALU.add, r=[cr_[q], self.mask8], w=[sp])
                        self.ts(G, sp.ap[:, 64:128], cr_[q].ap, self.mask8.ap[:, 2 * k + 1:2 * k + 2], ALU.mult, 0.0, ALU.add, r=[cr_[q], self.mask8], w=[sp])
                        ps = self.PS[5]
                        self.tr(ps.ap[:, 256 + q * 128:256 + (q + 1) * 128], sp.ap, self.ident.ap, r=[sp, self.ident], w=[ps])
                        src = ps.ap[:, 256 + q * 128:256 + (q + 1) * 128]
                        if q == 0:
                            self.cp("scalar", W.ap[:, wi_, 2, :], src, r=[ps], w=[W])
                            self.act(W.ap[:, wi_, 3, :], src, AF.Copy, r=[ps], w=[W], scale=-1.0)
                        else:
                            self.act(W.ap[:, wi_, 4, :], src, AF.Copy, r=[ps], w=[W], scale=-1.0)

        def table_thunks(tab, col):
            th = []
            add = th.append
            MAGIC = 12582912.0
            ti, tf, fr, s2, s4 = tmp2["ti"], tmp2["tf"], tmp2["fr"], tmp2["s2"], tmp2["s4"]
            sc1 = lambda t: t.ap[:, col:col + 1]
            add(lambda: self.act(tab["ta"].ap, Jrow.ap, AF.Identity, r=[Jrow, frac], w=[tab["ta"]], scale=sc1(frac)))
            add(lambda: self.act(tf.ap, tab["ta"].ap, AF.Identity, r=[tab["ta"]], w=[tf], bias=MAGIC))
            add(lambda: self.act(tf.ap, tf.ap, AF.Identity, r=[tf], w=[tf], bias=-MAGIC))
            add(lambda: self.tt(G, fr.ap, tab["ta"].ap, tf.ap, ALU.subtract, r=[tab["ta"], tf], w=[fr]))
            add(lambda: self.act(s2.ap, fr.ap, AF.Sin, r=[fr], w=[s2], scale=float(np.pi)))
            add(lambda: self.act(s4.ap, fr.ap, AF.Sin, r=[fr], w=[s4], scale=float(np.pi / 2)))
            add(lambda: self.act(s4.ap, s4.ap, AF.Square, r=[s4], w=[s4]))
            add(lambda: self.act(s4.ap, s4.ap, AF.Identity, r=[s4], w=[s4], scale=-4.0, bias=2.0))
            add(lambda: self.tt(G, tab["SIN"].ap, s2.ap, s4.ap, ALU.mult, r=[s2, s4], w=[tab["SIN"]]))
            add(lambda: self.act(s2.ap, s2.ap, AF.Square, r=[s2], w=[s2]))
            add(lambda: self.act(tab["COS"].ap, s2.ap, AF.Identity, r=[s2], w=[tab["COS"]], scale=-2.0, bias=1.0))
            for (c1, c2, dst) in ((cr, ci, "wr"), (cmr, ncpr, "bma"), (cpr, cmr, "apb")):
                add(lambda c1=c1: self.act(tab["ta"].ap, tab["COS"].ap, AF.Identity, r=[tab["COS"], c1], w=[tab["ta"]], scale=sc1(c1)))
                add(lambda c2=c2: self.act(tab["tb"].ap, tab["SIN"].ap, AF.Identity, r=[tab["SIN"], c2], w=[tab["tb"]], scale=sc1(c2)))
                add(lambda dst=dst: self.tt(G, tab[dst].ap, tab["ta"].ap, tab["tb"].ap, ALU.add, r=[tab["ta"], tab["tb"]], w=[tab[dst]]))
            return th

        items = []
        pd = 0
        for ft in range(NFT):
            for d in range(2):
                chunks = fwd_chunks if d == 0 else bwd_chunks
                for k in range(4):
                    for cidx, (s, n) in enumerate(chunks):
                        items.append(dict(ft=ft, d=d, k=k, cidx=cidx, s=s, n=n, pd=pd, last=(cidx == len(chunks) - 1),
                                          first_pd=(cidx == 0), first_y=(d == 0 and k == 0),
                                          ft_last=(d == 1 and k == 3 and cidx == len(chunks) - 1)))
                    pd += 1
        pending = []

        def stage_a(i, it_):
            ft, d, k, s, n = it_["ft"], it_["d"], it_["k"], it_["s"], it_["n"]
            if i == 0:
                emit_prep(0)
                for f in table_thunks(TAB[0], 0):
                    f()
            if d == 1 and k == 0 and it_["cidx"] == 0 and ft + 1 < NFT:
                emit_prep(ft + 1)
            if it_["cidx"] == 1 and it_["pd"] + 1 < 64:
                npd = it_["pd"] + 1
                nft, nd, nk = npd // 8, (npd // 4) % 2, npd % 4
                pending.extend(table_thunks(TAB[npd % 2], nd * 32 + nft * 4 + nk))
            if it_["cidx"] >= 1:
                ntake = len(pending) if it_["last"] else min(3, len(pending))
                for _ in range(ntake):
                    pending.pop(0)()
            tab = TAB[it_["pd"] % 2]
            Ui, W, wi_ = U[ft % 2], WTS[ft % 2], d * 4 + k
            wk = WK[i % 2]
            pre, pim, psu = self.PS[0], self.PS[1], self.PS[2]
            urhs = Ui.ap[:, s:s + n] if d == 0 else self.rev_ap(Ui.ap, s, n)
            self.mm(pre.ap[:, 0:n], W.ap[:, wi_, 0, :], urhs, True, True, r=[W, Ui], w=[pre])
            self.mm(pim.ap[:, 0:n], W.ap[:, wi_, 1, :], urhs, True, True, r=[W, Ui], w=[pim])
            self.mm(psu.ap[:, 0:n], W.ap[:, wi_, 5, :], urhs, True, True, r=[W, Ui], w=[psu])
            c_ = lambda t: t.ap[:, 0:n]
            self.cp("scalar", c_(wk["pss"]), psu.ap[:, 0:n], r=[psu], w=[wk["pss"]])
            self.tt(V, c_(wk["k2"]), pre.ap[:, 0:n], c_(tab["bma"]), ALU.mult, r=[pre, tab["bma"]], w=[wk["k2"]])
            self.tt(V, c_(wk["k3"]), pim.ap[:, 0:n], c_(tab["apb"]), ALU.mult, r=[pim, tab["apb"]], w=[wk["k3"]])
            self.tt(G, c_(wk["k1"]), c_(wk["pss"]), c_(tab["wr"]), ALU.mult, r=[wk["pss"], tab["wr"]], w=[wk["k1"]])
            self.tt(G, c_(wk["bre"]), c_(wk["k1"]), c_(wk["k3"]), ALU.subtract, r=[wk["k1"], wk["k3"]], w=[wk["bre"]])
            self.tt(G, c_(wk["bim"]), c_(wk["k1"]), c_(wk["k2"]), ALU.add, r=[wk["k1"], wk["k2"]], w=[wk["bim"]])

        def stage_b(i, it_):
            ft, d, k, s, n, cidx = it_["ft"], it_["d"], it_["k"], it_["s"], it_["n"], it_["cidx"]
            tab = TAB[it_["pd"] % 2]
            col = d * 32 + ft * 4 + k
            Ui, W, wi_ = U[ft % 2], WTS[ft % 2], d * 4 + k
            wk, mk, ini = WK[i % 2], MK[i % 2], init[i % 2]
            py = self.PS[3 + i % 2]
            c_ = lambda t: t.ap[:, 0:n]
            rcol = rr.ap[:, col:col + 1]
            rb = rcol.to_broadcast([128, n])
            if cidx == 0:
                i_re, i_im, ir = 0.0, 0.0, []
            else:
                pini = init[(i - 1) % 2]
                i_re, i_im, ir = pini.ap[:, 0:1], pini.ap[:, 1:2], [pini]
            gre_t, gim_t = self.PS[6], self.PS[7]
            self.E(V, lambda e, o=gre_t.ap[:, 0:n], d1=c_(wk["bre"]), i0=i_re, rb=rb: e.tensor_tensor_scan(out=o, data0=rb, data1=d1, initial=i0, op0=ALU.mult, op1=ALU.add),
                   r=[wk["bre"], rr] + ir, w=[gre_t])
            self.E(V, lambda e, o=gim_t.ap[:, 0:n], d1=c_(wk["bim"]), i0=i_im, rb=rb: e.tensor_tensor_scan(out=o, data0=rb, data1=d1, initial=i0, op0=ALU.mult, op1=ALU.add),
                   r=[wk["bim"], rr] + ir, w=[gim_t])
            if not it_["last"]:
                sQ, cQ = carry[n]
                sq_, cq_, nsq_ = sQ.ap[:, col:col + 1], cQ.ap[:, col:col + 1], nsQ[n].ap[:, col:col + 1]
                lc = lastc[i % 2]
                self.cp(V, lc.ap[:, 0:1], gre_t.ap[:, n - 1:n], r=[gre_t], w=[lc])
                self.cp(V, lc.ap[:, 1:2], gim_t.ap[:, n - 1:n], r=[gim_t], w=[lc])
                gre_l, gim_l = lc.ap[:, 0:1], lc.ap[:, 1:2]
                self.act(ini.ap[:, 2:3], gim_l, AF.Identity, r=[lc, nsQ[n]], w=[ini], scale=nsq_)
                self.act(ini.ap[:, 3:4], gim_l, AF.Identity, r=[lc, cQ], w=[ini], scale=cq_)
                self.act(ini.ap[:, 0:1], gre_l, AF.Identity, r=[lc, cQ, ini], w=[ini], scale=cq_, bias=ini.ap[:, 2:3])
                self.act(ini.ap[:, 1:2], gre_l, AF.Identity, r=[lc, sQ, ini], w=[ini], scale=sq_, bias=ini.ap[:, 3:4])
            self.tt(V, c_(mk["m1"]), gre_t.ap[:, 0:n], c_(tab["COS"]), ALU.mult, r=[gre_t, tab["COS"]], w=[mk["m1"]])
            self.tt(V, c_(mk["m3"]), gre_t.ap[:, 0:n], c_(tab["SIN"]), ALU.mult, r=[gre_t, tab["SIN"]], w=[mk["m3"]])
            self.tt(V, c_(mk["m2"]), gim_t.ap[:, 0:n], c_(tab["SIN"]), ALU.mult, r=[gim_t, tab["SIN"]], w=[mk["m2"]])
            self.tt(V, c_(mk["m4"]), gim_t.ap[:, 0:n], c_(tab["COS"]), ALU.mult, r=[gim_t, tab["COS"]], w=[mk["m4"]])
            for mi, (mn, wsel) in enumerate((("m1", 2), ("m2", 3), ("m3", 4), ("m4", 4))):
                mr = mk[mn].ap[:, 0:n] if d == 0 else self.rev_ap(mk[mn].ap, 0, n)
                self.mm(py.ap[:, 0:n], W.ap[:, wi_, wsel, :], mr, mi == 0, mi == 3, r=[W, mk[mn]], w=[py])
            if it_["first_y"]:
                self.cp("scalar", Yacc.ap[:, s:s + n], py.ap[:, 0:n], r=[py], w=[Yacc])
            else:
                self.tt(V, Yacc.ap[:, s:s + n], py.ap[:, 0:n], Yacc.ap[:, s:s + n], ALU.add, r=[py, Yacc], w=[Yacc])
            if it_["ft_last"]:
                self.stt(Yacc.ap, Ui.ap, dsk.ap[:, ft:ft + 1], Yacc.ap, ALU.mult, ALU.add, r=[Ui, dsk, Yacc], w=[Yacc])
                self.act(gt.ap, Yacc.ap, AF.Gelu_apprx_tanh, r=[Yacc], w=[gt])
                self.DMA("sync", GT.ap[ft * 128:(ft + 1) * 128, :], gt.ap, r=[gt], w=[GT])

        stage_a(0, items[0])
        for i in range(len(items)):
            if i + 1 < len(items):
                stage_a(i + 1, items[i + 1])
            stage_b(i, items[i])

    def proj_phase(self, XT, IN, Wd, w_ap, nkt, l, glu_bias=None):
        self.new_phase()
        Wt = 512
        ws = self.bf16(nkt * D, "pw", shape=(nkt, D))
        self.load_w_bf16(ws, w_ap, Wd, nkt)
        bg = None
        if glu_bias is not None:
            bg = self.f32(8, "bglu")
            self.DMA("sync", bg.ap, glu_bias[1].rearrange("(ft p) -> p ft", p=128), r=[glu_bias[0]], w=[bg], slow=True)
        ib = [self.bf16(nkt * Wt, "pi%d" % i, shape=(nkt, Wt)) for i in range(2)]
        xb = [self.f32(NFT * Wt, "px%d" % i, shape=(NFT, Wt)) for i in range(2)]
        sg = [self.f32(Wt, "psg%d" % i) for i in range(2)]
        mod = self.mod[l]
        for ci, (s, w, isctx) in enumerate(self.chunks()):
            cs = 1 if isctx else 0
            ibi, xbi = ib[ci % 2], xb[ci % 2]
            self.DMA("sync", ibi.ap[:, :, 0:w], IN.ap[:, s:s + w].rearrange("(kt p) t -> p kt t", p=128), r=[IN], w=[ibi])
            self.DMA("sync", xbi.ap[:, :, 0:w], XT.ap[:, s:s + w].rearrange("(ft p) t -> p ft t", p=128), r=[XT], w=[xbi])
            for fo in range(NFT):
                po = self.PS[fo % 4]
                for kt in range(nkt):
                    self.mm(po.ap[:, 0:w], ws.ap[:, kt, fo * 128:(fo + 1) * 128], ibi.ap[:, kt, 0:w], kt == 0, kt == nkt - 1, r=[ws, ibi], w=[po])
                gate = mod.ap[:, 2, fo, cs:cs + 1]
                if glu_bias is not None:
                    sgi = sg[fo % 2]
                    self.act(sgi.ap[:, 0:w], po.ap[:, 0:w], AF.Sigmoid, r=[po, bg], w=[sgi], bias=bg.ap[:, fo:fo + 1])
                    self.tt("gpsimd", sgi.ap[:, 0:w], sgi.ap[:, 0:w], ibi.ap[:, fo, 0:w], ALU.mult, r=[sgi, ibi], w=[sgi])
                    self.stt(xbi.ap[:, fo, 0:w], sgi.ap[:, 0:w], gate, xbi.ap[:, fo, 0:w], ALU.mult, ALU.add, r=[sgi, mod, xbi], w=[xbi])
                else:
                    self.stt(xbi.ap[:, fo, 0:w], po.ap[:, 0:w], gate, xbi.ap[:, fo, 0:w], ALU.mult, ALU.add, r=[po, mod, xbi], w=[xbi])
            self.DMA("sync", XT.ap[:, s:s + w].rearrange("(ft p) t -> p ft t", p=128), xbi.ap[:, :, 0:w], r=[xbi], w=[XT])


    def row_bcast(self, dst, row_ap, src_tl, n, rowtmp, func=None, scale=None):
        self.DMA("sync", rowtmp.ap[0:1, 0:n], row_ap, r=[src_tl], w=[rowtmp])
        for c0 in range(0, n, 512):
            w = min(512, n - c0)
            ps = self.PS[7]
            self.mm(ps.ap[:, 0:w], self.ones.ap[0:1, :], rowtmp.ap[0:1, c0:c0 + w], True, True, r=[self.ones, rowtmp], w=[ps])
            if func is None:
                self.cp("vector", dst.ap[:, c0:c0 + w], ps.ap[:, 0:w], r=[ps], w=[dst])
            else:
                self.act(dst.ap[:, c0:c0 + w], ps.ap[:, 0:w], func, r=[ps], w=[dst], scale=scale)

    def tri_mask(self, nm, base, cm, step, op):
        t = self.f32(128, nm)
        self.memset("gpsimd", t.ap, 1.0, [t])
        self.E("gpsimd", lambda e: e.affine_select(out=t.ap, in_=t.ap, pattern=[[step, 128]], compare_op=op, fill=0.0, base=base, channel_multiplier=cm), [t], [t])
        return t

    def ssd_phase_a(self, HT, P, SZ, XS, BC, DTR):
        self.new_phase()
        w_in = P["w_in"]
        Hres = self.bf16(8 * T, "Hres", shape=(8, T))
        for kt in range(8):
            self.DMA("sync", Hres.ap[:, kt, :], HT.ap[kt * 128:(kt + 1) * 128, :], r=[HT], w=[Hres])
        wz = self.bf16(8 * 2048, "wz", shape=(8, 2048))
        self.load_w_bf16(wz, w_in.ap[0], w_in, 8, 0, 2048)
        szb = [self.bf16(2048, "szb%d" % i) for i in range(2)]
        for tt in range(T // 128):
            sb = szb[tt % 2]
            for zc in range(4):
                ps = self.PS[zc]
                for kt in range(8):
                    self.mm(ps.ap, Hres.ap[:, kt, tt * 128:(tt + 1) * 128], wz.ap[:, kt, zc * 512:(zc + 1) * 512], kt == 0, kt == 7, r=[Hres, wz], w=[ps])
                self.act(sb.ap[:, zc * 512:(zc + 1) * 512], ps.ap, AF.Silu, r=[ps], w=[sb])
            self.DMA("sync", SZ.ap[tt * 128:(tt + 1) * 128, :], sb.ap, r=[sb], w=[SZ])
        self.new_phase()
        Hres = self.bf16(8 * T, "Hres2", shape=(8, T))
        for kt in range(8):
            self.DMA("sync", Hres.ap[:, kt, :], HT.ap[kt * 128:(kt + 1) * 128, :], r=[HT], w=[Hres])
        cw = self.f32(5 * 48, "cw", shape=(5, 48))
        cb = self.f32(48, "cb")
        for k in range(5):
            self.DMA("sync", cw.ap[:, k, :], P["conv_w"].ap[0, k].rearrange("(ot p) -> p ot", p=128), r=[P["conv_w"]], w=[cw], slow=True)
        self.DMA("sync", cb.ap, P["conv_b"].ap[0].rearrange("(ot p) -> p ot", p=128), r=[P["conv_b"]], w=[cb], slow=True)
        wch = [self.bf16(8 * 512, "wch%d" % i, shape=(8, 512)) for i in range(2)]
        xp = [self.bf16(T, "xp%d" % i) for i in range(2)]
        ob = [self.bf16(T, "ob%d" % i) for i in range(2)]
        dg = [self.bf16(5 * 128, "dg%d" % i, shape=(5, 128)) for i in range(2)]
        chunks = self.chunks()
        pi = 0
        for wc in range(12):
            wb = wch[wc % 2]
            self.load_w_bf16(wb, w_in.ap[0], w_in, 8, 2048 + wc * 512, 2048 + (wc + 1) * 512)
            for q in range(4):
                ot = wc * 4 + q
                xpi, obi, dgi = xp[ot % 2], ob[ot % 2], dg[ot % 2]
                for k in range(5):
                    self.ts("gpsimd", dgi.ap[:, k, :], self.identb.ap, cw.ap[:, k, ot:ot + 1], ALU.mult, 0.0, ALU.add, r=[self.identb, cw], w=[dgi])
                for ci, (s, w, isctx) in enumerate(chunks):
                    ps = self.PS[pi % 4]
                    pi += 1
                    for kt in range(8):
                        self.mm(ps.ap[:, 0:w], wb.ap[:, kt, q * 128:(q + 1) * 128], Hres.ap[:, kt, s:s + w], kt == 0, kt == 7, r=[wb, Hres], w=[ps])
                    self.cp("vector" if ci % 2 else "scalar", xpi.ap[:, s:s + w], ps.ap[:, 0:w], r=[ps], w=[xpi])
                for ci, (s, w, isctx) in enumerate(chunks):
                    q0, q1 = (0, LC) if isctx else (LC, T)
                    ps = self.PS[4 + ci % 4]
                    for ki, k in enumerate((2, 0, 1, 3, 4)):
                        o = k - 2
                        i0 = max(0, q0 - s - o)
                        i1 = min(w, q1 - s - o)
                        self.mm(ps.ap[:, i0:i1], dgi.ap[:, k, :], xpi.ap[:, s + i0 + o:s + i1 + o], ki == 0, ki == 4, r=[dgi, xpi], w=[ps])
                    self.act(obi.ap[:, s:s + w], ps.ap[:, 0:w], AF.Silu, r=[ps, cb], w=[obi], bias=cb.ap[:, ot:ot + 1])
                dst = XS.ap[ot * 128:(ot + 1) * 128, :] if ot < 16 else BC.ap[(ot - 16) * 128:(ot - 15) * 128, :]
                self.DMA("sync", dst, obi.ap, r=[obi], w=[XS if ot < 16 else BC])
        wdt = self.bf16(8 * 64, "wdt", shape=(8, 64))
        self.load_w_bf16(wdt, w_in.ap[0], w_in, 8, 8192, 8256)
        dtb = self.f32(T, "dtb")
        for ci, (s, w, isctx) in enumerate(chunks):
            ps = self.PS[ci % 4]
            for kt in range(8):
                self.mm(ps.ap[0:64, 0:w], wdt.ap[:, kt, :], Hres.ap[:, kt, s:s + w], kt == 0, kt == 7, r=[wdt, Hres], w=[ps])
            self.cp("vector", dtb.ap[0:64, s:s + w], ps.ap[0:64, 0:w], r=[ps], w=[dtb])
        self.DMA("sync", DTR.ap, dtb.ap[0:64, :], r=[dtb], w=[DTR])

    def ssd_phase_b(self, P, XS, BC, DTR, YF, YB):
        self.new_phase()
        V, G = "vector", "gpsimd"
        GT_ = self.tri_mask("mGT", 0, 1, -1, ALU.is_gt)
        LE_ = self.tri_mask("mLE", 0, -1, 1, ALU.is_ge)
        LT_ = self.tri_mask("mLT", 0, -1, 1, ALU.is_gt)
        GE_ = self.tri_mask("mGE", 0, 1, -1, ALU.is_ge)
        rowtmp = self.f32(64, "rowtmp")
        Arow = [self.f32(32, "Arow%d" % d) for d in range(2)]
        Brow = [self.f32(32, "Brow%d" % d) for d in range(2)]
        Drow = self.f32(32, "Drow")
        for d in range(2):
            self.row_bcast(Arow[d], P["a_log"].ap[0, d:d + 1, :], P["a_log"], 32, rowtmp, func=AF.Exp)
            self.ts(V, Arow[d].ap, Arow[d].ap, -1.0, ALU.mult, r=[Arow[d]], w=[Arow[d]])
            self.row_bcast(Brow[d], P["dt_bias"].ap[0, d:d + 1, :], P["dt_bias"], 32, rowtmp)
        self.row_bcast(Drow, P["d"].ap[0:1, :], P["d"], 32, rowtmp)
        H = self.f32(2048, "Hst", shape=(32, 64))
        Hb = self.bf16(2048, "Hb")
        NB = 2
        xsT = [self.bf16(16 * 128, "xsT%d" % i, shape=(16, 128)) for i in range(NB)]
        BTt = [self.bf16(8 * 128, "BT%d" % i, shape=(8, 128)) for i in range(NB)]
        CTt = [self.bf16(8 * 128, "CT%d" % i, shape=(8, 128)) for i in range(NB)]
        dtr = [self.f32(128, "dtr%d" % i) for i in range(NB)]
        ybuf = [self.bf16(2048, "ybuf%d" % i) for i in range(NB)]
        xdt = [self.bf16(2048, "xdt%d" % i, shape=(32, 64)) for i in range(NB)]
        xdte = [self.bf16(2048, "xdte%d" % i, shape=(32, 64)) for i in range(NB)]
        dskt = [self.f32(2048, "dskt%d" % i, shape=(32, 64)) for i in range(NB)]
        Btok = [self.bf16(1024, "Btok%d" % i, shape=(8, 128)) for i in range(NB)]
        dtt = [self.f32(32, "dtt%d" % i) for i in range(NB)]
        adt = [self.f32(32, "adt%d" % i) for i in range(NB)]
        dex = [self.f32(96, "dex%d" % i) for i in range(NB)]
        dtd = [self.f32(32, "dtd%d" % i) for i in range(NB)]
        Xh = [self.f32(512, "Xh%d" % i, shape=(4, 128)) for i in range(2)]
        Ld = [self.bf16(512, "Ld%d" % i, shape=(4, 128)) for i in range(2)]
        Mt = [self.bf16(512, "Mt%d" % i, shape=(4, 128)) for i in range(2)]
        CBm = [self.bf16(128, "CBm%d" % i) for i in range(2)]
        ytmp = [self.f32(256, "ytmp%d" % i, shape=(4, 64)) for i in range(2)]
        htmp = [self.f32(256, "htmp%d" % i, shape=(4, 64)) for i in range(2)]
        nchunk = T // 128
        work = []
        for d in range(2):
            order = list(range(nchunk)) if d == 0 else [1, 0] + list(range(nchunk - 1, 1, -1))
            for oi, c in enumerate(order):
                work.append((d, c, oi == 0))
        cfgd = {0: dict(mX=GT_, mE=LE_, mTE=GT_, mSeg=LE_, mCB=LE_), 1: dict(mX=LT_, mE=GE_, mTE=LT_, mSeg=GE_, mCB=GE_)}

        def prologue(wi):
            d, c, first = work[wi]
            cf = cfgd[d]
            b = wi % NB
            s0 = c * 128
            xi, bi, cti, dri = xsT[b], BTt[b], CTt[b], dtr[b]
            self.DMA("sync", xi.ap, XS.ap[:, s0:s0 + 128].rearrange("(t p) c -> p t c", p=128), r=[XS], w=[xi])
            self.DMA("sync", bi.ap, BC.ap[(d * 2) * 1024:(d * 2 + 1) * 1024, s0:s0 + 128].rearrange("(t p) c -> p t c", p=128), r=[BC], w=[bi])
            self.DMA("sync", cti.ap, BC.ap[(d * 2 + 1) * 1024:(d * 2 + 2) * 1024, s0:s0 + 128].rearrange("(t p) c -> p t c", p=128), r=[BC], w=[cti])
            self.DMA("sync", dri.ap[0:32, :], DTR.ap[d * 32:(d + 1) * 32, s0:s0 + 128], r=[DTR], w=[dri])
            psm = self.PS[6]
            dtt_, adt_, dex_, dtd_ = dtt[b], adt[b], dex[b], dtd[b]
            self.tr(psm.ap[:, 0:32], dri.ap[0:32, :], self.ident.ap[0:32, 0:32], r=[dri, self.ident], w=[psm])
            self.tt(V, dtt_.ap, psm.ap[:, 0:32], Brow[d].ap, ALU.add, r=[psm, Brow[d]], w=[dtt_])
            self.act(dtt_.ap, dtt_.ap, AF.Exp, r=[dtt_], w=[dtt_])
            self.act(dtt_.ap, dtt_.ap, AF.Ln, r=[dtt_, self.epsc], w=[dtt_], bias=self.epsc.ap[:, 2:3])
            self.tt(V, adt_.ap, dtt_.ap, Arow[d].ap, ALU.mult, r=[dtt_, Arow[d]], w=[adt_])
            self.mm(psm.ap[:, 32:64], self.ones.ap, adt_.ap, True, True, r=[self.ones, adt_], w=[psm])
            self.mm(psm.ap[:, 64:96], cf["mTE"].ap, adt_.ap, True, True, r=[cf["mTE"], adt_], w=[psm])
            self.mm(psm.ap[:, 96:128], cf["mE"].ap, adt_.ap, True, True, r=[cf["mE"], adt_], w=[psm])
            self.act(dex_.ap, psm.ap[:, 32:128], AF.Exp, r=[psm], w=[dex_])
            self.tt(V, dtd_.ap, dtt_.ap, dex_.ap[:, 32:64], ALU.mult, r=[dtt_, dex_], w=[dtd_])
            pst = self.PS[7]
            psb16 = self.PSB[7]
            for half in range(2):
                for q in range(8):
                    t_ = half * 8 + q
                    self.tr(psb16[:, q * 128:(q + 1) * 128], xi.ap[:, t_, :], self.identb.ap, r=[xi, self.identb], w=[pst])
                src = psb16[:, 0:1024].rearrange("p (h c) -> p h c", c=64)
                hs = slice(half * 16, (half + 1) * 16)
                bc = lambda col: col.unsqueeze(2).to_broadcast([128, 16, 64])
                self.tt(V, xdt[b].ap[:, hs, :], src, bc(dtt_.ap[:, hs]), ALU.mult, r=[pst, dtt_], w=[xdt[b]])
                self.tt(V, xdte[b].ap[:, hs, :], src, bc(dtd_.ap[:, hs]), ALU.mult, r=[pst, dtd_], w=[xdte[b]])
                if d == 0:
                    self.tt(V, dskt[b].ap[:, hs, :], src, bc(Drow.ap[:, hs]), ALU.mult, r=[pst, Drow], w=[dskt[b]])
            for g in range(8):
                self.tr(psb16[:, g * 128:(g + 1) * 128], bi.ap[:, g, :], self.identb.ap, r=[bi, self.identb], w=[pst])
            self.cp("scalar", Btok[b].ap, psb16[:, 0:1024].rearrange("p (g c) -> p g c", c=128), r=[pst], w=[Btok[b]])

        gi = [0]

        def s1(wi, g):
            d, c, first = work[wi]
            cf = cfgd[d]
            b = wi % NB
            k = (wi * 8 + g) % 2
            xh = Xh[k]
            for hh in range(4):
                h = g * 4 + hh
                self.ts(G, xh.ap[:, hh, :], cf["mX"].ap, adt[b].ap[:, h:h + 1], ALU.mult, 0.0, ALU.add, r=[cf["mX"], adt[b]], w=[xh])
            pcb = self.PS[k]
            self.mm(pcb.ap[:, 0:128], BTt[b].ap[:, g, :], CTt[b].ap[:, g, :], True, True, r=[BTt[b], CTt[b]], w=[pcb])
            psg = self.PS[4 + k]
            for hh in range(4):
                self.mm(psg.ap[:, hh * 128:(hh + 1) * 128], xh.ap[:, hh, :], cf["mSeg"].ap, True, True, r=[xh, cf["mSeg"]], w=[psg])

        def s2(wi, g):
            d, c, first = work[wi]
            cf = cfgd[d]
            k = (wi * 8 + g) % 2
            pcb, psg = self.PS[k], self.PS[4 + k]
            self.act(Ld[k].ap, psg.ap.rearrange("p (h c) -> p h c", c=128), AF.Exp, r=[psg], w=[Ld[k]])
            self.tt(V, CBm[k].ap, pcb.ap[:, 0:128], cf["mCB"].ap, ALU.mult, r=[pcb, cf["mCB"]], w=[CBm[k]])
            self.tt(V, Mt[k].ap, Ld[k].ap, CBm[k].ap.unsqueeze(1).to_broadcast([128, 4, 128]), ALU.mult, r=[Ld[k], CBm[k]], w=[Mt[k]])

        def s3(wi, g):
            d, c, first = work[wi]
            b = wi % NB
            k = (wi * 8 + g) % 2
            py = self.PS[2 + k]
            cti = CTt[b]
            for hh in range(4):
                h = g * 4 + hh
                self.mm(py.ap[:, hh * 64:(hh + 1) * 64], Mt[k].ap[:, hh, :], xdt[b].ap[:, h, :], True, True, r=[Mt[k], xdt[b]], w=[py])
                self.mm(py.ap[:, 256 + hh * 64:256 + (hh + 1) * 64], cti.ap[:, g, :], Hb.ap[:, h * 64:(h + 1) * 64], True, True, r=[cti, Hb], w=[py])
            gs = slice(g * 4, (g + 1) * 4)
            yt = ytmp[k]
            yb = ybuf[b]
            Eb = dex[b].ap[:, 64 + g * 4:64 + (g + 1) * 4].unsqueeze(2).to_broadcast([128, 4, 64])
            self.tt(V, yt.ap, py.ap[:, 256:512].rearrange("p (h c) -> p h c", c=64), Eb, ALU.mult, r=[py, dex[b]], w=[yt])
            if d == 0:
                self.tt(G, yt.ap, yt.ap, dskt[b].ap[:, gs, :], ALU.add, r=[yt, dskt[b]], w=[yt])
            self.tt(V, yb.ap[:, g * 256:(g + 1) * 256].rearrange("p (h c) -> p h c", c=64), py.ap[:, 0:256].rearrange("p (h c) -> p h c", c=64), yt.ap, ALU.add, r=[py, yt], w=[yb])
            pst2 = self.PS[6]
            self.mm(pst2.ap[:, 256:512], Btok[b].ap[:, g, :], xdte[b].ap[:, gs, :].rearrange("p h c -> p (h c)"), True, True, r=[Btok[b], xdte[b]], w=[pst2])
            ht = htmp[k]
            Db = dex[b].ap[:, g * 4:(g + 1) * 4].unsqueeze(2).to_broadcast([128, 4, 64])
            self.tt(G, ht.ap, H.ap[:, gs, :], Db, ALU.mult, r=[H, dex[b]], w=[ht])
            self.tt(V, H.ap[:, gs, :], pst2.ap[:, 256:512].rearrange("p (h c) -> p h c", c=64), ht.ap, ALU.add, r=[pst2, ht], w=[H])
            self.cp("scalar", Hb.ap[:, g * 256:(g + 1) * 256], H.ap[:, gs, :].rearrange("p h c -> p (h c)"), r=[H], w=[Hb])

        nw = len(work)
        prologue(0)
        s1(0, 0)
        for wi in range(nw):
            d, c, first = work[wi]
            if first:
                self.memset(V, H.ap, 0.0, [H])
                self.memset(V, Hb.ap, 0.0, [Hb])
            for g in range(8):
                if g == 4 and wi + 1 < nw:
                    prologue(wi + 1)
                if g + 1 < 8:
                    s1(wi, g + 1)
                elif wi + 1 < nw:
                    s1(wi + 1, 0)
                s2(wi, g)
                s3(wi, g)
            YO = YF if d == 0 else YB
            self.DMA("sync", YO.ap[c * 128:c * 128 + 128, :], ybuf[wi % NB].ap, r=[ybuf[wi % NB]], w=[YO])

    def ssd_phase_c(self, XT, P, SZ, YF, YB, l):
        self.new_phase()
        V, G = "vector", "gpsimd"
        wo = self.bf16(16 * D, "wo", shape=(16, D))
        self.load_w_bf16(wo, P["w_out"].ap[0], P["w_out"], 16)
        nrow = self.f32(2048, "nrow")
        rowtmp = self.f32(2048, "rowtmp2")
        self.row_bcast(nrow, P["norm"].ap[0:1, :], P["norm"], 2048, rowtmp)
        yf = [self.bf16(2048, "cyf%d" % i) for i in range(2)]
        ybk = [self.bf16(2048, "cyb%d" % i) for i in range(2)]
        sz = [self.bf16(2048, "csz%d" % i) for i in range(2)]
        yy = [self.f32(2048, "cyy%d" % i) for i in range(2)]
        sqj = self.f32(2048, "csq")
        gnb = [self.bf16(2048, "cgn%d" % i) for i in range(2)]
        ss = [self.f32(8, "css%d" % i) for i in range(2)]
        gnT = [self.bf16(16 * 512, "gnT%d" % i, shape=(16, 512)) for i in range(2)]
        xb = [self.f32(NFT * 512, "cx%d" % i, shape=(NFT, 512)) for i in range(2)]
        mod = self.mod[l]
        ti = 0
        for ci, (s, w, isctx) in enumerate(self.chunks()):
            cs = 1 if isctx else 0
            gT, xbi = gnT[ci % 2], xb[ci % 2]
            self.DMA("sync", xbi.ap[:, :, 0:w], XT.ap[:, s:s + w].rearrange("(ft p) t -> p ft t", p=128), r=[XT], w=[xbi])
            for tt in range(w // 128):
                t0 = s + tt * 128
                a, b, z_, y_, g_, s_ = yf[ti % 2], ybk[ti % 2], sz[ti % 2], yy[ti % 2], gnb[ti % 2], ss[ti % 2]
                ti += 1
                self.DMA("sync", a.ap, YF.ap[t0:t0 + 128, :], r=[YF], w=[a])
                self.DMA("sync", b.ap, YB.ap[t0:t0 + 128, :], r=[YB], w=[b])
                self.DMA("sync", z_.ap, SZ.ap[t0:t0 + 128, :], r=[SZ], w=[z_])
                self.tt(G, y_.ap, a.ap, b.ap, ALU.add, r=[a, b], w=[y_])
                self.tt(V, y_.ap, y_.ap, z_.ap, ALU.mult, r=[y_, z_], w=[y_])
                self.act(sqj.ap, y_.ap, AF.Square, r=[y_], w=[sqj, s_], accum=s_.ap[:, 0:1])
                self.act(s_.ap[:, 1:2], s_.ap[:, 0:1], AF.Sqrt, r=[s_, self.epsc], w=[s_], scale=1.0 / 2048, bias=self.epsc.ap[:, 0:1])
                self.E(V, lambda e, o=s_.ap[:, 2:3], i=s_.ap[:, 1:2]: e.reciprocal(out=o, in_=i), r=[s_], w=[s_])
                self.stt(g_.ap, y_.ap, s_.ap[:, 2:3], nrow.ap, ALU.mult, ALU.mult, r=[y_, s_, nrow], w=[g_])
                for half in range(2):
                    pst = self.PS[4 + (ti * 2 + half) % 4]
                    psb16 = self.PSB[4 + (ti * 2 + half) % 4]
                    for q in range(8):
                        kt = half * 8 + q
                        self.tr(psb16[:, q * 128:(q + 1) * 128], g_.ap[:, kt * 128:(kt + 1) * 128], self.identb.ap, r=[g_, self.identb], w=[pst])
                    self.cp("scalar" if half else "vector", gT.ap[:, half * 8:(half + 1) * 8, tt * 128:(tt + 1) * 128],
                            psb16[:, 0:1024].rearrange("p (k c) -> p k c", c=128), r=[pst], w=[gT])
            for fo in range(NFT):
                po = self.PS[fo % 4]
                for kt in range(16):
                    self.mm(po.ap[:, 0:w], wo.ap[:, kt, fo * 128:(fo + 1) * 128], gT.ap[:, kt, 0:w], kt == 0, kt == 15, r=[wo, gT], w=[po])
                self.stt(xbi.ap[:, fo, 0:w], po.ap[:, 0:w], mod.ap[:, 2, fo, cs:cs + 1], xbi.ap[:, fo, 0:w], ALU.mult, ALU.add, r=[po, mod, xbi], w=[xbi])
            self.DMA("sync", XT.ap[:, s:s + w].rearrange("(ft p) t -> p ft t", p=128), xbi.ap[:, :, 0:w], r=[xbi], w=[XT])


    def na_bias_table(self, rpb, BTD):
        self.new_phase()
        neg = self.f32(7680, "negt")
        self.memset("vector", neg.ap, -30000.0, [neg])
        self.DMA("sync", BTD.ap.rearrange("h w r c -> (h w r c)").rearrange("(p n) -> p n", p=128), neg.ap, r=[neg], w=[BTD])
        HS = 64 * 15 * 64
        for h in range(16):
            so = h * 465
            do = h * HS
            dst = bass.AP(BTD.ap.tensor, do + 8 * 960, [[64, 15], [961, 49], [1, 16]])
            src = bass.AP(rpb.ap.tensor, so + 7, [[31, 15], [0, 49], [1, 16]])
            self.DMA("sync", dst, src, r=[rpb], w=[BTD], slow=True)
            dst = bass.AP(BTD.ap.tensor, do, [[64, 15], [960, 8], [1, 16]])
            src = bass.AP(rpb.ap.tensor, so + 15, [[31, 15], [-1, 8], [1, 16]])
            self.DMA("sync", dst, src, r=[rpb], w=[BTD], slow=True)
            dst = bass.AP(BTD.ap.tensor, do + 57 * 960 + 48, [[64, 15], [960, 7], [1, 16]])
            src = bass.AP(rpb.ap.tensor, so + 6, [[31, 15], [-1, 7], [1, 16]])
            self.DMA("sync", dst, src, r=[rpb], w=[BTD], slow=True)

    def na_phase_a(self, HT, wqkv, QT, KT, VT):
        self.new_phase()
        Hres = self.bf16(8 * T, "Hres", shape=(8, T))
        for kt in range(8):
            self.DMA("sync", Hres.ap[:, kt, :], HT.ap[kt * 128:(kt + 1) * 128, :], r=[HT], w=[Hres])
        wv = self.bf16(8 * 1024, "wv", shape=(8, 1024))
        self.load_w_bf16(wv, wqkv.ap[0], wqkv, 8, 2048, 3072)
        vb = [self.bf16(1024, "vb%d" % i) for i in range(2)]
        for tt in range(T // 128):
            v_ = vb[tt % 2]
            for vc in range(2):
                ps = self.PS[vc + 2 * (tt % 2)]
                for kt in range(8):
                    self.mm(ps.ap, Hres.ap[:, kt, tt * 128:(tt + 1) * 128], wv.ap[:, kt, vc * 512:(vc + 1) * 512], kt == 0, kt == 7, r=[Hres, wv], w=[ps])
                self.cp("scalar" if vc else "vector", v_.ap[:, vc * 512:(vc + 1) * 512], ps.ap, r=[ps], w=[v_])
            self.DMA("sync", VT.ap[tt * 128:(tt + 1) * 128, :], v_.ap, r=[v_], w=[VT])
        wch = [self.bf16(8 * 512, "wq%d" % i, shape=(8, 512)) for i in range(2)]
        ob = [self.bf16(T, "qo%d" % i) for i in range(2)]
        chunks = self.chunks()
        pi = 0
        for wc in range(4):
            wb = wch[wc % 2]
            self.load_w_bf16(wb, wqkv.ap[0], wqkv, 8, wc * 512, (wc + 1) * 512)
            for q in range(4):
                ot = wc * 4 + q
                obi = ob[ot % 2]
                for ci, (s, w, isctx) in enumerate(chunks):
                    ps = self.PS[4 + pi % 4]
                    pi += 1
                    for kt in range(8):
                        self.mm(ps.ap[:, 0:w], wb.ap[:, kt, q * 128:(q + 1) * 128], Hres.ap[:, kt, s:s + w], kt == 0, kt == 7, r=[wb, Hres], w=[ps])
                    if ot < 8:
                        self.act(obi.ap[:, s:s + w], ps.ap[:, 0:w], AF.Copy, r=[ps], w=[obi], scale=0.125)
                    else:
                        self.cp("vector", obi.ap[:, s:s + w], ps.ap[:, 0:w], r=[ps], w=[obi])
                dst = QT.ap[ot * 128:(ot + 1) * 128, :] if ot < 8 else KT.ap[(ot - 8) * 128:(ot - 7) * 128, :]
                self.DMA("sync", dst, obi.ap, r=[obi], w=[QT if ot < 8 else KT])

    def na_phase_b(self, QT, KT, VT, BTD, YT):
        self.new_phase()
        V, G = "vector", "gpsimd"
        Qp = [self.bf16(T, "Qp%d" % i) for i in range(2)]
        Kp = [self.bf16(T, "Kp%d" % i) for i in range(2)]
        Ve = [self.bf16(34 * 128, "Ve%d" % i, shape=(34, 128)) for i in range(2)]
        Vo = [self.bf16(33 * 128, "Vo%d" % i, shape=(33, 128)) for i in range(2)]
        BTt = [self.f32(960, "BTt%d" % i) for i in range(2)]
        YTp = [self.bf16(T, "YTp%d" % i) for i in range(2)]
        Qbd = [self.bf16(128, "Qbd%d" % i) for i in range(2)]
        for q_ in Qbd:
            self.memset(G, q_.ap, 0.0, [q_])
        sc = [self.f32(768, "sc%d" % i) for i in range(2)]
        pe = [self.bf16(768, "pe%d" % i) for i in range(2)]
        pn = [self.bf16(768, "pn%d" % i) for i in range(2)]
        pT = [self.bf16(768, "pT%d" % i, shape=(6, 128)) for i in range(2)]
        st = [self.f32(8, "st%d" % i) for i in range(2)]
        blocks = []
        for hp in range(8):
            bl = [("c", i) for i in range(4)] + [("l", r) for r in range(64)]
            for bi, (kind, idx) in enumerate(bl):
                blocks.append((hp, kind, idx, bi == 0, bi == len(bl) - 1))

        def geom(kind, idx):
            if kind == "c":
                return idx * 64, 256, 0, 0
            r = idx
            start = min(max(r - 4, 0), 56)
            return LC + r * 64, 768, start - r + 7, LC + start * 64

        def load_pair(hp):
            h2 = hp % 2
            self.DMA("sync", Qp[h2].ap, QT.ap[hp * 128:(hp + 1) * 128, :], r=[QT], w=[Qp[h2]])
            self.DMA("sync", Kp[h2].ap, KT.ap[hp * 128:(hp + 1) * 128, :], r=[KT], w=[Kp[h2]])
            self.DMA("sync", Ve[h2].ap, VT.ap[:, hp * 128:(hp + 1) * 128].rearrange("(tt p) c -> p tt c", p=128), r=[VT], w=[Ve[h2]])
            self.DMA("sync", Vo[h2].ap, VT.ap[64:64 + 33 * 128, hp * 128:(hp + 1) * 128].rearrange("(tt p) c -> p tt c", p=128), r=[VT], w=[Vo[h2]])
            self.DMA("sync", BTt[h2].ap, BTD.ap[2 * hp:2 * hp + 2].rearrange("two w r c -> (two w) (r c)"), r=[BTD], w=[BTt[h2]])

        def s1(i):
            hp, kind, idx, pfirst, plast = blocks[i]
            h2 = hp % 2
            if pfirst:
                load_pair(hp)
            qpos, nk, ro0, kpos = geom(kind, idx)
            qb, sci, pei, sti = Qbd[i % 2], sc[i % 2], pe[i % 2], st[i % 2]
            ps_l, ps_c = self.PS[(i % 2) * 2], self.PS[(i % 2) * 2 + 1]
            self.cp(G, qb.ap[0:64, 0:64], Qp[h2].ap[0:64, qpos:qpos + 64], r=[Qp[h2]], w=[qb])
            self.cp(G, qb.ap[64:128, 64:128], Qp[h2].ap[64:128, qpos:qpos + 64], r=[Qp[h2]], w=[qb])
            if kind == "l":
                self.mm(ps_l.ap, qb.ap, Kp[h2].ap[:, kpos:kpos + 512], True, True, r=[qb, Kp[h2]], w=[ps_l])
                self.mm(ps_c.ap[:, 0:256], qb.ap, Kp[h2].ap[:, 0:256], True, True, r=[qb, Kp[h2]], w=[ps_c])
                self.tt(V, sci.ap[:, 0:512], ps_l.ap, BTt[h2].ap[:, ro0 * 64:ro0 * 64 + 512], ALU.add, r=[ps_l, BTt[h2]], w=[sci])
                self.cp("scalar", sci.ap[:, 512:768], ps_c.ap[:, 0:256], r=[ps_c], w=[sci])
            else:
                self.mm(ps_c.ap[:, 0:256], qb.ap, Kp[h2].ap[:, 0:256], True, True, r=[qb, Kp[h2]], w=[ps_c])
                self.cp("scalar", sci.ap[:, 0:256], ps_c.ap[:, 0:256], r=[ps_c], w=[sci])
            self.E(V, lambda e, o=sti.ap[:, 0:1], i_=sci.ap[:, 0:nk]: e.reduce_max(out=o, in_=i_, axis=AX.X), r=[sci], w=[sti])
            self.ts(V, sti.ap[:, 1:2], sti.ap[:, 0:1], -1.0, ALU.mult, r=[sti], w=[sti])
            self.act(pei.ap[:, 0:nk], sci.ap[:, 0:nk], AF.Exp, r=[sci, sti], w=[pei, sti], bias=sti.ap[:, 1:2], accum=sti.ap[:, 2:3])
            self.E(V, lambda e, o=sti.ap[:, 3:4], i_=sti.ap[:, 2:3]: e.reciprocal(out=o, in_=i_), r=[sti], w=[sti])

        def s2(i):
            hp, kind, idx, pfirst, plast = blocks[i]
            h2 = hp % 2
            qpos, nk, ro0, kpos = geom(kind, idx)
            pei, pni, pTi, sti = pe[i % 2], pn[i % 2], pT[i % 2], st[i % 2]
            pst, pstb = self.PS[4 + i % 2], self.PSB[4 + i % 2]
            pso = self.PS[6 + i % 2]
            self.ts(G, pni.ap[:, 0:nk], pei.ap[:, 0:nk], sti.ap[:, 3:4], ALU.mult, 0.0, ALU.add, r=[pei, sti], w=[pni])
            nkt = nk // 128
            for kt in range(nkt):
                self.tr(pstb[:, kt * 128:(kt + 1) * 128], pni.ap[:, kt * 128:(kt + 1) * 128], self.identb.ap, r=[pni, self.identb], w=[pst])
            self.cp("scalar", pTi.ap[:, 0:nkt, :], pstb[:, 0:nk].rearrange("p (k c) -> p k c", c=128), r=[pst], w=[pTi])
            for kt in range(nkt):
                if kind == "c":
                    vt = Ve[h2].ap[:, kt, :]
                elif kt >= 4:
                    vt = Ve[h2].ap[:, kt - 4, :]
                else:
                    tok0 = kpos + kt * 128
                    vt = Ve[h2].ap[:, tok0 // 128, :] if tok0 % 128 == 0 else Vo[h2].ap[:, (tok0 - 64) // 128, :]
                self.mm(pso.ap[:, 0:128], vt, pTi.ap[:, kt, :], kt == 0, kt == nkt - 1, r=[Ve[h2], Vo[h2], pTi], w=[pso])
            self.cp(V, YTp[h2].ap[0:64, qpos:qpos + 64], pso.ap[0:64, 0:64], r=[pso], w=[YTp[h2]])
            self.cp("scalar", YTp[h2].ap[64:128, qpos:qpos + 64], pso.ap[64:128, 64:128], r=[pso], w=[YTp[h2]])
            if plast:
                self.DMA("sync", YT.ap[hp * 128:(hp + 1) * 128, :], YTp[h2].ap, r=[YTp[h2]], w=[YT])

        nb_ = len(blocks)
        s1(0)
        for i in range(nb_):
            if i + 1 < nb_:
                s1(i + 1)
            s2(i)


def _build(cfg):
    nc = bass.Bass("TRN2", target_bir_lowering=False)
    kb = KB(nc, cfg)
    IN = lambda n, s: kb.dram_t(n, s, F32, kind="ExternalInput")
    x_in = IN("x", [LL, D])
    ctx_in = IN("ctx", [LC, D])
    c_in = IN("c", [D])
    cctx_in = IN("c_ctx", [D])
    ada_w = IN("ada_w", [DEPTH, D, 6 * D])
    ada_b = IN("ada_b", [DEPTH, 6 * D])
    norm_mix = IN("norm_mix", [DEPTH, D])
    norm_ffn = IN("norm_ffn", [DEPTH, D])
    norm_final = IN("norm_final", [D])
    w1 = IN("ffn_w1", [DEPTH, D, FH])
    w3 = IN("ffn_w3", [DEPTH, D, FH])
    w2 = IN("ffn_w2", [DEPTH, FH, D])
    S5 = {
        "lam_re": IN("s5_lam_re", [2, 2, 64, 64]), "lam_im": IN("s5_lam_im", [2, 2, 64, 64]),
        "log_step": IN("s5_log_step", [2, 2, 64]),
        "b_re": IN("s5_b_re", [2, 2, 64, 64, 16]), "b_im": IN("s5_b_im", [2, 2, 64, 64, 16]),
        "c_re": IN("s5_c_re", [2, 2, 64, 16, 64]), "c_im": IN("s5_c_im", [2, 2, 64, 16, 64]),
        "d": IN("s5_d", [2, D]), "w_glu": IN("s5_w_glu", [2, D, D]), "b_glu": IN("s5_b_glu", [2, D]),
    }
    SSD = {
        "w_in": IN("ssd_w_in", [1, D, 8256]), "conv_w": IN("ssd_conv_w", [1, 5, 6144]), "conv_b": IN("ssd_conv_b", [1, 6144]),
        "dt_bias": IN("ssd_dt_bias", [1, 2, 32]), "a_log": IN("ssd_a_log", [1, 2, 32]), "d": IN("ssd_d", [1, 32]),
        "norm": IN("ssd_norm", [1, 2048]), "w_out": IN("ssd_w_out", [1, 2048, D]),
    }
    NA = {"w_qkv": IN("na_w_qkv", [1, D, 3 * D]), "w_o": IN("na_w_o", [1, D, D]), "rpb": IN("na_rpb", [1, 16, 15, 31])}
    out_t = kb.dram_t("out", [LL, D], F32, kind="ExternalOutput")
    QT = kb.dram_t("QT", [D, T], BF16)
    KT = kb.dram_t("KT", [D, T], BF16)
    VT = kb.dram_t("VT", [T, D], BF16)
    YT = kb.dram_t("YT", [D, T], BF16)
    BTD = kb.dram_t("BTD", [16, 64, 15, 64], F32)
    SZ = kb.dram_t("SZ", [T, 2048], BF16)
    XS = kb.dram_t("XS", [2048, T], BF16)
    BC = kb.dram_t("BC", [4096, T], BF16)
    DTR = kb.dram_t("DTR", [64, T], F32)
    YF = kb.dram_t("YF", [T, 2048], BF16)
    YB = kb.dram_t("YB", [T, 2048], BF16)
    XT = kb.dram_t("XT", [D, T], F32)
    HT = kb.dram_t("HT", [D, T], BF16)
    GT = kb.dram_t("GT", [D, T], BF16)
    layers = cfg.get("layers", list(range(DEPTH)))
    kb.consts()
    kb.build_mask8()
    kb.adaln(c_in, cctx_in, ada_w, ada_b, norm_mix, norm_ffn, norm_final, layers)
    kb.prologue_transpose(x_in, ctx_in, XT)
    for l in layers:
        kind, j = l % 3, l // 3
        if cfg.get("mixer", True):
            kb.norm_layer(XT, HT, l, 0)
            if kind == 0:
                kb.s5_phase(HT, GT, S5, j)
                kb.proj_phase(XT, GT, S5["w_glu"], S5["w_glu"].ap[j], 8, l, glu_bias=(S5["b_glu"], S5["b_glu"].ap[j]))
            elif kind == 1:
                kb.ssd_phase_a(HT, SSD, SZ, XS, BC, DTR)
                kb.ssd_phase_b(SSD, XS, BC, DTR, YF, YB)
                kb.ssd_phase_c(XT, SSD, SZ, YF, YB, l)
            else:
                kb.na_bias_table(NA["rpb"], BTD)
                kb.na_phase_a(HT, NA["w_qkv"], QT, KT, VT)
                kb.na_phase_b(QT, KT, VT, BTD, YT)
                kb.proj_phase(XT, YT, NA["w_o"], NA["w_o"].ap[0], 8, l)
        if cfg.get("ffn", True):
            kb.norm_layer(XT, HT, l, 1)
            kb.ffn_phase(XT, HT, w1, w3, w2, l)
    kb.final_phase(XT, out_t)
    kb.p.fence("sync", kb.out_ops)
    kb.p.build()
    kb.es.close()
    return nc, kb


INPUT_NAMES = ["x", "ctx", "c", "c_ctx", "ada_w", "ada_b", "norm_mix", "norm_ffn", "norm_final", "ffn_w1", "ffn_w3", "ffn_w2",
               "s5_lam_re", "s5_lam_im", "s5_log_step", "s5_b_re", "s5_b_im", "s5_c_re", "s5_c_im", "s5_d", "s5_w_glu", "s5_b_glu",
               "ssd_w_in", "ssd_conv_w", "ssd_conv_b", "ssd_dt_bias", "ssd_a_log", "ssd_d", "ssd_norm", "ssd_w_out",
               "na_w_qkv", "na_w_o", "na_rpb"]


def kernel(**inputs):
    cfg = {}
    nc, kb = _build(cfg)
    n = 8
    in_maps = []
    for b in range(n):
        m = {}
        for k in INPUT_NAMES:
            v = np.ascontiguousarray(inputs[k], dtype=np.float32)
            if k in ("x", "ctx", "c"):
                v = np.ascontiguousarray(v[b])
            m[k] = v
        in_maps.append(m)
    res = run_bass_kernel_spmd(nc, in_maps, core_ids=list(range(n)))
    return np.stack([np.asarray(r["out"], dtype=np.float32) for r in res.results], axis=0)
```

```python
import numpy as np
from contextlib import ExitStack
import concourse.bass as bass
import concourse.mybir as mybir
from concourse.bass_utils import run_bass_kernel_spmd

F32 = mybir.dt.float32
BF16 = mybir.dt.bfloat16
I32 = mybir.dt.int32
AF = mybir.ActivationFunctionType
ALU = mybir.AluOpType
AX = mybir.AxisListType

ENGS = ("sync", "gpsimd", "scalar", "vector", "tensor")
NDMA_SEM = 24
SEM_EPOCH = 12000

D = 1024
LC = 256
LL = 4096
T = LC + LL
NFT = 8
FH = 2816
NHT = 22
DEPTH = 4
EPS = 1e-6
ARENA_F32 = 46 * 1024


class Buf:
    __slots__ = ("name", "w", "r", "psum")

    def __init__(self, name, psum=False):
        self.name = name
        self.w = None
        self.r = []
        self.psum = psum


class Op:
    __slots__ = ("eng", "fn", "idx", "deps", "need_inc", "val", "is_dma", "semi", "is_barrier")

    def __init__(self, eng, fn, is_dma):
        self.eng = eng
        self.fn = fn
        self.is_dma = is_dma
        self.deps = []
        self.need_inc = False
        self.val = 0
        self.semi = -1
        self.idx = -1
        self.is_barrier = False


class Prog:
    def __init__(self, nc):
        self.nc = nc
        self.ops = {e: [] for e in ENGS}
        self.nreal = {e: 0 for e in ENGS}
        self.es = ExitStack()
        self.engsem = {e: self.es.enter_context(nc.semaphore("s_" + e)) for e in ENGS}
        self.dmasem = [self.es.enter_context(nc.semaphore("d%d" % i)) for i in range(NDMA_SEM)]
        self.dma_last = [None] * NDMA_SEM
        self.dma_cnt = [0] * NDMA_SEM
        self.dma_rr = 0

    def emit(self, eng, fn, reads=(), writes=(), dma=False):
        op = Op(eng, fn, dma)
        op.idx = self.nreal[eng]
        self.nreal[eng] += 1
        deps = []
        for b in reads:
            if b.w is not None:
                deps.append(b.w)
            if b.psum:
                deps.extend(b.r)
        for b in writes:
            if b.w is not None:
                deps.append(b.w)
            deps.extend(b.r)
        if dma:
            k = self.dma_rr
            self.dma_rr = (k + 1) % NDMA_SEM
            if self.dma_last[k] is not None:
                deps.append(self.dma_last[k])
            self.dma_last[k] = op
            self.dma_cnt[k] += 16
            op.semi = k
            op.val = self.dma_cnt[k]
        seen = set()
        for d in deps:
            if d is op or id(d) in seen:
                continue
            seen.add(id(d))
            op.deps.append(d)
        for b in reads:
            if b.psum:
                b.w = op
                b.r = []
            else:
                b.r.append(op)
        for b in writes:
            b.w = op
            b.r = []
        self.ops[eng].append(op)
        return op

    def fence(self, eng, deps):
        op = Op(eng, None, False)
        op.idx = self.nreal[eng]
        op.deps = list(deps)
        self.ops[eng].append(op)
        return op

    def barrier(self):
        lasts = []
        for e in ENGS:
            for o in reversed(self.ops[e]):
                if o.fn is not None:
                    lasts.append(o)
                    break
        for o in self.dma_last:
            if o is not None:
                lasts.append(o)
        for e in ENGS:
            self.fence(e, lasts).is_barrier = True

    def _needs_wait(self, op, d):
        if d.is_dma:
            return True
        if d.eng != op.eng:
            return True
        if op.eng == "tensor":
            return False
        return (op.idx - d.idx) <= 2

    def build(self):
        nc = self.nc
        for e in ENGS:
            for op in self.ops[e]:
                for d in op.deps:
                    if not d.is_dma and self._needs_wait(op, d):
                        d.need_inc = True
        self.epoch_sems = {e: [self.engsem[e]] for e in ENGS}
        for e in ENGS:
            c = 0
            ep = 0
            for op in self.ops[e]:
                if op.fn is None:
                    if op.is_barrier and c > SEM_EPOCH:
                        ep += 1
                        c = 0
                        self.epoch_sems[e].append(self.es.enter_context(nc.semaphore("s_%s_%d" % (e, ep))))
                    continue
                if op.is_dma:
                    continue
                op.semi = ep
                if op.need_inc:
                    c += 1
                    op.val = c
        self.counts = {e: 0 for e in ENGS}

        def mk_body(e):
            def body(eng):
                waited = {}
                for op in self.ops[e]:
                    for d in op.deps:
                        if not self._needs_wait(op, d):
                            continue
                        if d.is_dma:
                            key = ("d", d.semi)
                            sem = self.dmasem[d.semi]
                        else:
                            key = ("e", d.eng, d.semi)
                            sem = self.epoch_sems[d.eng][d.semi]
                        if waited.get(key, 0) >= d.val:
                            continue
                        waited[key] = d.val
                        eng.wait_ge(sem, d.val)
                    if op.fn is None:
                        continue
                    inst = op.fn(eng)
                    self.counts[e] += 1
                    if op.is_dma:
                        inst.then_inc(self.dmasem[op.semi], 16)
                    elif op.need_inc:
                        inst.then_inc(self.epoch_sems[e][op.semi], 1)
            return body

        with nc.Block() as block:
            for e in ENGS:
                if self.ops[e]:
                    getattr(block, e)(mk_body(e))
        self.es.close()


class Tl:
    __slots__ = ("ap", "buf")

    def __init__(self, ap, buf):
        self.ap = ap
        self.buf = buf


class KB:
    def __init__(self, nc, cfg):
        self.nc = nc
        self.cfg = cfg
        self.p = Prog(nc)
        self.es = ExitStack()
        self.arena = self.es.enter_context(nc.sbuf_tensor("arena", [128, ARENA_F32], F32))
        self.arena_bf = self.arena.bitcast(BF16)
        self.psum = [self.es.enter_context(nc.psum_tensor("ps%d" % i, [128, 512], F32)) for i in range(8)]
        self.PS = [Tl(self.psum[i][:], Buf("ps%d" % i, psum=True)) for i in range(8)]
        self.PSB = [self.psum[i].bitcast(BF16) for i in range(8)]
        self.top = 0
        self.ptr = 0
        self.nb = 0
        self.dram = {}
        self.out_ops = []

    def _al(self, n_f32, persistent):
        n_f32 = (n_f32 + 7) // 8 * 8
        if persistent:
            assert self.ptr == self.top, "persistent alloc only between phases"
            off = self.top
            self.top += n_f32
            self.ptr = self.top
        else:
            off = self.ptr
            self.ptr += n_f32
        assert self.ptr <= ARENA_F32, "arena overflow %d" % self.ptr
        return off

    def f32(self, n, name=None, persistent=False, shape=None):
        off = self._al(n, persistent)
        ap = self.arena[:, off:off + n]
        if shape is not None:
            ap = self._reshape(ap, shape)
        self.nb += 1
        return Tl(ap, Buf(name or "t%d" % self.nb))

    def bf16(self, n, name=None, persistent=False, shape=None):
        off = self._al((n + 1) // 2, persistent)
        ap = self.arena_bf[:, 2 * off:2 * off + n]
        if shape is not None:
            ap = self._reshape(ap, shape)
        self.nb += 1
        return Tl(ap, Buf(name or "t%d" % self.nb))

    def i32(self, n, name=None):
        off = self._al(n, False)
        ap = self.arena.bitcast(I32)[:, off:off + n]
        self.nb += 1
        return Tl(ap, Buf(name or "t%d" % self.nb))

    @staticmethod
    def _reshape(ap, shape):
        if len(shape) == 2:
            return ap.rearrange("p (a b) -> p a b", b=shape[1])
        if len(shape) == 3:
            return ap.rearrange("p (a b c) -> p a b c", b=shape[1], c=shape[2])
        raise ValueError

    def new_phase(self):
        self.p.barrier()
        self.ptr = self.top

    def dram_t(self, name, shape, dt, kind="Internal"):
        t = self.nc.dram_tensor(name, shape, dt, kind=kind)
        tl = Tl(t.ap(), Buf(name))
        self.dram[name] = tl
        return tl

    def E(self, eng, fn, r=(), w=()):
        return self.p.emit(eng, fn, [t.buf for t in r], [t.buf for t in w])

    def DMA(self, eng, out_ap, in_ap, r=(), w=(), slow=False):
        if slow:
            fn = lambda e: e.dma_start(out=out_ap, in_=in_ap, allow_slow_non_contiguous=True)
        else:
            fn = lambda e: e.dma_start(out=out_ap, in_=in_ap)
        return self.p.emit(eng, fn, [t.buf for t in r], [t.buf for t in w], dma=True)

    def mm(self, ps_ap, lhsT, rhs, start, stop, r=(), w=()):
        return self.E("tensor", lambda e: e.matmul(ps_ap, lhsT=lhsT, rhs=rhs, start=start, stop=stop), r, w)

    def tr(self, ps_ap, in_ap, ident_ap, r=(), w=()):
        return self.E("tensor", lambda e: e.transpose(out=ps_ap, in_=in_ap, identity=ident_ap), r, w)

    def act(self, out, in_, func, r=(), w=(), scale=None, bias=None, accum=None):
        kw = {}
        if scale is not None:
            kw["scale"] = scale
        if bias is not None:
            kw["bias"] = bias
        if accum is not None:
            kw["accum_out"] = accum
        return self.E("scalar", lambda e: e.activation(out=out, in_=in_, func=func, **kw), r, w)

    def tt(self, eng, out, in0, in1, op, r=(), w=()):
        return self.E(eng, lambda e: e.tensor_tensor(out=out, in0=in0, in1=in1, op=op), r, w)

    def ts(self, eng, out, in0, s1, op0, s2=None, op1=None, r=(), w=(), accum=None):
        if op1 is None:
            return self.E(eng, lambda e: e.tensor_scalar(out=out, in0=in0, scalar1=s1, scalar2=None, op0=op0), r, w)
        if accum is not None:
            return self.E(eng, lambda e: e.tensor_scalar(out=out, in0=in0, scalar1=s1, scalar2=s2, op0=op0, op1=op1, accum_out=accum), r, w)
        return self.E(eng, lambda e: e.tensor_scalar(out=out, in0=in0, scalar1=s1, scalar2=s2, op0=op0, op1=op1), r, w)

    def stt(self, out, in0, scalar, in1, op0, op1, r=(), w=()):
        return self.E("vector", lambda e: e.scalar_tensor_tensor(out=out, in0=in0, scalar=scalar, in1=in1, op0=op0, op1=op1), r, w)

    def cp(self, eng, out, in_, r=(), w=()):
        if eng == "scalar":
            return self.act(out, in_, AF.Copy, r, w)
        return self.E(eng, lambda e: e.tensor_copy(out=out, in_=in_), r, w)

    def memset(self, eng, ap, val, w=()):
        return self.E(eng, lambda e: e.memset(ap, val), (), w)

    def consts(self):
        self.ident = self.f32(128, "ident", True)
        self.identb = self.bf16(128, "identb", True)
        self.ones = self.f32(128, "ones", True)
        self.epsc = self.f32(8, "epsc", True)
        self.memset("gpsimd", self.ident.ap, 0.0, [self.ident])
        idap = self.ident.ap
        self.E("gpsimd", lambda e: e.affine_select(out=idap, in_=idap, pattern=[[-1, 128]], compare_op=ALU.not_equal,
                                                   fill=1.0, base=0, channel_multiplier=1), [self.ident], [self.ident])
        self.cp("vector", self.identb.ap, self.ident.ap, [self.ident], [self.identb])
        self.memset("vector", self.ones.ap, 1.0, [self.ones])
        self.memset("vector", self.epsc.ap[:, 0:1], EPS, [self.epsc])
        self.memset("vector", self.epsc.ap[:, 1:2], 0.0, [self.epsc])
        self.memset("vector", self.epsc.ap[:, 2:3], 1.0, [self.epsc])

    @staticmethod
    def chunks(w_lat=512):
        ch = [(0, LC, True)]
        for s in range(LC, T, w_lat):
            ch.append((s, w_lat, False))
        return ch

    def prologue_transpose(self, x_in, ctx_in, XT):
        self.new_phase()
        xin = [self.f32(4 * D, "xin%d" % i, shape=(4, D)) for i in range(2)]
        stage = [self.f32(NFT * 512, "stg%d" % i, shape=(NFT, 512)) for i in range(2)]
        for ci, (s, w, isctx) in enumerate(self.chunks()):
            xi = xin[ci % 2]
            st = stage[ci % 2]
            ntt = w // 128
            if isctx:
                src = ctx_in.ap.rearrange("(tt p) f -> p tt f", p=128)
            else:
                src = x_in.ap[s - LC:s - LC + w, :].rearrange("(tt p) f -> p tt f", p=128)
            self.DMA("sync", xi.ap[:, 0:ntt, :], src, r=[x_in], w=[xi])
            for ft in range(NFT):
                ps = self.PS[ft]
                for tt in range(ntt):
                    self.tr(ps.ap[:, tt * 128:(tt + 1) * 128], xi.ap[:, tt, ft * 128:(ft + 1) * 128], self.ident.ap,
                            r=[xi, self.ident], w=[ps])
                self.cp("scalar" if ft % 2 else "vector", st.ap[:, ft, 0:w], ps.ap[:, 0:w], r=[ps], w=[st])
            dst = XT.ap[:, s:s + w].rearrange("(ft p) t -> p ft t", p=128)
            self.DMA("sync", dst, st.ap[:, :, 0:w], r=[st], w=[XT])

    def adaln(self, c_in, cctx_in, ada_w, ada_b, norm_mix, norm_ffn, norm_final, layers):
        self.mod = {}
        for l in layers:
            self.mod[l] = self.f32(96, "mod%d" % l, True, shape=(6, 8, 2))
        self.nw = self.f32(9 * 8, "nw", True, shape=(9, 8))
        self.AB = {}
        for l in layers:
            self.AB[l] = self.f32(4 * 16, "AB%d" % l, True, shape=(4, 8, 2))
        self.new_phase()
        sT = self.f32(16, "sT", shape=(8, 2))
        craw = self.f32(16, "craw", shape=(8, 2))
        self.DMA("sync", craw.ap[:, :, 0], c_in.ap.rearrange("(kt p) -> p kt", p=128), r=[c_in], w=[craw], slow=True)
        self.DMA("sync", craw.ap[:, :, 1], cctx_in.ap.rearrange("(kt p) -> p kt", p=128), r=[cctx_in], w=[craw], slow=True)
        self.act(sT.ap, craw.ap, AF.Silu, r=[craw], w=[sT])
        for k, nwt in enumerate([norm_mix, norm_ffn]):
            self.DMA("sync", self.nw.ap[:, 4 * k:4 * k + 4, :], nwt.ap.rearrange("l (ft p) -> p l ft", p=128), r=[nwt], w=[self.nw], slow=True)
        self.DMA("sync", self.nw.ap[:, 8, :], norm_final.ap.rearrange("(ft p) -> p ft", p=128), r=[norm_final], w=[self.nw], slow=True)
        wbuf = [self.f32(8 * 512, "adaw%d" % i, shape=(8, 512)) for i in range(3)]
        bbuf = [self.f32(512, "adab%d" % i) for i in range(3)]
        onesrow = self.ones.ap[0:1, 0:2]
        it = 0
        for l in layers:
            for cj in range(12):
                wb = wbuf[it % 3]
                bb = bbuf[it % 3]
                ps = self.PS[it % 4]
                it += 1
                self.DMA("sync", wb.ap, ada_w.ap[l, :, cj * 512:(cj + 1) * 512].rearrange("(kt p) n -> p kt n", p=128), r=[ada_w], w=[wb])
                self.DMA("sync", bb.ap[0:1, :], ada_b.ap[l:l + 1, cj * 512:(cj + 1) * 512], r=[ada_b], w=[bb])
                for jj in range(4):
                    j = cj * 4 + jj
                    o = ps.ap[:, jj * 2:jj * 2 + 2]
                    for kt in range(8):
                        self.mm(o, wb.ap[:, kt, jj * 128:(jj + 1) * 128], sT.ap[:, kt, :], kt == 0, False, r=[wb, sT], w=[ps])
                    self.mm(o, bb.ap[0:1, jj * 128:(jj + 1) * 128], onesrow, False, True, r=[bb, self.ones], w=[ps])
                m = cj * 4 // 8
                ft0 = (cj * 4) % 8
                self.cp("vector", self.mod[l].ap[:, m, ft0:ft0 + 4, :], ps.ap[:, 0:8].rearrange("p (a b) -> p a b", b=2), r=[ps], w=[self.mod[l]])
        for l in layers:
            for k, (mi, nwi) in enumerate([(1, l), (4, 4 + l)]):
                nwb = self.nw.ap[:, nwi, :].unsqueeze(2).to_broadcast([128, 8, 2])
                self.stt(self.AB[l].ap[:, k, :, :], self.mod[l].ap[:, mi, :, :], 1.0, nwb, ALU.add, ALU.mult, r=[self.mod[l], self.nw], w=[self.AB[l]])

    def norm_phase(self, XT, HT, A_sel, B_sel, deps_r):
        self.new_phase()
        xin = [self.f32(NFT * 512, "nx%d" % i, shape=(NFT, 512)) for i in range(2)]
        sq = [self.f32(NFT * 512, "nsq%d" % i, shape=(NFT, 512)) for i in range(2)]
        hb = [self.bf16(NFT * 512, "nh%d" % i, shape=(NFT, 512)) for i in range(2)]
        rt = [self.f32(512, "nrt%d" % i) for i in range(2)]
        chs = self.chunks()

        def load(ci):
            s, w, isctx = chs[ci]
            xi = xin[ci % 2]
            self.DMA("sync", xi.ap[:, :, 0:w], XT.ap[:, s:s + w].rearrange("(ft p) t -> p ft t", p=128), r=[XT], w=[xi])

        load(0)
        for ci, (s, w, isctx) in enumerate(chs):
            if ci + 1 < len(chs):
                load(ci + 1)
            xi, sqi, hbi, rti = xin[ci % 2], sq[ci % 2], hb[ci % 2], rt[ci % 2]
            ps = self.PS[ci % 2]
            self.act(sqi.ap[:, :, 0:w], xi.ap[:, :, 0:w], AF.Square, r=[xi], w=[sqi])
            for ft in range(NFT):
                self.mm(ps.ap[:, 0:w], self.ones.ap, sqi.ap[:, ft, 0:w], ft == 0, ft == NFT - 1, r=[self.ones, sqi], w=[ps])
            self.act(rti.ap[:, 0:w], ps.ap[:, 0:w], AF.Sqrt, r=[ps, self.epsc], w=[rti], scale=1.0 / D, bias=self.epsc.ap[:, 0:1])
            self.E("vector", lambda e, o=rti.ap[:, 0:w]: e.reciprocal(out=o, in_=o), r=[rti], w=[rti])
            for ft in range(NFT):
                a = A_sel(ft, isctx)
                b = B_sel(ft, isctx)
                self.stt(sqi.ap[:, ft, 0:w], xi.ap[:, ft, 0:w], a, rti.ap[:, 0:w], ALU.mult, ALU.mult, r=[xi, rti] + deps_r, w=[sqi])
                self.act(hbi.ap[:, ft, 0:w], sqi.ap[:, ft, 0:w], AF.Identity, r=[sqi] + deps_r, w=[hbi], bias=b)
            self.DMA("sync", HT.ap[:, s:s + w].rearrange("(ft p) t -> p ft t", p=128), hbi.ap[:, :, 0:w], r=[hbi], w=[HT])

    def norm_layer(self, XT, HT, l, which):
        AB = self.AB[l]
        mod = self.mod[l]
        k = 0 if which == 0 else 1
        smi = 0 if which == 0 else 3
        A_sel = lambda ft, isctx: AB.ap[:, k, ft, (1 if isctx else 0):(1 if isctx else 0) + 1]
        B_sel = lambda ft, isctx: mod.ap[:, smi, ft, (1 if isctx else 0):(1 if isctx else 0) + 1]
        self.norm_phase(XT, HT, A_sel, B_sel, [AB, mod])

    def final_phase(self, XT, out_t):
        self.new_phase()
        xin = [self.f32(NFT * 512, "fx%d" % i, shape=(NFT, 512)) for i in range(2)]
        sq = [self.f32(NFT * 512, "fsq%d" % i, shape=(NFT, 512)) for i in range(2)]
        rt = [self.f32(512, "frt%d" % i) for i in range(2)]
        ob = [self.f32(4 * D, "fo%d" % i, shape=(4, D)) for i in range(2)]
        ci = 0
        lat = [c for c in self.chunks() if not c[2]]

        def loadf(i):
            s, w, _ = lat[i]
            self.DMA("sync", xin[i % 2].ap, XT.ap[:, s:s + w].rearrange("(ft p) t -> p ft t", p=128), r=[XT], w=[xin[i % 2]])

        loadf(0)
        for (s, w, isctx) in lat:
            xi, sqi, rti, obi = xin[ci % 2], sq[ci % 2], rt[ci % 2], ob[ci % 2]
            ps = self.PS[ci % 2]
            ci += 1
            if ci < len(lat):
                loadf(ci)
            self.act(sqi.ap, xi.ap, AF.Square, r=[xi], w=[sqi])
            for ft in range(NFT):
                self.mm(ps.ap, self.ones.ap, sqi.ap[:, ft, :], ft == 0, ft == NFT - 1, r=[self.ones, sqi], w=[ps])
            self.act(rti.ap, ps.ap, AF.Sqrt, r=[ps, self.epsc], w=[rti], scale=1.0 / D, bias=self.epsc.ap[:, 0:1])
            self.E("vector", lambda e, o=rti.ap: e.reciprocal(out=o, in_=o), r=[rti], w=[rti])
            for ft in range(NFT):
                self.stt(sqi.ap[:, ft, :], xi.ap[:, ft, :], self.nw.ap[:, 8, ft:ft + 1], rti.ap, ALU.mult, ALU.mult, r=[xi, rti, self.nw], w=[sqi])
            for tt in range(4):
                for half in range(2):
                    pso = self.PS[2 + (tt * 2 + half) % 6]
                    for q in range(4):
                        ft = half * 4 + q
                        self.tr(pso.ap[:, q * 128:(q + 1) * 128], sqi.ap[:, ft, tt * 128:(tt + 1) * 128], self.ident.ap, r=[sqi, self.ident], w=[pso])
                    self.cp("scalar" if half else "vector", obi.ap[:, tt, half * 512:(half + 1) * 512], pso.ap, r=[pso], w=[obi])
            dst = out_t.ap[s - LC:s - LC + w, :].rearrange("(tt p) f -> p tt f", p=128)
            self.out_ops.append(self.DMA("sync", dst, obi.ap, r=[obi], w=[out_t]))

    def load_w_bf16(self, dst, src_ap, src_tl, nkt, col0=None, col1=None):
        for kt in range(nkt):
            s = src_ap[kt * 128:(kt + 1) * 128, :] if col0 is None else src_ap[kt * 128:(kt + 1) * 128, col0:col1]
            self.DMA("gpsimd", dst.ap[:, kt, :], s, r=[src_tl], w=[dst])

    def ffn_phase(self, XT, HT, w1, w3, w2, l):
        self.new_phase()
        W = 256
        w1s = self.bf16(8 * FH, "w1s", shape=(8, FH))
        w3s = self.bf16(8 * FH, "w3s", shape=(8, FH))
        w2s = self.bf16(NHT * D, "w2s", shape=(NHT, D))
        self.load_w_bf16(w1s, w1.ap[l], w1, 8)
        self.load_w_bf16(w3s, w3.ap[l], w3, 8)
        self.load_w_bf16(w2s, w2.ap[l], w2, NHT)
        hb = [self.bf16(NFT * W, "fh%d" % i, shape=(NFT, W)) for i in range(2)]
        xb = [self.f32(NFT * W, "fxx%d" % i, shape=(NFT, W)) for i in range(2)]
        sqb = self.f32(NFT * W, "fsq", shape=(NFT, W))
        rtb = [self.f32(W, "frt%d" % i) for i in range(2)]
        gb = [self.bf16(NHT * W, "fg%d" % i, shape=(NHT, W)) for i in range(1)]
        sl = [self.bf16(W, "fs%d" % i) for i in range(2)]
        mod = self.mod[l]
        AB = self.AB[l]
        ntile = T // W

        def load(ti):
            s = ti * W
            self.DMA("sync", xb[ti % 2].ap, XT.ap[:, s:s + W].rearrange("(ft p) t -> p ft t", p=128), r=[XT], w=[xb[ti % 2]])

        def norm(ti):
            s = ti * W
            cs = 1 if s < LC else 0
            hbi, xbi, rti = hb[ti % 2], xb[ti % 2], rtb[ti % 2]
            ps = self.PS[7]
            self.act(sqb.ap, xbi.ap, AF.Square, r=[xbi], w=[sqb])
            for ft in range(NFT):
                self.mm(ps.ap[:, 0:W], self.ones.ap, sqb.ap[:, ft, :], ft == 0, ft == NFT - 1, r=[self.ones, sqb], w=[ps])
            self.act(rti.ap, ps.ap[:, 0:W], AF.Sqrt, r=[ps, self.epsc], w=[rti], scale=1.0 / D, bias=self.epsc.ap[:, 0:1])
            self.E("vector", lambda e, o=rti.ap: e.reciprocal(out=o, in_=o), r=[rti], w=[rti])
            for ft in range(NFT):
                self.stt(sqb.ap[:, ft, :], xbi.ap[:, ft, :], AB.ap[:, 1, ft, cs:cs + 1], rti.ap, ALU.mult, ALU.mult, r=[xbi, rti, AB], w=[sqb])
                self.act(hbi.ap[:, ft, :], sqb.ap[:, ft, :], AF.Identity, r=[sqb, mod], w=[hbi], bias=mod.ap[:, 3, ft, cs:cs + 1])

        load(0)
        norm(0)
        for ti in range(ntile):
            s = ti * W
            cs = 1 if s < LC else 0
            hbi, xbi, gbi = hb[ti % 2], xb[ti % 2], gb[0]
            if ti + 1 < ntile:
                load(ti + 1)
            for j in range(NHT):
                pa = self.PS[(j % 2) * 2]
                pb = self.PS[(j % 2) * 2 + 1]
                for kt in range(8):
                    self.mm(pa.ap[:, 0:W], w1s.ap[:, kt, j * 128:(j + 1) * 128], hbi.ap[:, kt, :], kt == 0, kt == 7, r=[w1s, hbi], w=[pa])
                for kt in range(8):
                    self.mm(pb.ap[:, 0:W], w3s.ap[:, kt, j * 128:(j + 1) * 128], hbi.ap[:, kt, :], kt == 0, kt == 7, r=[w3s, hbi], w=[pb])
                sli = sl[j % 2]
                self.act(sli.ap, pa.ap[:, 0:W], AF.Silu, r=[pa], w=[sli])
                self.tt("vector", gbi.ap[:, j, :], pb.ap[:, 0:W], sli.ap, ALU.mult, r=[pb, sli], w=[gbi])
                if j == 10 and ti + 1 < ntile:
                    norm(ti + 1)
            for fo in range(NFT):
                po = self.PS[4 + fo % 3]
                for j in range(NHT):
                    self.mm(po.ap[:, 0:W], w2s.ap[:, j, fo * 128:(fo + 1) * 128], gbi.ap[:, j, :], j == 0, j == NHT - 1, r=[w2s, gbi], w=[po])
                self.stt(xbi.ap[:, fo, :], po.ap[:, 0:W], mod.ap[:, 5, fo, cs:cs + 1], xbi.ap[:, fo, :], ALU.mult, ALU.add, r=[po, mod, xbi], w=[xbi])
            self.DMA("sync", XT.ap[:, s:s + W].rearrange("(ft p) t -> p ft t", p=128), xbi.ap, r=[xbi], w=[XT])

    def rev_ap(self, ap2d, start, n):
        pstride = ap2d.ap[0][0]
        return bass.AP(ap2d.tensor, ap2d.offset + start + n - 1, [[pstride, 128], [-1, n]])

    def sincos_turns(self, eng, turns, n, osin, ocos, tmp, cast_eng="vector"):
        ti, tf, fr, s2, s4 = tmp["ti"], tmp["tf"], tmp["fr"], tmp["s2"], tmp["s4"]
        sl = lambda t: t.ap[:, 0:n]
        self.cp(cast_eng, sl(ti), sl(turns), r=[turns], w=[ti])
        self.cp(cast_eng, sl(tf), sl(ti), r=[ti], w=[tf])
        self.tt(eng, sl(fr), sl(turns), sl(tf), ALU.subtract, r=[turns, tf], w=[fr])
        self.act(sl(s2), sl(fr), AF.Sin, r=[fr], w=[s2], scale=float(np.pi))
        self.act(sl(s4), sl(fr), AF.Sin, r=[fr], w=[s4], scale=float(np.pi / 2))
        self.tt(eng, sl(s4), sl(s4), sl(s4), ALU.mult, r=[s4], w=[s4])
        self.ts(eng, sl(s4), sl(s4), -4.0, ALU.mult, 2.0, ALU.add, r=[s4], w=[s4])
        self.tt(eng, sl(osin), sl(s2), sl(s4), ALU.mult, r=[s2, s4], w=[osin])
        self.tt(eng, sl(s2), sl(s2), sl(s2), ALU.mult, r=[s2], w=[s2])
        self.ts(eng, sl(ocos), sl(s2), -2.0, ALU.mult, 1.0, ALU.add, r=[s2], w=[ocos])

    def build_mask8(self):
        self.mask8 = self.f32(8, "mask8", True)
        m = self.mask8.ap
        self.memset("gpsimd", m, 1.0, [self.mask8])
        self.E("gpsimd", lambda e: e.affine_select(out=m, in_=m, pattern=[[-16, 8]], compare_op=ALU.is_ge, fill=0.0, base=0, channel_multiplier=1), [self.mask8], [self.mask8])
        self.E("gpsimd", lambda e: e.affine_select(out=m, in_=m, pattern=[[16, 8]], compare_op=ALU.is_ge, fill=0.0, base=15, channel_multiplier=-1), [self.mask8], [self.mask8])

    def s5_phase(self, HT, GT, P, j):
        self.new_phase()
        V, G = "vector", "gpsimd"
        def sc_tile(nm):
            return self.f32(64, nm)
        lr, li, ls = sc_tile("lr"), sc_tile("li"), sc_tile("ls")
        for d in range(2):
            self.DMA("sync", lr.ap[:, d * 32:(d + 1) * 32], P["lam_re"].ap[j, d].rearrange("(p two) n -> (two n) p", two=2), r=[P["lam_re"]], w=[lr], slow=True)
            self.DMA("sync", li.ap[:, d * 32:(d + 1) * 32], P["lam_im"].ap[j, d].rearrange("(p two) n -> (two n) p", two=2), r=[P["lam_im"]], w=[li], slow=True)
        lsrow = self.f32(128, "lsrow")
        self.DMA("sync", lsrow.ap[0:1, :], P["log_step"].ap[j:j + 1].rearrange("o d g -> o (d g)"), r=[P["log_step"]], w=[lsrow])
        psb = self.PS[0]
        self.mm(psb.ap[:, 0:128], self.ones.ap[0:1, :], lsrow.ap[0:1, :], True, True, r=[self.ones, lsrow], w=[psb])
        for d in range(2):
            src = psb.ap[:, d * 64:(d + 1) * 64].rearrange("q (p two) -> q p two", two=2)
            self.cp(V, ls.ap[0:64, d * 32:(d + 1) * 32], src[0:64, :, 0], r=[psb], w=[ls])
            self.cp(V, ls.ap[64:128, d * 32:(d + 1) * 32], src[64:128, :, 1], r=[psb], w=[ls])
        step, zr, zi, rr, tq = sc_tile("step"), sc_tile("zr"), sc_tile("zi"), sc_tile("rr"), sc_tile("tq")
        tmp = {"ti": self.i32(512, "ti"), "tf": self.f32(512, "tf"), "fr": self.f32(512, "fr"), "s2": self.f32(512, "s2"), "s4": self.f32(512, "s4")}
        tmp2 = {"ti": self.i32(512, "ti2"), "tf": self.f32(512, "tf2"), "fr": self.f32(512, "fr2"), "s2": self.f32(512, "s22"), "s4": self.f32(512, "s42")}
        sphi, cphi, frac = sc_tile("sphi"), sc_tile("cphi"), sc_tile("frac")
        self.act(step.ap, ls.ap, AF.Exp, r=[ls], w=[step])
        self.tt(V, zr.ap, lr.ap, step.ap, ALU.mult, r=[lr, step], w=[zr])
        self.tt(V, zi.ap, li.ap, step.ap, ALU.mult, r=[li, step], w=[zi])
        self.act(rr.ap, zr.ap, AF.Exp, r=[zr], w=[rr])
        self.ts(V, tq.ap, zi.ap, float(1.0 / (2 * np.pi)), ALU.mult, r=[zi], w=[tq])
        self.sincos_turns(V, tq, 64, sphi, cphi, tmp)
        self.cp(V, frac.ap, tmp["fr"].ap[:, 0:64], r=[tmp["fr"]], w=[frac])
        carry = {}
        for Q in (256, 512):
            tQ, sQ, cQ = sc_tile("tQ%d" % Q), sc_tile("sQ%d" % Q), sc_tile("cQ%d" % Q)
            self.ts(V, tQ.ap, frac.ap, float(Q), ALU.mult, r=[frac], w=[tQ])
            self.sincos_turns(V, tQ, 64, sQ, cQ, tmp)
            carry[Q] = (sQ, cQ)
        ar, ai, den, u, cr, ci, t1s, t2s = [sc_tile(n) for n in ("ar", "ai", "den", "u", "cr", "ci", "t1s", "t2s")]
        self.tt(V, ar.ap, rr.ap, cphi.ap, ALU.mult, r=[rr, cphi], w=[ar])
        self.tt(V, ai.ap, rr.ap, sphi.ap, ALU.mult, r=[rr, sphi], w=[ai])
        self.tt(V, t1s.ap, lr.ap, lr.ap, ALU.mult, r=[lr], w=[t1s])
        self.tt(V, t2s.ap, li.ap, li.ap, ALU.mult, r=[li], w=[t2s])
        self.tt(V, den.ap, t1s.ap, t2s.ap, ALU.add, r=[t1s, t2s], w=[den])
        self.E(V, lambda e: e.reciprocal(out=den.ap, in_=den.ap), r=[den], w=[den])
        self.ts(V, u.ap, ar.ap, -1.0, ALU.add, r=[ar], w=[u])
        self.tt(V, t1s.ap, u.ap, lr.ap, ALU.mult, r=[u, lr], w=[t1s])
        self.tt(V, t2s.ap, ai.ap, li.ap, ALU.mult, r=[ai, li], w=[t2s])
        self.tt(V, t1s.ap, t1s.ap, t2s.ap, ALU.add, r=[t1s, t2s], w=[t1s])
        self.tt(V, cr.ap, t1s.ap, den.ap, ALU.mult, r=[t1s, den], w=[cr])
        self.tt(V, t1s.ap, ai.ap, lr.ap, ALU.mult, r=[ai, lr], w=[t1s])
        self.tt(V, t2s.ap, u.ap, li.ap, ALU.mult, r=[u, li], w=[t2s])
        self.tt(V, t1s.ap, t1s.ap, t2s.ap, ALU.subtract, r=[t1s, t2s], w=[t1s])
        self.tt(V, ci.ap, t1s.ap, den.ap, ALU.mult, r=[t1s, den], w=[ci])
        braw = {}
        for nm in ("b_re", "b_im"):
            braw[nm] = self.f32(2 * 32 * 16, "braw_" + nm, shape=(2, 32, 16))
            for d in range(2):
                self.DMA("sync", braw[nm].ap[:, d, :, :], P[nm].ap[j, d].rearrange("(p two) n h -> (two n) p h", two=2), r=[P[nm]], w=[braw[nm]], slow=True)
        dsk = self.f32(8, "dsk")
        self.DMA("sync", dsk.ap, P["d"].ap[j].rearrange("(ft p) -> p ft", p=128), r=[P["d"]], w=[dsk], slow=True)
        M1 = {}
        for k in range(4):
            for nm in ("b_re", "b_im"):
                M1[(k, nm)] = self.f32(128, "M1_%d%s" % (k, nm))
                self.memset(G, M1[(k, nm)].ap, 0.0, [M1[(k, nm)]])
        Jrow = self.f32(512, "Jrow")
        self.E(G, lambda e: e.iota(Jrow.ap, pattern=[[1, 512]], base=0, channel_multiplier=0, allow_small_or_imprecise_dtypes=True), (), [Jrow])
        WTS = [self.bf16(8 * 6 * 128, "wts%d" % i, shape=(8, 6, 128)) for i in range(2)]
        craw = [[self.f32(64, "craw%d_%d" % (i, q)) for q in range(2)] for i in range(2)]
        Spair = [self.f32(128, "Spair%d" % i) for i in range(2)]
        U = [self.bf16(T, "U%d" % i) for i in range(2)]
        Yacc = self.f32(T, "Yacc")
        gt = self.bf16(T, "gt")
        cmr, cpr, ncpr = sc_tile("cmr"), sc_tile("cpr"), sc_tile("ncpr")
        self.tt(V, cmr.ap, ci.ap, cr.ap, ALU.subtract, r=[ci, cr], w=[cmr])
        self.tt(V, cpr.ap, ci.ap, cr.ap, ALU.add, r=[ci, cr], w=[cpr])
        self.ts(V, ncpr.ap, cpr.ap, -1.0, ALU.mult, r=[cpr], w=[ncpr])
        lastc = [self.f32(2, "lastc%d" % i) for i in range(2)]
        nsQ = {}
        for Q in (256, 512):
            nsQ[Q] = sc_tile("nsQ%d" % Q)
            self.ts(V, nsQ[Q].ap, carry[Q][0].ap, -1.0, ALU.mult, r=[carry[Q][0]], w=[nsQ[Q]])
        TAB = [{n: self.f32(512, "%s%d" % (n, i)) for n in ("COS", "SIN", "wr", "bma", "apb", "ta", "tb")} for i in range(2)]
        WK = [{n: self.f32(512, "%s%d" % (n, i)) for n in ("k1", "k2", "k3", "pss", "bre", "bim")} for i in range(2)]
        MK = [{n: self.bf16(512, "%s%d" % (n, i)) for n in ("m1", "m2", "m3", "m4")} for i in range(2)]
        init = [self.f32(4, "init%d" % i) for i in range(2)]
        fwd_chunks = [(0, LC)] + [(s, 512) for s in range(LC, T, 512)]
        bwd_chunks = [(0, LC)] + [(T - 512 * (i + 1), 512) for i in range(8)]
        tgc = [0]

        def emit_prep(ft):
            Ui = U[ft % 2]
            self.DMA("sync", Ui.ap, HT.ap[ft * 128:(ft + 1) * 128, :], r=[HT], w=[Ui])
            W = WTS[ft % 2]
            for d in range(2):
                cr_ = craw[d]
                self.DMA("sync", cr_[0].ap, P["c_re"].ap[j, d, ft * 8:(ft + 1) * 8].rearrange("g h n -> (g h) n"), r=[P["c_re"]], w=[cr_[0]])
                self.DMA("sync", cr_[1].ap, P["c_im"].ap[j, d, ft * 8:(ft + 1) * 8].rearrange("g h n -> (g h) n"), r=[P["c_im"]], w=[cr_[1]])
                for k in range(4):
                    p_ = ft * 4 + k
                    wi_ = d * 4 + k
                    c1, c2 = 32 * k, 32 * k + 16
                    for bi, nm in enumerate(("b_re", "b_im")):
                        m1t = M1[(k, nm)]
                        self.cp(G, m1t.ap[0:64, c1:c1 + 16], braw[nm].ap[0:64, d, p_, :], r=[braw[nm]], w=[m1t])
                        self.cp(G, m1t.ap[64:128, c2:c2 + 16], braw[nm].ap[64:128, d, p_, :], r=[braw[nm]], w=[m1t])
                        ps = self.PS[5]
                        self.tr(ps.ap[:, bi * 128:(bi + 1) * 128], m1t.ap, self.ident.ap, r=[m1t, self.ident], w=[ps])
                        self.cp("scalar", W.ap[:, wi_, bi, :], ps.ap[:, bi * 128:(bi + 1) * 128], r=[ps], w=[W])
                    self.tt(G, W.ap[:, wi_, 5, :], W.ap[:, wi_, 0, :], W.ap[:, wi_, 1, :], ALU.add, r=[W], w=[W])
                    for q in range(2):
                        sp = Spair[q]
                        self.ts(G, sp.ap[:, 0:64], cr_[q].ap, self.mask8.ap[:, 2 * k:2 * k + 1], ALU.mult, 0.0, ALU.add, r=[cr_[q], self.mask8], w=[sp])
                        self.ts(G, sp.ap[:, 64:128], cr_[q].ap, self.mask8.ap[:, 2 * k + 1:2 * k + 2], ALU.mult, 0.0, ALU.add, r=[cr_[q], self.mask8], w=[sp])
                        ps = self.PS[5]
                        self.tr(ps.ap[:, 256 + q * 128:256 + (q + 1) * 128], sp.ap, self.ident.ap, r=[sp, self.ident], w=[ps])
                        src = ps.ap[:, 256 + q * 128:256 + (q + 1) * 128]
                        if q == 0:
                            self.cp("scalar", W.ap[:, wi_, 2, :], src, r=[ps], w=[W])
                            self.act(W.ap[:, wi_, 3, :], src, AF.Copy, r=[ps], w=[W], scale=-1.0)
                        else:
                            self.act(W.ap[:, wi_, 4, :], src, AF.Copy, r=[ps], w=[W], scale=-1.0)

        def table_thunks(tab, col):
            th = []
            add = th.append
            MAGIC = 12582912.0
            ti, tf, fr, s2, s4 = tmp2["ti"], tmp2["tf"], tmp2["fr"], tmp2["s2"], tmp2["s4"]
            sc1 = lambda t: t.ap[:, col:col + 1]
            add(lambda: self.act(tab["ta"].ap, Jrow.ap, AF.Identity, r=[Jrow, frac], w=[tab["ta"]], scale=sc1(frac)))
            add(lambda: self.act(tf.ap, tab["ta"].ap, AF.Identity, r=[tab["ta"]], w=[tf], bias=MAGIC))
            add(lambda: self.act(tf.ap, tf.ap, AF.Identity, r=[tf], w=[tf], bias=-MAGIC))
            add(lambda: self.tt(G, fr.ap, tab["ta"].ap, tf.ap, ALU.subtract, r=[tab["ta"], tf], w=[fr]))
            add(lambda: self.act(s2.ap, fr.ap, AF.Sin, r=[fr], w=[s2], scale=float(np.pi)))
            add(lambda: self.act(s4.ap, fr.ap, AF.Sin, r=[fr], w=[s4], scale=float(np.pi / 2)))
            add(lambda: self.act(s4.ap, s4.ap, AF.Square, r=[s4], w=[s4]))
            add(lambda: self.act(s4.ap, s4.ap, AF.Identity, r=[s4], w=[s4], scale=-4.0, bias=2.0))
            add(lambda: self.tt(G, tab["SIN"].ap, s2.ap, s4.ap, ALU.mult, r=[s2, s4], w=[tab["SIN"]]))
            add(lambda: self.act(s2.ap, s2.ap, AF.Square, r=[s2], w=[s2]))
            add(lambda: self.act(tab["COS"].ap, s2.ap, AF.Identity, r=[s2], w=[tab["COS"]], scale=-2.0, bias=1.0))
            for (c1, c2, dst) in ((cr, ci, "wr"), (cmr, ncpr, "bma"), (cpr, cmr, "apb")):
                add(lambda c1=c1: self.act(tab["ta"].ap, tab["COS"].ap, AF.Identity, r=[tab["COS"], c1], w=[tab["ta"]], scale=sc1(c1)))
                add(lambda c2=c2: self.act(tab["tb"].ap, tab["SIN"].ap, AF.Identity, r=[tab["SIN"], c2], w=[tab["tb"]], scale=sc1(c2)))
                add(lambda dst=dst: self.tt(G, tab[dst].ap, tab["ta"].ap, tab["tb"].ap, ALU.add, r=[tab["ta"], tab["tb"]], w=[tab[dst]]))
            return th

        items = []
        pd = 0
        for ft in range(NFT):
            for d in range(2):
                chunks = fwd_chunks if d == 0 else bwd_chunks
                for k in range(4):
                    for cidx, (s, n) in enumerate(chunks):
                        items.append(dict(ft=ft, d=d, k=k, cidx=cidx, s=s, n=n, pd=pd, last=(cidx == len(chunks) - 1),
                                          first_pd=(cidx == 0), first_y=(d == 0 and k == 0),
                                          ft_last=(d == 1 and k == 3 and cidx == len(chunks) - 1)))
                    pd += 1
        pending = []

        def stage_a(i, it_):
            ft, d, k, s, n = it_["ft"], it_["d"], it_["k"], it_["s"], it_["n"]
            if i == 0:
                emit_prep(0)
                for f in table_thunks(TAB[0], 0):
                    f()
            if d == 1 and k == 0 and it_["cidx"] == 0 and ft + 1 < NFT:
                emit_prep(ft + 1)
            if it_["cidx"] == 1 and it_["pd"] + 1 < 64:
                npd = it_["pd"] + 1
                nft, nd, nk = npd // 8, (npd // 4) % 2, npd % 4
                pending.extend(table_thunks(TAB[npd % 2], nd * 32 + nft * 4 + nk))
            if it_["cidx"] >= 1:
                ntake = len(pending) if it_["last"] else min(3, len(pending))
                for _ in range(ntake):
                    pending.pop(0)()
            tab = TAB[it_["pd"] % 2]
            Ui, W, wi_ = U[ft % 2], WTS[ft % 2], d * 4 + k
            wk = WK[i % 2]
            pre, pim, psu = self.PS[0], self.PS[1], self.PS[2]
            urhs = Ui.ap[:, s:s + n] if d == 0 else self.rev_ap(Ui.ap, s, n)
            self.mm(pre.ap[:, 0:n], W.ap[:, wi_, 0, :], urhs, True, True, r=[W, Ui], w=[pre])
            self.mm(pim.ap[:, 0:n], W.ap[:, wi_, 1, :], urhs, True, True, r=[W, Ui], w=[pim])
            self.mm(psu.ap[:, 0:n], W.ap[:, wi_, 5, :], urhs, True, True, r=[W, Ui], w=[psu])
            c_ = lambda t: t.ap[:, 0:n]
            self.cp("scalar", c_(wk["pss"]), psu.ap[:, 0:n], r=[psu], w=[wk["pss"]])
            self.tt(V, c_(wk["k2"]), pre.ap[:, 0:n], c_(tab["bma"]), ALU.mult, r=[pre, tab["bma"]], w=[wk["k2"]])
            self.tt(V, c_(wk["k3"]), pim.ap[:, 0:n], c_(tab["apb"]), ALU.mult, r=[pim, tab["apb"]], w=[wk["k3"]])
            self.tt(G, c_(wk["k1"]), c_(wk["pss"]), c_(tab["wr"]), ALU.mult, r=[wk["pss"], tab["wr"]], w=[wk["k1"]])
            self.tt(G, c_(wk["bre"]), c_(wk["k1"]), c_(wk["k3"]), ALU.subtract, r=[wk["k1"], wk["k3"]], w=[wk["bre"]])
            self.tt(G, c_(wk["bim"]), c_(wk["k1"]), c_(wk["k2"]), ALU.add, r=[wk["k1"], wk["k2"]], w=[wk["bim"]])

        def stage_b(i, it_):
            ft, d, k, s, n, cidx = it_["ft"], it_["d"], it_["k"], it_["s"], it_["n"], it_["cidx"]
            tab = TAB[it_["pd"] % 2]
            col = d * 32 + ft * 4 + k
            Ui, W, wi_ = U[ft % 2], WTS[ft % 2], d * 4 + k
            wk, mk, ini = WK[i % 2], MK[i % 2], init[i % 2]
            py = self.PS[3 + i % 2]
            c_ = lambda t: t.ap[:, 0:n]
            rcol = rr.ap[:, col:col + 1]
            rb = rcol.to_broadcast([128, n])
            if cidx == 0:
                i_re, i_im, ir = 0.0, 0.0, []
            else:
                pini = init[(i - 1) % 2]
                i_re, i_im, ir = pini.ap[:, 0:1], pini.ap[:, 1:2], [pini]
            gre_t, gim_t = self.PS[6], self.PS[7]
            self.E(V, lambda e, o=gre_t.ap[:, 0:n], d1=c_(wk["bre"]), i0=i_re, rb=rb: e.tensor_tensor_scan(out=o, data0=rb, data1=d1, initial=i0, op0=ALU.mult, op1=ALU.add),
                   r=[wk["bre"], rr] + ir, w=[gre_t])
            self.E(V, lambda e, o=gim_t.ap[:, 0:n], d1=c_(wk["bim"]), i0=i_im, rb=rb: e.tensor_tensor_scan(out=o, data0=rb, data1=d1, initial=i0, op0=ALU.mult, op1=ALU.add),
                   r=[wk["bim"], rr] + ir, w=[gim_t])
            if not it_["last"]:
                sQ, cQ = carry[n]
                sq_, cq_, nsq_ = sQ.ap[:, col:col + 1], cQ.ap[:, col:col + 1], nsQ[n].ap[:, col:col + 1]
                lc = lastc[i % 2]
                self.cp(V, lc.ap[:, 0:1], gre_t.ap[:, n - 1:n], r=[gre_t], w=[lc])
                self.cp(V, lc.ap[:, 1:2], gim_t.ap[:, n - 1:n], r=[gim_t], w=[lc])
                gre_l, gim_l = lc.ap[:, 0:1], lc.ap[:, 1:2]
                self.act(ini.ap[:, 2:3], gim_l, AF.Identity, r=[lc, nsQ[n]], w=[ini], scale=nsq_)
                self.act(ini.ap[:, 3:4], gim_l, AF.Identity, r=[lc, cQ], w=[ini], scale=cq_)
                self.act(ini.ap[:, 0:1], gre_l, AF.Identity, r=[lc, cQ, ini], w=[ini], scale=cq_, bias=ini.ap[:, 2:3])
                self.act(ini.ap[:, 1:2], gre_l, AF.Identity, r=[lc, sQ, ini], w=[ini], scale=sq_, bias=ini.ap[:, 3:4])
            self.tt(V, c_(mk["m1"]), gre_t.ap[:, 0:n], c_(tab["COS"]), ALU.mult, r=[gre_t, tab["COS"]], w=[mk["m1"]])
            self.tt(V, c_(mk["m3"]), gre_t.ap[:, 0:n], c_(tab["SIN"]), ALU.mult, r=[gre_t, tab["SIN"]], w=[mk["m3"]])
            self.tt(V, c_(mk["m2"]), gim_t.ap[:, 0:n], c_(tab["SIN"]), ALU.mult, r=[gim_t, tab["SIN"]], w=[mk["m2"]])
            self.tt(V, c_(mk["m4"]), gim_t.ap[:, 0:n], c_(tab["COS"]), ALU.mult, r=[gim_t, tab["COS"]], w=[mk["m4"]])
            for mi, (mn, wsel) in enumerate((("m1", 2), ("m2", 3), ("m3", 4), ("m4", 4))):
                mr = mk[mn].ap[:, 0:n] if d == 0 else self.rev_ap(mk[mn].ap, 0, n)
                self.mm(py.ap[:, 0:n], W.ap[:, wi_, wsel, :], mr, mi == 0, mi == 3, r=[W, mk[mn]], w=[py])
            if it_["first_y"]:
                self.cp("scalar", Yacc.ap[:, s:s + n], py.ap[:, 0:n], r=[py], w=[Yacc])
            else:
                self.tt(V, Yacc.ap[:, s:s + n], py.ap[:, 0:n], Yacc.ap[:, s:s + n], ALU.add, r=[py, Yacc], w=[Yacc])
            if it_["ft_last"]:
                self.stt(Yacc.ap, Ui.ap, dsk.ap[:, ft:ft + 1], Yacc.ap, ALU.mult, ALU.add, r=[Ui, dsk, Yacc], w=[Yacc])
                self.act(gt.ap, Yacc.ap, AF.Gelu_apprx_tanh, r=[Yacc], w=[gt])
                self.DMA("sync", GT.ap[ft * 128:(ft + 1) * 128, :], gt.ap, r=[gt], w=[GT])

        stage_a(0, items[0])
        for i in range(len(items)):
            if i + 1 < len(items):
                stage_a(i + 1, items[i + 1])
            stage_b(i, items[i])

    def proj_phase(self, XT, IN, Wd, w_ap, nkt, l, glu_bias=None):
        self.new_phase()
        Wt = 512
        ws = self.bf16(nkt * D, "pw", shape=(nkt, D))
        self.load_w_bf16(ws, w_ap, Wd, nkt)
        bg = None
        if glu_bias is not None:
            bg = self.f32(8, "bglu")
            self.DMA("sync", bg.ap, glu_bias[1].rearrange("(ft p) -> p ft", p=128), r=[glu_bias[0]], w=[bg], slow=True)
        ib = [self.bf16(nkt * Wt, "pi%d" % i, shape=(nkt, Wt)) for i in range(2)]
        xb = [self.f32(NFT * Wt, "px%d" % i, shape=(NFT, Wt)) for i in range(2)]
        sg = [self.f32(Wt, "psg%d" % i) for i in range(2)]
        mod = self.mod[l]
        chs = self.chunks()

        def load(ci):
            s, w, isctx = chs[ci]
            self.DMA("sync", ib[ci % 2].ap[:, :, 0:w], IN.ap[:, s:s + w].rearrange("(kt p) t -> p kt t", p=128), r=[IN], w=[ib[ci % 2]])
            self.DMA("sync", xb[ci % 2].ap[:, :, 0:w], XT.ap[:, s:s + w].rearrange("(ft p) t -> p ft t", p=128), r=[XT], w=[xb[ci % 2]])

        load(0)
        for ci, (s, w, isctx) in enumerate(chs):
            cs = 1 if isctx else 0
            ibi, xbi = ib[ci % 2], xb[ci % 2]
            if ci + 1 < len(chs):
                load(ci + 1)
            for fo in range(NFT):
                po = self.PS[fo % 4]
                for kt in range(nkt):
                    self.mm(po.ap[:, 0:w], ws.ap[:, kt, fo * 128:(fo + 1) * 128], ibi.ap[:, kt, 0:w], kt == 0, kt == nkt - 1, r=[ws, ibi], w=[po])
                gate = mod.ap[:, 2, fo, cs:cs + 1]
                if glu_bias is not None:
                    sgi = sg[fo % 2]
                    self.act(sgi.ap[:, 0:w], po.ap[:, 0:w], AF.Sigmoid, r=[po, bg], w=[sgi], bias=bg.ap[:, fo:fo + 1])
                    self.tt("gpsimd", sgi.ap[:, 0:w], sgi.ap[:, 0:w], ibi.ap[:, fo, 0:w], ALU.mult, r=[sgi, ibi], w=[sgi])
                    self.stt(xbi.ap[:, fo, 0:w], sgi.ap[:, 0:w], gate, xbi.ap[:, fo, 0:w], ALU.mult, ALU.add, r=[sgi, mod, xbi], w=[xbi])
                else:
                    self.stt(xbi.ap[:, fo, 0:w], po.ap[:, 0:w], gate, xbi.ap[:, fo, 0:w], ALU.mult, ALU.add, r=[po, mod, xbi], w=[xbi])
            self.DMA("sync", XT.ap[:, s:s + w].rearrange("(ft p) t -> p ft t", p=128), xbi.ap[:, :, 0:w], r=[xbi], w=[XT])


    def row_bcast(self, dst, row_ap, src_tl, n, rowtmp, func=None, scale=None):
        self.DMA("sync", rowtmp.ap[0:1, 0:n], row_ap, r=[src_tl], w=[rowtmp])
        for c0 in range(0, n, 512):
            w = min(512, n - c0)
            ps = self.PS[7]
            self.mm(ps.ap[:, 0:w], self.ones.ap[0:1, :], rowtmp.ap[0:1, c0:c0 + w], True, True, r=[self.ones, rowtmp], w=[ps])
            if func is None:
                self.cp("vector", dst.ap[:, c0:c0 + w], ps.ap[:, 0:w], r=[ps], w=[dst])
            else:
                self.act(dst.ap[:, c0:c0 + w], ps.ap[:, 0:w], func, r=[ps], w=[dst], scale=scale)

    def tri_mask(self, nm, base, cm, step, op):
        t = self.f32(128, nm)
        self.memset("gpsimd", t.ap, 1.0, [t])
        self.E("gpsimd", lambda e: e.affine_select(out=t.ap, in_=t.ap, pattern=[[step, 128]], compare_op=op, fill=0.0, base=base, channel_multiplier=cm), [t], [t])
        return t

    def ssd_phase_a(self, HT, P, SZ, XS, BC, DTR):
        self.new_phase()
        w_in = P["w_in"]
        Hres = self.bf16(8 * T, "Hres", shape=(8, T))
        for kt in range(8):
            self.DMA("sync", Hres.ap[:, kt, :], HT.ap[kt * 128:(kt + 1) * 128, :], r=[HT], w=[Hres])
        wz = self.bf16(8 * 2048, "wz", shape=(8, 2048))
        self.load_w_bf16(wz, w_in.ap[0], w_in, 8, 0, 2048)
        szb = [self.bf16(2048, "szb%d" % i) for i in range(2)]
        for tt in range(T // 128):
            sb = szb[tt % 2]
            for zc in range(4):
                ps = self.PS[zc]
                for kt in range(8):
                    self.mm(ps.ap, Hres.ap[:, kt, tt * 128:(tt + 1) * 128], wz.ap[:, kt, zc * 512:(zc + 1) * 512], kt == 0, kt == 7, r=[Hres, wz], w=[ps])
                self.act(sb.ap[:, zc * 512:(zc + 1) * 512], ps.ap, AF.Silu, r=[ps], w=[sb])
            self.DMA("sync", SZ.ap[tt * 128:(tt + 1) * 128, :], sb.ap, r=[sb], w=[SZ])
        self.new_phase()
        Hres = self.bf16(8 * T, "Hres2", shape=(8, T))
        for kt in range(8):
            self.DMA("sync", Hres.ap[:, kt, :], HT.ap[kt * 128:(kt + 1) * 128, :], r=[HT], w=[Hres])
        cw = self.f32(5 * 48, "cw", shape=(5, 48))
        cb = self.f32(48, "cb")
        for k in range(5):
            self.DMA("sync", cw.ap[:, k, :], P["conv_w"].ap[0, k].rearrange("(ot p) -> p ot", p=128), r=[P["conv_w"]], w=[cw], slow=True)
        self.DMA("sync", cb.ap, P["conv_b"].ap[0].rearrange("(ot p) -> p ot", p=128), r=[P["conv_b"]], w=[cb], slow=True)
        wch = [self.bf16(8 * 512, "wch%d" % i, shape=(8, 512)) for i in range(2)]
        xp = [self.bf16(T, "xp%d" % i) for i in range(2)]
        ob = [self.bf16(T, "ob%d" % i) for i in range(2)]
        dg = [self.bf16(5 * 128, "dg%d" % i, shape=(5, 128)) for i in range(2)]
        chunks = self.chunks()
        pi = 0
        for wc in range(12):
            wb = wch[wc % 2]
            self.load_w_bf16(wb, w_in.ap[0], w_in, 8, 2048 + wc * 512, 2048 + (wc + 1) * 512)
            for q in range(4):
                ot = wc * 4 + q
                xpi, obi, dgi = xp[ot % 2], ob[ot % 2], dg[ot % 2]
                for k in range(5):
                    self.ts("gpsimd", dgi.ap[:, k, :], self.identb.ap, cw.ap[:, k, ot:ot + 1], ALU.mult, 0.0, ALU.add, r=[self.identb, cw], w=[dgi])
                for ci, (s, w, isctx) in enumerate(chunks):
                    ps = self.PS[pi % 4]
                    pi += 1
                    for kt in range(8):
                        self.mm(ps.ap[:, 0:w], wb.ap[:, kt, q * 128:(q + 1) * 128], Hres.ap[:, kt, s:s + w], kt == 0, kt == 7, r=[wb, Hres], w=[ps])
                    self.cp("vector" if ci % 2 else "scalar", xpi.ap[:, s:s + w], ps.ap[:, 0:w], r=[ps], w=[xpi])
                for ci, (s, w, isctx) in enumerate(chunks):
                    q0, q1 = (0, LC) if isctx else (LC, T)
                    ps = self.PS[4 + ci % 4]
                    for ki, k in enumerate((2, 0, 1, 3, 4)):
                        o = k - 2
                        i0 = max(0, q0 - s - o)
                        i1 = min(w, q1 - s - o)
                        self.mm(ps.ap[:, i0:i1], dgi.ap[:, k, :], xpi.ap[:, s + i0 + o:s + i1 + o], ki == 0, ki == 4, r=[dgi, xpi], w=[ps])
                    self.act(obi.ap[:, s:s + w], ps.ap[:, 0:w], AF.Silu, r=[ps, cb], w=[obi], bias=cb.ap[:, ot:ot + 1])
                dst = XS.ap[ot * 128:(ot + 1) * 128, :] if ot < 16 else BC.ap[(ot - 16) * 128:(ot - 15) * 128, :]
                self.DMA("sync", dst, obi.ap, r=[obi], w=[XS if ot < 16 else BC])
        wdt = self.bf16(8 * 64, "wdt", shape=(8, 64))
        self.load_w_bf16(wdt, w_in.ap[0], w_in, 8, 8192, 8256)
        dtb = self.f32(T, "dtb")
        for ci, (s, w, isctx) in enumerate(chunks):
            ps = self.PS[ci % 4]
            for kt in range(8):
                self.mm(ps.ap[0:64, 0:w], wdt.ap[:, kt, :], Hres.ap[:, kt, s:s + w], kt == 0, kt == 7, r=[wdt, Hres], w=[ps])
            self.cp("vector", dtb.ap[0:64, s:s + w], ps.ap[0:64, 0:w], r=[ps], w=[dtb])
        self.DMA("sync", DTR.ap, dtb.ap[0:64, :], r=[dtb], w=[DTR])

    def ssd_phase_b(self, P, XS, BC, DTR, YF, YB):
        self.new_phase()
        V, G = "vector", "gpsimd"
        GT_ = self.tri_mask("mGT", 0, 1, -1, ALU.is_gt)
        LE_ = self.tri_mask("mLE", 0, -1, 1, ALU.is_ge)
        LT_ = self.tri_mask("mLT", 0, -1, 1, ALU.is_gt)
        GE_ = self.tri_mask("mGE", 0, 1, -1, ALU.is_ge)
        rowtmp = self.f32(64, "rowtmp")
        Arow = [self.f32(32, "Arow%d" % d) for d in range(2)]
        Brow = [self.f32(32, "Brow%d" % d) for d in range(2)]
        Drow = self.f32(32, "Drow")
        for d in range(2):
            self.row_bcast(Arow[d], P["a_log"].ap[0, d:d + 1, :], P["a_log"], 32, rowtmp, func=AF.Exp)
            self.ts(V, Arow[d].ap, Arow[d].ap, -1.0, ALU.mult, r=[Arow[d]], w=[Arow[d]])
            self.row_bcast(Brow[d], P["dt_bias"].ap[0, d:d + 1, :], P["dt_bias"], 32, rowtmp)
        self.row_bcast(Drow, P["d"].ap[0:1, :], P["d"], 32, rowtmp)
        H = self.f32(2048, "Hst", shape=(32, 64))
        Hb = self.bf16(2048, "Hb")
        NB = 2
        xsT = [self.bf16(16 * 128, "xsT%d" % i, shape=(16, 128)) for i in range(NB)]
        BTt = [self.bf16(8 * 128, "BT%d" % i, shape=(8, 128)) for i in range(NB)]
        CTt = [self.bf16(8 * 128, "CT%d" % i, shape=(8, 128)) for i in range(NB)]
        dtr = [self.f32(128, "dtr%d" % i) for i in range(NB)]
        ybuf = [self.bf16(2048, "ybuf%d" % i) for i in range(NB)]
        xdt = [self.bf16(2048, "xdt%d" % i, shape=(32, 64)) for i in range(NB)]
        xdte = [self.bf16(2048, "xdte%d" % i, shape=(32, 64)) for i in range(NB)]
        dskt = [self.f32(2048, "dskt%d" % i, shape=(32, 64)) for i in range(NB)]
        Btok = [self.bf16(1024, "Btok%d" % i, shape=(8, 128)) for i in range(NB)]
        dtt = [self.f32(32, "dtt%d" % i) for i in range(NB)]
        adt = [self.f32(32, "adt%d" % i) for i in range(NB)]
        dex = [self.f32(96, "dex%d" % i) for i in range(NB)]
        dtd = [self.f32(32, "dtd%d" % i) for i in range(NB)]
        Xh = [self.f32(512, "Xh%d" % i, shape=(4, 128)) for i in range(2)]
        Ld = [self.bf16(512, "Ld%d" % i, shape=(4, 128)) for i in range(2)]
        Mt = [self.bf16(512, "Mt%d" % i, shape=(4, 128)) for i in range(2)]
        CBm = [self.bf16(128, "CBm%d" % i) for i in range(2)]
        ytmp = [self.f32(256, "ytmp%d" % i, shape=(4, 64)) for i in range(2)]
        htmp = [self.f32(256, "htmp%d" % i, shape=(4, 64)) for i in range(2)]
        nchunk = T // 128
        work = []
        for d in range(2):
            order = list(range(nchunk)) if d == 0 else [1, 0] + list(range(nchunk - 1, 1, -1))
            for oi, c in enumerate(order):
                work.append((d, c, oi == 0))
        cfgd = {0: dict(mX=GT_, mE=LE_, mTE=GT_, mSeg=LE_, mCB=LE_), 1: dict(mX=LT_, mE=GE_, mTE=LT_, mSeg=GE_, mCB=GE_)}

        def prologue(wi):
            d, c, first = work[wi]
            cf = cfgd[d]
            b = wi % NB
            s0 = c * 128
            xi, bi, cti, dri = xsT[b], BTt[b], CTt[b], dtr[b]
            self.DMA("sync", xi.ap, XS.ap[:, s0:s0 + 128].rearrange("(t p) c -> p t c", p=128), r=[XS], w=[xi])
            self.DMA("sync", bi.ap, BC.ap[(d * 2) * 1024:(d * 2 + 1) * 1024, s0:s0 + 128].rearrange("(t p) c -> p t c", p=128), r=[BC], w=[bi])
            self.DMA("sync", cti.ap, BC.ap[(d * 2 + 1) * 1024:(d * 2 + 2) * 1024, s0:s0 + 128].rearrange("(t p) c -> p t c", p=128), r=[BC], w=[cti])
            self.DMA("sync", dri.ap[0:32, :], DTR.ap[d * 32:(d + 1) * 32, s0:s0 + 128], r=[DTR], w=[dri])
            psm = self.PS[6]
            dtt_, adt_, dex_, dtd_ = dtt[b], adt[b], dex[b], dtd[b]
            self.tr(psm.ap[:, 0:32], dri.ap[0:32, :], self.ident.ap[0:32, 0:32], r=[dri, self.ident], w=[psm])
            self.tt(V, dtt_.ap, psm.ap[:, 0:32], Brow[d].ap, ALU.add, r=[psm, Brow[d]], w=[dtt_])
            self.act(dtt_.ap, dtt_.ap, AF.Exp, r=[dtt_], w=[dtt_])
            self.act(dtt_.ap, dtt_.ap, AF.Ln, r=[dtt_, self.epsc], w=[dtt_], bias=self.epsc.ap[:, 2:3])
            self.tt(V, adt_.ap, dtt_.ap, Arow[d].ap, ALU.mult, r=[dtt_, Arow[d]], w=[adt_])
            self.mm(psm.ap[:, 32:64], self.ones.ap, adt_.ap, True, True, r=[self.ones, adt_], w=[psm])
            self.mm(psm.ap[:, 64:96], cf["mTE"].ap, adt_.ap, True, True, r=[cf["mTE"], adt_], w=[psm])
            self.mm(psm.ap[:, 96:128], cf["mE"].ap, adt_.ap, True, True, r=[cf["mE"], adt_], w=[psm])
            self.act(dex_.ap, psm.ap[:, 32:128], AF.Exp, r=[psm], w=[dex_])
            self.tt(V, dtd_.ap, dtt_.ap, dex_.ap[:, 32:64], ALU.mult, r=[dtt_, dex_], w=[dtd_])
            pst = self.PS[7]
            psb16 = self.PSB[7]
            for half in range(2):
                for q in range(8):
                    t_ = half * 8 + q
                    self.tr(psb16[:, q * 128:(q + 1) * 128], xi.ap[:, t_, :], self.identb.ap, r=[xi, self.identb], w=[pst])
                src = psb16[:, 0:1024].rearrange("p (h c) -> p h c", c=64)
                hs = slice(half * 16, (half + 1) * 16)
                bc = lambda col: col.unsqueeze(2).to_broadcast([128, 16, 64])
                self.tt(V, xdt[b].ap[:, hs, :], src, bc(dtt_.ap[:, hs]), ALU.mult, r=[pst, dtt_], w=[xdt[b]])
                self.tt(V, xdte[b].ap[:, hs, :], src, bc(dtd_.ap[:, hs]), ALU.mult, r=[pst, dtd_], w=[xdte[b]])
                if d == 0:
                    self.tt(V, dskt[b].ap[:, hs, :], src, bc(Drow.ap[:, hs]), ALU.mult, r=[pst, Drow], w=[dskt[b]])
            for g in range(8):
                self.tr(psb16[:, g * 128:(g + 1) * 128], bi.ap[:, g, :], self.identb.ap, r=[bi, self.identb], w=[pst])
            self.cp("scalar", Btok[b].ap, psb16[:, 0:1024].rearrange("p (g c) -> p g c", c=128), r=[pst], w=[Btok[b]])

        gi = [0]

        def s1(wi, g):
            d, c, first = work[wi]
            cf = cfgd[d]
            b = wi % NB
            k = (wi * 8 + g) % 2
            xh = Xh[k]
            for hh in range(4):
                h = g * 4 + hh
                self.ts(G, xh.ap[:, hh, :], cf["mX"].ap, adt[b].ap[:, h:h + 1], ALU.mult, 0.0, ALU.add, r=[cf["mX"], adt[b]], w=[xh])
            pcb = self.PS[k]
            self.mm(pcb.ap[:, 0:128], BTt[b].ap[:, g, :], CTt[b].ap[:, g, :], True, True, r=[BTt[b], CTt[b]], w=[pcb])
            psg = self.PS[4 + k]
            for hh in range(4):
                self.mm(psg.ap[:, hh * 128:(hh + 1) * 128], xh.ap[:, hh, :], cf["mSeg"].ap, True, True, r=[xh, cf["mSeg"]], w=[psg])

        def s2(wi, g):
            d, c, first = work[wi]
            cf = cfgd[d]
            k = (wi * 8 + g) % 2
            pcb, psg = self.PS[k], self.PS[4 + k]
            self.act(Ld[k].ap, psg.ap.rearrange("p (h c) -> p h c", c=128), AF.Exp, r=[psg], w=[Ld[k]])
            self.tt(V, CBm[k].ap, pcb.ap[:, 0:128], cf["mCB"].ap, ALU.mult, r=[pcb, cf["mCB"]], w=[CBm[k]])
            self.tt(V, Mt[k].ap, Ld[k].ap, CBm[k].ap.unsqueeze(1).to_broadcast([128, 4, 128]), ALU.mult, r=[Ld[k], CBm[k]], w=[Mt[k]])

        def s3(wi, g):
            d, c, first = work[wi]
            b = wi % NB
            k = (wi * 8 + g) % 2
            py = self.PS[2 + k]
            cti = CTt[b]
            for hh in range(4):
                h = g * 4 + hh
                self.mm(py.ap[:, hh * 64:(hh + 1) * 64], Mt[k].ap[:, hh, :], xdt[b].ap[:, h, :], True, True, r=[Mt[k], xdt[b]], w=[py])
                self.mm(py.ap[:, 256 + hh * 64:256 + (hh + 1) * 64], cti.ap[:, g, :], Hb.ap[:, h * 64:(h + 1) * 64], True, True, r=[cti, Hb], w=[py])
            gs = slice(g * 4, (g + 1) * 4)
            yt = ytmp[k]
            yb = ybuf[b]
            Eb = dex[b].ap[:, 64 + g * 4:64 + (g + 1) * 4].unsqueeze(2).to_broadcast([128, 4, 64])
            self.tt(V, yt.ap, py.ap[:, 256:512].rearrange("p (h c) -> p h c", c=64), Eb, ALU.mult, r=[py, dex[b]], w=[yt])
            if d == 0:
                self.tt(G, yt.ap, yt.ap, dskt[b].ap[:, gs, :], ALU.add, r=[yt, dskt[b]], w=[yt])
            self.tt(V, yb.ap[:, g * 256:(g + 1) * 256].rearrange("p (h c) -> p h c", c=64), py.ap[:, 0:256].rearrange("p (h c) -> p h c", c=64), yt.ap, ALU.add, r=[py, yt], w=[yb])
            pst2 = self.PS[6]
            self.mm(pst2.ap[:, 256:512], Btok[b].ap[:, g, :], xdte[b].ap[:, gs, :].rearrange("p h c -> p (h c)"), True, True, r=[Btok[b], xdte[b]], w=[pst2])
            ht = htmp[k]
            Db = dex[b].ap[:, g * 4:(g + 1) * 4].unsqueeze(2).to_broadcast([128, 4, 64])
            self.tt(G, ht.ap, H.ap[:, gs, :], Db, ALU.mult, r=[H, dex[b]], w=[ht])
            self.tt(V, H.ap[:, gs, :], pst2.ap[:, 256:512].rearrange("p (h c) -> p h c", c=64), ht.ap, ALU.add, r=[pst2, ht], w=[H])
            self.cp("scalar", Hb.ap[:, g * 256:(g + 1) * 256], H.ap[:, gs, :].rearrange("p h c -> p (h c)"), r=[H], w=[Hb])

        nw = len(work)
        prologue(0)
        s1(0, 0)
        for wi in range(nw):
            d, c, first = work[wi]
            if first:
                self.memset(V, H.ap, 0.0, [H])
                self.memset(V, Hb.ap, 0.0, [Hb])
            for g in range(8):
                if g == 4 and wi + 1 < nw:
                    prologue(wi + 1)
                if g + 1 < 8:
                    s1(wi, g + 1)
                elif wi + 1 < nw:
                    s1(wi + 1, 0)
                s2(wi, g)
                s3(wi, g)
            YO = YF if d == 0 else YB
            self.DMA("sync", YO.ap[c * 128:c * 128 + 128, :], ybuf[wi % NB].ap, r=[ybuf[wi % NB]], w=[YO])

    def ssd_phase_c(self, XT, P, SZ, YF, YB, l):
        self.new_phase()
        V, G = "vector", "gpsimd"
        wo = self.bf16(16 * D, "wo", shape=(16, D))
        self.load_w_bf16(wo, P["w_out"].ap[0], P["w_out"], 16)
        nrow = self.f32(2048, "nrow")
        rowtmp = self.f32(2048, "rowtmp2")
        self.row_bcast(nrow, P["norm"].ap[0:1, :], P["norm"], 2048, rowtmp)
        yf = [self.bf16(2048, "cyf%d" % i) for i in range(2)]
        ybk = [self.bf16(2048, "cyb%d" % i) for i in range(2)]
        sz = [self.bf16(2048, "csz%d" % i) for i in range(2)]
        yy = [self.f32(2048, "cyy%d" % i) for i in range(2)]
        sqj = self.f32(2048, "csq")
        gnb = [self.bf16(2048, "cgn%d" % i) for i in range(2)]
        ss = [self.f32(8, "css%d" % i) for i in range(2)]
        gnT = [self.bf16(16 * 512, "gnT%d" % i, shape=(16, 512)) for i in range(2)]
        xb = [self.f32(NFT * 512, "cx%d" % i, shape=(NFT, 512)) for i in range(2)]
        mod = self.mod[l]
        ti = 0
        chs = self.chunks()

        def load_tile(tix):
            t0 = tix * 128
            self.DMA("sync", yf[tix % 2].ap, YF.ap[t0:t0 + 128, :], r=[YF], w=[yf[tix % 2]])
            self.DMA("sync", ybk[tix % 2].ap, YB.ap[t0:t0 + 128, :], r=[YB], w=[ybk[tix % 2]])
            self.DMA("sync", sz[tix % 2].ap, SZ.ap[t0:t0 + 128, :], r=[SZ], w=[sz[tix % 2]])

        def load_x(ci):
            s, w, isctx = chs[ci]
            self.DMA("sync", xb[ci % 2].ap[:, :, 0:w], XT.ap[:, s:s + w].rearrange("(ft p) t -> p ft t", p=128), r=[XT], w=[xb[ci % 2]])

        load_tile(0)
        load_x(0)
        for ci, (s, w, isctx) in enumerate(chs):
            cs = 1 if isctx else 0
            gT, xbi = gnT[ci % 2], xb[ci % 2]
            if ci + 1 < len(chs):
                load_x(ci + 1)
            for tt in range(w // 128):
                t0 = s + tt * 128
                a, b, z_, y_, g_, s_ = yf[ti % 2], ybk[ti % 2], sz[ti % 2], yy[ti % 2], gnb[ti % 2], ss[ti % 2]
                ti += 1
                if ti < T // 128:
                    load_tile(ti)
                self.tt(G, y_.ap, a.ap, b.ap, ALU.add, r=[a, b], w=[y_])
                self.tt(V, y_.ap, y_.ap, z_.ap, ALU.mult, r=[y_, z_], w=[y_])
                self.act(sqj.ap, y_.ap, AF.Square, r=[y_], w=[sqj, s_], accum=s_.ap[:, 0:1])
                self.act(s_.ap[:, 1:2], s_.ap[:, 0:1], AF.Sqrt, r=[s_, self.epsc], w=[s_], scale=1.0 / 2048, bias=self.epsc.ap[:, 0:1])
                self.E(V, lambda e, o=s_.ap[:, 2:3], i=s_.ap[:, 1:2]: e.reciprocal(out=o, in_=i), r=[s_], w=[s_])
                self.stt(g_.ap, y_.ap, s_.ap[:, 2:3], nrow.ap, ALU.mult, ALU.mult, r=[y_, s_, nrow], w=[g_])
                for half in range(2):
                    pst = self.PS[4 + (ti * 2 + half) % 4]
                    psb16 = self.PSB[4 + (ti * 2 + half) % 4]
                    for q in range(8):
                        kt = half * 8 + q
                        self.tr(psb16[:, q * 128:(q + 1) * 128], g_.ap[:, kt * 128:(kt + 1) * 128], self.identb.ap, r=[g_, self.identb], w=[pst])
                    self.cp("scalar" if half else "vector", gT.ap[:, half * 8:(half + 1) * 8, tt * 128:(tt + 1) * 128],
                            psb16[:, 0:1024].rearrange("p (k c) -> p k c", c=128), r=[pst], w=[gT])
            for fo in range(NFT):
                po = self.PS[fo % 4]
                for kt in range(16):
                    self.mm(po.ap[:, 0:w], wo.ap[:, kt, fo * 128:(fo + 1) * 128], gT.ap[:, kt, 0:w], kt == 0, kt == 15, r=[wo, gT], w=[po])
                self.stt(xbi.ap[:, fo, 0:w], po.ap[:, 0:w], mod.ap[:, 2, fo, cs:cs + 1], xbi.ap[:, fo, 0:w], ALU.mult, ALU.add, r=[po, mod, xbi], w=[xbi])
            self.DMA("sync", XT.ap[:, s:s + w].rearrange("(ft p) t -> p ft t", p=128), xbi.ap[:, :, 0:w], r=[xbi], w=[XT])


    def na_bias_table(self, rpb, BTD):
        self.new_phase()
        neg = self.f32(7680, "negt")
        self.memset("vector", neg.ap, -30000.0, [neg])
        self.DMA("sync", BTD.ap.rearrange("h w r c -> (h w r c)").rearrange("(p n) -> p n", p=128), neg.ap, r=[neg], w=[BTD])
        HS = 64 * 15 * 64
        for h in range(16):
            so = h * 465
            do = h * HS
            dst = bass.AP(BTD.ap.tensor, do + 8 * 960, [[64, 15], [961, 49], [1, 16]])
            src = bass.AP(rpb.ap.tensor, so + 7, [[31, 15], [0, 49], [1, 16]])
            self.DMA("sync", dst, src, r=[rpb], w=[BTD], slow=True)
            dst = bass.AP(BTD.ap.tensor, do, [[64, 15], [960, 8], [1, 16]])
            src = bass.AP(rpb.ap.tensor, so + 15, [[31, 15], [-1, 8], [1, 16]])
            self.DMA("sync", dst, src, r=[rpb], w=[BTD], slow=True)
            dst = bass.AP(BTD.ap.tensor, do + 57 * 960 + 48, [[64, 15], [960, 7], [1, 16]])
            src = bass.AP(rpb.ap.tensor, so + 6, [[31, 15], [-1, 7], [1, 16]])
            self.DMA("sync", dst, src, r=[rpb], w=[BTD], slow=True)

    def na_phase_a(self, HT, wqkv, QT, KT, VT):
        self.new_phase()
        Hres = self.bf16(8 * T, "Hres", shape=(8, T))
        for kt in range(8):
            self.DMA("sync", Hres.ap[:, kt, :], HT.ap[kt * 128:(kt + 1) * 128, :], r=[HT], w=[Hres])
        wv = self.bf16(8 * 1024, "wv", shape=(8, 1024))
        self.load_w_bf16(wv, wqkv.ap[0], wqkv, 8, 2048, 3072)
        vb = [self.bf16(1024, "vb%d" % i) for i in range(2)]
        for tt in range(T // 128):
            v_ = vb[tt % 2]
            for vc in range(2):
                ps = self.PS[vc + 2 * (tt % 2)]
                for kt in range(8):
                    self.mm(ps.ap, Hres.ap[:, kt, tt * 128:(tt + 1) * 128], wv.ap[:, kt, vc * 512:(vc + 1) * 512], kt == 0, kt == 7, r=[Hres, wv], w=[ps])
                self.cp("scalar" if vc else "vector", v_.ap[:, vc * 512:(vc + 1) * 512], ps.ap, r=[ps], w=[v_])
            self.DMA("sync", VT.ap[tt * 128:(tt + 1) * 128, :], v_.ap, r=[v_], w=[VT])
        wch = [self.bf16(8 * 512, "wq%d" % i, shape=(8, 512)) for i in range(2)]
        ob = [self.bf16(T, "qo%d" % i) for i in range(2)]
        chunks = self.chunks()
        pi = 0
        for wc in range(4):
            wb = wch[wc % 2]
            self.load_w_bf16(wb, wqkv.ap[0], wqkv, 8, wc * 512, (wc + 1) * 512)
            for q in range(4):
                ot = wc * 4 + q
                obi = ob[ot % 2]
                for ci, (s, w, isctx) in enumerate(chunks):
                    ps = self.PS[4 + pi % 4]
                    pi += 1
                    for kt in range(8):
                        self.mm(ps.ap[:, 0:w], wb.ap[:, kt, q * 128:(q + 1) * 128], Hres.ap[:, kt, s:s + w], kt == 0, kt == 7, r=[wb, Hres], w=[ps])
                    if ot < 8:
                        self.act(obi.ap[:, s:s + w], ps.ap[:, 0:w], AF.Copy, r=[ps], w=[obi], scale=0.125)
                    else:
                        self.cp("vector", obi.ap[:, s:s + w], ps.ap[:, 0:w], r=[ps], w=[obi])
                dst = QT.ap[ot * 128:(ot + 1) * 128, :] if ot < 8 else KT.ap[(ot - 8) * 128:(ot - 7) * 128, :]
                self.DMA("sync", dst, obi.ap, r=[obi], w=[QT if ot < 8 else KT])

    def na_phase_b(self, QT, KT, VT, BTD, YT):
        self.new_phase()
        V, G = "vector", "gpsimd"
        Qp = [self.bf16(T, "Qp%d" % i) for i in range(2)]
        Kp = [self.bf16(T, "Kp%d" % i) for i in range(2)]
        Ve = [self.bf16(34 * 128, "Ve%d" % i, shape=(34, 128)) for i in range(2)]
        Vo = [self.bf16(33 * 128, "Vo%d" % i, shape=(33, 128)) for i in range(2)]
        BTt = [self.f32(960, "BTt%d" % i) for i in range(2)]
        YTp = [self.bf16(T, "YTp%d" % i) for i in range(2)]
        Qbd = [self.bf16(128, "Qbd%d" % i) for i in range(2)]
        for q_ in Qbd:
            self.memset(G, q_.ap, 0.0, [q_])
        sc = [self.f32(768, "sc%d" % i) for i in range(2)]
        pe = [self.bf16(768, "pe%d" % i) for i in range(2)]
        pn = [self.bf16(768, "pn%d" % i) for i in range(2)]
        pT = [self.bf16(768, "pT%d" % i, shape=(6, 128)) for i in range(2)]
        st = [self.f32(8, "st%d" % i) for i in range(2)]
        blocks = []
        for hp in range(8):
            bl = [("c", i) for i in range(4)] + [("l", r) for r in range(64)]
            for bi, (kind, idx) in enumerate(bl):
                blocks.append((hp, kind, idx, bi == 0, bi == len(bl) - 1))

        def geom(kind, idx):
            if kind == "c":
                return idx * 64, 256, 0, 0
            r = idx
            start = min(max(r - 4, 0), 56)
            return LC + r * 64, 768, start - r + 7, LC + start * 64

        def load_pair(hp):
            h2 = hp % 2
            self.DMA("sync", Qp[h2].ap, QT.ap[hp * 128:(hp + 1) * 128, :], r=[QT], w=[Qp[h2]])
            self.DMA("sync", Kp[h2].ap, KT.ap[hp * 128:(hp + 1) * 128, :], r=[KT], w=[Kp[h2]])
            self.DMA("sync", Ve[h2].ap, VT.ap[:, hp * 128:(hp + 1) * 128].rearrange("(tt p) c -> p tt c", p=128), r=[VT], w=[Ve[h2]])
            self.DMA("sync", Vo[h2].ap, VT.ap[64:64 + 33 * 128, hp * 128:(hp + 1) * 128].rearrange("(tt p) c -> p tt c", p=128), r=[VT], w=[Vo[h2]])
            self.DMA("sync", BTt[h2].ap, BTD.ap[2 * hp:2 * hp + 2].rearrange("two w r c -> (two w) (r c)"), r=[BTD], w=[BTt[h2]])

        def s1(i):
            hp, kind, idx, pfirst, plast = blocks[i]
            h2 = hp % 2
            if pfirst:
                load_pair(hp)
            qpos, nk, ro0, kpos = geom(kind, idx)
            qb, sci, pei, sti = Qbd[i % 2], sc[i % 2], pe[i % 2], st[i % 2]
            ps_l, ps_c = self.PS[(i % 2) * 2], self.PS[(i % 2) * 2 + 1]
            self.cp(G, qb.ap[0:64, 0:64], Qp[h2].ap[0:64, qpos:qpos + 64], r=[Qp[h2]], w=[qb])
            self.cp(G, qb.ap[64:128, 64:128], Qp[h2].ap[64:128, qpos:qpos + 64], r=[Qp[h2]], w=[qb])
            if kind == "l":
                self.mm(ps_l.ap, qb.ap, Kp[h2].ap[:, kpos:kpos + 512], True, True, r=[qb, Kp[h2]], w=[ps_l])
                self.mm(ps_c.ap[:, 0:256], qb.ap, Kp[h2].ap[:, 0:256], True, True, r=[qb, Kp[h2]], w=[ps_c])
                self.tt(V, sci.ap[:, 0:512], ps_l.ap, BTt[h2].ap[:, ro0 * 64:ro0 * 64 + 512], ALU.add, r=[ps_l, BTt[h2]], w=[sci])
                self.cp("scalar", sci.ap[:, 512:768], ps_c.ap[:, 0:256], r=[ps_c], w=[sci])
            else:
                self.mm(ps_c.ap[:, 0:256], qb.ap, Kp[h2].ap[:, 0:256], True, True, r=[qb, Kp[h2]], w=[ps_c])
                self.cp("scalar", sci.ap[:, 0:256], ps_c.ap[:, 0:256], r=[ps_c], w=[sci])
            self.E(V, lambda e, o=sti.ap[:, 0:1], i_=sci.ap[:, 0:nk]: e.reduce_max(out=o, in_=i_, axis=AX.X), r=[sci], w=[sti])
            self.ts(V, sti.ap[:, 1:2], sti.ap[:, 0:1], -1.0, ALU.mult, r=[sti], w=[sti])
            self.act(pei.ap[:, 0:nk], sci.ap[:, 0:nk], AF.Exp, r=[sci, sti], w=[pei, sti], bias=sti.ap[:, 1:2], accum=sti.ap[:, 2:3])
            self.E(V, lambda e, o=sti.ap[:, 3:4], i_=sti.ap[:, 2:3]: e.reciprocal(out=o, in_=i_), r=[sti], w=[sti])

        def s2(i):
            hp, kind, idx, pfirst, plast = blocks[i]
            h2 = hp % 2
            qpos, nk, ro0, kpos = geom(kind, idx)
            pei, pni, pTi, sti = pe[i % 2], pn[i % 2], pT[i % 2], st[i % 2]
            pst, pstb = self.PS[4 + i % 2], self.PSB[4 + i % 2]
            pso = self.PS[6 + i % 2]
            self.ts(G, pni.ap[:, 0:nk], pei.ap[:, 0:nk], sti.ap[:, 3:4], ALU.mult, 0.0, ALU.add, r=[pei, sti], w=[pni])
            nkt = nk // 128
            for kt in range(nkt):
                self.tr(pstb[:, kt * 128:(kt + 1) * 128], pni.ap[:, kt * 128:(kt + 1) * 128], self.identb.ap, r=[pni, self.identb], w=[pst])
            self.cp("scalar", pTi.ap[:, 0:nkt, :], pstb[:, 0:nk].rearrange("p (k c) -> p k c", c=128), r=[pst], w=[pTi])
            for kt in range(nkt):
                if kind == "c":
                    vt = Ve[h2].ap[:, kt, :]
                elif kt >= 4:
                    vt = Ve[h2].ap[:, kt - 4, :]
                else:
                    tok0 = kpos + kt * 128
                    vt = Ve[h2].ap[:, tok0 // 128, :] if tok0 % 128 == 0 else Vo[h2].ap[:, (tok0 - 64) // 128, :]
                self.mm(pso.ap[:, 0:128], vt, pTi.ap[:, kt, :], kt == 0, kt == nkt - 1, r=[Ve[h2], Vo[h2], pTi], w=[pso])
            self.cp(V, YTp[h2].ap[0:64, qpos:qpos + 64], pso.ap[0:64, 0:64], r=[pso], w=[YTp[h2]])
            self.cp("scalar", YTp[h2].ap[64:128, qpos:qpos + 64], pso.ap[64:128, 64:128], r=[pso], w=[YTp[h2]])
            if plast:
                self.DMA("sync", YT.ap[hp * 128:(hp + 1) * 128, :], YTp[h2].ap, r=[YTp[h2]], w=[YT])

        nb_ = len(blocks)
        s1(0)
        for i in range(nb_):
            if i + 1 < nb_:
                s1(i + 1)
            s2(i)


def _build(cfg):
    nc = bass.Bass("TRN2", target_bir_lowering=False)
    kb = KB(nc, cfg)
    IN = lambda n, s: kb.dram_t(n, s, F32, kind="ExternalInput")
    x_in = IN("x", [LL, D])
    ctx_in = IN("ctx", [LC, D])
    c_in = IN("c", [D])
    cctx_in = IN("c_ctx", [D])
    ada_w = IN("ada_w", [DEPTH, D, 6 * D])
    ada_b = IN("ada_b", [DEPTH, 6 * D])
    norm_mix = IN("norm_mix", [DEPTH, D])
    norm_ffn = IN("norm_ffn", [DEPTH, D])
    norm_final = IN("norm_final", [D])
    w1 = IN("ffn_w1", [DEPTH, D, FH])
    w3 = IN("ffn_w3", [DEPTH, D, FH])
    w2 = IN("ffn_w2", [DEPTH, FH, D])
    S5 = {
        "lam_re": IN("s5_lam_re", [2, 2, 64, 64]), "lam_im": IN("s5_lam_im", [2, 2, 64, 64]),
        "log_step": IN("s5_log_step", [2, 2, 64]),
        "b_re": IN("s5_b_re", [2, 2, 64, 64, 16]), "b_im": IN("s5_b_im", [2, 2, 64, 64, 16]),
        "c_re": IN("s5_c_re", [2, 2, 64, 16, 64]), "c_im": IN("s5_c_im", [2, 2, 64, 16, 64]),
        "d": IN("s5_d", [2, D]), "w_glu": IN("s5_w_glu", [2, D, D]), "b_glu": IN("s5_b_glu", [2, D]),
    }
    SSD = {
        "w_in": IN("ssd_w_in", [1, D, 8256]), "conv_w": IN("ssd_conv_w", [1, 5, 6144]), "conv_b": IN("ssd_conv_b", [1, 6144]),
        "dt_bias": IN("ssd_dt_bias", [1, 2, 32]), "a_log": IN("ssd_a_log", [1, 2, 32]), "d": IN("ssd_d", [1, 32]),
        "norm": IN("ssd_norm", [1, 2048]), "w_out": IN("ssd_w_out", [1, 2048, D]),
    }
    NA = {"w_qkv": IN("na_w_qkv", [1, D, 3 * D]), "w_o": IN("na_w_o", [1, D, D]), "rpb": IN("na_rpb", [1, 16, 15, 31])}
    out_t = kb.dram_t("out", [LL, D], F32, kind="ExternalOutput")
    QT = kb.dram_t("QT", [D, T], BF16)
    KT = kb.dram_t("KT", [D, T], BF16)
    VT = kb.dram_t("VT", [T, D], BF16)
    YT = kb.dram_t("YT", [D, T], BF16)
    BTD = kb.dram_t("BTD", [16, 64, 15, 64], F32)
    SZ = kb.dram_t("SZ", [T, 2048], BF16)
    XS = kb.dram_t("XS", [2048, T], BF16)
    BC = kb.dram_t("BC", [4096, T], BF16)
    DTR = kb.dram_t("DTR", [64, T], F32)
    YF = kb.dram_t("YF", [T, 2048], BF16)
    YB = kb.dram_t("YB", [T, 2048], BF16)
    XT = kb.dram_t("XT", [D, T], F32)
    HT = kb.dram_t("HT", [D, T], BF16)
    GT = kb.dram_t("GT", [D, T], BF16)
    layers = cfg.get("layers", list(range(DEPTH)))
    kb.consts()
    kb.build_mask8()
    kb.adaln(c_in, cctx_in, ada_w, ada_b, norm_mix, norm_ffn, norm_final, layers)
    kb.prologue_transpose(x_in, ctx_in, XT)
    for l in layers:
        kind, j = l % 3, l // 3
        if cfg.get("mixer", True):
            kb.norm_layer(XT, HT, l, 0)
            if kind == 0:
                kb.s5_phase(HT, GT, S5, j)
                kb.proj_phase(XT, GT, S5["w_glu"], S5["w_glu"].ap[j], 8, l, glu_bias=(S5["b_glu"], S5["b_glu"].ap[j]))
            elif kind == 1:
                kb.ssd_phase_a(HT, SSD, SZ, XS, BC, DTR)
                kb.ssd_phase_b(SSD, XS, BC, DTR, YF, YB)
                kb.ssd_phase_c(XT, SSD, SZ, YF, YB, l)
            else:
                kb.na_bias_table(NA["rpb"], BTD)
                kb.na_phase_a(HT, NA["w_qkv"], QT, KT, VT)
                kb.na_phase_b(QT, KT, VT, BTD, YT)
                kb.proj_phase(XT, YT, NA["w_o"], NA["w_o"].ap[0], 8, l)
        if cfg.get("ffn", True):
            kb.ffn_phase(XT, HT, w1, w3, w2, l)
    kb.final_phase(XT, out_t)
    kb.p.fence("sync", kb.out_ops)
    kb.p.build()
    kb.es.close()
    return nc, kb


INPUT_NAMES = ["x", "ctx", "c", "c_ctx", "ada_w", "ada_b", "norm_mix", "norm_ffn", "norm_final", "ffn_w1", "ffn_w3", "ffn_w2",
               "s5_lam_re", "s5_lam_im", "s5_log_step", "s5_b_re", "s5_b_im", "s5_c_re", "s5_c_im", "s5_d", "s5_w_glu", "s5_b_glu",
               "ssd_w_in", "ssd_conv_w", "ssd_conv_b", "ssd_dt_bias", "ssd_a_log", "ssd_d", "ssd_norm", "ssd_w_out",
               "na_w_qkv", "na_w_o", "na_rpb"]


def kernel(**inputs):
    cfg = {}
    nc, kb = _build(cfg)
    n = 8
    in_maps = []
    for b in range(n):
        m = {}
        for k in INPUT_NAMES:
            v = np.ascontiguousarray(inputs[k], dtype=np.float32)
            if k in ("x", "ctx", "c"):
                v = np.ascontiguousarray(v[b])
            m[k] = v
        in_maps.append(m)
    res = run_bass_kernel_spmd(nc, in_maps, core_ids=list(range(n)))
    return np.stack([np.asarray(r["out"], dtype=np.float32) for r in res.results], axis=0)
```

```python
import numpy as np
from contextlib import ExitStack
import concourse.bass as bass
import concourse.mybir as mybir
from concourse.bass_utils import run_bass_kernel_spmd

F32 = mybir.dt.float32
BF16 = mybir.dt.bfloat16
I32 = mybir.dt.int32
AF = mybir.ActivationFunctionType
ALU = mybir.AluOpType
AX = mybir.AxisListType

ENGS = ("sync", "gpsimd", "scalar", "vector", "tensor")
NDMA_SEM = 24
SEM_EPOCH = 12000

D = 1024
LC = 256
LL = 4096
T = LC + LL
NFT = 8
FH = 2816
NHT = 22
DEPTH = 4
EPS = 1e-6
ARENA_F32 = 46 * 1024


class Buf:
    __slots__ = ("name", "w", "r", "psum")

    def __init__(self, name, psum=False):
        self.name = name
        self.w = None
        self.r = []
        self.psum = psum


class Op:
    __slots__ = ("eng", "fn", "idx", "deps", "need_inc", "val", "is_dma", "semi", "is_barrier")

    def __init__(self, eng, fn, is_dma):
        self.eng = eng
        self.fn = fn
        self.is_dma = is_dma
        self.deps = []
        self.need_inc = False
        self.val = 0
        self.semi = -1
        self.idx = -1
        self.is_barrier = False


class Prog:
    def __init__(self, nc):
        self.nc = nc
        self.ops = {e: [] for e in ENGS}
        self.nreal = {e: 0 for e in ENGS}
        self.es = ExitStack()
        self.engsem = {e: self.es.enter_context(nc.semaphore("s_" + e)) for e in ENGS}
        self.dmasem = [self.es.enter_context(nc.semaphore("d%d" % i)) for i in range(NDMA_SEM)]
        self.dma_last = [None] * NDMA_SEM
        self.dma_cnt = [0] * NDMA_SEM
        self.dma_rr = 0

    def emit(self, eng, fn, reads=(), writes=(), dma=False):
        op = Op(eng, fn, dma)
        op.idx = self.nreal[eng]
        self.nreal[eng] += 1
        deps = []
        for b in reads:
            if b.w is not None:
                deps.append(b.w)
            if b.psum:
                deps.extend(b.r)
        for b in writes:
            if b.w is not None:
                deps.append(b.w)
            deps.extend(b.r)
        if dma:
            k = self.dma_rr
            self.dma_rr = (k + 1) % NDMA_SEM
            if self.dma_last[k] is not None:
                deps.append(self.dma_last[k])
            self.dma_last[k] = op
            self.dma_cnt[k] += 16
            op.semi = k
            op.val = self.dma_cnt[k]
        seen = set()
        for d in deps:
            if d is op or id(d) in seen:
                continue
            seen.add(id(d))
            op.deps.append(d)
        for b in reads:
            if b.psum:
                b.w = op
                b.r = []
            else:
                b.r.append(op)
        for b in writes:
            b.w = op
            b.r = []
        self.ops[eng].append(op)
        return op

    def fence(self, eng, deps):
        op = Op(eng, None, False)
        op.idx = self.nreal[eng]
        op.deps = list(deps)
        self.ops[eng].append(op)
        return op

    def barrier(self):
        lasts = []
        for e in ENGS:
            for o in reversed(self.ops[e]):
                if o.fn is not None:
                    lasts.append(o)
                    break
        for o in self.dma_last:
            if o is not None:
                lasts.append(o)
        for e in ENGS:
            self.fence(e, lasts).is_barrier = True

    def _needs_wait(self, op, d):
        if d.is_dma:
            return True
        if d.eng != op.eng:
            return True
        if op.eng == "tensor":
            return False
        return (op.idx - d.idx) <= 2

    def build(self):
        nc = self.nc
        for e in ENGS:
            for op in self.ops[e]:
                for d in op.deps:
                    if not d.is_dma and self._needs_wait(op, d):
                        d.need_inc = True
        self.epoch_sems = {e: [self.engsem[e]] for e in ENGS}
        for e in ENGS:
            c = 0
            ep = 0
            for op in self.ops[e]:
                if op.fn is None:
                    if op.is_barrier and c > SEM_EPOCH:
                        ep += 1
                        c = 0
                        self.epoch_sems[e].append(self.es.enter_context(nc.semaphore("s_%s_%d" % (e, ep))))
                    continue
                if op.is_dma:
                    continue
                op.semi = ep
                if op.need_inc:
                    c += 1
                    op.val = c
        self.counts = {e: 0 for e in ENGS}

        def mk_body(e):
            def body(eng):
                waited = {}
                for op in self.ops[e]:
                    for d in op.deps:
                        if not self._needs_wait(op, d):
                            continue
                        if d.is_dma:
                            key = ("d", d.semi)
                            sem = self.dmasem[d.semi]
                        else:
                            key = ("e", d.eng, d.semi)
                            sem = self.epoch_sems[d.eng][d.semi]
                        if waited.get(key, 0) >= d.val:
                            continue
                        waited[key] = d.val
                        eng.wait_ge(sem, d.val)
                    if op.fn is None:
                        continue
                    inst = op.fn(eng)
                    self.counts[e] += 1
                    if op.is_dma:
                        inst.then_inc(self.dmasem[op.semi], 16)
                    elif op.need_inc:
                        inst.then_inc(self.epoch_sems[e][op.semi], 1)
            return body

        with nc.Block() as block:
            for e in ENGS:
                if self.ops[e]:
                    getattr(block, e)(mk_body(e))
        self.es.close()


class Tl:
    __slots__ = ("ap", "buf")

    def __init__(self, ap, buf):
        self.ap = ap
        self.buf = buf


class KB:
    def __init__(self, nc, cfg):
        self.nc = nc
        self.cfg = cfg
        self.p = Prog(nc)
        self.es = ExitStack()
        self.arena = self.es.enter_context(nc.sbuf_tensor("arena", [128, ARENA_F32], F32))
        self.arena_bf = self.arena.bitcast(BF16)
        self.psum = [self.es.enter_context(nc.psum_tensor("ps%d" % i, [128, 512], F32)) for i in range(8)]
        self.PS = [Tl(self.psum[i][:], Buf("ps%d" % i, psum=True)) for i in range(8)]
        self.PSB = [self.psum[i].bitcast(BF16) for i in range(8)]
        self.top = 0
        self.ptr = 0
        self.nb = 0
        self.dram = {}
        self.out_ops = []

    def _al(self, n_f32, persistent):
        n_f32 = (n_f32 + 7) // 8 * 8
        if persistent:
            assert self.ptr == self.top, "persistent alloc only between phases"
            off = self.top
            self.top += n_f32
            self.ptr = self.top
        else:
            off = self.ptr
            self.ptr += n_f32
        assert self.ptr <= ARENA_F32, "arena overflow %d" % self.ptr
        return off

    def f32(self, n, name=None, persistent=False, shape=None):
        off = self._al(n, persistent)
        ap = self.arena[:, off:off + n]
        if shape is not None:
            ap = self._reshape(ap, shape)
        self.nb += 1
        return Tl(ap, Buf(name or "t%d" % self.nb))

    def bf16(self, n, name=None, persistent=False, shape=None):
        off = self._al((n + 1) // 2, persistent)
        ap = self.arena_bf[:, 2 * off:2 * off + n]
        if shape is not None:
            ap = self._reshape(ap, shape)
        self.nb += 1
        return Tl(ap, Buf(name or "t%d" % self.nb))

    def i32(self, n, name=None):
        off = self._al(n, False)
        ap = self.arena.bitcast(I32)[:, off:off + n]
        self.nb += 1
        return Tl(ap, Buf(name or "t%d" % self.nb))

    @staticmethod
    def _reshape(ap, shape):
        if len(shape) == 2:
            return ap.rearrange("p (a b) -> p a b", b=shape[1])
        if len(shape) == 3:
            return ap.rearrange("p (a b c) -> p a b c", b=shape[1], c=shape[2])
        raise ValueError

    def new_phase(self):
        self.p.barrier()
        self.ptr = self.top

    def dram_t(self, name, shape, dt, kind="Internal"):
        t = self.nc.dram_tensor(name, shape, dt, kind=kind)
        tl = Tl(t.ap(), Buf(name))
        self.dram[name] = tl
        return tl

    def E(self, eng, fn, r=(), w=()):
        return self.p.emit(eng, fn, [t.buf for t in r], [t.buf for t in w])

    def DMA(self, eng, out_ap, in_ap, r=(), w=(), slow=False):
        if slow:
            fn = lambda e: e.dma_start(out=out_ap, in_=in_ap, allow_slow_non_contiguous=True)
        else:
            fn = lambda e: e.dma_start(out=out_ap, in_=in_ap)
        return self.p.emit(eng, fn, [t.buf for t in r], [t.buf for t in w], dma=True)

    def mm(self, ps_ap, lhsT, rhs, start, stop, r=(), w=()):
        return self.E("tensor", lambda e: e.matmul(ps_ap, lhsT=lhsT, rhs=rhs, start=start, stop=stop), r, w)

    def tr(self, ps_ap, in_ap, ident_ap, r=(), w=()):
        return self.E("tensor", lambda e: e.transpose(out=ps_ap, in_=in_ap, identity=ident_ap), r, w)

    def act(self, out, in_, func, r=(), w=(), scale=None, bias=None, accum=None):
        kw = {}
        if scale is not None:
            kw["scale"] = scale
        if bias is not None:
            kw["bias"] = bias
        if accum is not None:
            kw["accum_out"] = accum
        return self.E("scalar", lambda e: e.activation(out=out, in_=in_, func=func, **kw), r, w)

    def tt(self, eng, out, in0, in1, op, r=(), w=()):
        return self.E(eng, lambda e: e.tensor_tensor(out=out, in0=in0, in1=in1, op=op), r, w)

    def ts(self, eng, out, in0, s1, op0, s2=None, op1=None, r=(), w=(), accum=None):
        if op1 is None:
            return self.E(eng, lambda e: e.tensor_scalar(out=out, in0=in0, scalar1=s1, scalar2=None, op0=op0), r, w)
        if accum is not None:
            return self.E(eng, lambda e: e.tensor_scalar(out=out, in0=in0, scalar1=s1, scalar2=s2, op0=op0, op1=op1, accum_out=accum), r, w)
        return self.E(eng, lambda e: e.tensor_scalar(out=out, in0=in0, scalar1=s1, scalar2=s2, op0=op0, op1=op1), r, w)

    def stt(self, out, in0, scalar, in1, op0, op1, r=(), w=()):
        return self.E("vector", lambda e: e.scalar_tensor_tensor(out=out, in0=in0, scalar=scalar, in1=in1, op0=op0, op1=op1), r, w)

    def cp(self, eng, out, in_, r=(), w=()):
        if eng == "scalar":
            return self.act(out, in_, AF.Copy, r, w)
        return self.E(eng, lambda e: e.tensor_copy(out=out, in_=in_), r, w)

    def memset(self, eng, ap, val, w=()):
        return self.E(eng, lambda e: e.memset(ap, val), (), w)

    def consts(self):
        self.ident = self.f32(128, "ident", True)
        self.identb = self.bf16(128, "identb", True)
        self.ones = self.f32(128, "ones", True)
        self.epsc = self.f32(8, "epsc", True)
        self.memset("gpsimd", self.ident.ap, 0.0, [self.ident])
        idap = self.ident.ap
        self.E("gpsimd", lambda e: e.affine_select(out=idap, in_=idap, pattern=[[-1, 128]], compare_op=ALU.not_equal,
                                                   fill=1.0, base=0, channel_multiplier=1), [self.ident], [self.ident])
        self.cp("vector", self.identb.ap, self.ident.ap, [self.ident], [self.identb])
        self.memset("vector", self.ones.ap, 1.0, [self.ones])
        self.memset("vector", self.epsc.ap[:, 0:1], EPS, [self.epsc])
        self.memset("vector", self.epsc.ap[:, 1:2], 0.0, [self.epsc])
        self.memset("vector", self.epsc.ap[:, 2:3], 1.0, [self.epsc])

    @staticmethod
    def chunks(w_lat=512):
        ch = [(0, LC, True)]
        for s in range(LC, T, w_lat):
            ch.append((s, w_lat, False))
        return ch

    def prologue_transpose(self, x_in, ctx_in, XT):
        self.new_phase()
        xin = [self.f32(4 * D, "xin%d" % i, shape=(4, D)) for i in range(2)]
        stage = [self.f32(NFT * 512, "stg%d" % i, shape=(NFT, 512)) for i in range(2)]
        for ci, (s, w, isctx) in enumerate(self.chunks()):
            xi = xin[ci % 2]
            st = stage[ci % 2]
            ntt = w // 128
            if isctx:
                src = ctx_in.ap.rearrange("(tt p) f -> p tt f", p=128)
            else:
                src = x_in.ap[s - LC:s - LC + w, :].rearrange("(tt p) f -> p tt f", p=128)
            self.DMA("sync", xi.ap[:, 0:ntt, :], src, r=[x_in], w=[xi])
            for ft in range(NFT):
                ps = self.PS[ft]
                for tt in range(ntt):
                    self.tr(ps.ap[:, tt * 128:(tt + 1) * 128], xi.ap[:, tt, ft * 128:(ft + 1) * 128], self.ident.ap,
                            r=[xi, self.ident], w=[ps])
                self.cp("scalar" if ft % 2 else "vector", st.ap[:, ft, 0:w], ps.ap[:, 0:w], r=[ps], w=[st])
            dst = XT.ap[:, s:s + w].rearrange("(ft p) t -> p ft t", p=128)
            self.DMA("sync", dst, st.ap[:, :, 0:w], r=[st], w=[XT])

    def adaln(self, c_in, cctx_in, ada_w, ada_b, norm_mix, norm_ffn, norm_final, layers):
        self.mod = {}
        for l in layers:
            self.mod[l] = self.f32(96, "mod%d" % l, True, shape=(6, 8, 2))
        self.nw = self.f32(9 * 8, "nw", True, shape=(9, 8))
        self.AB = {}
        for l in layers:
            self.AB[l] = self.f32(4 * 16, "AB%d" % l, True, shape=(4, 8, 2))
        self.new_phase()
        sT = self.f32(16, "sT", shape=(8, 2))
        craw = self.f32(16, "craw", shape=(8, 2))
        self.DMA("sync", craw.ap[:, :, 0], c_in.ap.rearrange("(kt p) -> p kt", p=128), r=[c_in], w=[craw], slow=True)
        self.DMA("sync", craw.ap[:, :, 1], cctx_in.ap.rearrange("(kt p) -> p kt", p=128), r=[cctx_in], w=[craw], slow=True)
        self.act(sT.ap, craw.ap, AF.Silu, r=[craw], w=[sT])
        for k, nwt in enumerate([norm_mix, norm_ffn]):
            self.DMA("sync", self.nw.ap[:, 4 * k:4 * k + 4, :], nwt.ap.rearrange("l (ft p) -> p l ft", p=128), r=[nwt], w=[self.nw], slow=True)
        self.DMA("sync", self.nw.ap[:, 8, :], norm_final.ap.rearrange("(ft p) -> p ft", p=128), r=[norm_final], w=[self.nw], slow=True)
        wbuf = [self.f32(8 * 512, "adaw%d" % i, shape=(8, 512)) for i in range(3)]
        bbuf = [self.f32(512, "adab%d" % i) for i in range(3)]
        onesrow = self.ones.ap[0:1, 0:2]
        it = 0
        for l in layers:
            for cj in range(12):
                wb = wbuf[it % 3]
                bb = bbuf[it % 3]
                ps = self.PS[it % 4]
                it += 1
                self.DMA("sync", wb.ap, ada_w.ap[l, :, cj * 512:(cj + 1) * 512].rearrange("(kt p) n -> p kt n", p=128), r=[ada_w], w=[wb])
                self.DMA("sync", bb.ap[0:1, :], ada_b.ap[l:l + 1, cj * 512:(cj + 1) * 512], r=[ada_b], w=[bb])
                for jj in range(4):
                    j = cj * 4 + jj
                    o = ps.ap[:, jj * 2:jj * 2 + 2]
                    for kt in range(8):
                        self.mm(o, wb.ap[:, kt, jj * 128:(jj + 1) * 128], sT.ap[:, kt, :], kt == 0, False, r=[wb, sT], w=[ps])
                    self.mm(o, bb.ap[0:1, jj * 128:(jj + 1) * 128], onesrow, False, True, r=[bb, self.ones], w=[ps])
                m = cj * 4 // 8
                ft0 = (cj * 4) % 8
                self.cp("vector", self.mod[l].ap[:, m, ft0:ft0 + 4, :], ps.ap[:, 0:8].rearrange("p (a b) -> p a b", b=2), r=[ps], w=[self.mod[l]])
        for l in layers:
            for k, (mi, nwi) in enumerate([(1, l), (4, 4 + l)]):
                nwb = self.nw.ap[:, nwi, :].unsqueeze(2).to_broadcast([128, 8, 2])
                self.stt(self.AB[l].ap[:, k, :, :], self.mod[l].ap[:, mi, :, :], 1.0, nwb, ALU.add, ALU.mult, r=[self.mod[l], self.nw], w=[self.AB[l]])

    def norm_phase(self, XT, HT, A_sel, B_sel, deps_r):
        self.new_phase()
        xin = [self.f32(NFT * 512, "nx%d" % i, shape=(NFT, 512)) for i in range(2)]
        sq = [self.f32(NFT * 512, "nsq%d" % i, shape=(NFT, 512)) for i in range(2)]
        hb = [self.bf16(NFT * 512, "nh%d" % i, shape=(NFT, 512)) for i in range(2)]
        rt = [self.f32(512, "nrt%d" % i) for i in range(2)]
        chs = self.chunks()

        def load(ci):
            s, w, isctx = chs[ci]
            xi = xin[ci % 2]
            self.DMA("sync", xi.ap[:, :, 0:w], XT.ap[:, s:s + w].rearrange("(ft p) t -> p ft t", p=128), r=[XT], w=[xi])

        load(0)
        for ci, (s, w, isctx) in enumerate(chs):
            if ci + 1 < len(chs):
                load(ci + 1)
            xi, sqi, hbi, rti = xin[ci % 2], sq[ci % 2], hb[ci % 2], rt[ci % 2]
            ps = self.PS[ci % 2]
            self.act(sqi.ap[:, :, 0:w], xi.ap[:, :, 0:w], AF.Square, r=[xi], w=[sqi])
            for ft in range(NFT):
                self.mm(ps.ap[:, 0:w], self.ones.ap, sqi.ap[:, ft, 0:w], ft == 0, ft == NFT - 1, r=[self.ones, sqi], w=[ps])
            self.act(rti.ap[:, 0:w], ps.ap[:, 0:w], AF.Sqrt, r=[ps, self.epsc], w=[rti], scale=1.0 / D, bias=self.epsc.ap[:, 0:1])
            self.E("vector", lambda e, o=rti.ap[:, 0:w]: e.reciprocal(out=o, in_=o), r=[rti], w=[rti])
            for ft in range(NFT):
                a = A_sel(ft, isctx)
                b = B_sel(ft, isctx)
                self.stt(sqi.ap[:, ft, 0:w], xi.ap[:, ft, 0:w], a, rti.ap[:, 0:w], ALU.mult, ALU.mult, r=[xi, rti] + deps_r, w=[sqi])
                self.act(hbi.ap[:, ft, 0:w], sqi.ap[:, ft, 0:w], AF.Identity, r=[sqi] + deps_r, w=[hbi], bias=b)
            self.DMA("sync", HT.ap[:, s:s + w].rearrange("(ft p) t -> p ft t", p=128), hbi.ap[:, :, 0:w], r=[hbi], w=[HT])

    def norm_layer(self, XT, HT, l, which):
        AB = self.AB[l]
        mod = self.mod[l]
        k = 0 if which == 0 else 1
        smi = 0 if which == 0 else 3
        A_sel = lambda ft, isctx: AB.ap[:, k, ft, (1 if isctx else 0):(1 if isctx else 0) + 1]
        B_sel = lambda ft, isctx: mod.ap[:, smi, ft, (1 if isctx else 0):(1 if isctx else 0) + 1]
        self.norm_phase(XT, HT, A_sel, B_sel, [AB, mod])

    def final_phase(self, XT, out_t):
        self.new_phase()
        xin = [self.f32(NFT * 512, "fx%d" % i, shape=(NFT, 512)) for i in range(2)]
        sq = [self.f32(NFT * 512, "fsq%d" % i, shape=(NFT, 512)) for i in range(2)]
        rt = [self.f32(512, "frt%d" % i) for i in range(2)]
        ob = [self.f32(4 * D, "fo%d" % i, shape=(4, D)) for i in range(2)]
        ci = 0
        lat = [c for c in self.chunks() if not c[2]]

        def loadf(i):
            s, w, _ = lat[i]
            self.DMA("sync", xin[i % 2].ap, XT.ap[:, s:s + w].rearrange("(ft p) t -> p ft t", p=128), r=[XT], w=[xin[i % 2]])

        loadf(0)
        for (s, w, isctx) in lat:
            xi, sqi, rti, obi = xin[ci % 2], sq[ci % 2], rt[ci % 2], ob[ci % 2]
            ps = self.PS[ci % 2]
            ci += 1
            if ci < len(lat):
                loadf(ci)
            self.act(sqi.ap, xi.ap, AF.Square, r=[xi], w=[sqi])
            for ft in range(NFT):
                self.mm(ps.ap, self.ones.ap, sqi.ap[:, ft, :], ft == 0, ft == NFT - 1, r=[self.ones, sqi], w=[ps])
            self.act(rti.ap, ps.ap, AF.Sqrt, r=[ps, self.epsc], w=[rti], scale=1.0 / D, bias=self.epsc.ap[:, 0:1])
            self.E("vector", lambda e, o=rti.ap: e.reciprocal(out=o, in_=o), r=[rti], w=[rti])
            for ft in range(NFT):
                self.stt(sqi.ap[:, ft, :], xi.ap[:, ft, :], self.nw.ap[:, 8, ft:ft + 1], rti.ap, ALU.mult, ALU.mult, r=[xi, rti, self.nw], w=[sqi])
            for tt in range(4):
                for half in range(2):
                    pso = self.PS[2 + (tt * 2 + half) % 6]
                    for q in range(4):
                        ft = half * 4 + q
                        self.tr(pso.ap[:, q * 128:(q + 1) * 128], sqi.ap[:, ft, tt * 128:(tt + 1) * 128], self.ident.ap, r=[sqi, self.ident], w=[pso])
                    self.cp("scalar" if half else "vector", obi.ap[:, tt, half * 512:(half + 1) * 512], pso.ap, r=[pso], w=[obi])
            dst = out_t.ap[s - LC:s - LC + w, :].rearrange("(tt p) f -> p tt f", p=128)
            self.out_ops.append(self.DMA("sync", dst, obi.ap, r=[obi], w=[out_t]))

    def load_w_bf16(self, dst, src_ap, src_tl, nkt, col0=None, col1=None):
        for kt in range(nkt):
            s = src_ap[kt * 128:(kt + 1) * 128, :] if col0 is None else src_ap[kt * 128:(kt + 1) * 128, col0:col1]
            self.DMA("gpsimd", dst.ap[:, kt, :], s, r=[src_tl], w=[dst])

    def ffn_phase(self, XT, HT, w1, w3, w2, l):
        self.new_phase()
        W = 256
        w1s = self.bf16(8 * FH, "w1s", shape=(8, FH))
        w3s = self.bf16(8 * FH, "w3s", shape=(8, FH))
        w2s = self.bf16(NHT * D, "w2s", shape=(NHT, D))
        self.load_w_bf16(w1s, w1.ap[l], w1, 8)
        self.load_w_bf16(w3s, w3.ap[l], w3, 8)
        self.load_w_bf16(w2s, w2.ap[l], w2, NHT)
        hb = [self.bf16(NFT * W, "fh%d" % i, shape=(NFT, W)) for i in range(2)]
        xb = [self.f32(NFT * W, "fxx%d" % i, shape=(NFT, W)) for i in range(2)]
        sqb = self.f32(NFT * W, "fsq", shape=(NFT, W))
        rtb = [self.f32(W, "frt%d" % i) for i in range(2)]
        gb = [self.bf16(NHT * W, "fg%d" % i, shape=(NHT, W)) for i in range(1)]
        sl = [self.bf16(W, "fs%d" % i) for i in range(2)]
        mod = self.mod[l]
        AB = self.AB[l]
        ntile = T // W

        def load(ti):
            s = ti * W
            self.DMA("sync", xb[ti % 2].ap, XT.ap[:, s:s + W].rearrange("(ft p) t -> p ft t", p=128), r=[XT], w=[xb[ti % 2]])

        def norm(ti):
            s = ti * W
            cs = 1 if s < LC else 0
            hbi, xbi, rti = hb[ti % 2], xb[ti % 2], rtb[ti % 2]
            ps = self.PS[7]
            self.act(sqb.ap, xbi.ap, AF.Square, r=[xbi], w=[sqb])
            for ft in range(NFT):
                self.mm(ps.ap[:, 0:W], self.ones.ap, sqb.ap[:, ft, :], ft == 0, ft == NFT - 1, r=[self.ones, sqb], w=[ps])
            self.act(rti.ap, ps.ap[:, 0:W], AF.Sqrt, r=[ps, self.epsc], w=[rti], scale=1.0 / D, bias=self.epsc.ap[:, 0:1])
            self.E("vector", lambda e, o=rti.ap: e.reciprocal(out=o, in_=o), r=[rti], w=[rti])
            for ft in range(NFT):
                self.stt(sqb.ap[:, ft, :], xbi.ap[:, ft, :], AB.ap[:, 1, ft, cs:cs + 1], rti.ap, ALU.mult, ALU.mult, r=[xbi, rti, AB], w=[sqb])
                self.act(hbi.ap[:, ft, :], sqb.ap[:, ft, :], AF.Identity, r=[sqb, mod], w=[hbi], bias=mod.ap[:, 3, ft, cs:cs + 1])

        load(0)
        norm(0)
        for ti in range(ntile):
            s = ti * W
            cs = 1 if s < LC else 0
            hbi, xbi, gbi = hb[ti % 2], xb[ti % 2], gb[0]
            if ti + 1 < ntile:
                load(ti + 1)
            for j in range(NHT):
                pa = self.PS[(j % 2) * 2]
                pb = self.PS[(j % 2) * 2 + 1]
                for kt in range(8):
                    self.mm(pa.ap[:, 0:W], w1s.ap[:, kt, j * 128:(j + 1) * 128], hbi.ap[:, kt, :], kt == 0, kt == 7, r=[w1s, hbi], w=[pa])
                for kt in range(8):
                    self.mm(pb.ap[:, 0:W], w3s.ap[:, kt, j * 128:(j + 1) * 128], hbi.ap[:, kt, :], kt == 0, kt == 7, r=[w3s, hbi], w=[pb])
                sli = sl[j % 2]
                self.act(sli.ap, pa.ap[:, 0:W], AF.Silu, r=[pa], w=[sli])
                self.tt("vector", gbi.ap[:, j, :], pb.ap[:, 0:W], sli.ap, ALU.mult, r=[pb, sli], w=[gbi])
                if j == 10 and ti + 1 < ntile:
                    norm(ti + 1)
            for fo in range(NFT):
                po = self.PS[4 + fo % 3]
                for j in range(NHT):
                    self.mm(po.ap[:, 0:W], w2s.ap[:, j, fo * 128:(fo + 1) * 128], gbi.ap[:, j, :], j == 0, j == NHT - 1, r=[w2s, gbi], w=[po])
                self.stt(xbi.ap[:, fo, :], po.ap[:, 0:W], mod.ap[:, 5, fo, cs:cs + 1], xbi.ap[:, fo, :], ALU.mult, ALU.add, r=[po, mod, xbi], w=[xbi])
            self.DMA("sync", XT.ap[:, s:s + W].rearrange("(ft p) t -> p ft t", p=128), xbi.ap, r=[xbi], w=[XT])

    def rev_ap(self, ap2d, start, n):
        pstride = ap2d.ap[0][0]
        return bass.AP(ap2d.tensor, ap2d.offset + start + n - 1, [[pstride, 128], [-1, n]])

    def sincos_turns(self, eng, turns, n, osin, ocos, tmp, cast_eng="vector"):
        ti, tf, fr, s2, s4 = tmp["ti"], tmp["tf"], tmp["fr"], tmp["s2"], tmp["s4"]
        sl = lambda t: t.ap[:, 0:n]
        self.cp(cast_eng, sl(ti), sl(turns), r=[turns], w=[ti])
        self.cp(cast_eng, sl(tf), sl(ti), r=[ti], w=[tf])
        self.tt(eng, sl(fr), sl(turns), sl(tf), ALU.subtract, r=[turns, tf], w=[fr])
        self.act(sl(s2), sl(fr), AF.Sin, r=[fr], w=[s2], scale=float(np.pi))
        self.act(sl(s4), sl(fr), AF.Sin, r=[fr], w=[s4], scale=float(np.pi / 2))
        self.tt(eng, sl(s4), sl(s4), sl(s4), ALU.mult, r=[s4], w=[s4])
        self.ts(eng, sl(s4), sl(s4), -4.0, ALU.mult, 2.0, ALU.add, r=[s4], w=[s4])
        self.tt(eng, sl(osin), sl(s2), sl(s4), ALU.mult, r=[s2, s4], w=[osin])
        self.tt(eng, sl(s2), sl(s2), sl(s2), ALU.mult, r=[s2], w=[s2])
        self.ts(eng, sl(ocos), sl(s2), -2.0, ALU.mult, 1.0, ALU.add, r=[s2], w=[ocos])

    def build_mask8(self):
        self.mask8 = self.f32(8, "mask8", True)
        m = self.mask8.ap
        self.memset("gpsimd", m, 1.0, [self.mask8])
        self.E("gpsimd", lambda e: e.affine_select(out=m, in_=m, pattern=[[-16, 8]], compare_op=ALU.is_ge, fill=0.0, base=0, channel_multiplier=1), [self.mask8], [self.mask8])
        self.E("gpsimd", lambda e: e.affine_select(out=m, in_=m, pattern=[[16, 8]], compare_op=ALU.is_ge, fill=0.0, base=15, channel_multiplier=-1), [self.mask8], [self.mask8])

    def s5_phase(self, HT, GT, P, j):
        self.new_phase()
        V, G = "vector", "gpsimd"
        def sc_tile(nm):
            return self.f32(64, nm)
        lr, li, ls = sc_tile("lr"), sc_tile("li"), sc_tile("ls")
        for d in range(2):
            self.DMA("sync", lr.ap[:, d * 32:(d + 1) * 32], P["lam_re"].ap[j, d].rearrange("(p two) n -> (two n) p", two=2), r=[P["lam_re"]], w=[lr], slow=True)
            self.DMA("sync", li.ap[:, d * 32:(d + 1) * 32], P["lam_im"].ap[j, d].rearrange("(p two) n -> (two n) p", two=2), r=[P["lam_im"]], w=[li], slow=True)
        lsrow = self.f32(128, "lsrow")
        self.DMA("sync", lsrow.ap[0:1, :], P["log_step"].ap[j:j + 1].rearrange("o d g -> o (d g)"), r=[P["log_step"]], w=[lsrow])
        psb = self.PS[0]
        self.mm(psb.ap[:, 0:128], self.ones.ap[0:1, :], lsrow.ap[0:1, :], True, True, r=[self.ones, lsrow], w=[psb])
        for d in range(2):
            src = psb.ap[:, d * 64:(d + 1) * 64].rearrange("q (p two) -> q p two", two=2)
            self.cp(V, ls.ap[0:64, d * 32:(d + 1) * 32], src[0:64, :, 0], r=[psb], w=[ls])
            self.cp(V, ls.ap[64:128, d * 32:(d + 1) * 32], src[64:128, :, 1], r=[psb], w=[ls])
        step, zr, zi, rr, tq = sc_tile("step"), sc_tile("zr"), sc_tile("zi"), sc_tile("rr"), sc_tile("tq")
        tmp = {"ti": self.i32(512, "ti"), "tf": self.f32(512, "tf"), "fr": self.f32(512, "fr"), "s2": self.f32(512, "s2"), "s4": self.f32(512, "s4")}
        tmp2 = {"ti": self.i32(512, "ti2"), "tf": self.f32(512, "tf2"), "fr": self.f32(512, "fr2"), "s2": self.f32(512, "s22"), "s4": self.f32(512, "s42")}
        sphi, cphi, frac = sc_tile("sphi"), sc_tile("cphi"), sc_tile("frac")
        self.act(step.ap, ls.ap, AF.Exp, r=[ls], w=[step])
        self.tt(V, zr.ap, lr.ap, step.ap, ALU.mult, r=[lr, step], w=[zr])
        self.tt(V, zi.ap, li.ap, step.ap, ALU.mult, r=[li, step], w=[zi])
        self.act(rr.ap, zr.ap, AF.Exp, r=[zr], w=[rr])
        self.ts(V, tq.ap, zi.ap, float(1.0 / (2 * np.pi)), ALU.mult, r=[zi], w=[tq])
        self.sincos_turns(V, tq, 64, sphi, cphi, tmp)
        self.cp(V, frac.ap, tmp["fr"].ap[:, 0:64], r=[tmp["fr"]], w=[frac])
        carry = {}
        for Q in (256, 512):
            tQ, sQ, cQ = sc_tile("tQ%d" % Q), sc_tile("sQ%d" % Q), sc_tile("cQ%d" % Q)
            self.ts(V, tQ.ap, frac.ap, float(Q), ALU.mult, r=[frac], w=[tQ])
            self.sincos_turns(V, tQ, 64, sQ, cQ, tmp)
            carry[Q] = (sQ, cQ)
        ar, ai, den, u, cr, ci, t1s, t2s = [sc_tile(n) for n in ("ar", "ai", "den", "u", "cr", "ci", "t1s", "t2s")]
        self.tt(V, ar.ap, rr.ap, cphi.ap, ALU.mult, r=[rr, cphi], w=[ar])
        self.tt(V, ai.ap, rr.ap, sphi.ap, ALU.mult, r=[rr, sphi], w=[ai])
        self.tt(V, t1s.ap, lr.ap, lr.ap, ALU.mult, r=[lr], w=[t1s])
        self.tt(V, t2s.ap, li.ap, li.ap, ALU.mult, r=[li], w=[t2s])
        self.tt(V, den.ap, t1s.ap, t2s.ap, ALU.add, r=[t1s, t2s], w=[den])
        self.E(V, lambda e: e.reciprocal(out=den.ap, in_=den.ap), r=[den], w=[den])
        self.ts(V, u.ap, ar.ap, -1.0, ALU.add, r=[ar], w=[u])
        self.tt(V, t1s.ap, u.ap, lr.ap, ALU.mult, r=[u, lr], w=[t1s])
        self.tt(V, t2s.ap, ai.ap, li.ap, ALU.mult, r=[ai, li], w=[t2s])
        self.tt(V, t1s.ap, t1s.ap, t2s.ap, ALU.add, r=[t1s, t2s], w=[t1s])
        self.tt(V, cr.ap, t1s.ap, den.ap, ALU.mult, r=[t1s, den], w=[cr])
        self.tt(V, t1s.ap, ai.ap, lr.ap, ALU.mult, r=[ai, lr], w=[t1s])
        self.tt(V, t2s.ap, u.ap, li.ap, ALU.mult, r=[u, li], w=[t2s])
        self.tt(V, t1s.ap, t1s.ap, t2s.ap, ALU.subtract, r=[t1s, t2s], w=[t1s])
        self.tt(V, ci.ap, t1s.ap, den.ap, ALU.mult, r=[t1s, den], w=[ci])
        braw = {}
        for nm in ("b_re", "b_im"):
            braw[nm] = self.f32(2 * 32 * 16, "braw_" + nm, shape=(2, 32, 16))
            for d in range(2):
                self.DMA("sync", braw[nm].ap[:, d, :, :], P[nm].ap[j, d].rearrange("(p two) n h -> (two n) p h", two=2), r=[P[nm]], w=[braw[nm]], slow=True)
        dsk = self.f32(8, "dsk")
        self.DMA("sync", dsk.ap, P["d"].ap[j].rearrange("(ft p) -> p ft", p=128), r=[P["d"]], w=[dsk], slow=True)
        M1 = {}
        for k in range(4):
            for nm in ("b_re", "b_im"):
                M1[(k, nm)] = self.f32(128, "M1_%d%s" % (k, nm))
                self.memset(G, M1[(k, nm)].ap, 0.0, [M1[(k, nm)]])
        Jrow = self.f32(512, "Jrow")
        self.E(G, lambda e: e.iota(Jrow.ap, pattern=[[1, 512]], base=0, channel_multiplier=0, allow_small_or_imprecise_dtypes=True), (), [Jrow])
        WTS = [self.bf16(8 * 6 * 128, "wts%d" % i, shape=(8, 6, 128)) for i in range(2)]
        craw = [[self.f32(64, "craw%d_%d" % (i, q)) for q in range(2)] for i in range(2)]
        Spair = [self.f32(128, "Spair%d" % i) for i in range(2)]
        U = [self.bf16(T, "U%d" % i) for i in range(2)]
        Yacc = self.f32(T, "Yacc")
        gt = self.bf16(T, "gt")
        cmr, cpr, ncpr = sc_tile("cmr"), sc_tile("cpr"), sc_tile("ncpr")
        self.tt(V, cmr.ap, ci.ap, cr.ap, ALU.subtract, r=[ci, cr], w=[cmr])
        self.tt(V, cpr.ap, ci.ap, cr.ap, ALU.add, r=[ci, cr], w=[cpr])
        self.ts(V, ncpr.ap, cpr.ap, -1.0, ALU.mult, r=[cpr], w=[ncpr])
        lastc = [self.f32(2, "lastc%d" % i) for i in range(2)]
        nsQ = {}
        for Q in (256, 512):
            nsQ[Q] = sc_tile("nsQ%d" % Q)
            self.ts(V, nsQ[Q].ap, carry[Q][0].ap, -1.0, ALU.mult, r=[carry[Q][0]], w=[nsQ[Q]])
        TAB = [{n: self.f32(512, "%s%d" % (n, i)) for n in ("COS", "SIN", "wr", "bma", "apb", "ta", "tb")} for i in range(2)]
        WK = [{n: self.f32(512, "%s%d" % (n, i)) for n in ("k1", "k2", "k3", "pss", "bre", "bim")} for i in range(2)]
        MK = [{n: self.bf16(512, "%s%d" % (n, i)) for n in ("m1", "m2", "m3", "m4")} for i in range(2)]
        init = [self.f32(4, "init%d" % i) for i in range(2)]
        fwd_chunks = [(0, LC)] + [(s, 512) for s in range(LC, T, 512)]
        bwd_chunks = [(0, LC)] + [(T - 512 * (i + 1), 512) for i in range(8)]
        tgc = [0]

        def emit_prep(ft):
            Ui = U[ft % 2]
            self.DMA("sync", Ui.ap, HT.ap[ft * 128:(ft + 1) * 128, :], r=[HT], w=[Ui])
            W = WTS[ft % 2]
            for d in range(2):
                cr_ = craw[d]
                self.DMA("sync", cr_[0].ap, P["c_re"].ap[j, d, ft * 8:(ft + 1) * 8].rearrange("g h n -> (g h) n"), r=[P["c_re"]], w=[cr_[0]])
                self.DMA("sync", cr_[1].ap, P["c_im"].ap[j, d, ft * 8:(ft + 1) * 8].rearrange("g h n -> (g h) n"), r=[P["c_im"]], w=[cr_[1]])
                for k in range(4):
                    p_ = ft * 4 + k
                    wi_ = d * 4 + k
                    c1, c2 = 32 * k, 32 * k + 16
                    for bi, nm in enumerate(("b_re", "b_im")):
                        m1t = M1[(k, nm)]
                        self.cp(G, m1t.ap[0:64, c1:c1 + 16], braw[nm].ap[0:64, d, p_, :], r=[braw[nm]], w=[m1t])
                        self.cp(G, m1t.ap[64:128, c2:c2 + 16], braw[nm].ap[64:128, d, p_, :], r=[braw[nm]], w=[m1t])
                        ps = self.PS[5]
                        self.tr(ps.ap[:, bi * 128:(bi + 1) * 128], m1t.ap, self.ident.ap, r=[m1t, self.ident], w=[ps])
                        self.cp("scalar", W.ap[:, wi_, bi, :], ps.ap[:, bi * 128:(bi + 1) * 128], r=[ps], w=[W])
                    self.tt(G, W.ap[:, wi_, 5, :], W.ap[:, wi_, 0, :], W.ap[:, wi_, 1, :], ALU.add, r=[W], w=[W])
                    for q in range(2):
                        sp = Spair[q]
                        self.ts(G, sp.ap[:, 0:64], cr_[q].ap, self.mask8.ap[:, 2 * k:2 * k + 1], ALU.mult, 0.0, ALU.add, r=[cr_[q], self.mask8], w=[sp])
                        self.ts(G, sp.ap[:, 64:128], cr_[q].ap, self.mask8.ap[:, 2 * k + 1:2 * k + 2], ALU.mult, 0.0, ALU.add, r=[cr_[q], self.mask8], w=[sp])
                        ps = self.PS[5]
                        self.tr(ps.ap[:, 256 + q * 128:256 + (q + 1) * 128], sp.ap, self.ident.ap, r=[sp, self.ident], w=[ps])
                        src = ps.ap[:, 256 + q * 128:256 + (q + 1) * 128]
                        if q == 0:
                            self.cp("scalar", W.ap[:, wi_, 2, :], src, r=[ps], w=[W])
                            self.act(W.ap[:, wi_, 3, :], src, AF.Copy, r=[ps], w=[W], scale=-1.0)
                        else:
                            self.act(W.ap[:, wi_, 4, :], src, AF.Copy, r=[ps], w=[W], scale=-1.0)

        def table_thunks(tab, col):
            th = []
            add = th.append
            MAGIC = 12582912.0
            ti, tf, fr, s2, s4 = tmp2["ti"], tmp2["tf"], tmp2["fr"], tmp2["s2"], tmp2["s4"]
            sc1 = lambda t: t.ap[:, col:col + 1]
            add(lambda: self.act(tab["ta"].ap, Jrow.ap, AF.Identity, r=[Jrow, frac], w=[tab["ta"]], scale=sc1(frac)))
            add(lambda: self.act(tf.ap, tab["ta"].ap, AF.Identity, r=[tab["ta"]], w=[tf], bias=MAGIC))
            add(lambda: self.act(tf.ap, tf.ap, AF.Identity, r=[tf], w=[tf], bias=-MAGIC))
            add(lambda: self.tt(G, fr.ap, tab["ta"].ap, tf.ap, ALU.subtract, r=[tab["ta"], tf], w=[fr]))
            add(lambda: self.act(s2.ap, fr.ap, AF.Sin, r=[fr], w=[s2], scale=float(np.pi)))
            add(lambda: self.act(s4.ap, fr.ap, AF.Sin, r=[fr], w=[s4], scale=float(np.pi / 2)))
            add(lambda: self.act(s4.ap, s4.ap, AF.Square, r=[s4], w=[s4]))
            add(lambda: self.act(s4.ap, s4.ap, AF.Identity, r=[s4], w=[s4], scale=-4.0, bias=2.0))
            add(lambda: self.tt(G, tab["SIN"].ap, s2.ap, s4.ap, ALU.mult, r=[s2, s4], w=[tab["SIN"]]))
            add(lambda: self.act(s2.ap, s2.ap, AF.Square, r=[s2], w=[s2]))
            add(lambda: self.act(tab["COS"].ap, s2.ap, AF.Identity, r=[s2], w=[tab["COS"]], scale=-2.0, bias=1.0))
            for (c1, c2, dst) in ((cr, ci, "wr"), (cmr, ncpr, "bma"), (cpr, cmr, "apb")):
                add(lambda c1=c1: self.act(tab["ta"].ap, tab["COS"].ap, AF.Identity, r=[tab["COS"], c1], w=[tab["ta"]], scale=sc1(c1)))
                add(lambda c2=c2: self.act(tab["tb"].ap, tab["SIN"].ap, AF.Identity, r=[tab["SIN"], c2], w=[tab["tb"]], scale=sc1(c2)))
                add(lambda dst=dst: self.tt(G, tab[dst].ap, tab["ta"].ap, tab["tb"].ap, ALU.add, r=[tab["ta"], tab["tb"]], w=[tab[dst]]))
            return th

        items = []
        pd = 0
        for ft in range(NFT):
            for d in range(2):
                chunks = fwd_chunks if d == 0 else bwd_chunks
                for k in range(4):
                    for cidx, (s, n) in enumerate(chunks):
                        items.append(dict(ft=ft, d=d, k=k, cidx=cidx, s=s, n=n, pd=pd, last=(cidx == len(chunks) - 1),
                                          first_pd=(cidx == 0), first_y=(d == 0 and k == 0),
                                          ft_last=(d == 1 and k == 3 and cidx == len(chunks) - 1)))
                    pd += 1
        pending = []

        def stage_a(i, it_):
            ft, d, k, s, n = it_["ft"], it_["d"], it_["k"], it_["s"], it_["n"]
            if i == 0:
                emit_prep(0)
                for f in table_thunks(TAB[0], 0):
                    f()
            if d == 1 and k == 0 and it_["cidx"] == 0 and ft + 1 < NFT:
                emit_prep(ft + 1)
            if it_["cidx"] == 1 and it_["pd"] + 1 < 64:
                npd = it_["pd"] + 1
                nft, nd, nk = npd // 8, (npd // 4) % 2, npd % 4
                pending.extend(table_thunks(TAB[npd % 2], nd * 32 + nft * 4 + nk))
            if it_["cidx"] >= 1:
                ntake = len(pending) if it_["last"] else min(3, len(pending))
                for _ in range(ntake):
                    pending.pop(0)()
            tab = TAB[it_["pd"] % 2]
            Ui, W, wi_ = U[ft % 2], WTS[ft % 2], d * 4 + k
            wk = WK[i % 2]
            pre, pim, psu = self.PS[0], self.PS[1], self.PS[2]
            urhs = Ui.ap[:, s:s + n] if d == 0 else self.rev_ap(Ui.ap, s, n)
            self.mm(pre.ap[:, 0:n], W.ap[:, wi_, 0, :], urhs, True, True, r=[W, Ui], w=[pre])
            self.mm(pim.ap[:, 0:n], W.ap[:, wi_, 1, :], urhs, True, True, r=[W, Ui], w=[pim])
            self.mm(psu.ap[:, 0:n], W.ap[:, wi_, 5, :], urhs, True, True, r=[W, Ui], w=[psu])
            c_ = lambda t: t.ap[:, 0:n]
            self.cp("scalar", c_(wk["pss"]), psu.ap[:, 0:n], r=[psu], w=[wk["pss"]])
            self.tt(V, c_(wk["k2"]), pre.ap[:, 0:n], c_(tab["bma"]), ALU.mult, r=[pre, tab["bma"]], w=[wk["k2"]])
            self.tt(V, c_(wk["k3"]), pim.ap[:, 0:n], c_(tab["apb"]), ALU.mult, r=[pim, tab["apb"]], w=[wk["k3"]])
            self.tt(G, c_(wk["k1"]), c_(wk["pss"]), c_(tab["wr"]), ALU.mult, r=[wk["pss"], tab["wr"]], w=[wk["k1"]])
            self.tt(G, c_(wk["bre"]), c_(wk["k1"]), c_(wk["k3"]), ALU.subtract, r=[wk["k1"], wk["k3"]], w=[wk["bre"]])
            self.tt(G, c_(wk["bim"]), c_(wk["k1"]), c_(wk["k2"]), ALU.add, r=[wk["k1"], wk["k2"]], w=[wk["bim"]])

        def stage_b(i, it_):
            ft, d, k, s, n, cidx = it_["ft"], it_["d"], it_["k"], it_["s"], it_["n"], it_["cidx"]
            tab = TAB[it_["pd"] % 2]
            col = d * 32 + ft * 4 + k
            Ui, W, wi_ = U[ft % 2], WTS[ft % 2], d * 4 + k
            wk, mk, ini = WK[i % 2], MK[i % 2], init[i % 2]
            py = self.PS[3 + i % 2]
            c_ = lambda t: t.ap[:, 0:n]
            rcol = rr.ap[:, col:col + 1]
            rb = rcol.to_broadcast([128, n])
            if cidx == 0:
                i_re, i_im, ir = 0.0, 0.0, []
            else:
                pini = init[(i - 1) % 2]
                i_re, i_im, ir = pini.ap[:, 0:1], pini.ap[:, 1:2], [pini]
            gre_t, gim_t = self.PS[6], self.PS[7]
            self.E(V, lambda e, o=gre_t.ap[:, 0:n], d1=c_(wk["bre"]), i0=i_re, rb=rb: e.tensor_tensor_scan(out=o, data0=rb, data1=d1, initial=i0, op0=ALU.mult, op1=ALU.add),
                   r=[wk["bre"], rr] + ir, w=[gre_t])
            self.E(V, lambda e, o=gim_t.ap[:, 0:n], d1=c_(wk["bim"]), i0=i_im, rb=rb: e.tensor_tensor_scan(out=o, data0=rb, data1=d1, initial=i0, op0=ALU.mult, op1=ALU.add),
                   r=[wk["bim"], rr] + ir, w=[gim_t])
            if not it_["last"]:
                sQ, cQ = carry[n]
                sq_, cq_, nsq_ = sQ.ap[:, col:col + 1], cQ.ap[:, col:col + 1], nsQ[n].ap[:, col:col + 1]
                lc = lastc[i % 2]
                self.cp(V, lc.ap[:, 0:1], gre_t.ap[:, n - 1:n], r=[gre_t], w=[lc])
                self.cp(V, lc.ap[:, 1:2], gim_t.ap[:, n - 1:n], r=[gim_t], w=[lc])
                gre_l, gim_l = lc.ap[:, 0:1], lc.ap[:, 1:2]
                self.act(ini.ap[:, 2:3], gim_l, AF.Identity, r=[lc, nsQ[n]], w=[ini], scale=nsq_)
                self.act(ini.ap[:, 3:4], gim_l, AF.Identity, r=[lc, cQ], w=[ini], scale=cq_)
                self.act(ini.ap[:, 0:1], gre_l, AF.Identity, r=[lc, cQ, ini], w=[ini], scale=cq_, bias=ini.ap[:, 2:3])
                self.act(ini.ap[:, 1:2], gre_l, AF.Identity, r=[lc, sQ, ini], w=[ini], scale=sq_, bias=ini.ap[:, 3:4])
            self.tt(V, c_(mk["m1"]), gre_t.ap[:, 0:n], c_(tab["COS"]), ALU.mult, r=[gre_t, tab["COS"]], w=[mk["m1"]])
            self.tt(V, c_(mk["m3"]), gre_t.ap[:, 0:n], c_(tab["SIN"]), ALU.mult, r=[gre_t, tab["SIN"]], w=[mk["m3"]])
            self.tt(V, c_(mk["m2"]), gim_t.ap[:, 0:n], c_(tab["SIN"]), ALU.mult, r=[gim_t, tab["SIN"]], w=[mk["m2"]])
            self.tt(V, c_(mk["m4"]), gim_t.ap[:, 0:n], c_(tab["COS"]), ALU.mult, r=[gim_t, tab["COS"]], w=[mk["m4"]])
            for mi, (mn, wsel) in enumerate((("m1", 2), ("m2", 3), ("m3", 4), ("m4", 4))):
                mr = mk[mn].ap[:, 0:n] if d == 0 else self.rev_ap(mk[mn].ap, 0, n)
                self.mm(py.ap[:, 0:n], W.ap[:, wi_, wsel, :], mr, mi == 0, mi == 3, r=[W, mk[mn]], w=[py])
            if it_["first_y"]:
                self.cp("scalar", Yacc.ap[:, s:s + n], py.ap[:, 0:n], r=[py], w=[Yacc])
            else:
                self.tt(V, Yacc.ap[:, s:s + n], py.ap[:, 0:n], Yacc.ap[:, s:s + n], ALU.add, r=[py, Yacc], w=[Yacc])
            if it_["ft_last"]:
                self.stt(Yacc.ap, Ui.ap, dsk.ap[:, ft:ft + 1], Yacc.ap, ALU.mult, ALU.add, r=[Ui, dsk, Yacc], w=[Yacc])
                self.act(gt.ap, Yacc.ap, AF.Gelu_apprx_tanh, r=[Yacc], w=[gt])
                self.DMA("sync", GT.ap[ft * 128:(ft + 1) * 128, :], gt.ap, r=[gt], w=[GT])

        stage_a(0, items[0])
        for i in range(len(items)):
            if i + 1 < len(items):
                stage_a(i + 1, items[i + 1])
            stage_b(i, items[i])

    def proj_phase(self, XT, IN, Wd, w_ap, nkt, l, glu_bias=None):
        self.new_phase()
        Wt = 512
        ws = self.bf16(nkt * D, "pw", shape=(nkt, D))
        self.load_w_bf16(ws, w_ap, Wd, nkt)
        bg = None
        if glu_bias is not None:
            bg = self.f32(8, "bglu")
            self.DMA("sync", bg.ap, glu_bias[1].rearrange("(ft p) -> p ft", p=128), r=[glu_bias[0]], w=[bg], slow=True)
        ib = [self.bf16(nkt * Wt, "pi%d" % i, shape=(nkt, Wt)) for i in range(2)]
        xb = [self.f32(NFT * Wt, "px%d" % i, shape=(NFT, Wt)) for i in range(2)]
        sg = [self.f32(Wt, "psg%d" % i) for i in range(2)]
        mod = self.mod[l]
        chs = self.chunks()

        def load(ci):
            s, w, isctx = chs[ci]
            self.DMA("sync", ib[ci % 2].ap[:, :, 0:w], IN.ap[:, s:s + w].rearrange("(kt p) t -> p kt t", p=128), r=[IN], w=[ib[ci % 2]])
            self.DMA("sync", xb[ci % 2].ap[:, :, 0:w], XT.ap[:, s:s + w].rearrange("(ft p) t -> p ft t", p=128), r=[XT], w=[xb[ci % 2]])

        load(0)
        for ci, (s, w, isctx) in enumerate(chs):
            cs = 1 if isctx else 0
            ibi, xbi = ib[ci % 2], xb[ci % 2]
            if ci + 1 < len(chs):
                load(ci + 1)
            for fo in range(NFT):
                po = self.PS[fo % 4]
                for kt in range(nkt):
                    self.mm(po.ap[:, 0:w], ws.ap[:, kt, fo * 128:(fo + 1) * 128], ibi.ap[:, kt, 0:w], kt == 0, kt == nkt - 1, r=[ws, ibi], w=[po])
                gate = mod.ap[:, 2, fo, cs:cs + 1]
                if glu_bias is not None:
                    sgi = sg[fo % 2]
                    self.act(sgi.ap[:, 0:w], po.ap[:, 0:w], AF.Sigmoid, r=[po, bg], w=[sgi], bias=bg.ap[:, fo:fo + 1])
                    self.tt("gpsimd", sgi.ap[:, 0:w], sgi.ap[:, 0:w], ibi.ap[:, fo, 0:w], ALU.mult, r=[sgi, ibi], w=[sgi])
                    self.stt(xbi.ap[:, fo, 0:w], sgi.ap[:, 0:w], gate, xbi.ap[:, fo, 0:w], ALU.mult, ALU.add, r=[sgi, mod, xbi], w=[xbi])
                else:
                    self.stt(xbi.ap[:, fo, 0:w], po.ap[:, 0:w], gate, xbi.ap[:, fo, 0:w], ALU.mult, ALU.add, r=[po, mod, xbi], w=[xbi])
            self.DMA("sync", XT.ap[:, s:s + w].rearrange("(ft p) t -> p ft t", p=128), xbi.ap[:, :, 0:w], r=[xbi], w=[XT])


    def row_bcast(self, dst, row_ap, src_tl, n, rowtmp, func=None, scale=None):
        self.DMA("sync", rowtmp.ap[0:1, 0:n], row_ap, r=[src_tl], w=[rowtmp])
        for c0 in range(0, n, 512):
            w = min(512, n - c0)
            ps = self.PS[7]
            self.mm(ps.ap[:, 0:w], self.ones.ap[0:1, :], rowtmp.ap[0:1, c0:c0 + w], True, True, r=[self.ones, rowtmp], w=[ps])
            if func is None:
                self.cp("vector", dst.ap[:, c0:c0 + w], ps.ap[:, 0:w], r=[ps], w=[dst])
            else:
                self.act(dst.ap[:, c0:c0 + w], ps.ap[:, 0:w], func, r=[ps], w=[dst], scale=scale)

    def tri_mask(self, nm, base, cm, step, op):
        t = self.f32(128, nm)
        self.memset("gpsimd", t.ap, 1.0, [t])
        self.E("gpsimd", lambda e: e.affine_select(out=t.ap, in_=t.ap, pattern=[[step, 128]], compare_op=op, fill=0.0, base=base, channel_multiplier=cm), [t], [t])
        return t

    def ssd_phase_a(self, HT, P, SZ, XS, BC, DTR):
        self.new_phase()
        w_in = P["w_in"]
        Hres = self.bf16(8 * T, "Hres", shape=(8, T))
        for kt in range(8):
            self.DMA("sync", Hres.ap[:, kt, :], HT.ap[kt * 128:(kt + 1) * 128, :], r=[HT], w=[Hres])
        wz = self.bf16(8 * 2048, "wz", shape=(8, 2048))
        self.load_w_bf16(wz, w_in.ap[0], w_in, 8, 0, 2048)
        szb = [self.bf16(2048, "szb%d" % i) for i in range(2)]
        for tt in range(T // 128):
            sb = szb[tt % 2]
            for zc in range(4):
                ps = self.PS[zc]
                for kt in range(8):
                    self.mm(ps.ap, Hres.ap[:, kt, tt * 128:(tt + 1) * 128], wz.ap[:, kt, zc * 512:(zc + 1) * 512], kt == 0, kt == 7, r=[Hres, wz], w=[ps])
                self.act(sb.ap[:, zc * 512:(zc + 1) * 512], ps.ap, AF.Silu, r=[ps], w=[sb])
            self.DMA("sync", SZ.ap[tt * 128:(tt + 1) * 128, :], sb.ap, r=[sb], w=[SZ])
        self.new_phase()
        Hres = self.bf16(8 * T, "Hres2", shape=(8, T))
        for kt in range(8):
            self.DMA("sync", Hres.ap[:, kt, :], HT.ap[kt * 128:(kt + 1) * 128, :], r=[HT], w=[Hres])
        cw = self.f32(5 * 48, "cw", shape=(5, 48))
        cb = self.f32(48, "cb")
        for k in range(5):
            self.DMA("sync", cw.ap[:, k, :], P["conv_w"].ap[0, k].rearrange("(ot p) -> p ot", p=128), r=[P["conv_w"]], w=[cw], slow=True)
        self.DMA("sync", cb.ap, P["conv_b"].ap[0].rearrange("(ot p) -> p ot", p=128), r=[P["conv_b"]], w=[cb], slow=True)
        wch = [self.bf16(8 * 512, "wch%d" % i, shape=(8, 512)) for i in range(2)]
        xp = [self.bf16(T, "xp%d" % i) for i in range(2)]
        ob = [self.bf16(T, "ob%d" % i) for i in range(2)]
        dg = [self.bf16(5 * 128, "dg%d" % i, shape=(5, 128)) for i in range(2)]
        chunks = self.chunks()
        pi = 0
        for wc in range(12):
            wb = wch[wc % 2]
            self.load_w_bf16(wb, w_in.ap[0], w_in, 8, 2048 + wc * 512, 2048 + (wc + 1) * 512)
            for q in range(4):
                ot = wc * 4 + q
                xpi, obi, dgi = xp[ot % 2], ob[ot % 2], dg[ot % 2]
                for k in range(5):
                    self.ts("gpsimd", dgi.ap[:, k, :], self.identb.ap, cw.ap[:, k, ot:ot + 1], ALU.mult, 0.0, ALU.add, r=[self.identb, cw], w=[dgi])
                for ci, (s, w, isctx) in enumerate(chunks):
                    ps = self.PS[pi % 4]
                    pi += 1
                    for kt in range(8):
                        self.mm(ps.ap[:, 0:w], wb.ap[:, kt, q * 128:(q + 1) * 128], Hres.ap[:, kt, s:s + w], kt == 0, kt == 7, r=[wb, Hres], w=[ps])
                    self.cp("vector" if ci % 2 else "scalar", xpi.ap[:, s:s + w], ps.ap[:, 0:w], r=[ps], w=[xpi])
                for ci, (s, w, isctx) in enumerate(chunks):
                    q0, q1 = (0, LC) if isctx else (LC, T)
                    ps = self.PS[4 + ci % 4]
                    for ki, k in enumerate((2, 0, 1, 3, 4)):
                        o = k - 2
                        i0 = max(0, q0 - s - o)
                        i1 = min(w, q1 - s - o)
                        self.mm(ps.ap[:, i0:i1], dgi.ap[:, k, :], xpi.ap[:, s + i0 + o:s + i1 + o], ki == 0, ki == 4, r=[dgi, xpi], w=[ps])
                    self.act(obi.ap[:, s:s + w], ps.ap[:, 0:w], AF.Silu, r=[ps, cb], w=[obi], bias=cb.ap[:, ot:ot + 1])
                dst = XS.ap[ot * 128:(ot + 1) * 128, :] if ot < 16 else BC.ap[(ot - 16) * 128:(ot - 15) * 128, :]
                self.DMA("sync", dst, obi.ap, r=[obi], w=[XS if ot < 16 else BC])
        wdt = self.bf16(8 * 64, "wdt", shape=(8, 64))
        self.load_w_bf16(wdt, w_in.ap[0], w_in, 8, 8192, 8256)
        dtb = self.f32(T, "dtb")
        for ci, (s, w, isctx) in enumerate(chunks):
            ps = self.PS[ci % 4]
            for kt in range(8):
                self.mm(ps.ap[0:64, 0:w], wdt.ap[:, kt, :], Hres.ap[:, kt, s:s + w], kt == 0, kt == 7, r=[wdt, Hres], w=[ps])
            self.cp("vector", dtb.ap[0:64, s:s + w], ps.ap[0:64, 0:w], r=[ps], w=[dtb])
        self.DMA("sync", DTR.ap, dtb.ap[0:64, :], r=[dtb], w=[DTR])

    def ssd_phase_b(self, P, XS, BC, DTR, YF, YB):
        self.new_phase()
        V, G = "vector", "gpsimd"
        GT_ = self.tri_mask("mGT", 0, 1, -1, ALU.is_gt)
        LE_ = self.tri_mask("mLE", 0, -1, 1, ALU.is_ge)
        LT_ = self.tri_mask("mLT", 0, -1, 1, ALU.is_gt)
        GE_ = self.tri_mask("mGE", 0, 1, -1, ALU.is_ge)
        rowtmp = self.f32(64, "rowtmp")
        Arow = [self.f32(32, "Arow%d" % d) for d in range(2)]
        Brow = [self.f32(32, "Brow%d" % d) for d in range(2)]
        Drow = self.f32(32, "Drow")
        for d in range(2):
            self.row_bcast(Arow[d], P["a_log"].ap[0, d:d + 1, :], P["a_log"], 32, rowtmp, func=AF.Exp)
            self.ts(V, Arow[d].ap, Arow[d].ap, -1.0, ALU.mult, r=[Arow[d]], w=[Arow[d]])
            self.row_bcast(Brow[d], P["dt_bias"].ap[0, d:d + 1, :], P["dt_bias"], 32, rowtmp)
        self.row_bcast(Drow, P["d"].ap[0:1, :], P["d"], 32, rowtmp)
        H = self.f32(2048, "Hst", shape=(32, 64))
        Hb = self.bf16(2048, "Hb")
        NB = 2
        xsT = [self.bf16(16 * 128, "xsT%d" % i, shape=(16, 128)) for i in range(NB)]
        BTt = [self.bf16(8 * 128, "BT%d" % i, shape=(8, 128)) for i in range(NB)]
        CTt = [self.bf16(8 * 128, "CT%d" % i, shape=(8, 128)) for i in range(NB)]
        dtr = [self.f32(128, "dtr%d" % i) for i in range(NB)]
        ybuf = [self.bf16(2048, "ybuf%d" % i) for i in range(NB)]
        xdt = [self.bf16(2048, "xdt%d" % i, shape=(32, 64)) for i in range(NB)]
        xdte = [self.bf16(2048, "xdte%d" % i, shape=(32, 64)) for i in range(NB)]
        dskt = [self.f32(2048, "dskt%d" % i, shape=(32, 64)) for i in range(NB)]
        Btok = [self.bf16(1024, "Btok%d" % i, shape=(8, 128)) for i in range(NB)]
        dtt = [self.f32(32, "dtt%d" % i) for i in range(NB)]
        adt = [self.f32(32, "adt%d" % i) for i in range(NB)]
        dex = [self.f32(96, "dex%d" % i) for i in range(NB)]
        dtd = [self.f32(32, "dtd%d" % i) for i in range(NB)]
        Xh = [self.f32(512, "Xh%d" % i, shape=(4, 128)) for i in range(2)]
        Ld = [self.bf16(512, "Ld%d" % i, shape=(4, 128)) for i in range(2)]
        Mt = [self.bf16(512, "Mt%d" % i, shape=(4, 128)) for i in range(2)]
        CBm = [self.bf16(128, "CBm%d" % i) for i in range(2)]
        ytmp = [self.f32(256, "ytmp%d" % i, shape=(4, 64)) for i in range(2)]
        htmp = [self.f32(256, "htmp%d" % i, shape=(4, 64)) for i in range(2)]
        nchunk = T // 128
        work = []
        for d in range(2):
            order = list(range(nchunk)) if d == 0 else [1, 0] + list(range(nchunk - 1, 1, -1))
            for oi, c in enumerate(order):
                work.append((d, c, oi == 0))
        cfgd = {0: dict(mX=GT_, mE=LE_, mTE=GT_, mSeg=LE_, mCB=LE_), 1: dict(mX=LT_, mE=GE_, mTE=LT_, mSeg=GE_, mCB=GE_)}

        def prologue(wi):
            d, c, first = work[wi]
            cf = cfgd[d]
            b = wi % NB
            s0 = c * 128
            xi, bi, cti, dri = xsT[b], BTt[b], CTt[b], dtr[b]
            self.DMA("sync", xi.ap, XS.ap[:, s0:s0 + 128].rearrange("(t p) c -> p t c", p=128), r=[XS], w=[xi])
            self.DMA("sync", bi.ap, BC.ap[(d * 2) * 1024:(d * 2 + 1) * 1024, s0:s0 + 128].rearrange("(t p) c -> p t c", p=128), r=[BC], w=[bi])
            self.DMA("sync", cti.ap, BC.ap[(d * 2 + 1) * 1024:(d * 2 + 2) * 1024, s0:s0 + 128].rearrange("(t p) c -> p t c", p=128), r=[BC], w=[cti])
            self.DMA("sync", dri.ap[0:32, :], DTR.ap[d * 32:(d + 1) * 32, s0:s0 + 128], r=[DTR], w=[dri])
            psm = self.PS[6]
            dtt_, adt_, dex_, dtd_ = dtt[b], adt[b], dex[b], dtd[b]
            self.tr(psm.ap[:, 0:32], dri.ap[0:32, :], self.ident.ap[0:32, 0:32], r=[dri, self.ident], w=[psm])
            self.tt(V, dtt_.ap, psm.ap[:, 0:32], Brow[d].ap, ALU.add, r=[psm, Brow[d]], w=[dtt_])
            self.act(dtt_.ap, dtt_.ap, AF.Exp, r=[dtt_], w=[dtt_])
            self.act(dtt_.ap, dtt_.ap, AF.Ln, r=[dtt_, self.epsc], w=[dtt_], bias=self.epsc.ap[:, 2:3])
            self.tt(V, adt_.ap, dtt_.ap, Arow[d].ap, ALU.mult, r=[dtt_, Arow[d]], w=[adt_])
            self.mm(psm.ap[:, 32:64], self.ones.ap, adt_.ap, True, True, r=[self.ones, adt_], w=[psm])
            self.mm(psm.ap[:, 64:96], cf["mTE"].ap, adt_.ap, True, True, r=[cf["mTE"], adt_], w=[psm])
            self.mm(psm.ap[:, 96:128], cf["mE"].ap, adt_.ap, True, True, r=[cf["mE"], adt_], w=[psm])
            self.act(dex_.ap, psm.ap[:, 32:128], AF.Exp, r=[psm], w=[dex_])
            self.tt(V, dtd_.ap, dtt_.ap, dex_.ap[:, 32:64], ALU.mult, r=[dtt_, dex_], w=[dtd_])
            pst = self.PS[7]
            psb16 = self.PSB[7]
            for half in range(2):
                for q in range(8):
                    t_ = half * 8 + q
                    self.tr(psb16[:, q * 128:(q + 1) * 128], xi.ap[:, t_, :], self.identb.ap, r=[xi, self.identb], w=[pst])
                src = psb16[:, 0:1024].rearrange("p (h c) -> p h c", c=64)
                hs = slice(half * 16, (half + 1) * 16)
                bc = lambda col: col.unsqueeze(2).to_broadcast([128, 16, 64])
                self.tt(V, xdt[b].ap[:, hs, :], src, bc(dtt_.ap[:, hs]), ALU.mult, r=[pst, dtt_], w=[xdt[b]])
                self.tt(V, xdte[b].ap[:, hs, :], src, bc(dtd_.ap[:, hs]), ALU.mult, r=[pst, dtd_], w=[xdte[b]])
                if d == 0:
                    self.tt(V, dskt[b].ap[:, hs, :], src, bc(Drow.ap[:, hs]), ALU.mult, r=[pst, Drow], w=[dskt[b]])
            for g in range(8):
                self.tr(psb16[:, g * 128:(g + 1) * 128], bi.ap[:, g, :], self.identb.ap, r=[bi, self.identb], w=[pst])
            self.cp("scalar", Btok[b].ap, psb16[:, 0:1024].rearrange("p (g c) -> p g c", c=128), r=[pst], w=[Btok[b]])

        gi = [0]

        def s1(wi, g):
            d, c, first = work[wi]
            cf = cfgd[d]
            b = wi % NB
            k = (wi * 8 + g) % 2
            xh = Xh[k]
            for hh in range(4):
                h = g * 4 + hh
                self.ts(G, xh.ap[:, hh, :], cf["mX"].ap, adt[b].ap[:, h:h + 1], ALU.mult, 0.0, ALU.add, r=[cf["mX"], adt[b]], w=[xh])
            pcb = self.PS[k]
            self.mm(pcb.ap[:, 0:128], BTt[b].ap[:, g, :], CTt[b].ap[:, g, :], True, True, r=[BTt[b], CTt[b]], w=[pcb])
            psg = self.PS[4 + k]
            for hh in range(4):
                self.mm(psg.ap[:, hh * 128:(hh + 1) * 128], xh.ap[:, hh, :], cf["mSeg"].ap, True, True, r=[xh, cf["mSeg"]], w=[psg])

        def s2(wi, g):
            d, c, first = work[wi]
            cf = cfgd[d]
            k = (wi * 8 + g) % 2
            pcb, psg = self.PS[k], self.PS[4 + k]
            self.act(Ld[k].ap, psg.ap.rearrange("p (h c) -> p h c", c=128), AF.Exp, r=[psg], w=[Ld[k]])
            self.tt(V, CBm[k].ap, pcb.ap[:, 0:128], cf["mCB"].ap, ALU.mult, r=[pcb, cf["mCB"]], w=[CBm[k]])
            self.tt(V, Mt[k].ap, Ld[k].ap, CBm[k].ap.unsqueeze(1).to_broadcast([128, 4, 128]), ALU.mult, r=[Ld[k], CBm[k]], w=[Mt[k]])

        def s3(wi, g):
            d, c, first = work[wi]
            b = wi % NB
            k = (wi * 8 + g) % 2
            py = self.PS[2 + k]
            cti = CTt[b]
            for hh in range(4):
                h = g * 4 + hh
                self.mm(py.ap[:, hh * 64:(hh + 1) * 64], Mt[k].ap[:, hh, :], xdt[b].ap[:, h, :], True, True, r=[Mt[k], xdt[b]], w=[py])
                self.mm(py.ap[:, 256 + hh * 64:256 + (hh + 1) * 64], cti.ap[:, g, :], Hb.ap[:, h * 64:(h + 1) * 64], True, True, r=[cti, Hb], w=[py])
            gs = slice(g * 4, (g + 1) * 4)
            yt = ytmp[k]
            yb = ybuf[b]
            Eb = dex[b].ap[:, 64 + g * 4:64 + (g + 1) * 4].unsqueeze(2).to_broadcast([128, 4, 64])
            self.tt(V, yt.ap, py.ap[:, 256:512].rearrange("p (h c) -> p h c", c=64), Eb, ALU.mult, r=[py, dex[b]], w=[yt])
            if d == 0:
                self.tt(G, yt.ap, yt.ap, dskt[b].ap[:, gs, :], ALU.add, r=[yt, dskt[b]], w=[yt])
            self.tt(V, yb.ap[:, g * 256:(g + 1) * 256].rearrange("p (h c) -> p h c", c=64), py.ap[:, 0:256].rearrange("p (h c) -> p h c", c=64), yt.ap, ALU.add, r=[py, yt], w=[yb])
            pst2 = self.PS[6]
            self.mm(pst2.ap[:, 256:512], Btok[b].ap[:, g, :], xdte[b].ap[:, gs, :].rearrange("p h c -> p (h c)"), True, True, r=[Btok[b], xdte[b]], w=[pst2])
            ht = htmp[k]
            Db = dex[b].ap[:, g * 4:(g + 1) * 4].unsqueeze(2).to_broadcast([128, 4, 64])
            self.tt(G, ht.ap, H.ap[:, gs, :], Db, ALU.mult, r=[H, dex[b]], w=[ht])
            self.tt(V, H.ap[:, gs, :], pst2.ap[:, 256:512].rearrange("p (h c) -> p h c", c=64), ht.ap, ALU.add, r=[pst2, ht], w=[H])
            self.cp("scalar", Hb.ap[:, g * 256:(g + 1) * 256], H.ap[:, gs, :].rearrange("p h c -> p (h c)"), r=[H], w=[Hb])

        nw = len(work)
        prologue(0)
        s1(0, 0)
        for wi in range(nw):
            d, c, first = work[wi]
            if first:
                self.memset(V, H.ap, 0.0, [H])
                self.memset(V, Hb.ap, 0.0, [Hb])
            for g in range(8):
                if g == 4 and wi + 1 < nw:
                    prologue(wi + 1)
                if g + 1 < 8:
                    s1(wi, g + 1)
                elif wi + 1 < nw:
                    s1(wi + 1, 0)
                s2(wi, g)
                s3(wi, g)
            YO = YF if d == 0 else YB
            self.DMA("sync", YO.ap[c * 128:c * 128 + 128, :], ybuf[wi % NB].ap, r=[ybuf[wi % NB]], w=[YO])

    def ssd_phase_c(self, XT, P, SZ, YF, YB, l):
        self.new_phase()
        V, G = "vector", "gpsimd"
        wo = self.bf16(16 * D, "wo", shape=(16, D))
        self.load_w_bf16(wo, P["w_out"].ap[0], P["w_out"], 16)
        nrow = self.f32(2048, "nrow")
        rowtmp = self.f32(2048, "rowtmp2")
        self.row_bcast(nrow, P["norm"].ap[0:1, :], P["norm"], 2048, rowtmp)
        yf = [self.bf16(2048, "cyf%d" % i) for i in range(2)]
        ybk = [self.bf16(2048, "cyb%d" % i) for i in range(2)]
        sz = [self.bf16(2048, "csz%d" % i) for i in range(2)]
        yy = [self.f32(2048, "cyy%d" % i) for i in range(2)]
        sqj = self.f32(2048, "csq")
        gnb = [self.bf16(2048, "cgn%d" % i) for i in range(2)]
        ss = [self.f32(8, "css%d" % i) for i in range(2)]
        gnT = [self.bf16(16 * 512, "gnT%d" % i, shape=(16, 512)) for i in range(2)]
        xb = [self.f32(NFT * 512, "cx%d" % i, shape=(NFT, 512)) for i in range(2)]
        mod = self.mod[l]
        ti = 0
        chs = self.chunks()

        def load_tile(tix):
            t0 = tix * 128
            self.DMA("sync", yf[tix % 2].ap, YF.ap[t0:t0 + 128, :], r=[YF], w=[yf[tix % 2]])
            self.DMA("sync", ybk[tix % 2].ap, YB.ap[t0:t0 + 128, :], r=[YB], w=[ybk[tix % 2]])
            self.DMA("sync", sz[tix % 2].ap, SZ.ap[t0:t0 + 128, :], r=[SZ], w=[sz[tix % 2]])

        def load_x(ci):
            s, w, isctx = chs[ci]
            self.DMA("sync", xb[ci % 2].ap[:, :, 0:w], XT.ap[:, s:s + w].rearrange("(ft p) t -> p ft t", p=128), r=[XT], w=[xb[ci % 2]])

        load_tile(0)
        load_x(0)
        for ci, (s, w, isctx) in enumerate(chs):
            cs = 1 if isctx else 0
            gT, xbi = gnT[ci % 2], xb[ci % 2]
            if ci + 1 < len(chs):
                load_x(ci + 1)
            for tt in range(w // 128):
                t0 = s + tt * 128
                a, b, z_, y_, g_, s_ = yf[ti % 2], ybk[ti % 2], sz[ti % 2], yy[ti % 2], gnb[ti % 2], ss[ti % 2]
                ti += 1
                if ti < T // 128:
                    load_tile(ti)
                self.tt(G, y_.ap, a.ap, b.ap, ALU.add, r=[a, b], w=[y_])
                self.tt(V, y_.ap, y_.ap, z_.ap, ALU.mult, r=[y_, z_], w=[y_])
                self.act(sqj.ap, y_.ap, AF.Square, r=[y_], w=[sqj, s_], accum=s_.ap[:, 0:1])
                self.act(s_.ap[:, 1:2], s_.ap[:, 0:1], AF.Sqrt, r=[s_, self.epsc], w=[s_], scale=1.0 / 2048, bias=self.epsc.ap[:, 0:1])
                self.E(V, lambda e, o=s_.ap[:, 2:3], i=s_.ap[:, 1:2]: e.reciprocal(out=o, in_=i), r=[s_], w=[s_])
                self.stt(g_.ap, y_.ap, s_.ap[:, 2:3], nrow.ap, ALU.mult, ALU.mult, r=[y_, s_, nrow], w=[g_])
                for half in range(2):
                    pst = self.PS[4 + (ti * 2 + half) % 4]
                    psb16 = self.PSB[4 + (ti * 2 + half) % 4]
                    for q in range(8):
                        kt = half * 8 + q
                        self.tr(psb16[:, q * 128:(q + 1) * 128], g_.ap[:, kt * 128:(kt + 1) * 128], self.identb.ap, r=[g_, self.identb], w=[pst])
                    self.cp("scalar" if half else "vector", gT.ap[:, half * 8:(half + 1) * 8, tt * 128:(tt + 1) * 128],
                            psb16[:, 0:1024].rearrange("p (k c) -> p k c", c=128), r=[pst], w=[gT])
            for fo in range(NFT):
                po = self.PS[fo % 4]
                for kt in range(16):
                    self.mm(po.ap[:, 0:w], wo.ap[:, kt, fo * 128:(fo + 1) * 128], gT.ap[:, kt, 0:w], kt == 0, kt == 15, r=[wo, gT], w=[po])
                self.stt(xbi.ap[:, fo, 0:w], po.ap[:, 0:w], mod.ap[:, 2, fo, cs:cs + 1], xbi.ap[:, fo, 0:w], ALU.mult, ALU.add, r=[po, mod, xbi], w=[xbi])
            self.DMA("sync", XT.ap[:, s:s + w].rearrange("(ft p) t -> p ft t", p=128), xbi.ap[:, :, 0:w], r=[xbi], w=[XT])


    def na_bias_table(self, rpb, BTD):
        self.new_phase()
        neg = self.f32(7680, "negt")
        self.memset("vector", neg.ap, -30000.0, [neg])
        self.DMA("sync", BTD.ap.rearrange("h w r c -> (h w r c)").rearrange("(p n) -> p n", p=128), neg.ap, r=[neg], w=[BTD])
        HS = 64 * 15 * 64
        for h in range(16):
            so = h * 465
            do = h * HS
            dst = bass.AP(BTD.ap.tensor, do + 8 * 960, [[64, 15], [961, 49], [1, 16]])
            src = bass.AP(rpb.ap.tensor, so + 7, [[31, 15], [0, 49], [1, 16]])
            self.DMA("sync", dst, src, r=[rpb], w=[BTD], slow=True)
            dst = bass.AP(BTD.ap.tensor, do, [[64, 15], [960, 8], [1, 16]])
            src = bass.AP(rpb.ap.tensor, so + 15, [[31, 15], [-1, 8], [1, 16]])
            self.DMA("sync", dst, src, r=[rpb], w=[BTD], slow=True)
            dst = bass.AP(BTD.ap.tensor, do + 57 * 960 + 48, [[64, 15], [960, 7], [1, 16]])
            src = bass.AP(rpb.ap.tensor, so + 6, [[31, 15], [-1, 7], [1, 16]])
            self.DMA("sync", dst, src, r=[rpb], w=[BTD], slow=True)

    def na_phase_a(self, HT, wqkv, QT, KT, VT):
        self.new_phase()
        Hres = self.bf16(8 * T, "Hres", shape=(8, T))
        for kt in range(8):
            self.DMA("sync", Hres.ap[:, kt, :], HT.ap[kt * 128:(kt + 1) * 128, :], r=[HT], w=[Hres])
        wv = self.bf16(8 * 1024, "wv", shape=(8, 1024))
        self.load_w_bf16(wv, wqkv.ap[0], wqkv, 8, 2048, 3072)
        vb = [self.bf16(1024, "vb%d" % i) for i in range(2)]
        for tt in range(T // 128):
            v_ = vb[tt % 2]
            for vc in range(2):
                ps = self.PS[vc + 2 * (tt % 2)]
                for kt in range(8):
                    self.mm(ps.ap, Hres.ap[:, kt, tt * 128:(tt + 1) * 128], wv.ap[:, kt, vc * 512:(vc + 1) * 512], kt == 0, kt == 7, r=[Hres, wv], w=[ps])
                self.cp("scalar" if vc else "vector", v_.ap[:, vc * 512:(vc + 1) * 512], ps.ap, r=[ps], w=[v_])
            self.DMA("sync", VT.ap[tt * 128:(tt + 1) * 128, :], v_.ap, r=[v_], w=[VT])
        wch = [self.bf16(8 * 512, "wq%d" % i, shape=(8, 512)) for i in range(2)]
        ob = [self.bf16(T, "qo%d" % i) for i in range(2)]
        chunks = self.chunks()
        pi = 0
        for wc in range(4):
            wb = wch[wc % 2]
            self.load_w_bf16(wb, wqkv.ap[0], wqkv, 8, wc * 512, (wc + 1) * 512)
            for q in range(4):
                ot = wc * 4 + q
                obi = ob[ot % 2]
                for ci, (s, w, isctx) in enumerate(chunks):
                    ps = self.PS[4 + pi % 4]
                    pi += 1
                    for kt in range(8):
                        self.mm(ps.ap[:, 0:w], wb.ap[:, kt, q * 128:(q + 1) * 128], Hres.ap[:, kt, s:s + w], kt == 0, kt == 7, r=[wb, Hres], w=[ps])
                    if ot < 8:
                        self.act(obi.ap[:, s:s + w], ps.ap[:, 0:w], AF.Copy, r=[ps], w=[obi], scale=0.125)
                    else:
                        self.cp("vector", obi.ap[:, s:s + w], ps.ap[:, 0:w], r=[ps], w=[obi])
                dst = QT.ap[ot * 128:(ot + 1) * 128, :] if ot < 8 else KT.ap[(ot - 8) * 128:(ot - 7) * 128, :]
                self.DMA("sync", dst, obi.ap, r=[obi], w=[QT if ot < 8 else KT])

    def na_phase_b(self, QT, KT, VT, BTD, YT):
        self.new_phase()
        V, G = "vector", "gpsimd"
        Qp = [self.bf16(T, "Qp%d" % i) for i in range(2)]
        Kp = [self.bf16(T, "Kp%d" % i) for i in range(2)]
        Ve = [self.bf16(34 * 128, "Ve%d" % i, shape=(34, 128)) for i in range(2)]
        Vo = [self.bf16(33 * 128, "Vo%d" % i, shape=(33, 128)) for i in range(2)]
        BTt = [self.f32(960, "BTt%d" % i) for i in range(2)]
        YTp = [self.bf16(T, "YTp%d" % i) for i in range(2)]
        NBLK = 68
        Qbd = [self.bf16(NBLK * 128, "Qbd%d" % i, shape=(NBLK, 128)) for i in range(2)]
        for q_ in Qbd:
            self.memset(G, q_.ap, 0.0, [q_])
        NP = 4
        sc = [self.f32(768, "sc%d" % i) for i in range(NP)]
        pe = [self.bf16(768, "pe%d" % i) for i in range(NP)]
        pn = [self.bf16(768, "pn%d" % i) for i in range(NP)]
        pT = [self.bf16(768, "pT%d" % i, shape=(6, 128)) for i in range(NP)]
        st = [self.f32(8, "st%d" % i) for i in range(NP)]
        blocks = []
        for hp in range(8):
            bl = [("c", i) for i in range(4)] + [("l", r) for r in range(64)]
            for bi, (kind, idx) in enumerate(bl):
                blocks.append((hp, kind, idx, bi == 0, bi == len(bl) - 1))

        def geom(kind, idx):
            if kind == "c":
                return idx * 64, 256, 0, 0
            r = idx
            start = min(max(r - 4, 0), 56)
            return LC + r * 64, 768, start - r + 7, LC + start * 64

        def load_pair(hp):
            h2 = hp % 2
            self.DMA("sync", Qp[h2].ap, QT.ap[hp * 128:(hp + 1) * 128, :], r=[QT], w=[Qp[h2]])
            self.DMA("sync", Kp[h2].ap, KT.ap[hp * 128:(hp + 1) * 128, :], r=[KT], w=[Kp[h2]])
            self.DMA("sync", Ve[h2].ap, VT.ap[:, hp * 128:(hp + 1) * 128].rearrange("(tt p) c -> p tt c", p=128), r=[VT], w=[Ve[h2]])
            self.DMA("sync", Vo[h2].ap, VT.ap[64:64 + 33 * 128, hp * 128:(hp + 1) * 128].rearrange("(tt p) c -> p tt c", p=128), r=[VT], w=[Vo[h2]])
            self.DMA("sync", BTt[h2].ap, BTD.ap[2 * hp:2 * hp + 2].rearrange("two w r c -> (two w) (r c)"), r=[BTD], w=[BTt[h2]])
            qv = Qp[h2].ap.rearrange("p (b c) -> p b c", c=64)
            self.cp(G, Qbd[h2].ap[0:64, :, 0:64], qv[0:64, :, :], r=[Qp[h2]], w=[Qbd[h2]])
            self.cp(G, Qbd[h2].ap[64:128, :, 64:128], qv[64:128, :, :], r=[Qp[h2]], w=[Qbd[h2]])

        def s1(i, late=None):
            hp, kind, idx, pfirst, plast = blocks[i]
            h2 = hp % 2
            if pfirst and late is not True:
                load_pair(hp)
            qpos, nk, ro0, kpos = geom(kind, idx)
            qbt, sci, pei, sti = Qbd[h2], sc[i % NP], pe[i % NP], st[i % NP]
            ps_l, ps_c = self.PS[(i % 2) * 2], self.PS[(i % 2) * 2 + 1]
            qb_ap = qbt.ap[:, qpos // 64, :]
            if late is None or late is False:
                if kind == "l":
                    self.mm(ps_l.ap, qb_ap, Kp[h2].ap[:, kpos:kpos + 512], True, True, r=[qbt, Kp[h2]], w=[ps_l])
                    self.mm(ps_c.ap[:, 0:256], qb_ap, Kp[h2].ap[:, 0:256], True, True, r=[qbt, Kp[h2]], w=[ps_c])
                    self.tt(V, sci.ap[:, 0:512], ps_l.ap, BTt[h2].ap[:, ro0 * 64:ro0 * 64 + 512], ALU.add, r=[ps_l, BTt[h2]], w=[sci])
                    self.cp("scalar", sci.ap[:, 512:768], ps_c.ap[:, 0:256], r=[ps_c], w=[sci])
                else:
                    self.mm(ps_c.ap[:, 0:256], qb_ap, Kp[h2].ap[:, 0:256], True, True, r=[qbt, Kp[h2]], w=[ps_c])
                    self.cp("scalar", sci.ap[:, 0:256], ps_c.ap[:, 0:256], r=[ps_c], w=[sci])
                if late is False:
                    return
            self.E(V, lambda e, o=sti.ap[:, 0:1], i_=sci.ap[:, 0:nk]: e.reduce_max(out=o, in_=i_, axis=AX.X), r=[sci], w=[sti])
            self.ts(V, sti.ap[:, 1:2], sti.ap[:, 0:1], -1.0, ALU.mult, r=[sti], w=[sti])
            self.act(pei.ap[:, 0:nk], sci.ap[:, 0:nk], AF.Exp, r=[sci, sti], w=[pei, sti], bias=sti.ap[:, 1:2], accum=sti.ap[:, 2:3])
            self.E(V, lambda e, o=sti.ap[:, 3:4], i_=sti.ap[:, 2:3]: e.reciprocal(out=o, in_=i_), r=[sti], w=[sti])

        def s2(i, part=None):
            hp, kind, idx, pfirst, plast = blocks[i]
            h2 = hp % 2
            qpos, nk, ro0, kpos = geom(kind, idx)
            pei, pni, pTi, sti = pe[i % NP], pn[i % NP], pT[i % NP], st[i % NP]
            pst, pstb = self.PS[4 + i % 2], self.PSB[4 + i % 2]
            pso = self.PS[6 + i % 2]
            nkt = nk // 128
            if part in (None, 0):
                self.ts(G, pni.ap[:, 0:nk], pei.ap[:, 0:nk], sti.ap[:, 3:4], ALU.mult, 0.0, ALU.add, r=[pei, sti], w=[pni])
                for kt in range(nkt):
                    self.tr(pstb[:, kt * 128:(kt + 1) * 128], pni.ap[:, kt * 128:(kt + 1) * 128], self.identb.ap, r=[pni, self.identb], w=[pst])
                self.cp("scalar", pTi.ap[:, 0:nkt, :], pstb[:, 0:nk].rearrange("p (k c) -> p k c", c=128), r=[pst], w=[pTi])
                if part == 0:
                    return
            for kt in range(nkt):
                if kind == "c":
                    vt = Ve[h2].ap[:, kt, :]
                elif kt >= 4:
                    vt = Ve[h2].ap[:, kt - 4, :]
                else:
                    tok0 = kpos + kt * 128
                    vt = Ve[h2].ap[:, tok0 // 128, :] if tok0 % 128 == 0 else Vo[h2].ap[:, (tok0 - 64) // 128, :]
                self.mm(pso.ap[:, 0:128], vt, pTi.ap[:, kt, :], kt == 0, kt == nkt - 1, r=[Ve[h2], Vo[h2], pTi], w=[pso])
            self.cp(V, YTp[h2].ap[0:64, qpos:qpos + 64], pso.ap[0:64, 0:64], r=[pso], w=[YTp[h2]])
            self.cp("scalar", YTp[h2].ap[64:128, qpos:qpos + 64], pso.ap[64:128, 64:128], r=[pso], w=[YTp[h2]])
            if plast:
                self.DMA("sync", YT.ap[hp * 128:(hp + 1) * 128, :], YTp[h2].ap, r=[YTp[h2]], w=[YT])

        nb_ = len(blocks)
        for step in range(nb_ + 3):
            a = step
            if a < nb_:
                s1(a, late=False)
            b_ = step - 1
            if 0 <= b_ < nb_:
                s1(b_, late=True)
            c_ = step - 2
            if 0 <= c_ < nb_:
                s2(c_, part=0)
            d_ = step - 3
            if 0 <= d_ < nb_:
                s2(d_, part=1)


def _build(cfg):
    nc = bass.Bass("TRN2", target_bir_lowering=False)
    kb = KB(nc, cfg)
    IN = lambda n, s: kb.dram_t(n, s, F32, kind="ExternalInput")
    x_in = IN("x", [LL, D])
    ctx_in = IN("ctx", [LC, D])
    c_in = IN("c", [D])
    cctx_in = IN("c_ctx", [D])
    ada_w = IN("ada_w", [DEPTH, D, 6 * D])
    ada_b = IN("ada_b", [DEPTH, 6 * D])
    norm_mix = IN("norm_mix", [DEPTH, D])
    norm_ffn = IN("norm_ffn", [DEPTH, D])
    norm_final = IN("norm_final", [D])
    w1 = IN("ffn_w1", [DEPTH, D, FH])
    w3 = IN("ffn_w3", [DEPTH, D, FH])
    w2 = IN("ffn_w2", [DEPTH, FH, D])
    S5 = {
        "lam_re": IN("s5_lam_re", [2, 2, 64, 64]), "lam_im": IN("s5_lam_im", [2, 2, 64, 64]),
        "log_step": IN("s5_log_step", [2, 2, 64]),
        "b_re": IN("s5_b_re", [2, 2, 64, 64, 16]), "b_im": IN("s5_b_im", [2, 2, 64, 64, 16]),
        "c_re": IN("s5_c_re", [2, 2, 64, 16, 64]), "c_im": IN("s5_c_im", [2, 2, 64, 16, 64]),
        "d": IN("s5_d", [2, D]), "w_glu": IN("s5_w_glu", [2, D, D]), "b_glu": IN("s5_b_glu", [2, D]),
    }
    SSD = {
        "w_in": IN("ssd_w_in", [1, D, 8256]), "conv_w": IN("ssd_conv_w", [1, 5, 6144]), "conv_b": IN("ssd_conv_b", [1, 6144]),
        "dt_bias": IN("ssd_dt_bias", [1, 2, 32]), "a_log": IN("ssd_a_log", [1, 2, 32]), "d": IN("ssd_d", [1, 32]),
        "norm": IN("ssd_norm", [1, 2048]), "w_out": IN("ssd_w_out", [1, 2048, D]),
    }
    NA = {"w_qkv": IN("na_w_qkv", [1, D, 3 * D]), "w_o": IN("na_w_o", [1, D, D]), "rpb": IN("na_rpb", [1, 16, 15, 31])}
    out_t = kb.dram_t("out", [LL, D], F32, kind="ExternalOutput")
    QT = kb.dram_t("QT", [D, T], BF16)
    KT = kb.dram_t("KT", [D, T], BF16)
    VT = kb.dram_t("VT", [T, D], BF16)
    YT = kb.dram_t("YT", [D, T], BF16)
    BTD = kb.dram_t("BTD", [16, 64, 15, 64], F32)
    SZ = kb.dram_t("SZ", [T, 2048], BF16)
    XS = kb.dram_t("XS", [2048, T], BF16)
    BC = kb.dram_t("BC", [4096, T], BF16)
    DTR = kb.dram_t("DTR", [64, T], F32)
    YF = kb.dram_t("YF", [T, 2048], BF16)
    YB = kb.dram_t("YB", [T, 2048], BF16)
    XT = kb.dram_t("XT", [D, T], F32)
    HT = kb.dram_t("HT", [D, T], BF16)
    GT = kb.dram_t("GT", [D, T], BF16)
    layers = cfg.get("layers", list(range(DEPTH)))
    kb.consts()
    kb.build_mask8()
    kb.adaln(c_in, cctx_in, ada_w, ada_b, norm_mix, norm_ffn, norm_final, layers)
    kb.prologue_transpose(x_in, ctx_in, XT)
    for l in layers:
        kind, j = l % 3, l // 3
        if cfg.get("mixer", True):
            kb.norm_layer(XT, HT, l, 0)
            if kind == 0:
                kb.s5_phase(HT, GT, S5, j)
                kb.proj_phase(XT, GT, S5["w_glu"], S5["w_glu"].ap[j], 8, l, glu_bias=(S5["b_glu"], S5["b_glu"].ap[j]))
            elif kind == 1:
                kb.ssd_phase_a(HT, SSD, SZ, XS, BC, DTR)
                kb.ssd_phase_b(SSD, XS, BC, DTR, YF, YB)
                kb.ssd_phase_c(XT, SSD, SZ, YF, YB, l)
            else:
                kb.na_bias_table(NA["rpb"], BTD)
                kb.na_phase_a(HT, NA["w_qkv"], QT, KT, VT)
                kb.na_phase_b(QT, KT, VT, BTD, YT)
                kb.proj_phase(XT, YT, NA["w_o"], NA["w_o"].ap[0], 8, l)
        if cfg.get("ffn", True):
            kb.ffn_phase(XT, HT, w1, w3, w2, l)
    kb.final_phase(XT, out_t)
    kb.p.fence("sync", kb.out_ops)
    kb.p.build()
    kb.es.close()
    return nc, kb


INPUT_NAMES = ["x", "ctx", "c", "c_ctx", "ada_w", "ada_b", "norm_mix", "norm_ffn", "norm_final", "ffn_w1", "ffn_w3", "ffn_w2",
               "s5_lam_re", "s5_lam_im", "s5_log_step", "s5_b_re", "s5_b_im", "s5_c_re", "s5_c_im", "s5_d", "s5_w_glu", "s5_b_glu",
               "ssd_w_in", "ssd_conv_w", "ssd_conv_b", "ssd_dt_bias", "ssd_a_log", "ssd_d", "ssd_norm", "ssd_w_out",
               "na_w_qkv", "na_w_o", "na_rpb"]


def kernel(**inputs):
    cfg = {}
    nc, kb = _build(cfg)
    n = 8
    in_maps = []
    for b in range(n):
        m = {}
        for k in INPUT_NAMES:
            v = np.ascontiguousarray(inputs[k], dtype=np.float32)
            if k in ("x", "ctx", "c"):
                v = np.ascontiguousarray(v[b])
            m[k] = v
        in_maps.append(m)
    res = run_bass_kernel_spmd(nc, in_maps, core_ids=list(range(n)))
    return np.stack([np.asarray(r["out"], dtype=np.float32) for r in res.results], axis=0)
```

```python
import numpy as np
from contextlib import ExitStack
import concourse.bass as bass
import concourse.mybir as mybir
from concourse.bass_utils import run_bass_kernel_spmd

F32 = mybir.dt.float32
BF16 = mybir.dt.bfloat16
I32 = mybir.dt.int32
AF = mybir.ActivationFunctionType
ALU = mybir.AluOpType
AX = mybir.AxisListType

ENGS = ("sync", "gpsimd", "scalar", "vector", "tensor")
NDMA_SEM = 24
SEM_EPOCH = 12000

D = 1024
LC = 256
LL = 4096
T = LC + LL
NFT = 8
FH = 2816
NHT = 22
DEPTH = 4
EPS = 1e-6
ARENA_F32 = 46 * 1024


class Buf:
    __slots__ = ("name", "w", "r", "psum")

    def __init__(self, name, psum=False):
        self.name = name
        self.w = None
        self.r = []
        self.psum = psum


class Op:
    __slots__ = ("eng", "fn", "idx", "deps", "need_inc", "val", "is_dma", "semi", "is_barrier", "bg")

    def __init__(self, eng, fn, is_dma):
        self.eng = eng
        self.fn = fn
        self.is_dma = is_dma
        self.deps = []
        self.need_inc = False
        self.val = 0
        self.semi = -1
        self.idx = -1
        self.is_barrier = False
        self.bg = False


class Prog:
    def __init__(self, nc):
        self.nc = nc
        self.ops = {e: [] for e in ENGS}
        self.nreal = {e: 0 for e in ENGS}
        self.es = ExitStack()
        self.engsem = {e: self.es.enter_context(nc.semaphore("s_" + e)) for e in ENGS}
        self.dmasem = [self.es.enter_context(nc.semaphore("d%d" % i)) for i in range(NDMA_SEM)]
        self.dma_last = [None] * NDMA_SEM
        self.dma_cnt = [0] * NDMA_SEM
        self.dma_rr = 0

    def emit(self, eng, fn, reads=(), writes=(), dma=False):
        op = Op(eng, fn, dma)
        op.idx = self.nreal[eng]
        self.nreal[eng] += 1
        deps = []
        for b in reads:
            if b.w is not None:
                deps.append(b.w)
            if b.psum:
                deps.extend(b.r)
        for b in writes:
            if b.w is not None:
                deps.append(b.w)
            deps.extend(b.r)
        if dma:
            k = self.dma_rr
            self.dma_rr = (k + 1) % NDMA_SEM
            if self.dma_last[k] is not None:
                deps.append(self.dma_last[k])
            self.dma_last[k] = op
            self.dma_cnt[k] += 16
            op.semi = k
            op.val = self.dma_cnt[k]
        seen = set()
        for d in deps:
            if d is op or id(d) in seen:
                continue
            seen.add(id(d))
            op.deps.append(d)
        for b in reads:
            if b.psum:
                b.w = op
                b.r = []
            else:
                b.r.append(op)
        for b in writes:
            b.w = op
            b.r = []
        self.ops[eng].append(op)
        return op

    def fence(self, eng, deps):
        op = Op(eng, None, False)
        op.idx = self.nreal[eng]
        op.deps = list(deps)
        self.ops[eng].append(op)
        return op

    def barrier(self):
        lasts = []
        for e in ENGS:
            for o in reversed(self.ops[e]):
                if o.fn is not None and not o.bg:
                    lasts.append(o)
                    break
        for o in self.dma_last:
            if o is not None and not o.bg:
                lasts.append(o)
        for e in ENGS:
            self.fence(e, lasts).is_barrier = True

    def _needs_wait(self, op, d):
        if d.is_dma:
            return True
        if d.eng != op.eng:
            return True
        if op.eng == "tensor":
            return False
        return (op.idx - d.idx) <= 2

    def build(self):
        nc = self.nc
        for e in ENGS:
            for op in self.ops[e]:
                for d in op.deps:
                    if not d.is_dma and self._needs_wait(op, d):
                        d.need_inc = True
        self.epoch_sems = {e: [self.engsem[e]] for e in ENGS}
        for e in ENGS:
            c = 0
            ep = 0
            for op in self.ops[e]:
                if op.fn is None:
                    if op.is_barrier and c > SEM_EPOCH:
                        ep += 1
                        c = 0
                        self.epoch_sems[e].append(self.es.enter_context(nc.semaphore("s_%s_%d" % (e, ep))))
                    continue
                if op.is_dma:
                    continue
                op.semi = ep
                if op.need_inc:
                    c += 1
                    op.val = c
        self.counts = {e: 0 for e in ENGS}

        def mk_body(e):
            def body(eng):
                waited = {}
                for op in self.ops[e]:
                    for d in op.deps:
                        if not self._needs_wait(op, d):
                            continue
                        if d.is_dma:
                            key = ("d", d.semi)
                            sem = self.dmasem[d.semi]
                        else:
                            key = ("e", d.eng, d.semi)
                            sem = self.epoch_sems[d.eng][d.semi]
                        if waited.get(key, 0) >= d.val:
                            continue
                        waited[key] = d.val
                        eng.wait_ge(sem, d.val)
                    if op.fn is None:
                        continue
                    inst = op.fn(eng)
                    self.counts[e] += 1
                    if op.is_dma:
                        inst.then_inc(self.dmasem[op.semi], 16)
                    elif op.need_inc:
                        inst.then_inc(self.epoch_sems[e][op.semi], 1)
            return body

        with nc.Block() as block:
            for e in ENGS:
                if self.ops[e]:
                    getattr(block, e)(mk_body(e))
        self.es.close()


class Tl:
    __slots__ = ("ap", "buf")

    def __init__(self, ap, buf):
        self.ap = ap
        self.buf = buf


class KB:
    def __init__(self, nc, cfg):
        self.nc = nc
        self.cfg = cfg
        self.p = Prog(nc)
        self.es = ExitStack()
        self.arena = self.es.enter_context(nc.sbuf_tensor("arena", [128, ARENA_F32], F32))
        self.arena_bf = self.arena.bitcast(BF16)
        self.psum = [self.es.enter_context(nc.psum_tensor("ps%d" % i, [128, 512], F32)) for i in range(8)]
        self.PS = [Tl(self.psum[i][:], Buf("ps%d" % i, psum=True)) for i in range(8)]
        self.PSB = [self.psum[i].bitcast(BF16) for i in range(8)]
        self.top = 0
        self.ptr = 0
        self.nb = 0
        self.dram = {}
        self.out_ops = []

    def _al(self, n_f32, persistent):
        n_f32 = (n_f32 + 7) // 8 * 8
        if persistent:
            assert self.ptr == self.top, "persistent alloc only between phases"
            off = self.top
            self.top += n_f32
            self.ptr = self.top
        else:
            off = self.ptr
            self.ptr += n_f32
        assert self.ptr <= ARENA_F32, "arena overflow %d" % self.ptr
        return off

    def f32(self, n, name=None, persistent=False, shape=None):
        off = self._al(n, persistent)
        ap = self.arena[:, off:off + n]
        if shape is not None:
            ap = self._reshape(ap, shape)
        self.nb += 1
        return Tl(ap, Buf(name or "t%d" % self.nb))

    def bf16(self, n, name=None, persistent=False, shape=None):
        off = self._al((n + 1) // 2, persistent)
        ap = self.arena_bf[:, 2 * off:2 * off + n]
        if shape is not None:
            ap = self._reshape(ap, shape)
        self.nb += 1
        return Tl(ap, Buf(name or "t%d" % self.nb))

    def i32(self, n, name=None):
        off = self._al(n, False)
        ap = self.arena.bitcast(I32)[:, off:off + n]
        self.nb += 1
        return Tl(ap, Buf(name or "t%d" % self.nb))

    @staticmethod
    def _reshape(ap, shape):
        if len(shape) == 2:
            return ap.rearrange("p (a b) -> p a b", b=shape[1])
        if len(shape) == 3:
            return ap.rearrange("p (a b c) -> p a b c", b=shape[1], c=shape[2])
        raise ValueError

    def new_phase(self):
        self.p.barrier()
        self.ptr = self.top

    def dram_t(self, name, shape, dt, kind="Internal"):
        t = self.nc.dram_tensor(name, shape, dt, kind=kind)
        tl = Tl(t.ap(), Buf(name))
        self.dram[name] = tl
        return tl

    def E(self, eng, fn, r=(), w=()):
        return self.p.emit(eng, fn, [t.buf for t in r], [t.buf for t in w])

    def DMA(self, eng, out_ap, in_ap, r=(), w=(), slow=False):
        if slow:
            fn = lambda e: e.dma_start(out=out_ap, in_=in_ap, allow_slow_non_contiguous=True)
        else:
            fn = lambda e: e.dma_start(out=out_ap, in_=in_ap)
        return self.p.emit(eng, fn, [t.buf for t in r], [t.buf for t in w], dma=True)

    def mm(self, ps_ap, lhsT, rhs, start, stop, r=(), w=()):
        return self.E("tensor", lambda e: e.matmul(ps_ap, lhsT=lhsT, rhs=rhs, start=start, stop=stop), r, w)

    def tr(self, ps_ap, in_ap, ident_ap, r=(), w=()):
        return self.E("tensor", lambda e: e.transpose(out=ps_ap, in_=in_ap, identity=ident_ap), r, w)

    def act(self, out, in_, func, r=(), w=(), scale=None, bias=None, accum=None):
        kw = {}
        if scale is not None:
            kw["scale"] = scale
        if bias is not None:
            kw["bias"] = bias
        if accum is not None:
            kw["accum_out"] = accum
        return self.E("scalar", lambda e: e.activation(out=out, in_=in_, func=func, **kw), r, w)

    def tt(self, eng, out, in0, in1, op, r=(), w=()):
        return self.E(eng, lambda e: e.tensor_tensor(out=out, in0=in0, in1=in1, op=op), r, w)

    def ts(self, eng, out, in0, s1, op0, s2=None, op1=None, r=(), w=(), accum=None):
        if op1 is None:
            return self.E(eng, lambda e: e.tensor_scalar(out=out, in0=in0, scalar1=s1, scalar2=None, op0=op0), r, w)
        if accum is not None:
            return self.E(eng, lambda e: e.tensor_scalar(out=out, in0=in0, scalar1=s1, scalar2=s2, op0=op0, op1=op1, accum_out=accum), r, w)
        return self.E(eng, lambda e: e.tensor_scalar(out=out, in0=in0, scalar1=s1, scalar2=s2, op0=op0, op1=op1), r, w)

    def stt(self, out, in0, scalar, in1, op0, op1, r=(), w=()):
        return self.E("vector", lambda e: e.scalar_tensor_tensor(out=out, in0=in0, scalar=scalar, in1=in1, op0=op0, op1=op1), r, w)

    def cp(self, eng, out, in_, r=(), w=()):
        if eng == "scalar":
            return self.act(out, in_, AF.Copy, r, w)
        return self.E(eng, lambda e: e.tensor_copy(out=out, in_=in_), r, w)

    def memset(self, eng, ap, val, w=()):
        return self.E(eng, lambda e: e.memset(ap, val), (), w)

    def consts(self):
        self.ident = self.f32(128, "ident", True)
        self.identb = self.bf16(128, "identb", True)
        self.ones = self.f32(128, "ones", True)
        self.epsc = self.f32(8, "epsc", True)
        self.memset("gpsimd", self.ident.ap, 0.0, [self.ident])
        idap = self.ident.ap
        self.E("gpsimd", lambda e: e.affine_select(out=idap, in_=idap, pattern=[[-1, 128]], compare_op=ALU.not_equal,
                                                   fill=1.0, base=0, channel_multiplier=1), [self.ident], [self.ident])
        self.cp("vector", self.identb.ap, self.ident.ap, [self.ident], [self.identb])
        self.memset("vector", self.ones.ap, 1.0, [self.ones])
        self.memset("vector", self.epsc.ap[:, 0:1], EPS, [self.epsc])
        self.memset("vector", self.epsc.ap[:, 1:2], 0.0, [self.epsc])
        self.memset("vector", self.epsc.ap[:, 2:3], 1.0, [self.epsc])

    @staticmethod
    def chunks(w_lat=512):
        ch = [(0, LC, True)]
        for s in range(LC, T, w_lat):
            ch.append((s, w_lat, False))
        return ch

    def prologue_transpose(self, x_in, ctx_in, XT):
        self.new_phase()
        xin = [self.f32(4 * D, "xin%d" % i, shape=(4, D)) for i in range(2)]
        stage = [self.f32(NFT * 512, "stg%d" % i, shape=(NFT, 512)) for i in range(2)]
        for ci, (s, w, isctx) in enumerate(self.chunks()):
            xi = xin[ci % 2]
            st = stage[ci % 2]
            ntt = w // 128
            if isctx:
                src = ctx_in.ap.rearrange("(tt p) f -> p tt f", p=128)
            else:
                src = x_in.ap[s - LC:s - LC + w, :].rearrange("(tt p) f -> p tt f", p=128)
            self.DMA("sync", xi.ap[:, 0:ntt, :], src, r=[x_in], w=[xi])
            for ft in range(NFT):
                ps = self.PS[ft]
                for tt in range(ntt):
                    self.tr(ps.ap[:, tt * 128:(tt + 1) * 128], xi.ap[:, tt, ft * 128:(ft + 1) * 128], self.ident.ap,
                            r=[xi, self.ident], w=[ps])
                self.cp("scalar" if ft % 2 else "vector", st.ap[:, ft, 0:w], ps.ap[:, 0:w], r=[ps], w=[st])
            dst = XT.ap[:, s:s + w].rearrange("(ft p) t -> p ft t", p=128)
            self.DMA("sync", dst, st.ap[:, :, 0:w], r=[st], w=[XT])

    def adaln(self, c_in, cctx_in, ada_w, ada_b, norm_mix, norm_ffn, norm_final, layers):
        self.mod = {}
        for l in layers:
            self.mod[l] = self.f32(96, "mod%d" % l, True, shape=(6, 8, 2))
        self.nw = self.f32(9 * 8, "nw", True, shape=(9, 8))
        self.AB = {}
        for l in layers:
            self.AB[l] = self.f32(4 * 16, "AB%d" % l, True, shape=(4, 8, 2))
        self.new_phase()
        sT = self.f32(16, "sT", shape=(8, 2))
        craw = self.f32(16, "craw", shape=(8, 2))
        self.DMA("sync", craw.ap[:, :, 0], c_in.ap.rearrange("(kt p) -> p kt", p=128), r=[c_in], w=[craw], slow=True)
        self.DMA("sync", craw.ap[:, :, 1], cctx_in.ap.rearrange("(kt p) -> p kt", p=128), r=[cctx_in], w=[craw], slow=True)
        self.act(sT.ap, craw.ap, AF.Silu, r=[craw], w=[sT])
        for k, nwt in enumerate([norm_mix, norm_ffn]):
            self.DMA("sync", self.nw.ap[:, 4 * k:4 * k + 4, :], nwt.ap.rearrange("l (ft p) -> p l ft", p=128), r=[nwt], w=[self.nw], slow=True)
        self.DMA("sync", self.nw.ap[:, 8, :], norm_final.ap.rearrange("(ft p) -> p ft", p=128), r=[norm_final], w=[self.nw], slow=True)
        wbuf = [self.f32(8 * 512, "adaw%d" % i, shape=(8, 512)) for i in range(3)]
        bbuf = [self.f32(512, "adab%d" % i) for i in range(3)]
        onesrow = self.ones.ap[0:1, 0:2]
        it = 0
        for l in layers:
            for cj in range(12):
                wb = wbuf[it % 3]
                bb = bbuf[it % 3]
                ps = self.PS[it % 4]
                it += 1
                self.DMA("sync", wb.ap, ada_w.ap[l, :, cj * 512:(cj + 1) * 512].rearrange("(kt p) n -> p kt n", p=128), r=[ada_w], w=[wb])
                self.DMA("sync", bb.ap[0:1, :], ada_b.ap[l:l + 1, cj * 512:(cj + 1) * 512], r=[ada_b], w=[bb])
                for jj in range(4):
                    j = cj * 4 + jj
                    o = ps.ap[:, jj * 2:jj * 2 + 2]
                    for kt in range(8):
                        self.mm(o, wb.ap[:, kt, jj * 128:(jj + 1) * 128], sT.ap[:, kt, :], kt == 0, False, r=[wb, sT], w=[ps])
                    self.mm(o, bb.ap[0:1, jj * 128:(jj + 1) * 128], onesrow, False, True, r=[bb, self.ones], w=[ps])
                m = cj * 4 // 8
                ft0 = (cj * 4) % 8
                self.cp("vector", self.mod[l].ap[:, m, ft0:ft0 + 4, :], ps.ap[:, 0:8].rearrange("p (a b) -> p a b", b=2), r=[ps], w=[self.mod[l]])
        for l in layers:
            for k, (mi, nwi) in enumerate([(1, l), (4, 4 + l)]):
                nwb = self.nw.ap[:, nwi, :].unsqueeze(2).to_broadcast([128, 8, 2])
                self.stt(self.AB[l].ap[:, k, :, :], self.mod[l].ap[:, mi, :, :], 1.0, nwb, ALU.add, ALU.mult, r=[self.mod[l], self.nw], w=[self.AB[l]])

    def norm_phase(self, XT, HT, A_sel, B_sel, deps_r):
        self.new_phase()
        xin = [self.f32(NFT * 512, "nx%d" % i, shape=(NFT, 512)) for i in range(2)]
        sq = [self.f32(NFT * 512, "nsq%d" % i, shape=(NFT, 512)) for i in range(2)]
        hb = [self.bf16(NFT * 512, "nh%d" % i, shape=(NFT, 512)) for i in range(2)]
        rt = [self.f32(512, "nrt%d" % i) for i in range(2)]
        chs = self.chunks()

        def load(ci):
            s, w, isctx = chs[ci]
            xi = xin[ci % 2]
            self.DMA("sync", xi.ap[:, :, 0:w], XT.ap[:, s:s + w].rearrange("(ft p) t -> p ft t", p=128), r=[XT], w=[xi])

        load(0)
        for ci, (s, w, isctx) in enumerate(chs):
            if ci + 1 < len(chs):
                load(ci + 1)
            xi, sqi, hbi, rti = xin[ci % 2], sq[ci % 2], hb[ci % 2], rt[ci % 2]
            ps = self.PS[ci % 2]
            self.act(sqi.ap[:, :, 0:w], xi.ap[:, :, 0:w], AF.Square, r=[xi], w=[sqi])
            for ft in range(NFT):
                self.mm(ps.ap[:, 0:w], self.ones.ap, sqi.ap[:, ft, 0:w], ft == 0, ft == NFT - 1, r=[self.ones, sqi], w=[ps])
            self.act(rti.ap[:, 0:w], ps.ap[:, 0:w], AF.Sqrt, r=[ps, self.epsc], w=[rti], scale=1.0 / D, bias=self.epsc.ap[:, 0:1])
            self.E("vector", lambda e, o=rti.ap[:, 0:w]: e.reciprocal(out=o, in_=o), r=[rti], w=[rti])
            for ft in range(NFT):
                a = A_sel(ft, isctx)
                b = B_sel(ft, isctx)
                self.stt(sqi.ap[:, ft, 0:w], xi.ap[:, ft, 0:w], a, rti.ap[:, 0:w], ALU.mult, ALU.mult, r=[xi, rti] + deps_r, w=[sqi])
                self.act(hbi.ap[:, ft, 0:w], sqi.ap[:, ft, 0:w], AF.Identity, r=[sqi] + deps_r, w=[hbi], bias=b)
            self.DMA("sync", HT.ap[:, s:s + w].rearrange("(ft p) t -> p ft t", p=128), hbi.ap[:, :, 0:w], r=[hbi], w=[HT])

    def norm_layer(self, XT, HT, l, which):
        AB = self.AB[l]
        mod = self.mod[l]
        k = 0 if which == 0 else 1
        smi = 0 if which == 0 else 3
        A_sel = lambda ft, isctx: AB.ap[:, k, ft, (1 if isctx else 0):(1 if isctx else 0) + 1]
        B_sel = lambda ft, isctx: mod.ap[:, smi, ft, (1 if isctx else 0):(1 if isctx else 0) + 1]
        self.norm_phase(XT, HT, A_sel, B_sel, [AB, mod])

    def final_phase(self, XT, out_t):
        self.new_phase()
        xin = [self.f32(NFT * 512, "fx%d" % i, shape=(NFT, 512)) for i in range(2)]
        sq = [self.f32(NFT * 512, "fsq%d" % i, shape=(NFT, 512)) for i in range(2)]
        rt = [self.f32(512, "frt%d" % i) for i in range(2)]
        ob = [self.f32(4 * D, "fo%d" % i, shape=(4, D)) for i in range(2)]
        ci = 0
        lat = [c for c in self.chunks() if not c[2]]

        def loadf(i):
            s, w, _ = lat[i]
            self.DMA("sync", xin[i % 2].ap, XT.ap[:, s:s + w].rearrange("(ft p) t -> p ft t", p=128), r=[XT], w=[xin[i % 2]])

        loadf(0)
        for (s, w, isctx) in lat:
            xi, sqi, rti, obi = xin[ci % 2], sq[ci % 2], rt[ci % 2], ob[ci % 2]
            ps = self.PS[ci % 2]
            ci += 1
            if ci < len(lat):
                loadf(ci)
            self.act(sqi.ap, xi.ap, AF.Square, r=[xi], w=[sqi])
            for ft in range(NFT):
                self.mm(ps.ap, self.ones.ap, sqi.ap[:, ft, :], ft == 0, ft == NFT - 1, r=[self.ones, sqi], w=[ps])
            self.act(rti.ap, ps.ap, AF.Sqrt, r=[ps, self.epsc], w=[rti], scale=1.0 / D, bias=self.epsc.ap[:, 0:1])
            self.E("vector", lambda e, o=rti.ap: e.reciprocal(out=o, in_=o), r=[rti], w=[rti])
            for ft in range(NFT):
                self.stt(sqi.ap[:, ft, :], xi.ap[:, ft, :], self.nw.ap[:, 8, ft:ft + 1], rti.ap, ALU.mult, ALU.mult, r=[xi, rti, self.nw], w=[sqi])
            for tt in range(4):
                for half in range(2):
                    pso = self.PS[2 + (tt * 2 + half) % 6]
                    for q in range(4):
                        ft = half * 4 + q
                        self.tr(pso.ap[:, q * 128:(q + 1) * 128], sqi.ap[:, ft, tt * 128:(tt + 1) * 128], self.ident.ap, r=[sqi, self.ident], w=[pso])
                    self.cp("scalar" if half else "vector", obi.ap[:, tt, half * 512:(half + 1) * 512], pso.ap, r=[pso], w=[obi])
            dst = out_t.ap[s - LC:s - LC + w, :].rearrange("(tt p) f -> p tt f", p=128)
            self.out_ops.append(self.DMA("sync", dst, obi.ap, r=[obi], w=[out_t]))

    def load_w_bf16(self, dst, src_ap, src_tl, nkt, col0=None, col1=None):
        for kt in range(nkt):
            s = src_ap[kt * 128:(kt + 1) * 128, :] if col0 is None else src_ap[kt * 128:(kt + 1) * 128, col0:col1]
            self.DMA("gpsimd", dst.ap[:, kt, :], s, r=[src_tl], w=[dst])

    def ffn_weights(self, w1, w3, w2, l, persistent, bg):
        top0 = self.top
        w1s = self.bf16(8 * FH, "w1s", persistent=persistent, shape=(8, FH))
        w3s = self.bf16(8 * FH, "w3s", persistent=persistent, shape=(8, FH))
        w2s = self.bf16(NHT * D, "w2s", persistent=persistent, shape=(NHT, D))
        hk = {"w1": [], "w3": [], "w2": []}
        for (nm, dst, src, nkt) in (("w1", w1s, w1, 8), ("w3", w3s, w3, 8), ("w2", w2s, w2, NHT)):
            for kt in range(nkt):
                tl = Tl(dst.ap, Buf("%s_%d" % (nm, kt)))
                op = self.DMA("gpsimd", dst.ap[:, kt, :], src.ap[l][kt * 128:(kt + 1) * 128, :], r=[src], w=[tl])
                op.bg = bg
                hk[nm].append(tl)
        return dict(w1s=w1s, w3s=w3s, w2s=w2s, hk=hk, top0=top0)

    def ffn_phase(self, XT, HT, w1, w3, w2, l, wts=None):
        self.new_phase()
        W = 256
        if wts is None:
            wts = self.ffn_weights(w1, w3, w2, l, False, False)
        w1s, w3s, w2s, hk = wts["w1s"], wts["w3s"], wts["w2s"], wts["hk"]
        hb = [self.bf16(NFT * W, "fh%d" % i, shape=(NFT, W)) for i in range(2)]
        xb = [self.f32(NFT * W, "fxx%d" % i, shape=(NFT, W)) for i in range(2)]
        sqb = self.f32(NFT * W, "fsq", shape=(NFT, W))
        rtb = [self.f32(W, "frt%d" % i) for i in range(2)]
        gb = [self.bf16(NHT * W, "fg%d" % i, shape=(NHT, W)) for i in range(1)]
        sl = [self.bf16(W, "fs%d" % i) for i in range(2)]
        mod = self.mod[l]
        AB = self.AB[l]
        ntile = T // W

        def load(ti):
            s = ti * W
            self.DMA("sync", xb[ti % 2].ap, XT.ap[:, s:s + W].rearrange("(ft p) t -> p ft t", p=128), r=[XT], w=[xb[ti % 2]])

        def norm(ti):
            s = ti * W
            cs = 1 if s < LC else 0
            hbi, xbi, rti = hb[ti % 2], xb[ti % 2], rtb[ti % 2]
            ps = self.PS[7]
            self.act(sqb.ap, xbi.ap, AF.Square, r=[xbi], w=[sqb])
            for ft in range(NFT):
                self.mm(ps.ap[:, 0:W], self.ones.ap, sqb.ap[:, ft, :], ft == 0, ft == NFT - 1, r=[self.ones, sqb], w=[ps])
            self.act(rti.ap, ps.ap[:, 0:W], AF.Sqrt, r=[ps, self.epsc], w=[rti], scale=1.0 / D, bias=self.epsc.ap[:, 0:1])
            self.E("vector", lambda e, o=rti.ap: e.reciprocal(out=o, in_=o), r=[rti], w=[rti])
            for ft in range(NFT):
                self.stt(sqb.ap[:, ft, :], xbi.ap[:, ft, :], AB.ap[:, 1, ft, cs:cs + 1], rti.ap, ALU.mult, ALU.mult, r=[xbi, rti, AB], w=[sqb])
                self.act(hbi.ap[:, ft, :], sqb.ap[:, ft, :], AF.Identity, r=[sqb, mod], w=[hbi], bias=mod.ap[:, 3, ft, cs:cs + 1])

        load(0)
        norm(0)
        for ti in range(ntile):
            s = ti * W
            cs = 1 if s < LC else 0
            hbi, xbi, gbi = hb[ti % 2], xb[ti % 2], gb[0]
            if ti + 1 < ntile:
                load(ti + 1)
            for j in range(NHT):
                pa = self.PS[(j % 2) * 2]
                pb = self.PS[(j % 2) * 2 + 1]
                for kt in range(8):
                    self.mm(pa.ap[:, 0:W], w1s.ap[:, kt, j * 128:(j + 1) * 128], hbi.ap[:, kt, :], kt == 0, kt == 7, r=[hk["w1"][kt], hbi], w=[pa])
                for kt in range(8):
                    self.mm(pb.ap[:, 0:W], w3s.ap[:, kt, j * 128:(j + 1) * 128], hbi.ap[:, kt, :], kt == 0, kt == 7, r=[hk["w3"][kt], hbi], w=[pb])
                sli = sl[j % 2]
                self.act(sli.ap, pa.ap[:, 0:W], AF.Silu, r=[pa], w=[sli])
                self.tt("vector", gbi.ap[:, j, :], pb.ap[:, 0:W], sli.ap, ALU.mult, r=[pb, sli], w=[gbi])
                if j == 10 and ti + 1 < ntile:
                    norm(ti + 1)
            for fo in range(NFT):
                po = self.PS[4 + fo % 3]
                for j in range(NHT):
                    self.mm(po.ap[:, 0:W], w2s.ap[:, j, fo * 128:(fo + 1) * 128], gbi.ap[:, j, :], j == 0, j == NHT - 1, r=[hk["w2"][j], gbi], w=[po])
                self.stt(xbi.ap[:, fo, :], po.ap[:, 0:W], mod.ap[:, 5, fo, cs:cs + 1], xbi.ap[:, fo, :], ALU.mult, ALU.add, r=[po, mod, xbi], w=[xbi])
            self.DMA("sync", XT.ap[:, s:s + W].rearrange("(ft p) t -> p ft t", p=128), xbi.ap, r=[xbi], w=[XT])

    def rev_ap(self, ap2d, start, n):
        pstride = ap2d.ap[0][0]
        return bass.AP(ap2d.tensor, ap2d.offset + start + n - 1, [[pstride, 128], [-1, n]])

    def sincos_turns(self, eng, turns, n, osin, ocos, tmp, cast_eng="vector"):
        ti, tf, fr, s2, s4 = tmp["ti"], tmp["tf"], tmp["fr"], tmp["s2"], tmp["s4"]
        sl = lambda t: t.ap[:, 0:n]
        self.cp(cast_eng, sl(ti), sl(turns), r=[turns], w=[ti])
        self.cp(cast_eng, sl(tf), sl(ti), r=[ti], w=[tf])
        self.tt(eng, sl(fr), sl(turns), sl(tf), ALU.subtract, r=[turns, tf], w=[fr])
        self.act(sl(s2), sl(fr), AF.Sin, r=[fr], w=[s2], scale=float(np.pi))
        self.act(sl(s4), sl(fr), AF.Sin, r=[fr], w=[s4], scale=float(np.pi / 2))
        self.tt(eng, sl(s4), sl(s4), sl(s4), ALU.mult, r=[s4], w=[s4])
        self.ts(eng, sl(s4), sl(s4), -4.0, ALU.mult, 2.0, ALU.add, r=[s4], w=[s4])
        self.tt(eng, sl(osin), sl(s2), sl(s4), ALU.mult, r=[s2, s4], w=[osin])
        self.tt(eng, sl(s2), sl(s2), sl(s2), ALU.mult, r=[s2], w=[s2])
        self.ts(eng, sl(ocos), sl(s2), -2.0, ALU.mult, 1.0, ALU.add, r=[s2], w=[ocos])

    def build_mask8(self):
        self.mask8 = self.f32(8, "mask8", True)
        m = self.mask8.ap
        self.memset("gpsimd", m, 1.0, [self.mask8])
        self.E("gpsimd", lambda e: e.affine_select(out=m, in_=m, pattern=[[-16, 8]], compare_op=ALU.is_ge, fill=0.0, base=0, channel_multiplier=1), [self.mask8], [self.mask8])
        self.E("gpsimd", lambda e: e.affine_select(out=m, in_=m, pattern=[[16, 8]], compare_op=ALU.is_ge, fill=0.0, base=15, channel_multiplier=-1), [self.mask8], [self.mask8])

    def s5_phase(self, HT, GT, P, j):
        self.new_phase()
        V, G = "vector", "gpsimd"
        def sc_tile(nm):
            return self.f32(64, nm)
        lr, li, ls = sc_tile("lr"), sc_tile("li"), sc_tile("ls")
        for d in range(2):
            self.DMA("sync", lr.ap[:, d * 32:(d + 1) * 32], P["lam_re"].ap[j, d].rearrange("(p two) n -> (two n) p", two=2), r=[P["lam_re"]], w=[lr], slow=True)
            self.DMA("sync", li.ap[:, d * 32:(d + 1) * 32], P["lam_im"].ap[j, d].rearrange("(p two) n -> (two n) p", two=2), r=[P["lam_im"]], w=[li], slow=True)
        lsrow = self.f32(128, "lsrow")
        self.DMA("sync", lsrow.ap[0:1, :], P["log_step"].ap[j:j + 1].rearrange("o d g -> o (d g)"), r=[P["log_step"]], w=[lsrow])
        psb = self.PS[0]
        self.mm(psb.ap[:, 0:128], self.ones.ap[0:1, :], lsrow.ap[0:1, :], True, True, r=[self.ones, lsrow], w=[psb])
        for d in range(2):
            src = psb.ap[:, d * 64:(d + 1) * 64].rearrange("q (p two) -> q p two", two=2)
            self.cp(V, ls.ap[0:64, d * 32:(d + 1) * 32], src[0:64, :, 0], r=[psb], w=[ls])
            self.cp(V, ls.ap[64:128, d * 32:(d + 1) * 32], src[64:128, :, 1], r=[psb], w=[ls])
        step, zr, zi, rr, tq = sc_tile("step"), sc_tile("zr"), sc_tile("zi"), sc_tile("rr"), sc_tile("tq")
        tmp = {"ti": self.i32(512, "ti"), "tf": self.f32(512, "tf"), "fr": self.f32(512, "fr"), "s2": self.f32(512, "s2"), "s4": self.f32(512, "s4")}
        tmp2 = {"ti": self.i32(512, "ti2"), "tf": self.f32(512, "tf2"), "fr": self.f32(512, "fr2"), "s2": self.f32(512, "s22"), "s4": self.f32(512, "s42")}
        sphi, cphi, frac = sc_tile("sphi"), sc_tile("cphi"), sc_tile("frac")
        self.act(step.ap, ls.ap, AF.Exp, r=[ls], w=[step])
        self.tt(V, zr.ap, lr.ap, step.ap, ALU.mult, r=[lr, step], w=[zr])
        self.tt(V, zi.ap, li.ap, step.ap, ALU.mult, r=[li, step], w=[zi])
        self.act(rr.ap, zr.ap, AF.Exp, r=[zr], w=[rr])
        self.ts(V, tq.ap, zi.ap, float(1.0 / (2 * np.pi)), ALU.mult, r=[zi], w=[tq])
        self.sincos_turns(V, tq, 64, sphi, cphi, tmp)
        self.cp(V, frac.ap, tmp["fr"].ap[:, 0:64], r=[tmp["fr"]], w=[frac])
        carry = {}
        for Q in (256, 512):
            tQ, sQ, cQ = sc_tile("tQ%d" % Q), sc_tile("sQ%d" % Q), sc_tile("cQ%d" % Q)
            self.ts(V, tQ.ap, frac.ap, float(Q), ALU.mult, r=[frac], w=[tQ])
            self.sincos_turns(V, tQ, 64, sQ, cQ, tmp)
            carry[Q] = (sQ, cQ)
        ar, ai, den, u, cr, ci, t1s, t2s = [sc_tile(n) for n in ("ar", "ai", "den", "u", "cr", "ci", "t1s", "t2s")]
        self.tt(V, ar.ap, rr.ap, cphi.ap, ALU.mult, r=[rr, cphi], w=[ar])
        self.tt(V, ai.ap, rr.ap, sphi.ap, ALU.mult, r=[rr, sphi], w=[ai])
        self.tt(V, t1s.ap, lr.ap, lr.ap, ALU.mult, r=[lr], w=[t1s])
        self.tt(V, t2s.ap, li.ap, li.ap, ALU.mult, r=[li], w=[t2s])
        self.tt(V, den.ap, t1s.ap, t2s.ap, ALU.add, r=[t1s, t2s], w=[den])
        self.E(V, lambda e: e.reciprocal(out=den.ap, in_=den.ap), r=[den], w=[den])
        self.ts(V, u.ap, ar.ap, -1.0, ALU.add, r=[ar], w=[u])
        self.tt(V, t1s.ap, u.ap, lr.ap, ALU.mult, r=[u, lr], w=[t1s])
        self.tt(V, t2s.ap, ai.ap, li.ap, ALU.mult, r=[ai, li], w=[t2s])
        self.tt(V, t1s.ap, t1s.ap, t2s.ap, ALU.add, r=[t1s, t2s], w=[t1s])
        self.tt(V, cr.ap, t1s.ap, den.ap, ALU.mult, r=[t1s, den], w=[cr])
        self.tt(V, t1s.ap, ai.ap, lr.ap, ALU.mult, r=[ai, lr], w=[t1s])
        self.tt(V, t2s.ap, u.ap, li.ap, ALU.mult, r=[u, li], w=[t2s])
        self.tt(V, t1s.ap, t1s.ap, t2s.ap, ALU.subtract, r=[t1s, t2s], w=[t1s])
        self.tt(V, ci.ap, t1s.ap, den.ap, ALU.mult, r=[t1s, den], w=[ci])
        braw = {}
        for nm in ("b_re", "b_im"):
            braw[nm] = self.f32(2 * 32 * 16, "braw_" + nm, shape=(2, 32, 16))
            for d in range(2):
                self.DMA("sync", braw[nm].ap[:, d, :, :], P[nm].ap[j, d].rearrange("(p two) n h -> (two n) p h", two=2), r=[P[nm]], w=[braw[nm]], slow=True)
        dsk = self.f32(8, "dsk")
        self.DMA("sync", dsk.ap, P["d"].ap[j].rearrange("(ft p) -> p ft", p=128), r=[P["d"]], w=[dsk], slow=True)
        M1 = {}
        for k in range(4):
            for nm in ("b_re", "b_im"):
                M1[(k, nm)] = self.f32(128, "M1_%d%s" % (k, nm))
                self.memset(G, M1[(k, nm)].ap, 0.0, [M1[(k, nm)]])
        Jrow = self.f32(512, "Jrow")
        self.E(G, lambda e: e.iota(Jrow.ap, pattern=[[1, 512]], base=0, channel_multiplier=0, allow_small_or_imprecise_dtypes=True), (), [Jrow])
        WTS = [self.bf16(8 * 6 * 128, "wts%d" % i, shape=(8, 6, 128)) for i in range(2)]
        craw = [[self.f32(64, "craw%d_%d" % (i, q)) for q in range(2)] for i in range(2)]
        Spair = [self.f32(128, "Spair%d" % i) for i in range(2)]
        U = [self.bf16(T, "U%d" % i) for i in range(2)]
        Yacc = self.f32(T, "Yacc")
        gt = self.bf16(T, "gt")
        cmr, cpr, ncpr = sc_tile("cmr"), sc_tile("cpr"), sc_tile("ncpr")
        self.tt(V, cmr.ap, ci.ap, cr.ap, ALU.subtract, r=[ci, cr], w=[cmr])
        self.tt(V, cpr.ap, ci.ap, cr.ap, ALU.add, r=[ci, cr], w=[cpr])
        self.ts(V, ncpr.ap, cpr.ap, -1.0, ALU.mult, r=[cpr], w=[ncpr])
        lastc = [self.f32(2, "lastc%d" % i) for i in range(2)]
        nsQ = {}
        for Q in (256, 512):
            nsQ[Q] = sc_tile("nsQ%d" % Q)
            self.ts(V, nsQ[Q].ap, carry[Q][0].ap, -1.0, ALU.mult, r=[carry[Q][0]], w=[nsQ[Q]])
        TAB = [{n: self.f32(512, "%s%d" % (n, i)) for n in ("COS", "SIN", "wr", "bma", "apb", "ta", "tb")} for i in range(2)]
        WK = [{n: self.f32(512, "%s%d" % (n, i)) for n in ("k1", "k2", "k3", "pss", "bre", "bim")} for i in range(2)]
        MK = [{n: self.bf16(512, "%s%d" % (n, i)) for n in ("m1", "m2", "m3", "m4")} for i in range(2)]
        init = [self.f32(4, "init%d" % i) for i in range(2)]
        fwd_chunks = [(0, LC)] + [(s, 512) for s in range(LC, T, 512)]
        bwd_chunks = [(0, LC)] + [(T - 512 * (i + 1), 512) for i in range(8)]
        tgc = [0]

        def emit_prep(ft):
            Ui = U[ft % 2]
            self.DMA("sync", Ui.ap, HT.ap[ft * 128:(ft + 1) * 128, :], r=[HT], w=[Ui])
            W = WTS[ft % 2]
            for d in range(2):
                cr_ = craw[d]
                self.DMA("sync", cr_[0].ap, P["c_re"].ap[j, d, ft * 8:(ft + 1) * 8].rearrange("g h n -> (g h) n"), r=[P["c_re"]], w=[cr_[0]])
                self.DMA("sync", cr_[1].ap, P["c_im"].ap[j, d, ft * 8:(ft + 1) * 8].rearrange("g h n -> (g h) n"), r=[P["c_im"]], w=[cr_[1]])
                for k in range(4):
                    p_ = ft * 4 + k
                    wi_ = d * 4 + k
                    c1, c2 = 32 * k, 32 * k + 16
                    for bi, nm in enumerate(("b_re", "b_im")):
                        m1t = M1[(k, nm)]
                        self.cp(G, m1t.ap[0:64, c1:c1 + 16], braw[nm].ap[0:64, d, p_, :], r=[braw[nm]], w=[m1t])
                        self.cp(G, m1t.ap[64:128, c2:c2 + 16], braw[nm].ap[64:128, d, p_, :], r=[braw[nm]], w=[m1t])
                        ps = self.PS[5]
                        self.tr(ps.ap[:, bi * 128:(bi + 1) * 128], m1t.ap, self.ident.ap, r=[m1t, self.ident], w=[ps])
                        self.cp("scalar", W.ap[:, wi_, bi, :], ps.ap[:, bi * 128:(bi + 1) * 128], r=[ps], w=[W])
                    self.tt(G, W.ap[:, wi_, 5, :], W.ap[:, wi_, 0, :], W.ap[:, wi_, 1, :], ALU.add, r=[W], w=[W])
                    for q in range(2):
                        sp = Spair[q]
                        self.ts(G, sp.ap[:, 0:64], cr_[q].ap, self.mask8.ap[:, 2 * k:2 * k + 1], ALU.mult, 0.0, ALU.add, r=[cr_[q], self.mask8], w=[sp])
                        self.ts(G, sp.ap[:, 64:128], cr_[q].ap, self.mask8.ap[:, 2 * k + 1:2 * k + 2], ALU.mult, 0.0, ALU.add, r=[cr_[q], self.mask8], w=[sp])
                        ps = self.PS[5]
                        self.tr(ps.ap[:, 256 + q * 128:256 + (q + 1) * 128], sp.ap, self.ident.ap, r=[sp, self.ident], w=[ps])
                        src = ps.ap[:, 256 + q * 128:256 + (q + 1) * 128]
                        if q == 0:
                            self.cp("scalar", W.ap[:, wi_, 2, :], src, r=[ps], w=[W])
                            self.act(W.ap[:, wi_, 3, :], src, AF.Copy, r=[ps], w=[W], scale=-1.0)
                        else:
                            self.act(W.ap[:, wi_, 4, :], src, AF.Copy, r=[ps], w=[W], scale=-1.0)

        def table_thunks(tab, col):
            th = []
            add = th.append
            MAGIC = 12582912.0
            ti, tf, fr, s2, s4 = tmp2["ti"], tmp2["tf"], tmp2["fr"], tmp2["s2"], tmp2["s4"]
            sc1 = lambda t: t.ap[:, col:col + 1]
            add(lambda: self.act(tab["ta"].ap, Jrow.ap, AF.Identity, r=[Jrow, frac], w=[tab["ta"]], scale=sc1(frac)))
            add(lambda: self.act(tf.ap, tab["ta"].ap, AF.Identity, r=[tab["ta"]], w=[tf], bias=MAGIC))
            add(lambda: self.act(tf.ap, tf.ap, AF.Identity, r=[tf], w=[tf], bias=-MAGIC))
            add(lambda: self.tt(G, fr.ap, tab["ta"].ap, tf.ap, ALU.subtract, r=[tab["ta"], tf], w=[fr]))
            add(lambda: self.act(s2.ap, fr.ap, AF.Sin, r=[fr], w=[s2], scale=float(np.pi)))
            add(lambda: self.act(s4.ap, fr.ap, AF.Sin, r=[fr], w=[s4], scale=float(np.pi / 2)))
            add(lambda: self.act(s4.ap, s4.ap, AF.Square, r=[s4], w=[s4]))
            add(lambda: self.act(s4.ap, s4.ap, AF.Identity, r=[s4], w=[s4], scale=-4.0, bias=2.0))
            add(lambda: self.tt(G, tab["SIN"].ap, s2.ap, s4.ap, ALU.mult, r=[s2, s4], w=[tab["SIN"]]))
            add(lambda: self.act(s2.ap, s2.ap, AF.Square, r=[s2], w=[s2]))
            add(lambda: self.act(tab["COS"].ap, s2.ap, AF.Identity, r=[s2], w=[tab["COS"]], scale=-2.0, bias=1.0))
            for (c1, c2, dst) in ((cr, ci, "wr"), (cmr, ncpr, "bma"), (cpr, cmr, "apb")):
                add(lambda c1=c1: self.act(tab["ta"].ap, tab["COS"].ap, AF.Identity, r=[tab["COS"], c1], w=[tab["ta"]], scale=sc1(c1)))
                add(lambda c2=c2: self.act(tab["tb"].ap, tab["SIN"].ap, AF.Identity, r=[tab["SIN"], c2], w=[tab["tb"]], scale=sc1(c2)))
                add(lambda dst=dst: self.tt(G, tab[dst].ap, tab["ta"].ap, tab["tb"].ap, ALU.add, r=[tab["ta"], tab["tb"]], w=[tab[dst]]))
            return th

        items = []
        pd = 0
        for ft in range(NFT):
            for d in range(2):
                chunks = fwd_chunks if d == 0 else bwd_chunks
                for k in range(4):
                    for cidx, (s, n) in enumerate(chunks):
                        items.append(dict(ft=ft, d=d, k=k, cidx=cidx, s=s, n=n, pd=pd, last=(cidx == len(chunks) - 1),
                                          first_pd=(cidx == 0), first_y=(d == 0 and k == 0),
                                          ft_last=(d == 1 and k == 3 and cidx == len(chunks) - 1)))
                    pd += 1
        pending = []

        def stage_a(i, it_):
            ft, d, k, s, n = it_["ft"], it_["d"], it_["k"], it_["s"], it_["n"]
            if i == 0:
                emit_prep(0)
                for f in table_thunks(TAB[0], 0):
                    f()
            if d == 1 and k == 0 and it_["cidx"] == 0 and ft + 1 < NFT:
                emit_prep(ft + 1)
            if it_["cidx"] == 1 and it_["pd"] + 1 < 64:
                npd = it_["pd"] + 1
                nft, nd, nk = npd // 8, (npd // 4) % 2, npd % 4
                pending.extend(table_thunks(TAB[npd % 2], nd * 32 + nft * 4 + nk))
            if it_["cidx"] >= 1:
                ntake = len(pending) if it_["last"] else min(3, len(pending))
                for _ in range(ntake):
                    pending.pop(0)()
            tab = TAB[it_["pd"] % 2]
            Ui, W, wi_ = U[ft % 2], WTS[ft % 2], d * 4 + k
            wk = WK[i % 2]
            pre, pim, psu = self.PS[0], self.PS[1], self.PS[2]
            urhs = Ui.ap[:, s:s + n] if d == 0 else self.rev_ap(Ui.ap, s, n)
            self.mm(pre.ap[:, 0:n], W.ap[:, wi_, 0, :], urhs, True, True, r=[W, Ui], w=[pre])
            self.mm(pim.ap[:, 0:n], W.ap[:, wi_, 1, :], urhs, True, True, r=[W, Ui], w=[pim])
            self.mm(psu.ap[:, 0:n], W.ap[:, wi_, 5, :], urhs, True, True, r=[W, Ui], w=[psu])
            c_ = lambda t: t.ap[:, 0:n]
            self.cp("scalar", c_(wk["pss"]), psu.ap[:, 0:n], r=[psu], w=[wk["pss"]])
            self.tt(V, c_(wk["k2"]), pre.ap[:, 0:n], c_(tab["bma"]), ALU.mult, r=[pre, tab["bma"]], w=[wk["k2"]])
            self.tt(V, c_(wk["k3"]), pim.ap[:, 0:n], c_(tab["apb"]), ALU.mult, r=[pim, tab["apb"]], w=[wk["k3"]])
            self.tt(G, c_(wk["k1"]), c_(wk["pss"]), c_(tab["wr"]), ALU.mult, r=[wk["pss"], tab["wr"]], w=[wk["k1"]])
            self.tt(G, c_(wk["bre"]), c_(wk["k1"]), c_(wk["k3"]), ALU.subtract, r=[wk["k1"], wk["k3"]], w=[wk["bre"]])
            self.tt(G, c_(wk["bim"]), c_(wk["k1"]), c_(wk["k2"]), ALU.add, r=[wk["k1"], wk["k2"]], w=[wk["bim"]])

        def stage_b(i, it_):
            ft, d, k, s, n, cidx = it_["ft"], it_["d"], it_["k"], it_["s"], it_["n"], it_["cidx"]
            tab = TAB[it_["pd"] % 2]
            col = d * 32 + ft * 4 + k
            Ui, W, wi_ = U[ft % 2], WTS[ft % 2], d * 4 + k
            wk, mk, ini = WK[i % 2], MK[i % 2], init[i % 2]
            py = self.PS[3 + i % 2]
            c_ = lambda t: t.ap[:, 0:n]
            rcol = rr.ap[:, col:col + 1]
            rb = rcol.to_broadcast([128, n])
            if cidx == 0:
                i_re, i_im, ir = 0.0, 0.0, []
            else:
                pini = init[(i - 1) % 2]
                i_re, i_im, ir = pini.ap[:, 0:1], pini.ap[:, 1:2], [pini]
            gre_t, gim_t = self.PS[6], self.PS[7]
            self.E(V, lambda e, o=gre_t.ap[:, 0:n], d1=c_(wk["bre"]), i0=i_re, rb=rb: e.tensor_tensor_scan(out=o, data0=rb, data1=d1, initial=i0, op0=ALU.mult, op1=ALU.add),
                   r=[wk["bre"], rr] + ir, w=[gre_t])
            self.E(V, lambda e, o=gim_t.ap[:, 0:n], d1=c_(wk["bim"]), i0=i_im, rb=rb: e.tensor_tensor_scan(out=o, data0=rb, data1=d1, initial=i0, op0=ALU.mult, op1=ALU.add),
                   r=[wk["bim"], rr] + ir, w=[gim_t])
            if not it_["last"]:
                sQ, cQ = carry[n]
                sq_, cq_, nsq_ = sQ.ap[:, col:col + 1], cQ.ap[:, col:col + 1], nsQ[n].ap[:, col:col + 1]
                lc = lastc[i % 2]
                self.cp(V, lc.ap[:, 0:1], gre_t.ap[:, n - 1:n], r=[gre_t], w=[lc])
                self.cp(V, lc.ap[:, 1:2], gim_t.ap[:, n - 1:n], r=[gim_t], w=[lc])
                gre_l, gim_l = lc.ap[:, 0:1], lc.ap[:, 1:2]
                self.act(ini.ap[:, 2:3], gim_l, AF.Identity, r=[lc, nsQ[n]], w=[ini], scale=nsq_)
                self.act(ini.ap[:, 3:4], gim_l, AF.Identity, r=[lc, cQ], w=[ini], scale=cq_)
                self.act(ini.ap[:, 0:1], gre_l, AF.Identity, r=[lc, cQ, ini], w=[ini], scale=cq_, bias=ini.ap[:, 2:3])
                self.act(ini.ap[:, 1:2], gre_l, AF.Identity, r=[lc, sQ, ini], w=[ini], scale=sq_, bias=ini.ap[:, 3:4])
            self.tt(V, c_(mk["m1"]), gre_t.ap[:, 0:n], c_(tab["COS"]), ALU.mult, r=[gre_t, tab["COS"]], w=[mk["m1"]])
            self.tt(V, c_(mk["m3"]), gre_t.ap[:, 0:n], c_(tab["SIN"]), ALU.mult, r=[gre_t, tab["SIN"]], w=[mk["m3"]])
            self.tt(V, c_(mk["m2"]), gim_t.ap[:, 0:n], c_(tab["SIN"]), ALU.mult, r=[gim_t, tab["SIN"]], w=[mk["m2"]])
            self.tt(V, c_(mk["m4"]), gim_t.ap[:, 0:n], c_(tab["COS"]), ALU.mult, r=[gim_t, tab["COS"]], w=[mk["m4"]])
            for mi, (mn, wsel) in enumerate((("m1", 2), ("m2", 3), ("m3", 4), ("m4", 4))):
                mr = mk[mn].ap[:, 0:n] if d == 0 else self.rev_ap(mk[mn].ap, 0, n)
                self.mm(py.ap[:, 0:n], W.ap[:, wi_, wsel, :], mr, mi == 0, mi == 3, r=[W, mk[mn]], w=[py])
            if it_["first_y"]:
                self.cp("scalar", Yacc.ap[:, s:s + n], py.ap[:, 0:n], r=[py], w=[Yacc])
            else:
                self.tt(V, Yacc.ap[:, s:s + n], py.ap[:, 0:n], Yacc.ap[:, s:s + n], ALU.add, r=[py, Yacc], w=[Yacc])
            if it_["ft_last"]:
                self.stt(Yacc.ap, Ui.ap, dsk.ap[:, ft:ft + 1], Yacc.ap, ALU.mult, ALU.add, r=[Ui, dsk, Yacc], w=[Yacc])
                self.act(gt.ap, Yacc.ap, AF.Gelu_apprx_tanh, r=[Yacc], w=[gt])
                self.DMA("sync", GT.ap[ft * 128:(ft + 1) * 128, :], gt.ap, r=[gt], w=[GT])

        stage_a(0, items[0])
        for i in range(len(items)):
            if i + 1 < len(items):
                stage_a(i + 1, items[i + 1])
            stage_b(i, items[i])

    def proj_phase(self, XT, IN, Wd, w_ap, nkt, l, glu_bias=None, Wt=512):
        self.new_phase()
        ws = self.bf16(nkt * D, "pw", shape=(nkt, D))
        self.load_w_bf16(ws, w_ap, Wd, nkt)
        bg = None
        if glu_bias is not None:
            bg = self.f32(8, "bglu")
            self.DMA("sync", bg.ap, glu_bias[1].rearrange("(ft p) -> p ft", p=128), r=[glu_bias[0]], w=[bg], slow=True)
        ib = [self.bf16(nkt * Wt, "pi%d" % i, shape=(nkt, Wt)) for i in range(2)]
        xb = [self.f32(NFT * Wt, "px%d" % i, shape=(NFT, Wt)) for i in range(2)]
        sg = [self.f32(Wt, "psg%d" % i) for i in range(2)]
        mod = self.mod[l]
        chs = self.chunks(Wt)

        def load(ci):
            s, w, isctx = chs[ci]
            self.DMA("sync", ib[ci % 2].ap[:, :, 0:w], IN.ap[:, s:s + w].rearrange("(kt p) t -> p kt t", p=128), r=[IN], w=[ib[ci % 2]])
            self.DMA("sync", xb[ci % 2].ap[:, :, 0:w], XT.ap[:, s:s + w].rearrange("(ft p) t -> p ft t", p=128), r=[XT], w=[xb[ci % 2]])

        load(0)
        for ci, (s, w, isctx) in enumerate(chs):
            cs = 1 if isctx else 0
            ibi, xbi = ib[ci % 2], xb[ci % 2]
            if ci + 1 < len(chs):
                load(ci + 1)
            for fo in range(NFT):
                po = self.PS[fo % 4]
                for kt in range(nkt):
                    self.mm(po.ap[:, 0:w], ws.ap[:, kt, fo * 128:(fo + 1) * 128], ibi.ap[:, kt, 0:w], kt == 0, kt == nkt - 1, r=[ws, ibi], w=[po])
                gate = mod.ap[:, 2, fo, cs:cs + 1]
                if glu_bias is not None:
                    sgi = sg[fo % 2]
                    self.act(sgi.ap[:, 0:w], po.ap[:, 0:w], AF.Sigmoid, r=[po, bg], w=[sgi], bias=bg.ap[:, fo:fo + 1])
                    self.tt("gpsimd", sgi.ap[:, 0:w], sgi.ap[:, 0:w], ibi.ap[:, fo, 0:w], ALU.mult, r=[sgi, ibi], w=[sgi])
                    self.stt(xbi.ap[:, fo, 0:w], sgi.ap[:, 0:w], gate, xbi.ap[:, fo, 0:w], ALU.mult, ALU.add, r=[sgi, mod, xbi], w=[xbi])
                else:
                    self.stt(xbi.ap[:, fo, 0:w], po.ap[:, 0:w], gate, xbi.ap[:, fo, 0:w], ALU.mult, ALU.add, r=[po, mod, xbi], w=[xbi])
            self.DMA("sync", XT.ap[:, s:s + w].rearrange("(ft p) t -> p ft t", p=128), xbi.ap[:, :, 0:w], r=[xbi], w=[XT])


    def row_bcast(self, dst, row_ap, src_tl, n, rowtmp, func=None, scale=None):
        self.DMA("sync", rowtmp.ap[0:1, 0:n], row_ap, r=[src_tl], w=[rowtmp])
        for c0 in range(0, n, 512):
            w = min(512, n - c0)
            ps = self.PS[7]
            self.mm(ps.ap[:, 0:w], self.ones.ap[0:1, :], rowtmp.ap[0:1, c0:c0 + w], True, True, r=[self.ones, rowtmp], w=[ps])
            if func is None:
                self.cp("vector", dst.ap[:, c0:c0 + w], ps.ap[:, 0:w], r=[ps], w=[dst])
            else:
                self.act(dst.ap[:, c0:c0 + w], ps.ap[:, 0:w], func, r=[ps], w=[dst], scale=scale)

    def tri_mask(self, nm, base, cm, step, op):
        t = self.f32(128, nm)
        self.memset("gpsimd", t.ap, 1.0, [t])
        self.E("gpsimd", lambda e: e.affine_select(out=t.ap, in_=t.ap, pattern=[[step, 128]], compare_op=op, fill=0.0, base=base, channel_multiplier=cm), [t], [t])
        return t

    def ssd_phase_a(self, HT, P, SZ, XS, BC, DTR):
        self.new_phase()
        w_in = P["w_in"]
        Hres = self.bf16(8 * T, "Hres", shape=(8, T))
        for kt in range(8):
            self.DMA("sync", Hres.ap[:, kt, :], HT.ap[kt * 128:(kt + 1) * 128, :], r=[HT], w=[Hres])
        wz = self.bf16(8 * 2048, "wz", shape=(8, 2048))
        self.load_w_bf16(wz, w_in.ap[0], w_in, 8, 0, 2048)
        szb = [self.bf16(2048, "szb%d" % i) for i in range(2)]
        for tt in range(T // 128):
            sb = szb[tt % 2]
            for zc in range(4):
                ps = self.PS[zc]
                for kt in range(8):
                    self.mm(ps.ap, Hres.ap[:, kt, tt * 128:(tt + 1) * 128], wz.ap[:, kt, zc * 512:(zc + 1) * 512], kt == 0, kt == 7, r=[Hres, wz], w=[ps])
                self.act(sb.ap[:, zc * 512:(zc + 1) * 512], ps.ap, AF.Silu, r=[ps], w=[sb])
            self.DMA("sync", SZ.ap[tt * 128:(tt + 1) * 128, :], sb.ap, r=[sb], w=[SZ])
        self.new_phase()
        Hres = self.bf16(8 * T, "Hres2", shape=(8, T))
        for kt in range(8):
            self.DMA("sync", Hres.ap[:, kt, :], HT.ap[kt * 128:(kt + 1) * 128, :], r=[HT], w=[Hres])
        cw = self.f32(5 * 48, "cw", shape=(5, 48))
        cb = self.f32(48, "cb")
        for k in range(5):
            self.DMA("sync", cw.ap[:, k, :], P["conv_w"].ap[0, k].rearrange("(ot p) -> p ot", p=128), r=[P["conv_w"]], w=[cw], slow=True)
        self.DMA("sync", cb.ap, P["conv_b"].ap[0].rearrange("(ot p) -> p ot", p=128), r=[P["conv_b"]], w=[cb], slow=True)
        wch = [self.bf16(8 * 512, "wch%d" % i, shape=(8, 512)) for i in range(2)]
        xp = [self.bf16(T, "xp%d" % i) for i in range(2)]
        ob = [self.bf16(T, "ob%d" % i) for i in range(2)]
        dg = [self.bf16(5 * 128, "dg%d" % i, shape=(5, 128)) for i in range(2)]
        chunks = self.chunks()
        pi = 0
        for wc in range(12):
            wb = wch[wc % 2]
            self.load_w_bf16(wb, w_in.ap[0], w_in, 8, 2048 + wc * 512, 2048 + (wc + 1) * 512)
            for q in range(4):
                ot = wc * 4 + q
                xpi, obi, dgi = xp[ot % 2], ob[ot % 2], dg[ot % 2]
                for k in range(5):
                    self.ts("gpsimd", dgi.ap[:, k, :], self.identb.ap, cw.ap[:, k, ot:ot + 1], ALU.mult, 0.0, ALU.add, r=[self.identb, cw], w=[dgi])
                for ci, (s, w, isctx) in enumerate(chunks):
                    ps = self.PS[pi % 4]
                    pi += 1
                    for kt in range(8):
                        self.mm(ps.ap[:, 0:w], wb.ap[:, kt, q * 128:(q + 1) * 128], Hres.ap[:, kt, s:s + w], kt == 0, kt == 7, r=[wb, Hres], w=[ps])
                    self.cp("vector" if ci % 2 else "scalar", xpi.ap[:, s:s + w], ps.ap[:, 0:w], r=[ps], w=[xpi])
                for ci, (s, w, isctx) in enumerate(chunks):
                    q0, q1 = (0, LC) if isctx else (LC, T)
                    ps = self.PS[4 + ci % 4]
                    for ki, k in enumerate((2, 0, 1, 3, 4)):
                        o = k - 2
                        i0 = max(0, q0 - s - o)
                        i1 = min(w, q1 - s - o)
                        self.mm(ps.ap[:, i0:i1], dgi.ap[:, k, :], xpi.ap[:, s + i0 + o:s + i1 + o], ki == 0, ki == 4, r=[dgi, xpi], w=[ps])
                    self.act(obi.ap[:, s:s + w], ps.ap[:, 0:w], AF.Silu, r=[ps, cb], w=[obi], bias=cb.ap[:, ot:ot + 1])
                dst = XS.ap[ot * 128:(ot + 1) * 128, :] if ot < 16 else BC.ap[(ot - 16) * 128:(ot - 15) * 128, :]
                self.DMA("sync", dst, obi.ap, r=[obi], w=[XS if ot < 16 else BC])
        wdt = self.bf16(8 * 64, "wdt", shape=(8, 64))
        self.load_w_bf16(wdt, w_in.ap[0], w_in, 8, 8192, 8256)
        dtb = self.f32(T, "dtb")
        for ci, (s, w, isctx) in enumerate(chunks):
            ps = self.PS[ci % 4]
            for kt in range(8):
                self.mm(ps.ap[0:64, 0:w], wdt.ap[:, kt, :], Hres.ap[:, kt, s:s + w], kt == 0, kt == 7, r=[wdt, Hres], w=[ps])
            self.cp("vector", dtb.ap[0:64, s:s + w], ps.ap[0:64, 0:w], r=[ps], w=[dtb])
        self.DMA("sync", DTR.ap, dtb.ap[0:64, :], r=[dtb], w=[DTR])

    def ssd_phase_b(self, P, XS, BC, DTR, YF, YB):
        self.new_phase()
        V, G = "vector", "gpsimd"
        GT_ = self.tri_mask("mGT", 0, 1, -1, ALU.is_gt)
        LE_ = self.tri_mask("mLE", 0, -1, 1, ALU.is_ge)
        LT_ = self.tri_mask("mLT", 0, -1, 1, ALU.is_gt)
        GE_ = self.tri_mask("mGE", 0, 1, -1, ALU.is_ge)
        rowtmp = self.f32(64, "rowtmp")
        Arow = [self.f32(32, "Arow%d" % d) for d in range(2)]
        Brow = [self.f32(32, "Brow%d" % d) for d in range(2)]
        Drow = self.f32(32, "Drow")
        for d in range(2):
            self.row_bcast(Arow[d], P["a_log"].ap[0, d:d + 1, :], P["a_log"], 32, rowtmp, func=AF.Exp)
            self.ts(V, Arow[d].ap, Arow[d].ap, -1.0, ALU.mult, r=[Arow[d]], w=[Arow[d]])
            self.row_bcast(Brow[d], P["dt_bias"].ap[0, d:d + 1, :], P["dt_bias"], 32, rowtmp)
        self.row_bcast(Drow, P["d"].ap[0:1, :], P["d"], 32, rowtmp)
        H = self.f32(2048, "Hst", shape=(32, 64))
        Hb = self.bf16(2048, "Hb")
        Hg = [Tl(H.ap, Buf("Hg%d" % g)) for g in range(8)]
        Hbg = [Tl(Hb.ap, Buf("Hbg%d" % g)) for g in range(8)]
        NB = 2
        xsT = [self.bf16(16 * 128, "xsT%d" % i, shape=(16, 128)) for i in range(NB)]
        BTt = [self.bf16(8 * 128, "BT%d" % i, shape=(8, 128)) for i in range(NB)]
        CTt = [self.bf16(8 * 128, "CT%d" % i, shape=(8, 128)) for i in range(NB)]
        dtr = [self.f32(128, "dtr%d" % i) for i in range(NB)]
        ybuf = [self.bf16(2048, "ybuf%d" % i) for i in range(NB)]
        xdt = [self.bf16(2048, "xdt%d" % i, shape=(32, 64)) for i in range(NB)]
        xdte = [self.bf16(2048, "xdte%d" % i, shape=(32, 64)) for i in range(NB)]
        dskt = [self.f32(2048, "dskt%d" % i, shape=(32, 64)) for i in range(NB)]
        Btok = [self.bf16(1024, "Btok%d" % i, shape=(8, 128)) for i in range(NB)]
        dtt = [self.f32(32, "dtt%d" % i) for i in range(NB)]
        adt = [self.f32(32, "adt%d" % i) for i in range(NB)]
        dex = [self.f32(96, "dex%d" % i) for i in range(NB)]
        dtd = [self.f32(32, "dtd%d" % i) for i in range(NB)]
        Xh = [self.f32(512, "Xh%d" % i, shape=(4, 128)) for i in range(2)]
        Ld = [self.bf16(512, "Ld%d" % i, shape=(4, 128)) for i in range(2)]
        Mt = [self.bf16(512, "Mt%d" % i, shape=(4, 128)) for i in range(2)]
        CBm = [self.bf16(128, "CBm%d" % i) for i in range(2)]
        ytmp = [self.f32(256, "ytmp%d" % i, shape=(4, 64)) for i in range(2)]
        htmp = [self.f32(256, "htmp%d" % i, shape=(4, 64)) for i in range(2)]
        nchunk = T // 128
        work = []
        for d in range(2):
            order = list(range(nchunk)) if d == 0 else [1, 0] + list(range(nchunk - 1, 1, -1))
            for oi, c in enumerate(order):
                work.append((d, c, oi == 0))
        cfgd = {0: dict(mX=GT_, mE=LE_, mTE=GT_, mSeg=LE_, mCB=LE_), 1: dict(mX=LT_, mE=GE_, mTE=LT_, mSeg=GE_, mCB=GE_)}

        def prologue(wi):
            d, c, first = work[wi]
            cf = cfgd[d]
            b = wi % NB
            s0 = c * 128
            xi, bi, cti, dri = xsT[b], BTt[b], CTt[b], dtr[b]
            self.DMA("sync", xi.ap, XS.ap[:, s0:s0 + 128].rearrange("(t p) c -> p t c", p=128), r=[XS], w=[xi])
            self.DMA("sync", bi.ap, BC.ap[(d * 2) * 1024:(d * 2 + 1) * 1024, s0:s0 + 128].rearrange("(t p) c -> p t c", p=128), r=[BC], w=[bi])
            self.DMA("sync", cti.ap, BC.ap[(d * 2 + 1) * 1024:(d * 2 + 2) * 1024, s0:s0 + 128].rearrange("(t p) c -> p t c", p=128), r=[BC], w=[cti])
            self.DMA("sync", dri.ap[0:32, :], DTR.ap[d * 32:(d + 1) * 32, s0:s0 + 128], r=[DTR], w=[dri])
            psm = self.PS[6]
            dtt_, adt_, dex_, dtd_ = dtt[b], adt[b], dex[b], dtd[b]
            self.tr(psm.ap[:, 0:32], dri.ap[0:32, :], self.ident.ap[0:32, 0:32], r=[dri, self.ident], w=[psm])
            self.tt(V, dtt_.ap, psm.ap[:, 0:32], Brow[d].ap, ALU.add, r=[psm, Brow[d]], w=[dtt_])
            self.act(dtt_.ap, dtt_.ap, AF.Exp, r=[dtt_], w=[dtt_])
            self.act(dtt_.ap, dtt_.ap, AF.Ln, r=[dtt_, self.epsc], w=[dtt_], bias=self.epsc.ap[:, 2:3])
            self.tt(V, adt_.ap, dtt_.ap, Arow[d].ap, ALU.mult, r=[dtt_, Arow[d]], w=[adt_])
            self.mm(psm.ap[:, 32:64], self.ones.ap, adt_.ap, True, True, r=[self.ones, adt_], w=[psm])
            self.mm(psm.ap[:, 64:96], cf["mTE"].ap, adt_.ap, True, True, r=[cf["mTE"], adt_], w=[psm])
            self.mm(psm.ap[:, 96:128], cf["mE"].ap, adt_.ap, True, True, r=[cf["mE"], adt_], w=[psm])
            self.act(dex_.ap, psm.ap[:, 32:128], AF.Exp, r=[psm], w=[dex_])
            self.tt(V, dtd_.ap, dtt_.ap, dex_.ap[:, 32:64], ALU.mult, r=[dtt_, dex_], w=[dtd_])
            pst = self.PS[7]
            psb16 = self.PSB[7]
            for half in range(2):
                for q in range(8):
                    t_ = half * 8 + q
                    self.tr(psb16[:, q * 128:(q + 1) * 128], xi.ap[:, t_, :], self.identb.ap, r=[xi, self.identb], w=[pst])
                src = psb16[:, 0:1024].rearrange("p (h c) -> p h c", c=64)
                hs = slice(half * 16, (half + 1) * 16)
                bc = lambda col: col.unsqueeze(2).to_broadcast([128, 16, 64])
                self.tt(V, xdt[b].ap[:, hs, :], src, bc(dtt_.ap[:, hs]), ALU.mult, r=[pst, dtt_], w=[xdt[b]])
                self.tt(V, xdte[b].ap[:, hs, :], src, bc(dtd_.ap[:, hs]), ALU.mult, r=[pst, dtd_], w=[xdte[b]])
                if d == 0:
                    self.tt(V, dskt[b].ap[:, hs, :], src, bc(Drow.ap[:, hs]), ALU.mult, r=[pst, Drow], w=[dskt[b]])
            for g in range(8):
                self.tr(psb16[:, g * 128:(g + 1) * 128], bi.ap[:, g, :], self.identb.ap, r=[bi, self.identb], w=[pst])
            self.cp("scalar", Btok[b].ap, psb16[:, 0:1024].rearrange("p (g c) -> p g c", c=128), r=[pst], w=[Btok[b]])

        gi = [0]

        def s1(wi, g):
            d, c, first = work[wi]
            cf = cfgd[d]
            b = wi % NB
            k = (wi * 8 + g) % 2
            xh = Xh[k]
            for hh in range(4):
                h = g * 4 + hh
                self.ts(G, xh.ap[:, hh, :], cf["mX"].ap, adt[b].ap[:, h:h + 1], ALU.mult, 0.0, ALU.add, r=[cf["mX"], adt[b]], w=[xh])
            pcb = self.PS[k]
            self.mm(pcb.ap[:, 0:128], BTt[b].ap[:, g, :], CTt[b].ap[:, g, :], True, True, r=[BTt[b], CTt[b]], w=[pcb])
            psg = self.PS[4 + k]
            for hh in range(4):
                self.mm(psg.ap[:, hh * 128:(hh + 1) * 128], xh.ap[:, hh, :], cf["mSeg"].ap, True, True, r=[xh, cf["mSeg"]], w=[psg])

        def s2(wi, g):
            d, c, first = work[wi]
            cf = cfgd[d]
            k = (wi * 8 + g) % 2
            pcb, psg = self.PS[k], self.PS[4 + k]
            self.act(Ld[k].ap, psg.ap.rearrange("p (h c) -> p h c", c=128), AF.Exp, r=[psg], w=[Ld[k]])
            self.tt(V, CBm[k].ap, pcb.ap[:, 0:128], cf["mCB"].ap, ALU.mult, r=[pcb, cf["mCB"]], w=[CBm[k]])
            self.tt(V, Mt[k].ap, Ld[k].ap, CBm[k].ap.unsqueeze(1).to_broadcast([128, 4, 128]), ALU.mult, r=[Ld[k], CBm[k]], w=[Mt[k]])

        def s3(wi, g):
            d, c, first = work[wi]
            b = wi % NB
            k = (wi * 8 + g) % 2
            py = self.PS[2 + k]
            cti = CTt[b]
            for hh in range(4):
                h = g * 4 + hh
                self.mm(py.ap[:, hh * 64:(hh + 1) * 64], Mt[k].ap[:, hh, :], xdt[b].ap[:, h, :], True, True, r=[Mt[k], xdt[b]], w=[py])
                self.mm(py.ap[:, 256 + hh * 64:256 + (hh + 1) * 64], cti.ap[:, g, :], Hb.ap[:, h * 64:(h + 1) * 64], True, True, r=[cti, Hbg[g]], w=[py])
            gs = slice(g * 4, (g + 1) * 4)
            yt = ytmp[k]
            yb = ybuf[b]
            Eb = dex[b].ap[:, 64 + g * 4:64 + (g + 1) * 4].unsqueeze(2).to_broadcast([128, 4, 64])
            self.tt(V, yt.ap, py.ap[:, 256:512].rearrange("p (h c) -> p h c", c=64), Eb, ALU.mult, r=[py, dex[b]], w=[yt])
            if d == 0:
                self.tt(G, yt.ap, yt.ap, dskt[b].ap[:, gs, :], ALU.add, r=[yt, dskt[b]], w=[yt])
            self.tt(V, yb.ap[:, g * 256:(g + 1) * 256].rearrange("p (h c) -> p h c", c=64), py.ap[:, 0:256].rearrange("p (h c) -> p h c", c=64), yt.ap, ALU.add, r=[py, yt], w=[yb])
            pst2 = self.PS[6]
            self.mm(pst2.ap[:, 256:512], Btok[b].ap[:, g, :], xdte[b].ap[:, gs, :].rearrange("p h c -> p (h c)"), True, True, r=[Btok[b], xdte[b]], w=[pst2])
            ht = htmp[k]
            Db = dex[b].ap[:, g * 4:(g + 1) * 4].unsqueeze(2).to_broadcast([128, 4, 64])
            self.tt(G, ht.ap, H.ap[:, gs, :], Db, ALU.mult, r=[Hg[g], dex[b]], w=[ht])
            self.tt(V, H.ap[:, gs, :], pst2.ap[:, 256:512].rearrange("p (h c) -> p h c", c=64), ht.ap, ALU.add, r=[pst2, ht], w=[Hg[g]])
            self.cp("scalar", Hb.ap[:, g * 256:(g + 1) * 256], H.ap[:, gs, :].rearrange("p h c -> p (h c)"), r=[Hg[g]], w=[Hbg[g]])

        nw = len(work)
        prologue(0)
        s1(0, 0)
        for wi in range(nw):
            d, c, first = work[wi]
            if first:
                self.memset(V, H.ap, 0.0, Hg)
                self.memset(V, Hb.ap, 0.0, Hbg)
            for g in range(8):
                if g == 4 and wi + 1 < nw:
                    prologue(wi + 1)
                if g + 1 < 8:
                    s1(wi, g + 1)
                elif wi + 1 < nw:
                    s1(wi + 1, 0)
                s2(wi, g)
                s3(wi, g)
            YO = YF if d == 0 else YB
            self.DMA("sync", YO.ap[c * 128:c * 128 + 128, :], ybuf[wi % NB].ap, r=[ybuf[wi % NB]], w=[YO])

    def ssd_phase_c(self, XT, P, SZ, YF, YB, l):
        self.new_phase()
        V, G = "vector", "gpsimd"
        wo = self.bf16(16 * D, "wo", shape=(16, D))
        self.load_w_bf16(wo, P["w_out"].ap[0], P["w_out"], 16)
        nrow = self.f32(2048, "nrow")
        rowtmp = self.f32(2048, "rowtmp2")
        self.row_bcast(nrow, P["norm"].ap[0:1, :], P["norm"], 2048, rowtmp)
        yf = [self.bf16(2048, "cyf%d" % i) for i in range(2)]
        ybk = [self.bf16(2048, "cyb%d" % i) for i in range(2)]
        sz = [self.bf16(2048, "csz%d" % i) for i in range(2)]
        yy = [self.f32(2048, "cyy%d" % i) for i in range(2)]
        sqj = self.f32(2048, "csq")
        gnb = [self.bf16(2048, "cgn%d" % i) for i in range(2)]
        ss = [self.f32(8, "css%d" % i) for i in range(2)]
        gnT = [self.bf16(16 * 512, "gnT%d" % i, shape=(16, 512)) for i in range(2)]
        xb = [self.f32(NFT * 512, "cx%d" % i, shape=(NFT, 512)) for i in range(2)]
        mod = self.mod[l]
        ti = 0
        chs = self.chunks()

        def load_tile(tix):
            t0 = tix * 128
            self.DMA("sync", yf[tix % 2].ap, YF.ap[t0:t0 + 128, :], r=[YF], w=[yf[tix % 2]])
            self.DMA("sync", ybk[tix % 2].ap, YB.ap[t0:t0 + 128, :], r=[YB], w=[ybk[tix % 2]])
            self.DMA("sync", sz[tix % 2].ap, SZ.ap[t0:t0 + 128, :], r=[SZ], w=[sz[tix % 2]])

        def load_x(ci):
            s, w, isctx = chs[ci]
            self.DMA("sync", xb[ci % 2].ap[:, :, 0:w], XT.ap[:, s:s + w].rearrange("(ft p) t -> p ft t", p=128), r=[XT], w=[xb[ci % 2]])

        load_tile(0)
        load_x(0)
        for ci, (s, w, isctx) in enumerate(chs):
            cs = 1 if isctx else 0
            gT, xbi = gnT[ci % 2], xb[ci % 2]
            if ci + 1 < len(chs):
                load_x(ci + 1)
            for tt in range(w // 128):
                t0 = s + tt * 128
                a, b, z_, y_, g_, s_ = yf[ti % 2], ybk[ti % 2], sz[ti % 2], yy[ti % 2], gnb[ti % 2], ss[ti % 2]
                ti += 1
                if ti < T // 128:
                    load_tile(ti)
                self.tt(G, y_.ap, a.ap, b.ap, ALU.add, r=[a, b], w=[y_])
                self.tt(V, y_.ap, y_.ap, z_.ap, ALU.mult, r=[y_, z_], w=[y_])
                self.act(sqj.ap, y_.ap, AF.Square, r=[y_], w=[sqj, s_], accum=s_.ap[:, 0:1])
                self.act(s_.ap[:, 1:2], s_.ap[:, 0:1], AF.Sqrt, r=[s_, self.epsc], w=[s_], scale=1.0 / 2048, bias=self.epsc.ap[:, 0:1])
                self.E(V, lambda e, o=s_.ap[:, 2:3], i=s_.ap[:, 1:2]: e.reciprocal(out=o, in_=i), r=[s_], w=[s_])
                self.stt(g_.ap, y_.ap, s_.ap[:, 2:3], nrow.ap, ALU.mult, ALU.mult, r=[y_, s_, nrow], w=[g_])
                for half in range(2):
                    pst = self.PS[4 + (ti * 2 + half) % 4]
                    psb16 = self.PSB[4 + (ti * 2 + half) % 4]
                    for q in range(8):
                        kt = half * 8 + q
                        self.tr(psb16[:, q * 128:(q + 1) * 128], g_.ap[:, kt * 128:(kt + 1) * 128], self.identb.ap, r=[g_, self.identb], w=[pst])
                    self.cp("scalar" if half else "vector", gT.ap[:, half * 8:(half + 1) * 8, tt * 128:(tt + 1) * 128],
                            psb16[:, 0:1024].rearrange("p (k c) -> p k c", c=128), r=[pst], w=[gT])
            for fo in range(NFT):
                po = self.PS[fo % 4]
                for kt in range(16):
                    self.mm(po.ap[:, 0:w], wo.ap[:, kt, fo * 128:(fo + 1) * 128], gT.ap[:, kt, 0:w], kt == 0, kt == 15, r=[wo, gT], w=[po])
                self.stt(xbi.ap[:, fo, 0:w], po.ap[:, 0:w], mod.ap[:, 2, fo, cs:cs + 1], xbi.ap[:, fo, 0:w], ALU.mult, ALU.add, r=[po, mod, xbi], w=[xbi])
            self.DMA("sync", XT.ap[:, s:s + w].rearrange("(ft p) t -> p ft t", p=128), xbi.ap[:, :, 0:w], r=[xbi], w=[XT])


    def na_bias_table(self, rpb, BTD):
        self.new_phase()
        neg = self.f32(7680, "negt")
        self.memset("vector", neg.ap, -30000.0, [neg])
        self.DMA("sync", BTD.ap.rearrange("h w r c -> (h w r c)").rearrange("(p n) -> p n", p=128), neg.ap, r=[neg], w=[BTD])
        HS = 64 * 15 * 64
        for h in range(16):
            so = h * 465
            do = h * HS
            dst = bass.AP(BTD.ap.tensor, do + 8 * 960, [[64, 15], [961, 49], [1, 16]])
            src = bass.AP(rpb.ap.tensor, so + 7, [[31, 15], [0, 49], [1, 16]])
            self.DMA("sync", dst, src, r=[rpb], w=[BTD], slow=True)
            dst = bass.AP(BTD.ap.tensor, do, [[64, 15], [960, 8], [1, 16]])
            src = bass.AP(rpb.ap.tensor, so + 15, [[31, 15], [-1, 8], [1, 16]])
            self.DMA("sync", dst, src, r=[rpb], w=[BTD], slow=True)
            dst = bass.AP(BTD.ap.tensor, do + 57 * 960 + 48, [[64, 15], [960, 7], [1, 16]])
            src = bass.AP(rpb.ap.tensor, so + 6, [[31, 15], [-1, 7], [1, 16]])
            self.DMA("sync", dst, src, r=[rpb], w=[BTD], slow=True)

    def na_phase_a(self, HT, wqkv, QT, KT, VT):
        self.new_phase()
        Hres = self.bf16(8 * T, "Hres", shape=(8, T))
        for kt in range(8):
            self.DMA("sync", Hres.ap[:, kt, :], HT.ap[kt * 128:(kt + 1) * 128, :], r=[HT], w=[Hres])
        wv = self.bf16(8 * 1024, "wv", shape=(8, 1024))
        self.load_w_bf16(wv, wqkv.ap[0], wqkv, 8, 2048, 3072)
        vb = [self.bf16(1024, "vb%d" % i) for i in range(2)]
        for tt in range(T // 128):
            v_ = vb[tt % 2]
            for vc in range(2):
                ps = self.PS[vc + 2 * (tt % 2)]
                for kt in range(8):
                    self.mm(ps.ap, Hres.ap[:, kt, tt * 128:(tt + 1) * 128], wv.ap[:, kt, vc * 512:(vc + 1) * 512], kt == 0, kt == 7, r=[Hres, wv], w=[ps])
                self.cp("scalar" if vc else "vector", v_.ap[:, vc * 512:(vc + 1) * 512], ps.ap, r=[ps], w=[v_])
            self.DMA("sync", VT.ap[tt * 128:(tt + 1) * 128, :], v_.ap, r=[v_], w=[VT])
        wch = [self.bf16(8 * 512, "wq%d" % i, shape=(8, 512)) for i in range(2)]
        ob = [self.bf16(T, "qo%d" % i) for i in range(2)]
        chunks = self.chunks()
        pi = 0
        for wc in range(4):
            wb = wch[wc % 2]
            self.load_w_bf16(wb, wqkv.ap[0], wqkv, 8, wc * 512, (wc + 1) * 512)
            for q in range(4):
                ot = wc * 4 + q
                obi = ob[ot % 2]
                for ci, (s, w, isctx) in enumerate(chunks):
                    ps = self.PS[4 + pi % 4]
                    pi += 1
                    for kt in range(8):
                        self.mm(ps.ap[:, 0:w], wb.ap[:, kt, q * 128:(q + 1) * 128], Hres.ap[:, kt, s:s + w], kt == 0, kt == 7, r=[wb, Hres], w=[ps])
                    if ot < 8:
                        self.act(obi.ap[:, s:s + w], ps.ap[:, 0:w], AF.Copy, r=[ps], w=[obi], scale=0.125)
                    else:
                        self.cp("vector", obi.ap[:, s:s + w], ps.ap[:, 0:w], r=[ps], w=[obi])
                dst = QT.ap[ot * 128:(ot + 1) * 128, :] if ot < 8 else KT.ap[(ot - 8) * 128:(ot - 7) * 128, :]
                self.DMA("sync", dst, obi.ap, r=[obi], w=[QT if ot < 8 else KT])

    def na_phase_b(self, QT, KT, VT, BTD, YT):
        self.new_phase()
        V, G = "vector", "gpsimd"
        Qp = [self.bf16(T, "Qp%d" % i) for i in range(2)]
        Kp = [self.bf16(T, "Kp%d" % i) for i in range(2)]
        Ve = [self.bf16(34 * 128, "Ve%d" % i, shape=(34, 128)) for i in range(2)]
        Vo = [self.bf16(33 * 128, "Vo%d" % i, shape=(33, 128)) for i in range(2)]
        BTt = [self.f32(960, "BTt%d" % i) for i in range(2)]
        YTp = [self.bf16(T, "YTp%d" % i) for i in range(2)]
        NBLK = 68
        Qbd = [self.bf16(NBLK * 128, "Qbd%d" % i, shape=(NBLK, 128)) for i in range(2)]
        for q_ in Qbd:
            self.memset(G, q_.ap, 0.0, [q_])
        NP = 4
        sc = [self.f32(768, "sc%d" % i) for i in range(NP)]
        pe = [self.bf16(768, "pe%d" % i) for i in range(NP)]
        pn = [self.bf16(768, "pn%d" % i) for i in range(NP)]
        pT = [self.bf16(768, "pT%d" % i, shape=(6, 128)) for i in range(NP)]
        st = [self.f32(8, "st%d" % i) for i in range(NP)]
        blocks = []
        for hp in range(8):
            bl = [("c", i) for i in range(4)] + [("l", r) for r in range(64)]
            for bi, (kind, idx) in enumerate(bl):
                blocks.append((hp, kind, idx, bi == 0, bi == len(bl) - 1))

        def geom(kind, idx):
            if kind == "c":
                return idx * 64, 256, 0, 0
            r = idx
            start = min(max(r - 4, 0), 56)
            return LC + r * 64, 768, start - r + 7, LC + start * 64

        def load_pair(hp):
            h2 = hp % 2
            self.DMA("sync", Qp[h2].ap, QT.ap[hp * 128:(hp + 1) * 128, :], r=[QT], w=[Qp[h2]])
            self.DMA("sync", Kp[h2].ap, KT.ap[hp * 128:(hp + 1) * 128, :], r=[KT], w=[Kp[h2]])
            self.DMA("sync", Ve[h2].ap, VT.ap[:, hp * 128:(hp + 1) * 128].rearrange("(tt p) c -> p tt c", p=128), r=[VT], w=[Ve[h2]])
            self.DMA("sync", Vo[h2].ap, VT.ap[64:64 + 33 * 128, hp * 128:(hp + 1) * 128].rearrange("(tt p) c -> p tt c", p=128), r=[VT], w=[Vo[h2]])
            self.DMA("sync", BTt[h2].ap, BTD.ap[2 * hp:2 * hp + 2].rearrange("two w r c -> (two w) (r c)"), r=[BTD], w=[BTt[h2]])
            qv = Qp[h2].ap.rearrange("p (b c) -> p b c", c=64)
            self.cp(G, Qbd[h2].ap[0:64, :, 0:64], qv[0:64, :, :], r=[Qp[h2]], w=[Qbd[h2]])
            self.cp(G, Qbd[h2].ap[64:128, :, 64:128], qv[64:128, :, :], r=[Qp[h2]], w=[Qbd[h2]])

        def s1(i, late=None):
            hp, kind, idx, pfirst, plast = blocks[i]
            h2 = hp % 2
            if pfirst and late is not True:
                load_pair(hp)
            qpos, nk, ro0, kpos = geom(kind, idx)
            qbt, sci, pei, sti = Qbd[h2], sc[i % NP], pe[i % NP], st[i % NP]
            ps_l, ps_c = self.PS[(i % 2) * 2], self.PS[(i % 2) * 2 + 1]
            qb_ap = qbt.ap[:, qpos // 64, :]
            if late is None or late is False:
                if kind == "l":
                    self.mm(ps_l.ap, qb_ap, Kp[h2].ap[:, kpos:kpos + 512], True, True, r=[qbt, Kp[h2]], w=[ps_l])
                    self.mm(ps_c.ap[:, 0:256], qb_ap, Kp[h2].ap[:, 0:256], True, True, r=[qbt, Kp[h2]], w=[ps_c])
                    self.tt(V, sci.ap[:, 0:512], ps_l.ap, BTt[h2].ap[:, ro0 * 64:ro0 * 64 + 512], ALU.add, r=[ps_l, BTt[h2]], w=[sci])
                    self.cp("scalar", sci.ap[:, 512:768], ps_c.ap[:, 0:256], r=[ps_c], w=[sci])
                else:
                    self.mm(ps_c.ap[:, 0:256], qb_ap, Kp[h2].ap[:, 0:256], True, True, r=[qbt, Kp[h2]], w=[ps_c])
                    self.cp("scalar", sci.ap[:, 0:256], ps_c.ap[:, 0:256], r=[ps_c], w=[sci])
                if late is False:
                    return
            self.E(V, lambda e, o=sti.ap[:, 0:1], i_=sci.ap[:, 0:nk]: e.reduce_max(out=o, in_=i_, axis=AX.X), r=[sci], w=[sti])
            self.ts(V, sti.ap[:, 1:2], sti.ap[:, 0:1], -1.0, ALU.mult, r=[sti], w=[sti])
            self.act(pei.ap[:, 0:nk], sci.ap[:, 0:nk], AF.Exp, r=[sci, sti], w=[pei, sti], bias=sti.ap[:, 1:2], accum=sti.ap[:, 2:3])
            self.E(V, lambda e, o=sti.ap[:, 3:4], i_=sti.ap[:, 2:3]: e.reciprocal(out=o, in_=i_), r=[sti], w=[sti])

        def s2(i, part=None):
            hp, kind, idx, pfirst, plast = blocks[i]
            h2 = hp % 2
            qpos, nk, ro0, kpos = geom(kind, idx)
            pei, pni, pTi, sti = pe[i % NP], pn[i % NP], pT[i % NP], st[i % NP]
            pst, pstb = self.PS[4 + i % 2], self.PSB[4 + i % 2]
            pso = self.PS[6 + i % 2]
            nkt = nk // 128
            if part in (None, 0):
                self.ts(G, pni.ap[:, 0:nk], pei.ap[:, 0:nk], sti.ap[:, 3:4], ALU.mult, 0.0, ALU.add, r=[pei, sti], w=[pni])
                for kt in range(nkt):
                    self.tr(pstb[:, kt * 128:(kt + 1) * 128], pni.ap[:, kt * 128:(kt + 1) * 128], self.identb.ap, r=[pni, self.identb], w=[pst])
                self.cp("scalar", pTi.ap[:, 0:nkt, :], pstb[:, 0:nk].rearrange("p (k c) -> p k c", c=128), r=[pst], w=[pTi])
                if part == 0:
                    return
            for kt in range(nkt):
                if kind == "c":
                    vt = Ve[h2].ap[:, kt, :]
                elif kt >= 4:
                    vt = Ve[h2].ap[:, kt - 4, :]
                else:
                    tok0 = kpos + kt * 128
                    vt = Ve[h2].ap[:, tok0 // 128, :] if tok0 % 128 == 0 else Vo[h2].ap[:, (tok0 - 64) // 128, :]
                self.mm(pso.ap[:, 0:128], vt, pTi.ap[:, kt, :], kt == 0, kt == nkt - 1, r=[Ve[h2], Vo[h2], pTi], w=[pso])
            self.cp(V, YTp[h2].ap[0:64, qpos:qpos + 64], pso.ap[0:64, 0:64], r=[pso], w=[YTp[h2]])
            self.cp("scalar", YTp[h2].ap[64:128, qpos:qpos + 64], pso.ap[64:128, 64:128], r=[pso], w=[YTp[h2]])
            if plast:
                self.DMA("sync", YT.ap[hp * 128:(hp + 1) * 128, :], YTp[h2].ap, r=[YTp[h2]], w=[YT])

        nb_ = len(blocks)
        for step in range(nb_ + 3):
            a = step
            if a < nb_:
                s1(a, late=False)
            b_ = step - 1
            if 0 <= b_ < nb_:
                s1(b_, late=True)
            c_ = step - 2
            if 0 <= c_ < nb_:
                s2(c_, part=0)
            d_ = step - 3
            if 0 <= d_ < nb_:
                s2(d_, part=1)


def _build(cfg):
    nc = bass.Bass("TRN2", target_bir_lowering=False)
    kb = KB(nc, cfg)
    IN = lambda n, s: kb.dram_t(n, s, F32, kind="ExternalInput")
    x_in = IN("x", [LL, D])
    ctx_in = IN("ctx", [LC, D])
    c_in = IN("c", [D])
    cctx_in = IN("c_ctx", [D])
    ada_w = IN("ada_w", [DEPTH, D, 6 * D])
    ada_b = IN("ada_b", [DEPTH, 6 * D])
    norm_mix = IN("norm_mix", [DEPTH, D])
    norm_ffn = IN("norm_ffn", [DEPTH, D])
    norm_final = IN("norm_final", [D])
    w1 = IN("ffn_w1", [DEPTH, D, FH])
    w3 = IN("ffn_w3", [DEPTH, D, FH])
    w2 = IN("ffn_w2", [DEPTH, FH, D])
    S5 = {
        "lam_re": IN("s5_lam_re", [2, 2, 64, 64]), "lam_im": IN("s5_lam_im", [2, 2, 64, 64]),
        "log_step": IN("s5_log_step", [2, 2, 64]),
        "b_re": IN("s5_b_re", [2, 2, 64, 64, 16]), "b_im": IN("s5_b_im", [2, 2, 64, 64, 16]),
        "c_re": IN("s5_c_re", [2, 2, 64, 16, 64]), "c_im": IN("s5_c_im", [2, 2, 64, 16, 64]),
        "d": IN("s5_d", [2, D]), "w_glu": IN("s5_w_glu", [2, D, D]), "b_glu": IN("s5_b_glu", [2, D]),
    }
    SSD = {
        "w_in": IN("ssd_w_in", [1, D, 8256]), "conv_w": IN("ssd_conv_w", [1, 5, 6144]), "conv_b": IN("ssd_conv_b", [1, 6144]),
        "dt_bias": IN("ssd_dt_bias", [1, 2, 32]), "a_log": IN("ssd_a_log", [1, 2, 32]), "d": IN("ssd_d", [1, 32]),
        "norm": IN("ssd_norm", [1, 2048]), "w_out": IN("ssd_w_out", [1, 2048, D]),
    }
    NA = {"w_qkv": IN("na_w_qkv", [1, D, 3 * D]), "w_o": IN("na_w_o", [1, D, D]), "rpb": IN("na_rpb", [1, 16, 15, 31])}
    out_t = kb.dram_t("out", [LL, D], F32, kind="ExternalOutput")
    QT = kb.dram_t("QT", [D, T], BF16)
    KT = kb.dram_t("KT", [D, T], BF16)
    VT = kb.dram_t("VT", [T, D], BF16)
    YT = kb.dram_t("YT", [D, T], BF16)
    BTD = kb.dram_t("BTD", [16, 64, 15, 64], F32)
    SZ = kb.dram_t("SZ", [T, 2048], BF16)
    XS = kb.dram_t("XS", [2048, T], BF16)
    BC = kb.dram_t("BC", [4096, T], BF16)
    DTR = kb.dram_t("DTR", [64, T], F32)
    YF = kb.dram_t("YF", [T, 2048], BF16)
    YB = kb.dram_t("YB", [T, 2048], BF16)
    XT = kb.dram_t("XT", [D, T], F32)
    HT = kb.dram_t("HT", [D, T], BF16)
    GT = kb.dram_t("GT", [D, T], BF16)
    layers = cfg.get("layers", list(range(DEPTH)))
    kb.consts()
    kb.build_mask8()
    kb.adaln(c_in, cctx_in, ada_w, ada_b, norm_mix, norm_ffn, norm_final, layers)
    kb.prologue_transpose(x_in, ctx_in, XT)
    for l in layers:
        kind, j = l % 3, l // 3
        pre_w = None
        if cfg.get("mixer", True):
            kb.norm_layer(XT, HT, l, 0)
            if kind == 0:
                kb.s5_phase(HT, GT, S5, j)
                if cfg.get("ffn", True):
                    kb.new_phase()
                    pre_w = kb.ffn_weights(w1, w3, w2, l, True, True)
                kb.proj_phase(XT, GT, S5["w_glu"], S5["w_glu"].ap[j], 8, l, glu_bias=(S5["b_glu"], S5["b_glu"].ap[j]), Wt=256 if pre_w else 512)
            elif kind == 1:
                kb.ssd_phase_a(HT, SSD, SZ, XS, BC, DTR)
                kb.ssd_phase_b(SSD, XS, BC, DTR, YF, YB)
                kb.ssd_phase_c(XT, SSD, SZ, YF, YB, l)
            else:
                kb.na_bias_table(NA["rpb"], BTD)
                kb.na_phase_a(HT, NA["w_qkv"], QT, KT, VT)
                kb.na_phase_b(QT, KT, VT, BTD, YT)
                if cfg.get("ffn", True):
                    kb.new_phase()
                    pre_w = kb.ffn_weights(w1, w3, w2, l, True, True)
                kb.proj_phase(XT, YT, NA["w_o"], NA["w_o"].ap[0], 8, l, Wt=256 if pre_w else 512)
        if cfg.get("ffn", True):
            kb.ffn_phase(XT, HT, w1, w3, w2, l, wts=pre_w)
            if pre_w is not None:
                kb.top = pre_w["top0"]
    kb.final_phase(XT, out_t)
    kb.p.fence("sync", kb.out_ops)
    kb.p.build()
    kb.es.close()
    return nc, kb


INPUT_NAMES = ["x", "ctx", "c", "c_ctx", "ada_w", "ada_b", "norm_mix", "norm_ffn", "norm_final", "ffn_w1", "ffn_w3", "ffn_w2",
               "s5_lam_re", "s5_lam_im", "s5_log_step", "s5_b_re", "s5_b_im", "s5_c_re", "s5_c_im", "s5_d", "s5_w_glu", "s5_b_glu",
               "ssd_w_in", "ssd_conv_w", "ssd_conv_b", "ssd_dt_bias", "ssd_a_log", "ssd_d", "ssd_norm", "ssd_w_out",
               "na_w_qkv", "na_w_o", "na_rpb"]


def kernel(**inputs):
    cfg = {}
    nc, kb = _build(cfg)
    n = 8
    in_maps = []
    for b in range(n):
        m = {}
        for k in INPUT_NAMES:
            v = np.ascontiguousarray(inputs[k], dtype=np.float32)
            if k in ("x", "ctx", "c"):
                v = np.ascontiguousarray(v[b])
            m[k] = v
        in_maps.append(m)
    res = run_bass_kernel_spmd(nc, in_maps, core_ids=list(range(n)))
    return np.stack([np.asarray(r["out"], dtype=np.float32) for r in res.results], axis=0)
```

```python
import numpy as np
from contextlib import ExitStack
import concourse.bass as bass
import concourse.mybir as mybir
from concourse.bass_utils import run_bass_kernel_spmd

F32 = mybir.dt.float32
BF16 = mybir.dt.bfloat16
I32 = mybir.dt.int32
AF = mybir.ActivationFunctionType
ALU = mybir.AluOpType
AX = mybir.AxisListType

ENGS = ("sync", "gpsimd", "scalar", "vector", "tensor")
NDMA_SEM = 24
SEM_EPOCH = 12000

D = 1024
LC = 256
LL = 4096
T = LC + LL
NFT = 8
FH = 2816
NHT = 22
DEPTH = 4
EPS = 1e-6
ARENA_F32 = 46 * 1024


class Buf:
    __slots__ = ("name", "w", "r", "psum")

    def __init__(self, name, psum=False):
        self.name = name
        self.w = None
        self.r = []
        self.psum = psum


class Op:
    __slots__ = ("eng", "fn", "idx", "deps", "need_inc", "val", "is_dma", "semi", "is_barrier", "bg")

    def __init__(self, eng, fn, is_dma):
        self.eng = eng
        self.fn = fn
        self.is_dma = is_dma
        self.deps = []
        self.need_inc = False
        self.val = 0
        self.semi = -1
        self.idx = -1
        self.is_barrier = False
        self.bg = False


class Prog:
    def __init__(self, nc):
        self.nc = nc
        self.ops = {e: [] for e in ENGS}
        self.nreal = {e: 0 for e in ENGS}
        self.es = ExitStack()
        self.engsem = {e: self.es.enter_context(nc.semaphore("s_" + e)) for e in ENGS}
        self.dmasem = [self.es.enter_context(nc.semaphore("d%d" % i)) for i in range(NDMA_SEM)]
        self.dma_last = [None] * NDMA_SEM
        self.dma_cnt = [0] * NDMA_SEM
        self.dma_rr = 0

    def emit(self, eng, fn, reads=(), writes=(), dma=False):
        op = Op(eng, fn, dma)
        op.idx = self.nreal[eng]
        self.nreal[eng] += 1
        deps = []
        for b in reads:
            if b.w is not None:
                deps.append(b.w)
            if b.psum:
                deps.extend(b.r)
        for b in writes:
            if b.w is not None:
                deps.append(b.w)
            deps.extend(b.r)
        if dma:
            k = self.dma_rr
            self.dma_rr = (k + 1) % NDMA_SEM
            if self.dma_last[k] is not None:
                deps.append(self.dma_last[k])
            self.dma_last[k] = op
            self.dma_cnt[k] += 16
            op.semi = k
            op.val = self.dma_cnt[k]
        seen = set()
        for d in deps:
            if d is op or id(d) in seen:
                continue
            seen.add(id(d))
            op.deps.append(d)
        for b in reads:
            if b.psum:
                b.w = op
                b.r = []
            else:
                b.r.append(op)
        for b in writes:
            b.w = op
            b.r = []
        self.ops[eng].append(op)
        return op

    def fence(self, eng, deps):
        op = Op(eng, None, False)
        op.idx = self.nreal[eng]
        op.deps = list(deps)
        self.ops[eng].append(op)
        return op

    def barrier(self):
        lasts = []
        for e in ENGS:
            for o in reversed(self.ops[e]):
                if o.fn is not None and not o.bg:
                    lasts.append(o)
                    break
        for o in self.dma_last:
            if o is not None and not o.bg:
                lasts.append(o)
        for e in ENGS:
            self.fence(e, lasts).is_barrier = True

    def _needs_wait(self, op, d):
        if d.is_dma:
            return True
        if d.eng != op.eng:
            return True
        if op.eng == "tensor":
            return False
        return (op.idx - d.idx) <= 2

    def build(self):
        nc = self.nc
        for e in ENGS:
            for op in self.ops[e]:
                for d in op.deps:
                    if not d.is_dma and self._needs_wait(op, d):
                        d.need_inc = True
        self.epoch_sems = {e: [self.engsem[e]] for e in ENGS}
        for e in ENGS:
            c = 0
            ep = 0
            for op in self.ops[e]:
                if op.fn is None:
                    if op.is_barrier and c > SEM_EPOCH:
                        ep += 1
                        c = 0
                        self.epoch_sems[e].append(self.es.enter_context(nc.semaphore("s_%s_%d" % (e, ep))))
                    continue
                if op.is_dma:
                    continue
                op.semi = ep
                if op.need_inc:
                    c += 1
                    op.val = c
        self.counts = {e: 0 for e in ENGS}

        def mk_body(e):
            def body(eng):
                waited = {}
                for op in self.ops[e]:
                    for d in op.deps:
                        if not self._needs_wait(op, d):
                            continue
                        if d.is_dma:
                            key = ("d", d.semi)
                            sem = self.dmasem[d.semi]
                        else:
                            key = ("e", d.eng, d.semi)
                            sem = self.epoch_sems[d.eng][d.semi]
                        if waited.get(key, 0) >= d.val:
                            continue
                        waited[key] = d.val
                        eng.wait_ge(sem, d.val)
                    if op.fn is None:
                        continue
                    inst = op.fn(eng)
                    self.counts[e] += 1
                    if op.is_dma:
                        inst.then_inc(self.dmasem[op.semi], 16)
                    elif op.need_inc:
                        inst.then_inc(self.epoch_sems[e][op.semi], 1)
            return body

        with nc.Block() as block:
            for e in ENGS:
                if self.ops[e]:
                    getattr(block, e)(mk_body(e))
        self.es.close()


class Tl:
    __slots__ = ("ap", "buf")

    def __init__(self, ap, buf):
        self.ap = ap
        self.buf = buf


class KB:
    def __init__(self, nc, cfg):
        self.nc = nc
        self.cfg = cfg
        self.p = Prog(nc)
        self.es = ExitStack()
        self.arena = self.es.enter_context(nc.sbuf_tensor("arena", [128, ARENA_F32], F32))
        self.arena_bf = self.arena.bitcast(BF16)
        self.psum = [self.es.enter_context(nc.psum_tensor("ps%d" % i, [128, 512], F32)) for i in range(8)]
        self.PS = [Tl(self.psum[i][:], Buf("ps%d" % i, psum=True)) for i in range(8)]
        self.PSB = [self.psum[i].bitcast(BF16) for i in range(8)]
        self.top = 0
        self.ptr = 0
        self.nb = 0
        self.dram = {}
        self.out_ops = []

    def _al(self, n_f32, persistent):
        n_f32 = (n_f32 + 7) // 8 * 8
        if persistent:
            assert self.ptr == self.top, "persistent alloc only between phases"
            off = self.top
            self.top += n_f32
            self.ptr = self.top
        else:
            off = self.ptr
            self.ptr += n_f32
        assert self.ptr <= ARENA_F32, "arena overflow %d" % self.ptr
        return off

    def f32(self, n, name=None, persistent=False, shape=None):
        off = self._al(n, persistent)
        ap = self.arena[:, off:off + n]
        if shape is not None:
            ap = self._reshape(ap, shape)
        self.nb += 1
        return Tl(ap, Buf(name or "t%d" % self.nb))

    def bf16(self, n, name=None, persistent=False, shape=None):
        off = self._al((n + 1) // 2, persistent)
        ap = self.arena_bf[:, 2 * off:2 * off + n]
        if shape is not None:
            ap = self._reshape(ap, shape)
        self.nb += 1
        return Tl(ap, Buf(name or "t%d" % self.nb))

    def i32(self, n, name=None):
        off = self._al(n, False)
        ap = self.arena.bitcast(I32)[:, off:off + n]
        self.nb += 1
        return Tl(ap, Buf(name or "t%d" % self.nb))

    @staticmethod
    def _reshape(ap, shape):
        if len(shape) == 2:
            return ap.rearrange("p (a b) -> p a b", b=shape[1])
        if len(shape) == 3:
            return ap.rearrange("p (a b c) -> p a b c", b=shape[1], c=shape[2])
        raise ValueError

    def new_phase(self):
        self.p.barrier()
        self.ptr = self.top

    def dram_t(self, name, shape, dt, kind="Internal"):
        t = self.nc.dram_tensor(name, shape, dt, kind=kind)
        tl = Tl(t.ap(), Buf(name))
        self.dram[name] = tl
        return tl

    def E(self, eng, fn, r=(), w=()):
        return self.p.emit(eng, fn, [t.buf for t in r], [t.buf for t in w])

    def DMA(self, eng, out_ap, in_ap, r=(), w=(), slow=False):
        if slow:
            fn = lambda e: e.dma_start(out=out_ap, in_=in_ap, allow_slow_non_contiguous=True)
        else:
            fn = lambda e: e.dma_start(out=out_ap, in_=in_ap)
        return self.p.emit(eng, fn, [t.buf for t in r], [t.buf for t in w], dma=True)

    def mm(self, ps_ap, lhsT, rhs, start, stop, r=(), w=()):
        return self.E("tensor", lambda e: e.matmul(ps_ap, lhsT=lhsT, rhs=rhs, start=start, stop=stop), r, w)

    def tr(self, ps_ap, in_ap, ident_ap, r=(), w=()):
        return self.E("tensor", lambda e: e.transpose(out=ps_ap, in_=in_ap, identity=ident_ap), r, w)

    def act(self, out, in_, func, r=(), w=(), scale=None, bias=None, accum=None):
        kw = {}
        if scale is not None:
            kw["scale"] = scale
        if bias is not None:
            kw["bias"] = bias
        if accum is not None:
            kw["accum_out"] = accum
        return self.E("scalar", lambda e: e.activation(out=out, in_=in_, func=func, **kw), r, w)

    def tt(self, eng, out, in0, in1, op, r=(), w=()):
        return self.E(eng, lambda e: e.tensor_tensor(out=out, in0=in0, in1=in1, op=op), r, w)

    def ts(self, eng, out, in0, s1, op0, s2=None, op1=None, r=(), w=(), accum=None):
        if op1 is None:
            return self.E(eng, lambda e: e.tensor_scalar(out=out, in0=in0, scalar1=s1, scalar2=None, op0=op0), r, w)
        if accum is not None:
            return self.E(eng, lambda e: e.tensor_scalar(out=out, in0=in0, scalar1=s1, scalar2=s2, op0=op0, op1=op1, accum_out=accum), r, w)
        return self.E(eng, lambda e: e.tensor_scalar(out=out, in0=in0, scalar1=s1, scalar2=s2, op0=op0, op1=op1), r, w)

    def stt(self, out, in0, scalar, in1, op0, op1, r=(), w=()):
        return self.E("vector", lambda e: e.scalar_tensor_tensor(out=out, in0=in0, scalar=scalar, in1=in1, op0=op0, op1=op1), r, w)

    def cp(self, eng, out, in_, r=(), w=()):
        if eng == "scalar":
            return self.act(out, in_, AF.Copy, r, w)
        return self.E(eng, lambda e: e.tensor_copy(out=out, in_=in_), r, w)

    def memset(self, eng, ap, val, w=()):
        return self.E(eng, lambda e: e.memset(ap, val), (), w)

    def consts(self):
        self.ident = self.f32(128, "ident", True)
        self.identb = self.bf16(128, "identb", True)
        self.ones = self.f32(128, "ones", True)
        self.epsc = self.f32(8, "epsc", True)
        self.memset("gpsimd", self.ident.ap, 0.0, [self.ident])
        idap = self.ident.ap
        self.E("gpsimd", lambda e: e.affine_select(out=idap, in_=idap, pattern=[[-1, 128]], compare_op=ALU.not_equal,
                                                   fill=1.0, base=0, channel_multiplier=1), [self.ident], [self.ident])
        self.cp("vector", self.identb.ap, self.ident.ap, [self.ident], [self.identb])
        self.memset("vector", self.ones.ap, 1.0, [self.ones])
        self.memset("vector", self.epsc.ap[:, 0:1], EPS, [self.epsc])
        self.memset("vector", self.epsc.ap[:, 1:2], 0.0, [self.epsc])
        self.memset("vector", self.epsc.ap[:, 2:3], 1.0, [self.epsc])

    @staticmethod
    def chunks(w_lat=512):
        ch = [(0, LC, True)]
        for s in range(LC, T, w_lat):
            ch.append((s, w_lat, False))
        return ch

    def prologue_transpose(self, x_in, ctx_in, XT):
        self.new_phase()
        xin = [self.f32(4 * D, "xin%d" % i, shape=(4, D)) for i in range(2)]
        stage = [self.f32(NFT * 512, "stg%d" % i, shape=(NFT, 512)) for i in range(2)]
        for ci, (s, w, isctx) in enumerate(self.chunks()):
            xi = xin[ci % 2]
            st = stage[ci % 2]
            ntt = w // 128
            if isctx:
                src = ctx_in.ap.rearrange("(tt p) f -> p tt f", p=128)
            else:
                src = x_in.ap[s - LC:s - LC + w, :].rearrange("(tt p) f -> p tt f", p=128)
            self.DMA("sync", xi.ap[:, 0:ntt, :], src, r=[x_in], w=[xi])
            for ft in range(NFT):
                ps = self.PS[ft]
                for tt in range(ntt):
                    self.tr(ps.ap[:, tt * 128:(tt + 1) * 128], xi.ap[:, tt, ft * 128:(ft + 1) * 128], self.ident.ap,
                            r=[xi, self.ident], w=[ps])
                self.cp("scalar" if ft % 2 else "vector", st.ap[:, ft, 0:w], ps.ap[:, 0:w], r=[ps], w=[st])
            dst = XT.ap[:, s:s + w].rearrange("(ft p) t -> p ft t", p=128)
            self.DMA("sync", dst, st.ap[:, :, 0:w], r=[st], w=[XT])

    def adaln(self, c_in, cctx_in, ada_w, ada_b, norm_mix, norm_ffn, norm_final, layers):
        self.mod = {}
        for l in layers:
            self.mod[l] = self.f32(96, "mod%d" % l, True, shape=(6, 8, 2))
        self.nw = self.f32(9 * 8, "nw", True, shape=(9, 8))
        self.AB = {}
        for l in layers:
            self.AB[l] = self.f32(4 * 16, "AB%d" % l, True, shape=(4, 8, 2))
        self.new_phase()
        sT = self.f32(16, "sT", shape=(8, 2))
        craw = self.f32(16, "craw", shape=(8, 2))
        self.DMA("sync", craw.ap[:, :, 0], c_in.ap.rearrange("(kt p) -> p kt", p=128), r=[c_in], w=[craw], slow=True)
        self.DMA("sync", craw.ap[:, :, 1], cctx_in.ap.rearrange("(kt p) -> p kt", p=128), r=[cctx_in], w=[craw], slow=True)
        self.act(sT.ap, craw.ap, AF.Silu, r=[craw], w=[sT])
        for k, nwt in enumerate([norm_mix, norm_ffn]):
            self.DMA("sync", self.nw.ap[:, 4 * k:4 * k + 4, :], nwt.ap.rearrange("l (ft p) -> p l ft", p=128), r=[nwt], w=[self.nw], slow=True)
        self.DMA("sync", self.nw.ap[:, 8, :], norm_final.ap.rearrange("(ft p) -> p ft", p=128), r=[norm_final], w=[self.nw], slow=True)
        wbuf = [self.f32(8 * 512, "adaw%d" % i, shape=(8, 512)) for i in range(3)]
        bbuf = [self.f32(512, "adab%d" % i) for i in range(3)]
        onesrow = self.ones.ap[0:1, 0:2]
        it = 0
        for l in layers:
            for cj in range(12):
                wb = wbuf[it % 3]
                bb = bbuf[it % 3]
                ps = self.PS[it % 4]
                it += 1
                self.DMA("sync", wb.ap, ada_w.ap[l, :, cj * 512:(cj + 1) * 512].rearrange("(kt p) n -> p kt n", p=128), r=[ada_w], w=[wb])
                self.DMA("sync", bb.ap[0:1, :], ada_b.ap[l:l + 1, cj * 512:(cj + 1) * 512], r=[ada_b], w=[bb])
                for jj in range(4):
                    j = cj * 4 + jj
                    o = ps.ap[:, jj * 2:jj * 2 + 2]
                    for kt in range(8):
                        self.mm(o, wb.ap[:, kt, jj * 128:(jj + 1) * 128], sT.ap[:, kt, :], kt == 0, False, r=[wb, sT], w=[ps])
                    self.mm(o, bb.ap[0:1, jj * 128:(jj + 1) * 128], onesrow, False, True, r=[bb, self.ones], w=[ps])
                m = cj * 4 // 8
                ft0 = (cj * 4) % 8
                self.cp("vector", self.mod[l].ap[:, m, ft0:ft0 + 4, :], ps.ap[:, 0:8].rearrange("p (a b) -> p a b", b=2), r=[ps], w=[self.mod[l]])
        for l in layers:
            for k, (mi, nwi) in enumerate([(1, l), (4, 4 + l)]):
                nwb = self.nw.ap[:, nwi, :].unsqueeze(2).to_broadcast([128, 8, 2])
                self.stt(self.AB[l].ap[:, k, :, :], self.mod[l].ap[:, mi, :, :], 1.0, nwb, ALU.add, ALU.mult, r=[self.mod[l], self.nw], w=[self.AB[l]])

    def norm_phase(self, XT, HT, A_sel, B_sel, deps_r):
        self.new_phase()
        xin = [self.f32(NFT * 512, "nx%d" % i, shape=(NFT, 512)) for i in range(2)]
        sq = [self.f32(NFT * 512, "nsq%d" % i, shape=(NFT, 512)) for i in range(2)]
        hb = [self.bf16(NFT * 512, "nh%d" % i, shape=(NFT, 512)) for i in range(2)]
        rt = [self.f32(512, "nrt%d" % i) for i in range(2)]
        chs = self.chunks()

        def load(ci):
            s, w, isctx = chs[ci]
            xi = xin[ci % 2]
            self.DMA("sync", xi.ap[:, :, 0:w], XT.ap[:, s:s + w].rearrange("(ft p) t -> p ft t", p=128), r=[XT], w=[xi])

        load(0)
        for ci, (s, w, isctx) in enumerate(chs):
            if ci + 1 < len(chs):
                load(ci + 1)
            xi, sqi, hbi, rti = xin[ci % 2], sq[ci % 2], hb[ci % 2], rt[ci % 2]
            ps = self.PS[ci % 2]
            self.act(sqi.ap[:, :, 0:w], xi.ap[:, :, 0:w], AF.Square, r=[xi], w=[sqi])
            for ft in range(NFT):
                self.mm(ps.ap[:, 0:w], self.ones.ap, sqi.ap[:, ft, 0:w], ft == 0, ft == NFT - 1, r=[self.ones, sqi], w=[ps])
            self.act(rti.ap[:, 0:w], ps.ap[:, 0:w], AF.Sqrt, r=[ps, self.epsc], w=[rti], scale=1.0 / D, bias=self.epsc.ap[:, 0:1])
            self.E("vector", lambda e, o=rti.ap[:, 0:w]: e.reciprocal(out=o, in_=o), r=[rti], w=[rti])
            for ft in range(NFT):
                a = A_sel(ft, isctx)
                b = B_sel(ft, isctx)
                self.stt(sqi.ap[:, ft, 0:w], xi.ap[:, ft, 0:w], a, rti.ap[:, 0:w], ALU.mult, ALU.mult, r=[xi, rti] + deps_r, w=[sqi])
                self.act(hbi.ap[:, ft, 0:w], sqi.ap[:, ft, 0:w], AF.Identity, r=[sqi] + deps_r, w=[hbi], bias=b)
            self.DMA("sync", HT.ap[:, s:s + w].rearrange("(ft p) t -> p ft t", p=128), hbi.ap[:, :, 0:w], r=[hbi], w=[HT])

    def norm_layer(self, XT, HT, l, which):
        AB = self.AB[l]
        mod = self.mod[l]
        k = 0 if which == 0 else 1
        smi = 0 if which == 0 else 3
        A_sel = lambda ft, isctx: AB.ap[:, k, ft, (1 if isctx else 0):(1 if isctx else 0) + 1]
        B_sel = lambda ft, isctx: mod.ap[:, smi, ft, (1 if isctx else 0):(1 if isctx else 0) + 1]
        self.norm_phase(XT, HT, A_sel, B_sel, [AB, mod])

    def final_phase(self, XT, out_t):
        self.new_phase()
        xin = [self.f32(NFT * 512, "fx%d" % i, shape=(NFT, 512)) for i in range(2)]
        sq = [self.f32(NFT * 512, "fsq%d" % i, shape=(NFT, 512)) for i in range(2)]
        rt = [self.f32(512, "frt%d" % i) for i in range(2)]
        ob = [self.f32(4 * D, "fo%d" % i, shape=(4, D)) for i in range(2)]
        ci = 0
        lat = [c for c in self.chunks() if not c[2]]

        def loadf(i):
            s, w, _ = lat[i]
            self.DMA("sync", xin[i % 2].ap, XT.ap[:, s:s + w].rearrange("(ft p) t -> p ft t", p=128), r=[XT], w=[xin[i % 2]])

        loadf(0)
        for (s, w, isctx) in lat:
            xi, sqi, rti, obi = xin[ci % 2], sq[ci % 2], rt[ci % 2], ob[ci % 2]
            ps = self.PS[ci % 2]
            ci += 1
            if ci < len(lat):
                loadf(ci)
            self.act(sqi.ap, xi.ap, AF.Square, r=[xi], w=[sqi])
            for ft in range(NFT):
                self.mm(ps.ap, self.ones.ap, sqi.ap[:, ft, :], ft == 0, ft == NFT - 1, r=[self.ones, sqi], w=[ps])
            self.act(rti.ap, ps.ap, AF.Sqrt, r=[ps, self.epsc], w=[rti], scale=1.0 / D, bias=self.epsc.ap[:, 0:1])
            self.E("vector", lambda e, o=rti.ap: e.reciprocal(out=o, in_=o), r=[rti], w=[rti])
            for ft in range(NFT):
                self.stt(sqi.ap[:, ft, :], xi.ap[:, ft, :], self.nw.ap[:, 8, ft:ft + 1], rti.ap, ALU.mult, ALU.mult, r=[xi, rti, self.nw], w=[sqi])
            for tt in range(4):
                for half in range(2):
                    pso = self.PS[2 + (tt * 2 + half) % 6]
                    for q in range(4):
                        ft = half * 4 + q
                        self.tr(pso.ap[:, q * 128:(q + 1) * 128], sqi.ap[:, ft, tt * 128:(tt + 1) * 128], self.ident.ap, r=[sqi, self.ident], w=[pso])
                    self.cp("scalar" if half else "vector", obi.ap[:, tt, half * 512:(half + 1) * 512], pso.ap, r=[pso], w=[obi])
            dst = out_t.ap[s - LC:s - LC + w, :].rearrange("(tt p) f -> p tt f", p=128)
            self.out_ops.append(self.DMA("sync", dst, obi.ap, r=[obi], w=[out_t]))

    def load_w_bf16(self, dst, src_ap, src_tl, nkt, col0=None, col1=None):
        for kt in range(nkt):
            s = src_ap[kt * 128:(kt + 1) * 128, :] if col0 is None else src_ap[kt * 128:(kt + 1) * 128, col0:col1]
            self.DMA("gpsimd", dst.ap[:, kt, :], s, r=[src_tl], w=[dst])

    def ffn_weights(self, w1, w3, w2, l, persistent, bg):
        top0 = self.top
        w1s = self.bf16(8 * FH, "w1s", persistent=persistent, shape=(8, FH))
        w3s = self.bf16(8 * FH, "w3s", persistent=persistent, shape=(8, FH))
        w2s = self.bf16(NHT * D, "w2s", persistent=persistent, shape=(NHT, D))
        hk = {"w1": [], "w3": [], "w2": []}
        for (nm, dst, src, nkt) in (("w1", w1s, w1, 8), ("w3", w3s, w3, 8), ("w2", w2s, w2, NHT)):
            for kt in range(nkt):
                tl = Tl(dst.ap, Buf("%s_%d" % (nm, kt)))
                op = self.DMA("gpsimd", dst.ap[:, kt, :], src.ap[l][kt * 128:(kt + 1) * 128, :], r=[src], w=[tl])
                op.bg = bg
                hk[nm].append(tl)
        return dict(w1s=w1s, w3s=w3s, w2s=w2s, hk=hk, top0=top0)

    def ffn_phase(self, XT, HT, w1, w3, w2, l, wts=None):
        self.new_phase()
        W = 256
        if wts is None:
            wts = self.ffn_weights(w1, w3, w2, l, False, False)
        w1s, w3s, w2s, hk = wts["w1s"], wts["w3s"], wts["w2s"], wts["hk"]
        hb = [self.bf16(NFT * W, "fh%d" % i, shape=(NFT, W)) for i in range(2)]
        xb = [self.f32(NFT * W, "fxx%d" % i, shape=(NFT, W)) for i in range(2)]
        sqb = self.f32(NFT * W, "fsq", shape=(NFT, W))
        rtb = [self.f32(W, "frt%d" % i) for i in range(2)]
        gb = [self.bf16(NHT * W, "fg%d" % i, shape=(NHT, W)) for i in range(1)]
        sl = [self.bf16(W, "fs%d" % i) for i in range(2)]
        mod = self.mod[l]
        AB = self.AB[l]
        ntile = T // W

        def load(ti):
            s = ti * W
            self.DMA("sync", xb[ti % 2].ap, XT.ap[:, s:s + W].rearrange("(ft p) t -> p ft t", p=128), r=[XT], w=[xb[ti % 2]])

        def norm(ti):
            s = ti * W
            cs = 1 if s < LC else 0
            hbi, xbi, rti = hb[ti % 2], xb[ti % 2], rtb[ti % 2]
            ps = self.PS[7]
            self.act(sqb.ap, xbi.ap, AF.Square, r=[xbi], w=[sqb])
            for ft in range(NFT):
                self.mm(ps.ap[:, 0:W], self.ones.ap, sqb.ap[:, ft, :], ft == 0, ft == NFT - 1, r=[self.ones, sqb], w=[ps])
            self.act(rti.ap, ps.ap[:, 0:W], AF.Sqrt, r=[ps, self.epsc], w=[rti], scale=1.0 / D, bias=self.epsc.ap[:, 0:1])
            self.E("vector", lambda e, o=rti.ap: e.reciprocal(out=o, in_=o), r=[rti], w=[rti])
            for ft in range(NFT):
                self.stt(sqb.ap[:, ft, :], xbi.ap[:, ft, :], AB.ap[:, 1, ft, cs:cs + 1], rti.ap, ALU.mult, ALU.mult, r=[xbi, rti, AB], w=[sqb])
                self.act(hbi.ap[:, ft, :], sqb.ap[:, ft, :], AF.Identity, r=[sqb, mod], w=[hbi], bias=mod.ap[:, 3, ft, cs:cs + 1])

        load(0)
        norm(0)
        for ti in range(ntile):
            s = ti * W
            cs = 1 if s < LC else 0
            hbi, xbi, gbi = hb[ti % 2], xb[ti % 2], gb[0]
            if ti + 1 < ntile:
                load(ti + 1)
            for j in range(NHT):
                pa = self.PS[(j % 2) * 2]
                pb = self.PS[(j % 2) * 2 + 1]
                for kt in range(8):
                    self.mm(pa.ap[:, 0:W], w1s.ap[:, kt, j * 128:(j + 1) * 128], hbi.ap[:, kt, :], kt == 0, kt == 7, r=[hk["w1"][kt], hbi], w=[pa])
                for kt in range(8):
                    self.mm(pb.ap[:, 0:W], w3s.ap[:, kt, j * 128:(j + 1) * 128], hbi.ap[:, kt, :], kt == 0, kt == 7, r=[hk["w3"][kt], hbi], w=[pb])
                sli = sl[j % 2]
                self.act(sli.ap, pa.ap[:, 0:W], AF.Silu, r=[pa], w=[sli])
                self.tt("vector", gbi.ap[:, j, :], pb.ap[:, 0:W], sli.ap, ALU.mult, r=[pb, sli], w=[gbi])
                if j == 10 and ti + 1 < ntile:
                    norm(ti + 1)
            for fo in range(NFT):
                po = self.PS[4 + fo % 3]
                for j in range(NHT):
                    self.mm(po.ap[:, 0:W], w2s.ap[:, j, fo * 128:(fo + 1) * 128], gbi.ap[:, j, :], j == 0, j == NHT - 1, r=[hk["w2"][j], gbi], w=[po])
                self.stt(xbi.ap[:, fo, :], po.ap[:, 0:W], mod.ap[:, 5, fo, cs:cs + 1], xbi.ap[:, fo, :], ALU.mult, ALU.add, r=[po, mod, xbi], w=[xbi])
            self.DMA("sync", XT.ap[:, s:s + W].rearrange("(ft p) t -> p ft t", p=128), xbi.ap, r=[xbi], w=[XT])

    def rev_ap(self, ap2d, start, n):
        pstride = ap2d.ap[0][0]
        return bass.AP(ap2d.tensor, ap2d.offset + start + n - 1, [[pstride, 128], [-1, n]])

    def sincos_turns(self, eng, turns, n, osin, ocos, tmp, cast_eng="vector"):
        ti, tf, fr, s2, s4 = tmp["ti"], tmp["tf"], tmp["fr"], tmp["s2"], tmp["s4"]
        sl = lambda t: t.ap[:, 0:n]
        self.cp(cast_eng, sl(ti), sl(turns), r=[turns], w=[ti])
        self.cp(cast_eng, sl(tf), sl(ti), r=[ti], w=[tf])
        self.tt(eng, sl(fr), sl(turns), sl(tf), ALU.subtract, r=[turns, tf], w=[fr])
        self.act(sl(s2), sl(fr), AF.Sin, r=[fr], w=[s2], scale=float(np.pi))
        self.act(sl(s4), sl(fr), AF.Sin, r=[fr], w=[s4], scale=float(np.pi / 2))
        self.tt(eng, sl(s4), sl(s4), sl(s4), ALU.mult, r=[s4], w=[s4])
        self.ts(eng, sl(s4), sl(s4), -4.0, ALU.mult, 2.0, ALU.add, r=[s4], w=[s4])
        self.tt(eng, sl(osin), sl(s2), sl(s4), ALU.mult, r=[s2, s4], w=[osin])
        self.tt(eng, sl(s2), sl(s2), sl(s2), ALU.mult, r=[s2], w=[s2])
        self.ts(eng, sl(ocos), sl(s2), -2.0, ALU.mult, 1.0, ALU.add, r=[s2], w=[ocos])

    def build_mask8(self):
        self.mask8 = self.f32(8, "mask8", True)
        m = self.mask8.ap
        self.memset("gpsimd", m, 1.0, [self.mask8])
        self.E("gpsimd", lambda e: e.affine_select(out=m, in_=m, pattern=[[-16, 8]], compare_op=ALU.is_ge, fill=0.0, base=0, channel_multiplier=1), [self.mask8], [self.mask8])
        self.E("gpsimd", lambda e: e.affine_select(out=m, in_=m, pattern=[[16, 8]], compare_op=ALU.is_ge, fill=0.0, base=15, channel_multiplier=-1), [self.mask8], [self.mask8])

    def s5_phase(self, HT, GT, P, j):
        self.new_phase()
        V, G = "vector", "gpsimd"
        def sc_tile(nm):
            return self.f32(64, nm)
        lr, li, ls = sc_tile("lr"), sc_tile("li"), sc_tile("ls")
        for d in range(2):
            self.DMA("sync", lr.ap[:, d * 32:(d + 1) * 32], P["lam_re"].ap[j, d].rearrange("(p two) n -> (two n) p", two=2), r=[P["lam_re"]], w=[lr], slow=True)
            self.DMA("sync", li.ap[:, d * 32:(d + 1) * 32], P["lam_im"].ap[j, d].rearrange("(p two) n -> (two n) p", two=2), r=[P["lam_im"]], w=[li], slow=True)
        lsrow = self.f32(128, "lsrow")
        self.DMA("sync", lsrow.ap[0:1, :], P["log_step"].ap[j:j + 1].rearrange("o d g -> o (d g)"), r=[P["log_step"]], w=[lsrow])
        psb = self.PS[0]
        self.mm(psb.ap[:, 0:128], self.ones.ap[0:1, :], lsrow.ap[0:1, :], True, True, r=[self.ones, lsrow], w=[psb])
        for d in range(2):
            src = psb.ap[:, d * 64:(d + 1) * 64].rearrange("q (p two) -> q p two", two=2)
            self.cp(V, ls.ap[0:64, d * 32:(d + 1) * 32], src[0:64, :, 0], r=[psb], w=[ls])
            self.cp(V, ls.ap[64:128, d * 32:(d + 1) * 32], src[64:128, :, 1], r=[psb], w=[ls])
        step, zr, zi, rr, tq = sc_tile("step"), sc_tile("zr"), sc_tile("zi"), sc_tile("rr"), sc_tile("tq")
        tmp = {"ti": self.i32(512, "ti"), "tf": self.f32(512, "tf"), "fr": self.f32(512, "fr"), "s2": self.f32(512, "s2"), "s4": self.f32(512, "s4")}
        tmp2 = {"ti": self.i32(512, "ti2"), "tf": self.f32(512, "tf2"), "fr": self.f32(512, "fr2"), "s2": self.f32(512, "s22"), "s4": self.f32(512, "s42")}
        sphi, cphi, frac = sc_tile("sphi"), sc_tile("cphi"), sc_tile("frac")
        self.act(step.ap, ls.ap, AF.Exp, r=[ls], w=[step])
        self.tt(V, zr.ap, lr.ap, step.ap, ALU.mult, r=[lr, step], w=[zr])
        self.tt(V, zi.ap, li.ap, step.ap, ALU.mult, r=[li, step], w=[zi])
        self.act(rr.ap, zr.ap, AF.Exp, r=[zr], w=[rr])
        self.ts(V, tq.ap, zi.ap, float(1.0 / (2 * np.pi)), ALU.mult, r=[zi], w=[tq])
        self.sincos_turns(V, tq, 64, sphi, cphi, tmp)
        self.cp(V, frac.ap, tmp["fr"].ap[:, 0:64], r=[tmp["fr"]], w=[frac])
        carry = {}
        for Q in (256, 512):
            tQ, sQ, cQ = sc_tile("tQ%d" % Q), sc_tile("sQ%d" % Q), sc_tile("cQ%d" % Q)
            self.ts(V, tQ.ap, frac.ap, float(Q), ALU.mult, r=[frac], w=[tQ])
            self.sincos_turns(V, tQ, 64, sQ, cQ, tmp)
            carry[Q] = (sQ, cQ)
        ar, ai, den, u, cr, ci, t1s, t2s = [sc_tile(n) for n in ("ar", "ai", "den", "u", "cr", "ci", "t1s", "t2s")]
        self.tt(V, ar.ap, rr.ap, cphi.ap, ALU.mult, r=[rr, cphi], w=[ar])
        self.tt(V, ai.ap, rr.ap, sphi.ap, ALU.mult, r=[rr, sphi], w=[ai])
        self.tt(V, t1s.ap, lr.ap, lr.ap, ALU.mult, r=[lr], w=[t1s])
        self.tt(V, t2s.ap, li.ap, li.ap, ALU.mult, r=[li], w=[t2s])
        self.tt(V, den.ap, t1s.ap, t2s.ap, ALU.add, r=[t1s, t2s], w=[den])
        self.E(V, lambda e: e.reciprocal(out=den.ap, in_=den.ap), r=[den], w=[den])
        self.ts(V, u.ap, ar.ap, -1.0, ALU.add, r=[ar], w=[u])
        self.tt(V, t1s.ap, u.ap, lr.ap, ALU.mult, r=[u, lr], w=[t1s])
        self.tt(V, t2s.ap, ai.ap, li.ap, ALU.mult, r=[ai, li], w=[t2s])
        self.tt(V, t1s.ap, t1s.ap, t2s.ap, ALU.add, r=[t1s, t2s], w=[t1s])
        self.tt(V, cr.ap, t1s.ap, den.ap, ALU.mult, r=[t1s, den], w=[cr])
        self.tt(V, t1s.ap, ai.ap, lr.ap, ALU.mult, r=[ai, lr], w=[t1s])
        self.tt(V, t2s.ap, u.ap, li.ap, ALU.mult, r=[u, li], w=[t2s])
        self.tt(V, t1s.ap, t1s.ap, t2s.ap, ALU.subtract, r=[t1s, t2s], w=[t1s])
        self.tt(V, ci.ap, t1s.ap, den.ap, ALU.mult, r=[t1s, den], w=[ci])
        braw = {}
        for nm in ("b_re", "b_im"):
            braw[nm] = self.f32(2 * 32 * 16, "braw_" + nm, shape=(2, 32, 16))
            for d in range(2):
                self.DMA("sync", braw[nm].ap[:, d, :, :], P[nm].ap[j, d].rearrange("(p two) n h -> (two n) p h", two=2), r=[P[nm]], w=[braw[nm]], slow=True)
        dsk = self.f32(8, "dsk")
        self.DMA("sync", dsk.ap, P["d"].ap[j].rearrange("(ft p) -> p ft", p=128), r=[P["d"]], w=[dsk], slow=True)
        M1 = {}
        for k in range(4):
            for nm in ("b_re", "b_im"):
                M1[(k, nm)] = self.f32(128, "M1_%d%s" % (k, nm))
                self.memset(G, M1[(k, nm)].ap, 0.0, [M1[(k, nm)]])
        Jrow = self.f32(512, "Jrow")
        self.E(G, lambda e: e.iota(Jrow.ap, pattern=[[1, 512]], base=0, channel_multiplier=0, allow_small_or_imprecise_dtypes=True), (), [Jrow])
        WTS = [self.bf16(8 * 6 * 128, "wts%d" % i, shape=(8, 6, 128)) for i in range(2)]
        craw = [[self.f32(64, "craw%d_%d" % (i, q)) for q in range(2)] for i in range(2)]
        Spair = [self.f32(128, "Spair%d" % i) for i in range(2)]
        U = [self.bf16(T, "U%d" % i) for i in range(2)]
        Yacc = self.f32(T, "Yacc")
        gt = self.bf16(T, "gt")
        cmr, cpr, ncpr = sc_tile("cmr"), sc_tile("cpr"), sc_tile("ncpr")
        self.tt(V, cmr.ap, ci.ap, cr.ap, ALU.subtract, r=[ci, cr], w=[cmr])
        self.tt(V, cpr.ap, ci.ap, cr.ap, ALU.add, r=[ci, cr], w=[cpr])
        self.ts(V, ncpr.ap, cpr.ap, -1.0, ALU.mult, r=[cpr], w=[ncpr])
        lastc = [self.f32(2, "lastc%d" % i) for i in range(2)]
        nsQ = {}
        for Q in (256, 512):
            nsQ[Q] = sc_tile("nsQ%d" % Q)
            self.ts(V, nsQ[Q].ap, carry[Q][0].ap, -1.0, ALU.mult, r=[carry[Q][0]], w=[nsQ[Q]])
        TAB = [{n: self.f32(512, "%s%d" % (n, i)) for n in ("COS", "SIN", "wr", "bma", "apb", "ta", "tb")} for i in range(2)]
        WK = [{n: self.f32(512, "%s%d" % (n, i)) for n in ("k1", "k2", "k3", "pss", "bre", "bim")} for i in range(2)]
        MK = [{n: self.bf16(512, "%s%d" % (n, i)) for n in ("m1", "m2", "m3", "m4")} for i in range(2)]
        init = [self.f32(4, "init%d" % i) for i in range(2)]
        fwd_chunks = [(0, LC)] + [(s, 512) for s in range(LC, T, 512)]
        bwd_chunks = [(0, LC)] + [(T - 512 * (i + 1), 512) for i in range(8)]
        tgc = [0]

        def emit_prep(ft):
            Ui = U[ft % 2]
            self.DMA("sync", Ui.ap, HT.ap[ft * 128:(ft + 1) * 128, :], r=[HT], w=[Ui])
            W = WTS[ft % 2]
            for d in range(2):
                cr_ = craw[d]
                self.DMA("sync", cr_[0].ap, P["c_re"].ap[j, d, ft * 8:(ft + 1) * 8].rearrange("g h n -> (g h) n"), r=[P["c_re"]], w=[cr_[0]])
                self.DMA("sync", cr_[1].ap, P["c_im"].ap[j, d, ft * 8:(ft + 1) * 8].rearrange("g h n -> (g h) n"), r=[P["c_im"]], w=[cr_[1]])
                for k in range(4):
                    p_ = ft * 4 + k
                    wi_ = d * 4 + k
                    c1, c2 = 32 * k, 32 * k + 16
                    for bi, nm in enumerate(("b_re", "b_im")):
                        m1t = M1[(k, nm)]
                        self.cp(G, m1t.ap[0:64, c1:c1 + 16], braw[nm].ap[0:64, d, p_, :], r=[braw[nm]], w=[m1t])
                        self.cp(G, m1t.ap[64:128, c2:c2 + 16], braw[nm].ap[64:128, d, p_, :], r=[braw[nm]], w=[m1t])
                        ps = self.PS[5]
                        self.tr(ps.ap[:, bi * 128:(bi + 1) * 128], m1t.ap, self.ident.ap, r=[m1t, self.ident], w=[ps])
                        self.cp("scalar", W.ap[:, wi_, bi, :], ps.ap[:, bi * 128:(bi + 1) * 128], r=[ps], w=[W])
                    self.tt(G, W.ap[:, wi_, 5, :], W.ap[:, wi_, 0, :], W.ap[:, wi_, 1, :], ALU.add, r=[W], w=[W])
                    for q in range(2):
                        sp = Spair[q]
                        self.ts(G, sp.ap[:, 0:64], cr_[q].ap, self.mask8.ap[:, 2 * k:2 * k + 1], ALU.mult, 0.0, ALU.add, r=[cr_[q], self.mask8], w=[sp])
                        self.ts(G, sp.ap[:, 64:128], cr_[q].ap, self.mask8.ap[:, 2 * k + 1:2 * k + 2], ALU.mult, 0.0, ALU.add, r=[cr_[q], self.mask8], w=[sp])
                        ps = self.PS[5]
                        self.tr(ps.ap[:, 256 + q * 128:256 + (q + 1) * 128], sp.ap, self.ident.ap, r=[sp, self.ident], w=[ps])
                        src = ps.ap[:, 256 + q * 128:256 + (q + 1) * 128]
                        if q == 0:
                            self.cp("scalar", W.ap[:, wi_, 2, :], src, r=[ps], w=[W])
                            self.act(W.ap[:, wi_, 3, :], src, AF.Copy, r=[ps], w=[W], scale=-1.0)
                        else:
                            self.act(W.ap[:, wi_, 4, :], src, AF.Copy, r=[ps], w=[W], scale=-1.0)

        def table_thunks(tab, col):
            th = []
            add = th.append
            MAGIC = 12582912.0
            ti, tf, fr, s2, s4 = tmp2["ti"], tmp2["tf"], tmp2["fr"], tmp2["s2"], tmp2["s4"]
            sc1 = lambda t: t.ap[:, col:col + 1]
            add(lambda: self.act(tab["ta"].ap, Jrow.ap, AF.Identity, r=[Jrow, frac], w=[tab["ta"]], scale=sc1(frac)))
            add(lambda: self.act(tf.ap, tab["ta"].ap, AF.Identity, r=[tab["ta"]], w=[tf], bias=MAGIC))
            add(lambda: self.act(tf.ap, tf.ap, AF.Identity, r=[tf], w=[tf], bias=-MAGIC))
            add(lambda: self.tt(G, fr.ap, tab["ta"].ap, tf.ap, ALU.subtract, r=[tab["ta"], tf], w=[fr]))
            add(lambda: self.act(s2.ap, fr.ap, AF.Sin, r=[fr], w=[s2], scale=float(np.pi)))
            add(lambda: self.act(s4.ap, fr.ap, AF.Sin, r=[fr], w=[s4], scale=float(np.pi / 2)))
            add(lambda: self.act(s4.ap, s4.ap, AF.Square, r=[s4], w=[s4]))
            add(lambda: self.act(s4.ap, s4.ap, AF.Identity, r=[s4], w=[s4], scale=-4.0, bias=2.0))
            add(lambda: self.tt(G, tab["SIN"].ap, s2.ap, s4.ap, ALU.mult, r=[s2, s4], w=[tab["SIN"]]))
            add(lambda: self.act(s2.ap, s2.ap, AF.Square, r=[s2], w=[s2]))
            add(lambda: self.act(tab["COS"].ap, s2.ap, AF.Identity, r=[s2], w=[tab["COS"]], scale=-2.0, bias=1.0))
            for (c1, c2, dst) in ((cr, ci, "wr"), (cmr, ncpr, "bma"), (cpr, cmr, "apb")):
                add(lambda c1=c1: self.act(tab["ta"].ap, tab["COS"].ap, AF.Identity, r=[tab["COS"], c1], w=[tab["ta"]], scale=sc1(c1)))
                add(lambda c2=c2: self.act(tab["tb"].ap, tab["SIN"].ap, AF.Identity, r=[tab["SIN"], c2], w=[tab["tb"]], scale=sc1(c2)))
                add(lambda dst=dst: self.tt(G, tab[dst].ap, tab["ta"].ap, tab["tb"].ap, ALU.add, r=[tab["ta"], tab["tb"]], w=[tab[dst]]))
            return th

        items = []
        pd = 0
        for ft in range(NFT):
            for d in range(2):
                chunks = fwd_chunks if d == 0 else bwd_chunks
                for k in range(4):
                    for cidx, (s, n) in enumerate(chunks):
                        items.append(dict(ft=ft, d=d, k=k, cidx=cidx, s=s, n=n, pd=pd, last=(cidx == len(chunks) - 1),
                                          first_pd=(cidx == 0), first_y=(d == 0 and k == 0),
                                          ft_last=(d == 1 and k == 3 and cidx == len(chunks) - 1)))
                    pd += 1
        pending = []

        def stage_a(i, it_):
            ft, d, k, s, n = it_["ft"], it_["d"], it_["k"], it_["s"], it_["n"]
            if i == 0:
                emit_prep(0)
                for f in table_thunks(TAB[0], 0):
                    f()
            if d == 1 and k == 0 and it_["cidx"] == 0 and ft + 1 < NFT:
                emit_prep(ft + 1)
            if it_["cidx"] == 1 and it_["pd"] + 1 < 64:
                npd = it_["pd"] + 1
                nft, nd, nk = npd // 8, (npd // 4) % 2, npd % 4
                pending.extend(table_thunks(TAB[npd % 2], nd * 32 + nft * 4 + nk))
            if it_["cidx"] >= 1:
                ntake = len(pending) if it_["last"] else min(3, len(pending))
                for _ in range(ntake):
                    pending.pop(0)()
            tab = TAB[it_["pd"] % 2]
            Ui, W, wi_ = U[ft % 2], WTS[ft % 2], d * 4 + k
            wk = WK[i % 2]
            pre, pim, psu = self.PS[0], self.PS[1], self.PS[2]
            urhs = Ui.ap[:, s:s + n] if d == 0 else self.rev_ap(Ui.ap, s, n)
            self.mm(pre.ap[:, 0:n], W.ap[:, wi_, 0, :], urhs, True, True, r=[W, Ui], w=[pre])
            self.mm(pim.ap[:, 0:n], W.ap[:, wi_, 1, :], urhs, True, True, r=[W, Ui], w=[pim])
            self.mm(psu.ap[:, 0:n], W.ap[:, wi_, 5, :], urhs, True, True, r=[W, Ui], w=[psu])
            c_ = lambda t: t.ap[:, 0:n]
            self.cp("scalar", c_(wk["pss"]), psu.ap[:, 0:n], r=[psu], w=[wk["pss"]])
            self.tt(V, c_(wk["k2"]), pre.ap[:, 0:n], c_(tab["bma"]), ALU.mult, r=[pre, tab["bma"]], w=[wk["k2"]])
            self.tt(V, c_(wk["k3"]), pim.ap[:, 0:n], c_(tab["apb"]), ALU.mult, r=[pim, tab["apb"]], w=[wk["k3"]])
            self.tt(G, c_(wk["k1"]), c_(wk["pss"]), c_(tab["wr"]), ALU.mult, r=[wk["pss"], tab["wr"]], w=[wk["k1"]])
            self.tt(G, c_(wk["bre"]), c_(wk["k1"]), c_(wk["k3"]), ALU.subtract, r=[wk["k1"], wk["k3"]], w=[wk["bre"]])
            self.tt(G, c_(wk["bim"]), c_(wk["k1"]), c_(wk["k2"]), ALU.add, r=[wk["k1"], wk["k2"]], w=[wk["bim"]])

        def stage_b(i, it_):
            ft, d, k, s, n, cidx = it_["ft"], it_["d"], it_["k"], it_["s"], it_["n"], it_["cidx"]
            tab = TAB[it_["pd"] % 2]
            col = d * 32 + ft * 4 + k
            Ui, W, wi_ = U[ft % 2], WTS[ft % 2], d * 4 + k
            wk, mk, ini = WK[i % 2], MK[i % 2], init[i % 2]
            py = self.PS[3 + i % 2]
            c_ = lambda t: t.ap[:, 0:n]
            rcol = rr.ap[:, col:col + 1]
            rb = rcol.to_broadcast([128, n])
            if cidx == 0:
                i_re, i_im, ir = 0.0, 0.0, []
            else:
                pini = init[(i - 1) % 2]
                i_re, i_im, ir = pini.ap[:, 0:1], pini.ap[:, 1:2], [pini]
            gre_t, gim_t = self.PS[6], self.PS[7]
            self.E(V, lambda e, o=gre_t.ap[:, 0:n], d1=c_(wk["bre"]), i0=i_re, rb=rb: e.tensor_tensor_scan(out=o, data0=rb, data1=d1, initial=i0, op0=ALU.mult, op1=ALU.add),
                   r=[wk["bre"], rr] + ir, w=[gre_t])
            self.E(V, lambda e, o=gim_t.ap[:, 0:n], d1=c_(wk["bim"]), i0=i_im, rb=rb: e.tensor_tensor_scan(out=o, data0=rb, data1=d1, initial=i0, op0=ALU.mult, op1=ALU.add),
                   r=[wk["bim"], rr] + ir, w=[gim_t])
            if not it_["last"]:
                sQ, cQ = carry[n]
                sq_, cq_, nsq_ = sQ.ap[:, col:col + 1], cQ.ap[:, col:col + 1], nsQ[n].ap[:, col:col + 1]
                lc = lastc[i % 2]
                self.cp(V, lc.ap[:, 0:1], gre_t.ap[:, n - 1:n], r=[gre_t], w=[lc])
                self.cp(V, lc.ap[:, 1:2], gim_t.ap[:, n - 1:n], r=[gim_t], w=[lc])
                gre_l, gim_l = lc.ap[:, 0:1], lc.ap[:, 1:2]
                self.act(ini.ap[:, 2:3], gim_l, AF.Identity, r=[lc, nsQ[n]], w=[ini], scale=nsq_)
                self.act(ini.ap[:, 3:4], gim_l, AF.Identity, r=[lc, cQ], w=[ini], scale=cq_)
                self.act(ini.ap[:, 0:1], gre_l, AF.Identity, r=[lc, cQ, ini], w=[ini], scale=cq_, bias=ini.ap[:, 2:3])
                self.act(ini.ap[:, 1:2], gre_l, AF.Identity, r=[lc, sQ, ini], w=[ini], scale=sq_, bias=ini.ap[:, 3:4])
            self.tt(V, c_(mk["m1"]), gre_t.ap[:, 0:n], c_(tab["COS"]), ALU.mult, r=[gre_t, tab["COS"]], w=[mk["m1"]])
            self.tt(V, c_(mk["m3"]), gre_t.ap[:, 0:n], c_(tab["SIN"]), ALU.mult, r=[gre_t, tab["SIN"]], w=[mk["m3"]])
            self.tt(V, c_(mk["m2"]), gim_t.ap[:, 0:n], c_(tab["SIN"]), ALU.mult, r=[gim_t, tab["SIN"]], w=[mk["m2"]])
            self.tt(V, c_(mk["m4"]), gim_t.ap[:, 0:n], c_(tab["COS"]), ALU.mult, r=[gim_t, tab["COS"]], w=[mk["m4"]])
            for mi, (mn, wsel) in enumerate((("m1", 2), ("m2", 3), ("m3", 4), ("m4", 4))):
                mr = mk[mn].ap[:, 0:n] if d == 0 else self.rev_ap(mk[mn].ap, 0, n)
                self.mm(py.ap[:, 0:n], W.ap[:, wi_, wsel, :], mr, mi == 0, mi == 3, r=[W, mk[mn]], w=[py])
            if it_["first_y"]:
                self.cp("scalar", Yacc.ap[:, s:s + n], py.ap[:, 0:n], r=[py], w=[Yacc])
            else:
                self.tt(V, Yacc.ap[:, s:s + n], py.ap[:, 0:n], Yacc.ap[:, s:s + n], ALU.add, r=[py, Yacc], w=[Yacc])
            if it_["ft_last"]:
                self.stt(Yacc.ap, Ui.ap, dsk.ap[:, ft:ft + 1], Yacc.ap, ALU.mult, ALU.add, r=[Ui, dsk, Yacc], w=[Yacc])
                self.act(gt.ap, Yacc.ap, AF.Gelu_apprx_tanh, r=[Yacc], w=[gt])
                self.DMA("sync", GT.ap[ft * 128:(ft + 1) * 128, :], gt.ap, r=[gt], w=[GT])

        stage_a(0, items[0])
        for i in range(len(items)):
            if i + 1 < len(items):
                stage_a(i + 1, items[i + 1])
            stage_b(i, items[i])

    def proj_phase(self, XT, IN, Wd, w_ap, nkt, l, glu_bias=None, Wt=512):
        self.new_phase()
        ws = self.bf16(nkt * D, "pw", shape=(nkt, D))
        self.load_w_bf16(ws, w_ap, Wd, nkt)
        bg = None
        if glu_bias is not None:
            bg = self.f32(8, "bglu")
            self.DMA("sync", bg.ap, glu_bias[1].rearrange("(ft p) -> p ft", p=128), r=[glu_bias[0]], w=[bg], slow=True)
        ib = [self.bf16(nkt * Wt, "pi%d" % i, shape=(nkt, Wt)) for i in range(2)]
        xb = [self.f32(NFT * Wt, "px%d" % i, shape=(NFT, Wt)) for i in range(2)]
        sg = [self.f32(Wt, "psg%d" % i) for i in range(2)]
        mod = self.mod[l]
        chs = self.chunks(Wt)

        def load(ci):
            s, w, isctx = chs[ci]
            self.DMA("sync", ib[ci % 2].ap[:, :, 0:w], IN.ap[:, s:s + w].rearrange("(kt p) t -> p kt t", p=128), r=[IN], w=[ib[ci % 2]])
            self.DMA("sync", xb[ci % 2].ap[:, :, 0:w], XT.ap[:, s:s + w].rearrange("(ft p) t -> p ft t", p=128), r=[XT], w=[xb[ci % 2]])

        load(0)
        for ci, (s, w, isctx) in enumerate(chs):
            cs = 1 if isctx else 0
            ibi, xbi = ib[ci % 2], xb[ci % 2]
            if ci + 1 < len(chs):
                load(ci + 1)
            for fo in range(NFT):
                po = self.PS[fo % 4]
                for kt in range(nkt):
                    self.mm(po.ap[:, 0:w], ws.ap[:, kt, fo * 128:(fo + 1) * 128], ibi.ap[:, kt, 0:w], kt == 0, kt == nkt - 1, r=[ws, ibi], w=[po])
                gate = mod.ap[:, 2, fo, cs:cs + 1]
                if glu_bias is not None:
                    sgi = sg[fo % 2]
                    self.act(sgi.ap[:, 0:w], po.ap[:, 0:w], AF.Sigmoid, r=[po, bg], w=[sgi], bias=bg.ap[:, fo:fo + 1])
                    self.tt("gpsimd", sgi.ap[:, 0:w], sgi.ap[:, 0:w], ibi.ap[:, fo, 0:w], ALU.mult, r=[sgi, ibi], w=[sgi])
                    self.stt(xbi.ap[:, fo, 0:w], sgi.ap[:, 0:w], gate, xbi.ap[:, fo, 0:w], ALU.mult, ALU.add, r=[sgi, mod, xbi], w=[xbi])
                else:
                    self.stt(xbi.ap[:, fo, 0:w], po.ap[:, 0:w], gate, xbi.ap[:, fo, 0:w], ALU.mult, ALU.add, r=[po, mod, xbi], w=[xbi])
            self.DMA("sync", XT.ap[:, s:s + w].rearrange("(ft p) t -> p ft t", p=128), xbi.ap[:, :, 0:w], r=[xbi], w=[XT])


    def row_bcast(self, dst, row_ap, src_tl, n, rowtmp, func=None, scale=None):
        self.DMA("sync", rowtmp.ap[0:1, 0:n], row_ap, r=[src_tl], w=[rowtmp])
        for c0 in range(0, n, 512):
            w = min(512, n - c0)
            ps = self.PS[7]
            self.mm(ps.ap[:, 0:w], self.ones.ap[0:1, :], rowtmp.ap[0:1, c0:c0 + w], True, True, r=[self.ones, rowtmp], w=[ps])
            if func is None:
                self.cp("vector", dst.ap[:, c0:c0 + w], ps.ap[:, 0:w], r=[ps], w=[dst])
            else:
                self.act(dst.ap[:, c0:c0 + w], ps.ap[:, 0:w], func, r=[ps], w=[dst], scale=scale)

    def tri_mask(self, nm, base, cm, step, op):
        t = self.f32(128, nm)
        self.memset("gpsimd", t.ap, 1.0, [t])
        self.E("gpsimd", lambda e: e.affine_select(out=t.ap, in_=t.ap, pattern=[[step, 128]], compare_op=op, fill=0.0, base=base, channel_multiplier=cm), [t], [t])
        return t

    def ssd_phase_a(self, HT, P, SZ, XS, BC, DTR):
        self.new_phase()
        w_in = P["w_in"]
        Hres = self.bf16(8 * T, "Hres", shape=(8, T))
        for kt in range(8):
            self.DMA("sync", Hres.ap[:, kt, :], HT.ap[kt * 128:(kt + 1) * 128, :], r=[HT], w=[Hres])
        wz = self.bf16(8 * 2048, "wz", shape=(8, 2048))
        self.load_w_bf16(wz, w_in.ap[0], w_in, 8, 0, 2048)
        szb = [self.bf16(2048, "szb%d" % i) for i in range(2)]
        for tt in range(T // 128):
            sb = szb[tt % 2]
            for zc in range(4):
                ps = self.PS[zc]
                for kt in range(8):
                    self.mm(ps.ap, Hres.ap[:, kt, tt * 128:(tt + 1) * 128], wz.ap[:, kt, zc * 512:(zc + 1) * 512], kt == 0, kt == 7, r=[Hres, wz], w=[ps])
                self.act(sb.ap[:, zc * 512:(zc + 1) * 512], ps.ap, AF.Silu, r=[ps], w=[sb])
            self.DMA("sync", SZ.ap[tt * 128:(tt + 1) * 128, :], sb.ap, r=[sb], w=[SZ])
        self.new_phase()
        Hres = self.bf16(8 * T, "Hres2", shape=(8, T))
        for kt in range(8):
            self.DMA("sync", Hres.ap[:, kt, :], HT.ap[kt * 128:(kt + 1) * 128, :], r=[HT], w=[Hres])
        cw = self.f32(5 * 48, "cw", shape=(5, 48))
        cb = self.f32(48, "cb")
        for k in range(5):
            self.DMA("sync", cw.ap[:, k, :], P["conv_w"].ap[0, k].rearrange("(ot p) -> p ot", p=128), r=[P["conv_w"]], w=[cw], slow=True)
        self.DMA("sync", cb.ap, P["conv_b"].ap[0].rearrange("(ot p) -> p ot", p=128), r=[P["conv_b"]], w=[cb], slow=True)
        wch = [self.bf16(8 * 512, "wch%d" % i, shape=(8, 512)) for i in range(2)]
        xp = [self.bf16(T, "xp%d" % i) for i in range(2)]
        ob = [self.bf16(T, "ob%d" % i) for i in range(2)]
        dg = [self.bf16(5 * 128, "dg%d" % i, shape=(5, 128)) for i in range(2)]
        chunks = self.chunks()
        pi = 0
        for wc in range(12):
            wb = wch[wc % 2]
            self.load_w_bf16(wb, w_in.ap[0], w_in, 8, 2048 + wc * 512, 2048 + (wc + 1) * 512)
            for q in range(4):
                ot = wc * 4 + q
                xpi, obi, dgi = xp[ot % 2], ob[ot % 2], dg[ot % 2]
                for k in range(5):
                    self.ts("gpsimd", dgi.ap[:, k, :], self.identb.ap, cw.ap[:, k, ot:ot + 1], ALU.mult, 0.0, ALU.add, r=[self.identb, cw], w=[dgi])
                for ci, (s, w, isctx) in enumerate(chunks):
                    ps = self.PS[pi % 4]
                    pi += 1
                    for kt in range(8):
                        self.mm(ps.ap[:, 0:w], wb.ap[:, kt, q * 128:(q + 1) * 128], Hres.ap[:, kt, s:s + w], kt == 0, kt == 7, r=[wb, Hres], w=[ps])
                    self.cp("vector" if ci % 2 else "scalar", xpi.ap[:, s:s + w], ps.ap[:, 0:w], r=[ps], w=[xpi])
                for ci, (s, w, isctx) in enumerate(chunks):
                    q0, q1 = (0, LC) if isctx else (LC, T)
                    ps = self.PS[4 + ci % 4]
                    for ki, k in enumerate((2, 0, 1, 3, 4)):
                        o = k - 2
                        i0 = max(0, q0 - s - o)
                        i1 = min(w, q1 - s - o)
                        self.mm(ps.ap[:, i0:i1], dgi.ap[:, k, :], xpi.ap[:, s + i0 + o:s + i1 + o], ki == 0, ki == 4, r=[dgi, xpi], w=[ps])
                    self.act(obi.ap[:, s:s + w], ps.ap[:, 0:w], AF.Silu, r=[ps, cb], w=[obi], bias=cb.ap[:, ot:ot + 1])
                dst = XS.ap[ot * 128:(ot + 1) * 128, :] if ot < 16 else BC.ap[(ot - 16) * 128:(ot - 15) * 128, :]
                self.DMA("sync", dst, obi.ap, r=[obi], w=[XS if ot < 16 else BC])
        wdt = self.bf16(8 * 64, "wdt", shape=(8, 64))
        self.load_w_bf16(wdt, w_in.ap[0], w_in, 8, 8192, 8256)
        dtb = self.f32(T, "dtb")
        for ci, (s, w, isctx) in enumerate(chunks):
            ps = self.PS[ci % 4]
            for kt in range(8):
                self.mm(ps.ap[0:64, 0:w], wdt.ap[:, kt, :], Hres.ap[:, kt, s:s + w], kt == 0, kt == 7, r=[wdt, Hres], w=[ps])
            self.cp("vector", dtb.ap[0:64, s:s + w], ps.ap[0:64, 0:w], r=[ps], w=[dtb])
        self.DMA("sync", DTR.ap, dtb.ap[0:64, :], r=[dtb], w=[DTR])

    def ssd_phase_b(self, P, XS, BC, DTR, YF, YB):
        self.new_phase()
        V, G = "vector", "gpsimd"
        GT_ = self.tri_mask("mGT", 0, 1, -1, ALU.is_gt)
        LE_ = self.tri_mask("mLE", 0, -1, 1, ALU.is_ge)
        LT_ = self.tri_mask("mLT", 0, -1, 1, ALU.is_gt)
        GE_ = self.tri_mask("mGE", 0, 1, -1, ALU.is_ge)
        rowtmp = self.f32(64, "rowtmp")
        Arow = [self.f32(32, "Arow%d" % d) for d in range(2)]
        Brow = [self.f32(32, "Brow%d" % d) for d in range(2)]
        Drow = self.f32(32, "Drow")
        for d in range(2):
            self.row_bcast(Arow[d], P["a_log"].ap[0, d:d + 1, :], P["a_log"], 32, rowtmp, func=AF.Exp)
            self.ts(V, Arow[d].ap, Arow[d].ap, -1.0, ALU.mult, r=[Arow[d]], w=[Arow[d]])
            self.row_bcast(Brow[d], P["dt_bias"].ap[0, d:d + 1, :], P["dt_bias"], 32, rowtmp)
        self.row_bcast(Drow, P["d"].ap[0:1, :], P["d"], 32, rowtmp)
        H = self.f32(2048, "Hst", shape=(32, 64))
        Hb = self.bf16(2048, "Hb")
        Hg = [Tl(H.ap, Buf("Hg%d" % g)) for g in range(8)]
        Hbg = [Tl(Hb.ap, Buf("Hbg%d" % g)) for g in range(8)]
        NB = 2
        xsT = [self.bf16(16 * 128, "xsT%d" % i, shape=(16, 128)) for i in range(NB)]
        BTt = [self.bf16(8 * 128, "BT%d" % i, shape=(8, 128)) for i in range(NB)]
        CTt = [self.bf16(8 * 128, "CT%d" % i, shape=(8, 128)) for i in range(NB)]
        dtr = [self.f32(128, "dtr%d" % i) for i in range(NB)]
        ybuf = [self.bf16(2048, "ybuf%d" % i) for i in range(NB)]
        xdt = [self.bf16(2048, "xdt%d" % i, shape=(32, 64)) for i in range(NB)]
        xdte = [self.bf16(2048, "xdte%d" % i, shape=(32, 64)) for i in range(NB)]
        dskt = [self.f32(2048, "dskt%d" % i, shape=(32, 64)) for i in range(NB)]
        Btok = [self.bf16(1024, "Btok%d" % i, shape=(8, 128)) for i in range(NB)]
        dtt = [self.f32(32, "dtt%d" % i) for i in range(NB)]
        adt = [self.f32(32, "adt%d" % i) for i in range(NB)]
        dex = [self.f32(96, "dex%d" % i) for i in range(NB)]
        dtd = [self.f32(32, "dtd%d" % i) for i in range(NB)]
        Xh = [self.f32(512, "Xh%d" % i, shape=(4, 128)) for i in range(2)]
        Ld = [self.bf16(512, "Ld%d" % i, shape=(4, 128)) for i in range(2)]
        Mt = [self.bf16(512, "Mt%d" % i, shape=(4, 128)) for i in range(2)]
        CBm = [self.bf16(128, "CBm%d" % i) for i in range(2)]
        ytmp = [self.f32(256, "ytmp%d" % i, shape=(4, 64)) for i in range(2)]
        htmp = [self.f32(256, "htmp%d" % i, shape=(4, 64)) for i in range(2)]
        nchunk = T // 128
        work = []
        for d in range(2):
            order = list(range(nchunk)) if d == 0 else [1, 0] + list(range(nchunk - 1, 1, -1))
            for oi, c in enumerate(order):
                work.append((d, c, oi == 0))
        cfgd = {0: dict(mX=GT_, mE=LE_, mTE=GT_, mSeg=LE_, mCB=LE_), 1: dict(mX=LT_, mE=GE_, mTE=LT_, mSeg=GE_, mCB=GE_)}

        def prologue(wi):
            d, c, first = work[wi]
            cf = cfgd[d]
            b = wi % NB
            s0 = c * 128
            xi, bi, cti, dri = xsT[b], BTt[b], CTt[b], dtr[b]
            self.DMA("sync", xi.ap, XS.ap[:, s0:s0 + 128].rearrange("(t p) c -> p t c", p=128), r=[XS], w=[xi])
            self.DMA("sync", bi.ap, BC.ap[(d * 2) * 1024:(d * 2 + 1) * 1024, s0:s0 + 128].rearrange("(t p) c -> p t c", p=128), r=[BC], w=[bi])
            self.DMA("sync", cti.ap, BC.ap[(d * 2 + 1) * 1024:(d * 2 + 2) * 1024, s0:s0 + 128].rearrange("(t p) c -> p t c", p=128), r=[BC], w=[cti])
            self.DMA("sync", dri.ap[0:32, :], DTR.ap[d * 32:(d + 1) * 32, s0:s0 + 128], r=[DTR], w=[dri])
            psm = self.PS[6]
            dtt_, adt_, dex_, dtd_ = dtt[b], adt[b], dex[b], dtd[b]
            self.tr(psm.ap[:, 0:32], dri.ap[0:32, :], self.ident.ap[0:32, 0:32], r=[dri, self.ident], w=[psm])
            self.tt(V, dtt_.ap, psm.ap[:, 0:32], Brow[d].ap, ALU.add, r=[psm, Brow[d]], w=[dtt_])
            self.act(dtt_.ap, dtt_.ap, AF.Exp, r=[dtt_], w=[dtt_])
            self.act(dtt_.ap, dtt_.ap, AF.Ln, r=[dtt_, self.epsc], w=[dtt_], bias=self.epsc.ap[:, 2:3])
            self.tt(V, adt_.ap, dtt_.ap, Arow[d].ap, ALU.mult, r=[dtt_, Arow[d]], w=[adt_])
            self.mm(psm.ap[:, 32:64], self.ones.ap, adt_.ap, True, True, r=[self.ones, adt_], w=[psm])
            self.mm(psm.ap[:, 64:96], cf["mTE"].ap, adt_.ap, True, True, r=[cf["mTE"], adt_], w=[psm])
            self.mm(psm.ap[:, 96:128], cf["mE"].ap, adt_.ap, True, True, r=[cf["mE"], adt_], w=[psm])
            self.act(dex_.ap, psm.ap[:, 32:128], AF.Exp, r=[psm], w=[dex_])
            self.tt(V, dtd_.ap, dtt_.ap, dex_.ap[:, 32:64], ALU.mult, r=[dtt_, dex_], w=[dtd_])
            pst = self.PS[7]
            psb16 = self.PSB[7]
            for half in range(2):
                for q in range(8):
                    t_ = half * 8 + q
                    self.tr(psb16[:, q * 128:(q + 1) * 128], xi.ap[:, t_, :], self.identb.ap, r=[xi, self.identb], w=[pst])
                src = psb16[:, 0:1024].rearrange("p (h c) -> p h c", c=64)
                hs = slice(half * 16, (half + 1) * 16)
                bc = lambda col: col.unsqueeze(2).to_broadcast([128, 16, 64])
                self.tt(V, xdt[b].ap[:, hs, :], src, bc(dtt_.ap[:, hs]), ALU.mult, r=[pst, dtt_], w=[xdt[b]])
                self.tt(V, xdte[b].ap[:, hs, :], src, bc(dtd_.ap[:, hs]), ALU.mult, r=[pst, dtd_], w=[xdte[b]])
                if d == 0:
                    self.tt(V, dskt[b].ap[:, hs, :], src, bc(Drow.ap[:, hs]), ALU.mult, r=[pst, Drow], w=[dskt[b]])
            for g in range(8):
                self.tr(psb16[:, g * 128:(g + 1) * 128], bi.ap[:, g, :], self.identb.ap, r=[bi, self.identb], w=[pst])
            self.cp("scalar", Btok[b].ap, psb16[:, 0:1024].rearrange("p (g c) -> p g c", c=128), r=[pst], w=[Btok[b]])

        gi = [0]

        def s1(wi, g):
            d, c, first = work[wi]
            cf = cfgd[d]
            b = wi % NB
            k = (wi * 8 + g) % 2
            xh = Xh[k]
            for hh in range(4):
                h = g * 4 + hh
                self.ts(G, xh.ap[:, hh, :], cf["mX"].ap, adt[b].ap[:, h:h + 1], ALU.mult, 0.0, ALU.add, r=[cf["mX"], adt[b]], w=[xh])
            pcb = self.PS[k]
            self.mm(pcb.ap[:, 0:128], BTt[b].ap[:, g, :], CTt[b].ap[:, g, :], True, True, r=[BTt[b], CTt[b]], w=[pcb])
            psg = self.PS[4 + k]
            for hh in range(4):
                self.mm(psg.ap[:, hh * 128:(hh + 1) * 128], xh.ap[:, hh, :], cf["mSeg"].ap, True, True, r=[xh, cf["mSeg"]], w=[psg])

        def s2(wi, g):
            d, c, first = work[wi]
            cf = cfgd[d]
            k = (wi * 8 + g) % 2
            pcb, psg = self.PS[k], self.PS[4 + k]
            self.act(Ld[k].ap, psg.ap.rearrange("p (h c) -> p h c", c=128), AF.Exp, r=[psg], w=[Ld[k]])
            self.tt(V, CBm[k].ap, pcb.ap[:, 0:128], cf["mCB"].ap, ALU.mult, r=[pcb, cf["mCB"]], w=[CBm[k]])
            self.tt(V, Mt[k].ap, Ld[k].ap, CBm[k].ap.unsqueeze(1).to_broadcast([128, 4, 128]), ALU.mult, r=[Ld[k], CBm[k]], w=[Mt[k]])

        def s3(wi, g):
            d, c, first = work[wi]
            b = wi % NB
            k = (wi * 8 + g) % 2
            py = self.PS[2 + k]
            cti = CTt[b]
            for hh in range(4):
                h = g * 4 + hh
                self.mm(py.ap[:, hh * 64:(hh + 1) * 64], Mt[k].ap[:, hh, :], xdt[b].ap[:, h, :], True, True, r=[Mt[k], xdt[b]], w=[py])
                self.mm(py.ap[:, 256 + hh * 64:256 + (hh + 1) * 64], cti.ap[:, g, :], Hb.ap[:, h * 64:(h + 1) * 64], True, True, r=[cti, Hbg[g]], w=[py])
            gs = slice(g * 4, (g + 1) * 4)
            yt = ytmp[k]
            yb = ybuf[b]
            Eb = dex[b].ap[:, 64 + g * 4:64 + (g + 1) * 4].unsqueeze(2).to_broadcast([128, 4, 64])
            self.tt(V, yt.ap, py.ap[:, 256:512].rearrange("p (h c) -> p h c", c=64), Eb, ALU.mult, r=[py, dex[b]], w=[yt])
            if d == 0:
                self.tt(G, yt.ap, yt.ap, dskt[b].ap[:, gs, :], ALU.add, r=[yt, dskt[b]], w=[yt])
            self.tt(V, yb.ap[:, g * 256:(g + 1) * 256].rearrange("p (h c) -> p h c", c=64), py.ap[:, 0:256].rearrange("p (h c) -> p h c", c=64), yt.ap, ALU.add, r=[py, yt], w=[yb])
            pst2 = self.PS[6]
            self.mm(pst2.ap[:, 256:512], Btok[b].ap[:, g, :], xdte[b].ap[:, gs, :].rearrange("p h c -> p (h c)"), True, True, r=[Btok[b], xdte[b]], w=[pst2])
            ht = htmp[k]
            Db = dex[b].ap[:, g * 4:(g + 1) * 4].unsqueeze(2).to_broadcast([128, 4, 64])
            self.tt(G, ht.ap, H.ap[:, gs, :], Db, ALU.mult, r=[Hg[g], dex[b]], w=[ht])
            self.tt(V, H.ap[:, gs, :], pst2.ap[:, 256:512].rearrange("p (h c) -> p h c", c=64), ht.ap, ALU.add, r=[pst2, ht], w=[Hg[g]])
            self.cp("scalar", Hb.ap[:, g * 256:(g + 1) * 256], H.ap[:, gs, :].rearrange("p h c -> p (h c)"), r=[Hg[g]], w=[Hbg[g]])

        nw = len(work)
        prologue(0)
        s1(0, 0)
        for wi in range(nw):
            d, c, first = work[wi]
            if first:
                self.memset(V, H.ap, 0.0, Hg)
                self.memset(V, Hb.ap, 0.0, Hbg)
            for g in range(8):
                if g == 4 and wi + 1 < nw:
                    prologue(wi + 1)
                if g + 1 < 8:
                    s1(wi, g + 1)
                elif wi + 1 < nw:
                    s1(wi + 1, 0)
                s2(wi, g)
                s3(wi, g)
            YO = YF if d == 0 else YB
            self.DMA("sync", YO.ap[c * 128:c * 128 + 128, :], ybuf[wi % NB].ap, r=[ybuf[wi % NB]], w=[YO])

    def ssd_phase_c(self, XT, P, SZ, YF, YB, l):
        self.new_phase()
        V, G = "vector", "gpsimd"
        wo = self.bf16(16 * D, "wo", shape=(16, D))
        self.load_w_bf16(wo, P["w_out"].ap[0], P["w_out"], 16)
        nrow = self.f32(2048, "nrow")
        rowtmp = self.f32(2048, "rowtmp2")
        self.row_bcast(nrow, P["norm"].ap[0:1, :], P["norm"], 2048, rowtmp)
        yf = [self.bf16(2048, "cyf%d" % i) for i in range(2)]
        ybk = [self.bf16(2048, "cyb%d" % i) for i in range(2)]
        sz = [self.bf16(2048, "csz%d" % i) for i in range(2)]
        yy = [self.f32(2048, "cyy%d" % i) for i in range(2)]
        sqj = self.f32(2048, "csq")
        gnb = [self.bf16(2048, "cgn%d" % i) for i in range(2)]
        ss = [self.f32(8, "css%d" % i) for i in range(2)]
        gnT = [self.bf16(16 * 512, "gnT%d" % i, shape=(16, 512)) for i in range(2)]
        xb = [self.f32(NFT * 512, "cx%d" % i, shape=(NFT, 512)) for i in range(2)]
        mod = self.mod[l]
        ti = 0
        chs = self.chunks()

        def load_tile(tix):
            t0 = tix * 128
            self.DMA("sync", yf[tix % 2].ap, YF.ap[t0:t0 + 128, :], r=[YF], w=[yf[tix % 2]])
            self.DMA("sync", ybk[tix % 2].ap, YB.ap[t0:t0 + 128, :], r=[YB], w=[ybk[tix % 2]])
            self.DMA("sync", sz[tix % 2].ap, SZ.ap[t0:t0 + 128, :], r=[SZ], w=[sz[tix % 2]])

        def load_x(ci):
            s, w, isctx = chs[ci]
            self.DMA("sync", xb[ci % 2].ap[:, :, 0:w], XT.ap[:, s:s + w].rearrange("(ft p) t -> p ft t", p=128), r=[XT], w=[xb[ci % 2]])

        load_tile(0)
        load_x(0)
        for ci, (s, w, isctx) in enumerate(chs):
            cs = 1 if isctx else 0
            gT, xbi = gnT[ci % 2], xb[ci % 2]
            if ci + 1 < len(chs):
                load_x(ci + 1)
            for tt in range(w // 128):
                t0 = s + tt * 128
                a, b, z_, y_, g_, s_ = yf[ti % 2], ybk[ti % 2], sz[ti % 2], yy[ti % 2], gnb[ti % 2], ss[ti % 2]
                ti += 1
                if ti < T // 128:
                    load_tile(ti)
                self.tt(G, y_.ap, a.ap, b.ap, ALU.add, r=[a, b], w=[y_])
                self.tt(V, y_.ap, y_.ap, z_.ap, ALU.mult, r=[y_, z_], w=[y_])
                self.act(sqj.ap, y_.ap, AF.Square, r=[y_], w=[sqj, s_], accum=s_.ap[:, 0:1])
                self.act(s_.ap[:, 1:2], s_.ap[:, 0:1], AF.Sqrt, r=[s_, self.epsc], w=[s_], scale=1.0 / 2048, bias=self.epsc.ap[:, 0:1])
                self.E(V, lambda e, o=s_.ap[:, 2:3], i=s_.ap[:, 1:2]: e.reciprocal(out=o, in_=i), r=[s_], w=[s_])
                self.stt(g_.ap, y_.ap, s_.ap[:, 2:3], nrow.ap, ALU.mult, ALU.mult, r=[y_, s_, nrow], w=[g_])
                for half in range(2):
                    pst = self.PS[4 + (ti * 2 + half) % 4]
                    psb16 = self.PSB[4 + (ti * 2 + half) % 4]
                    for q in range(8):
                        kt = half * 8 + q
                        self.tr(psb16[:, q * 128:(q + 1) * 128], g_.ap[:, kt * 128:(kt + 1) * 128], self.identb.ap, r=[g_, self.identb], w=[pst])
                    self.cp("scalar" if half else "vector", gT.ap[:, half * 8:(half + 1) * 8, tt * 128:(tt + 1) * 128],
                            psb16[:, 0:1024].rearrange("p (k c) -> p k c", c=128), r=[pst], w=[gT])
            for fo in range(NFT):
                po = self.PS[fo % 4]
                for kt in range(16):
                    self.mm(po.ap[:, 0:w], wo.ap[:, kt, fo * 128:(fo + 1) * 128], gT.ap[:, kt, 0:w], kt == 0, kt == 15, r=[wo, gT], w=[po])
                self.stt(xbi.ap[:, fo, 0:w], po.ap[:, 0:w], mod.ap[:, 2, fo, cs:cs + 1], xbi.ap[:, fo, 0:w], ALU.mult, ALU.add, r=[po, mod, xbi], w=[xbi])
            self.DMA("sync", XT.ap[:, s:s + w].rearrange("(ft p) t -> p ft t", p=128), xbi.ap[:, :, 0:w], r=[xbi], w=[XT])


    def na_bias_table(self, rpb, BTD):
        self.new_phase()
        neg = self.f32(7680, "negt")
        self.memset("vector", neg.ap, -30000.0, [neg])
        self.DMA("sync", BTD.ap.rearrange("h w r c -> (h w r c)").rearrange("(p n) -> p n", p=128), neg.ap, r=[neg], w=[BTD])
        HS = 64 * 15 * 64
        for h in range(16):
            so = h * 465
            do = h * HS
            dst = bass.AP(BTD.ap.tensor, do + 8 * 960, [[64, 15], [961, 49], [1, 16]])
            src = bass.AP(rpb.ap.tensor, so + 7, [[31, 15], [0, 49], [1, 16]])
            self.DMA("sync", dst, src, r=[rpb], w=[BTD], slow=True)
            dst = bass.AP(BTD.ap.tensor, do, [[64, 15], [960, 8], [1, 16]])
            src = bass.AP(rpb.ap.tensor, so + 15, [[31, 15], [-1, 8], [1, 16]])
            self.DMA("sync", dst, src, r=[rpb], w=[BTD], slow=True)
            dst = bass.AP(BTD.ap.tensor, do + 57 * 960 + 48, [[64, 15], [960, 7], [1, 16]])
            src = bass.AP(rpb.ap.tensor, so + 6, [[31, 15], [-1, 7], [1, 16]])
            self.DMA("sync", dst, src, r=[rpb], w=[BTD], slow=True)

    def na_phase_a(self, HT, wqkv, QT, KT, VT):
        self.new_phase()
        Hres = self.bf16(8 * T, "Hres", shape=(8, T))
        for kt in range(8):
            self.DMA("sync", Hres.ap[:, kt, :], HT.ap[kt * 128:(kt + 1) * 128, :], r=[HT], w=[Hres])
        wv = self.bf16(8 * 1024, "wv", shape=(8, 1024))
        self.load_w_bf16(wv, wqkv.ap[0], wqkv, 8, 2048, 3072)
        vb = [self.bf16(1024, "vb%d" % i) for i in range(2)]
        for tt in range(T // 128):
            v_ = vb[tt % 2]
            for vc in range(2):
                ps = self.PS[vc + 2 * (tt % 2)]
                for kt in range(8):
                    self.mm(ps.ap, Hres.ap[:, kt, tt * 128:(tt + 1) * 128], wv.ap[:, kt, vc * 512:(vc + 1) * 512], kt == 0, kt == 7, r=[Hres, wv], w=[ps])
                self.cp("scalar" if vc else "vector", v_.ap[:, vc * 512:(vc + 1) * 512], ps.ap, r=[ps], w=[v_])
            self.DMA("sync", VT.ap[tt * 128:(tt + 1) * 128, :], v_.ap, r=[v_], w=[VT])
        wch = [self.bf16(8 * 512, "wq%d" % i, shape=(8, 512)) for i in range(2)]
        ob = [self.bf16(T, "qo%d" % i) for i in range(2)]
        chunks = self.chunks()
        pi = 0
        for wc in range(4):
            wb = wch[wc % 2]
            self.load_w_bf16(wb, wqkv.ap[0], wqkv, 8, wc * 512, (wc + 1) * 512)
            for q in range(4):
                ot = wc * 4 + q
                obi = ob[ot % 2]
                for ci, (s, w, isctx) in enumerate(chunks):
                    ps = self.PS[4 + pi % 4]
                    pi += 1
                    for kt in range(8):
                        self.mm(ps.ap[:, 0:w], wb.ap[:, kt, q * 128:(q + 1) * 128], Hres.ap[:, kt, s:s + w], kt == 0, kt == 7, r=[wb, Hres], w=[ps])
                    if ot < 8:
                        self.act(obi.ap[:, s:s + w], ps.ap[:, 0:w], AF.Copy, r=[ps], w=[obi], scale=0.125)
                    else:
                        self.cp("vector", obi.ap[:, s:s + w], ps.ap[:, 0:w], r=[ps], w=[obi])
                dst = QT.ap[ot * 128:(ot + 1) * 128, :] if ot < 8 else KT.ap[(ot - 8) * 128:(ot - 7) * 128, :]
                self.DMA("sync", dst, obi.ap, r=[obi], w=[QT if ot < 8 else KT])

    def na_phase_b(self, QT, KT, VT, BTD, YT):
        self.new_phase()
        V, G = "vector", "gpsimd"
        Qp = [self.bf16(T, "Qp%d" % i) for i in range(2)]
        Kp = [self.bf16(T, "Kp%d" % i) for i in range(2)]
        Ve = [self.bf16(34 * 128, "Ve%d" % i, shape=(34, 128)) for i in range(2)]
        Vo = [self.bf16(33 * 128, "Vo%d" % i, shape=(33, 128)) for i in range(2)]
        BTt = [self.f32(960, "BTt%d" % i) for i in range(2)]
        YTp = [self.bf16(T, "YTp%d" % i) for i in range(2)]
        NBLK = 68
        Qbd = [self.bf16(NBLK * 128, "Qbd%d" % i, shape=(NBLK, 128)) for i in range(2)]
        for q_ in Qbd:
            self.memset(G, q_.ap, 0.0, [q_])
        NP = 4
        sc = [self.f32(768, "sc%d" % i) for i in range(NP)]
        pe = [self.bf16(768, "pe%d" % i) for i in range(NP)]
        pn = [self.bf16(768, "pn%d" % i) for i in range(NP)]
        pT = [self.bf16(768, "pT%d" % i, shape=(6, 128)) for i in range(NP)]
        st = [self.f32(8, "st%d" % i) for i in range(NP)]
        blocks = []
        for hp in range(8):
            bl = [("c", i) for i in range(4)] + [("l", r) for r in range(64)]
            for bi, (kind, idx) in enumerate(bl):
                blocks.append((hp, kind, idx, bi == 0, bi == len(bl) - 1))

        def geom(kind, idx):
            if kind == "c":
                return idx * 64, 256, 0, 0
            r = idx
            start = min(max(r - 4, 0), 56)
            return LC + r * 64, 768, start - r + 7, LC + start * 64

        def load_pair(hp):
            h2 = hp % 2
            self.DMA("sync", Qp[h2].ap, QT.ap[hp * 128:(hp + 1) * 128, :], r=[QT], w=[Qp[h2]])
            self.DMA("sync", Kp[h2].ap, KT.ap[hp * 128:(hp + 1) * 128, :], r=[KT], w=[Kp[h2]])
            self.DMA("sync", Ve[h2].ap, VT.ap[:, hp * 128:(hp + 1) * 128].rearrange("(tt p) c -> p tt c", p=128), r=[VT], w=[Ve[h2]])
            self.DMA("sync", Vo[h2].ap, VT.ap[64:64 + 33 * 128, hp * 128:(hp + 1) * 128].rearrange("(tt p) c -> p tt c", p=128), r=[VT], w=[Vo[h2]])
            self.DMA("sync", BTt[h2].ap, BTD.ap[2 * hp:2 * hp + 2].rearrange("two w r c -> (two w) (r c)"), r=[BTD], w=[BTt[h2]])
            qv = Qp[h2].ap.rearrange("p (b c) -> p b c", c=64)
            self.cp(G, Qbd[h2].ap[0:64, :, 0:64], qv[0:64, :, :], r=[Qp[h2]], w=[Qbd[h2]])
            self.cp(G, Qbd[h2].ap[64:128, :, 64:128], qv[64:128, :, :], r=[Qp[h2]], w=[Qbd[h2]])

        def s1(i, late=None):
            hp, kind, idx, pfirst, plast = blocks[i]
            h2 = hp % 2
            if pfirst and late is not True:
                load_pair(hp)
            qpos, nk, ro0, kpos = geom(kind, idx)
            qbt, sci, pei, sti = Qbd[h2], sc[i % NP], pe[i % NP], st[i % NP]
            ps_l, ps_c = self.PS[(i % 2) * 2], self.PS[(i % 2) * 2 + 1]
            qb_ap = qbt.ap[:, qpos // 64, :]
            if late is None or late is False:
                if kind == "l":
                    self.mm(ps_l.ap, qb_ap, Kp[h2].ap[:, kpos:kpos + 512], True, True, r=[qbt, Kp[h2]], w=[ps_l])
                    self.mm(ps_c.ap[:, 0:256], qb_ap, Kp[h2].ap[:, 0:256], True, True, r=[qbt, Kp[h2]], w=[ps_c])
                    self.tt(V, sci.ap[:, 0:512], ps_l.ap, BTt[h2].ap[:, ro0 * 64:ro0 * 64 + 512], ALU.add, r=[ps_l, BTt[h2]], w=[sci])
                    self.cp("scalar", sci.ap[:, 512:768], ps_c.ap[:, 0:256], r=[ps_c], w=[sci])
                else:
                    self.mm(ps_c.ap[:, 0:256], qb_ap, Kp[h2].ap[:, 0:256], True, True, r=[qbt, Kp[h2]], w=[ps_c])
                    self.cp("scalar", sci.ap[:, 0:256], ps_c.ap[:, 0:256], r=[ps_c], w=[sci])
                if late is False:
                    return
            self.E(V, lambda e, o=sti.ap[:, 0:1], i_=sci.ap[:, 0:nk]: e.reduce_max(out=o, in_=i_, axis=AX.X), r=[sci], w=[sti])
            self.ts(V, sti.ap[:, 1:2], sti.ap[:, 0:1], -1.0, ALU.mult, r=[sti], w=[sti])
            self.act(pei.ap[:, 0:nk], sci.ap[:, 0:nk], AF.Exp, r=[sci, sti], w=[pei, sti], bias=sti.ap[:, 1:2], accum=sti.ap[:, 2:3])
            self.E(V, lambda e, o=sti.ap[:, 3:4], i_=sti.ap[:, 2:3]: e.reciprocal(out=o, in_=i_), r=[sti], w=[sti])

        def s2(i, part=None):
            hp, kind, idx, pfirst, plast = blocks[i]
            h2 = hp % 2
            qpos, nk, ro0, kpos = geom(kind, idx)
            pei, pni, pTi, sti = pe[i % NP], pn[i % NP], pT[i % NP], st[i % NP]
            pst, pstb = self.PS[4 + i % 2], self.PSB[4 + i % 2]
            pso = self.PS[6 + i % 2]
            nkt = nk // 128
            if part in (None, 0):
                self.ts(G, pni.ap[:, 0:nk], pei.ap[:, 0:nk], sti.ap[:, 3:4], ALU.mult, 0.0, ALU.add, r=[pei, sti], w=[pni])
                for kt in range(nkt):
                    self.tr(pstb[:, kt * 128:(kt + 1) * 128], pni.ap[:, kt * 128:(kt + 1) * 128], self.identb.ap, r=[pni, self.identb], w=[pst])
                self.cp("scalar", pTi.ap[:, 0:nkt, :], pstb[:, 0:nk].rearrange("p (k c) -> p k c", c=128), r=[pst], w=[pTi])
                if part == 0:
                    return
            for kt in range(nkt):
                if kind == "c":
                    vt = Ve[h2].ap[:, kt, :]
                elif kt >= 4:
                    vt = Ve[h2].ap[:, kt - 4, :]
                else:
                    tok0 = kpos + kt * 128
                    vt = Ve[h2].ap[:, tok0 // 128, :] if tok0 % 128 == 0 else Vo[h2].ap[:, (tok0 - 64) // 128, :]
                self.mm(pso.ap[:, 0:128], vt, pTi.ap[:, kt, :], kt == 0, kt == nkt - 1, r=[Ve[h2], Vo[h2], pTi], w=[pso])
            self.cp(V, YTp[h2].ap[0:64, qpos:qpos + 64], pso.ap[0:64, 0:64], r=[pso], w=[YTp[h2]])
            self.cp("scalar", YTp[h2].ap[64:128, qpos:qpos + 64], pso.ap[64:128, 64:128], r=[pso], w=[YTp[h2]])
            if plast:
                self.DMA("sync", YT.ap[hp * 128:(hp + 1) * 128, :], YTp[h2].ap, r=[YTp[h2]], w=[YT])

        nb_ = len(blocks)
        for step in range(nb_ + 3):
            a = step
            if a < nb_:
                s1(a, late=False)
            b_ = step - 1
            if 0 <= b_ < nb_:
                s1(b_, late=True)
            c_ = step - 2
            if 0 <= c_ < nb_:
                s2(c_, part=0)
            d_ = step - 3
            if 0 <= d_ < nb_:
                s2(d_, part=1)


def _build(cfg):
    nc = bass.Bass("TRN2", target_bir_lowering=False)
    kb = KB(nc, cfg)
    IN = lambda n, s: kb.dram_t(n, s, F32, kind="ExternalInput")
    x_in = IN("x", [LL, D])
    ctx_in = IN("ctx", [LC, D])
    c_in = IN("c", [D])
    cctx_in = IN("c_ctx", [D])
    ada_w = IN("ada_w", [DEPTH, D, 6 * D])
    ada_b = IN("ada_b", [DEPTH, 6 * D])
    norm_mix = IN("norm_mix", [DEPTH, D])
    norm_ffn = IN("norm_ffn", [DEPTH, D])
    norm_final = IN("norm_final", [D])
    w1 = IN("ffn_w1", [DEPTH, D, FH])
    w3 = IN("ffn_w3", [DEPTH, D, FH])
    w2 = IN("ffn_w2", [DEPTH, FH, D])
    S5 = {
        "lam_re": IN("s5_lam_re", [2, 2, 64, 64]), "lam_im": IN("s5_lam_im", [2, 2, 64, 64]),
        "log_step": IN("s5_log_step", [2, 2, 64]),
        "b_re": IN("s5_b_re", [2, 2, 64, 64, 16]), "b_im": IN("s5_b_im", [2, 2, 64, 64, 16]),
        "c_re": IN("s5_c_re", [2, 2, 64, 16, 64]), "c_im": IN("s5_c_im", [2, 2, 64, 16, 64]),
        "d": IN("s5_d", [2, D]), "w_glu": IN("s5_w_glu", [2, D, D]), "b_glu": IN("s5_b_glu", [2, D]),
    }
    SSD = {
        "w_in": IN("ssd_w_in", [1, D, 8256]), "conv_w": IN("ssd_conv_w", [1, 5, 6144]), "conv_b": IN("ssd_conv_b", [1, 6144]),
        "dt_bias": IN("ssd_dt_bias", [1, 2, 32]), "a_log": IN("ssd_a_log", [1, 2, 32]), "d": IN("ssd_d", [1, 32]),
        "norm": IN("ssd_norm", [1, 2048]), "w_out": IN("ssd_w_out", [1, 2048, D]),
    }
    NA = {"w_qkv": IN("na_w_qkv", [1, D, 3 * D]), "w_o": IN("na_w_o", [1, D, D]), "rpb": IN("na_rpb", [1, 16, 15, 31])}
    out_t = kb.dram_t("out", [LL, D], F32, kind="ExternalOutput")
    QT = kb.dram_t("QT", [D, T], BF16)
    KT = kb.dram_t("KT", [D, T], BF16)
    VT = kb.dram_t("VT", [T, D], BF16)
    YT = kb.dram_t("YT", [D, T], BF16)
    BTD = kb.dram_t("BTD", [16, 64, 15, 64], F32)
    SZ = kb.dram_t("SZ", [T, 2048], BF16)
    XS = kb.dram_t("XS", [2048, T], BF16)
    BC = kb.dram_t("BC", [4096, T], BF16)
    DTR = kb.dram_t("DTR", [64, T], F32)
    YF = kb.dram_t("YF", [T, 2048], BF16)
    YB = kb.dram_t("YB", [T, 2048], BF16)
    XT = kb.dram_t("XT", [D, T], F32)
    HT = kb.dram_t("HT", [D, T], BF16)
    GT = kb.dram_t("GT", [D, T], BF16)
    layers = cfg.get("layers", list(range(DEPTH)))
    kb.consts()
    kb.build_mask8()
    kb.adaln(c_in, cctx_in, ada_w, ada_b, norm_mix, norm_ffn, norm_final, layers)
    kb.prologue_transpose(x_in, ctx_in, XT)
    for l in layers:
        kind, j = l % 3, l // 3
        pre_w = None
        if cfg.get("mixer", True):
            kb.norm_layer(XT, HT, l, 0)
            if kind == 0:
                kb.s5_phase(HT, GT, S5, j)
                if cfg.get("ffn", True) and cfg.get("prefetch_ffn", False):
                    kb.new_phase()
                    pre_w = kb.ffn_weights(w1, w3, w2, l, True, True)
                kb.proj_phase(XT, GT, S5["w_glu"], S5["w_glu"].ap[j], 8, l, glu_bias=(S5["b_glu"], S5["b_glu"].ap[j]), Wt=256 if pre_w else 512)
            elif kind == 1:
                kb.ssd_phase_a(HT, SSD, SZ, XS, BC, DTR)
                kb.ssd_phase_b(SSD, XS, BC, DTR, YF, YB)
                kb.ssd_phase_c(XT, SSD, SZ, YF, YB, l)
            else:
                kb.na_bias_table(NA["rpb"], BTD)
                kb.na_phase_a(HT, NA["w_qkv"], QT, KT, VT)
                kb.na_phase_b(QT, KT, VT, BTD, YT)
                if cfg.get("ffn", True) and cfg.get("prefetch_ffn", False):
                    kb.new_phase()
                    pre_w = kb.ffn_weights(w1, w3, w2, l, True, True)
                kb.proj_phase(XT, YT, NA["w_o"], NA["w_o"].ap[0], 8, l, Wt=256 if pre_w else 512)
        if cfg.get("ffn", True):
            kb.ffn_phase(XT, HT, w1, w3, w2, l, wts=pre_w)
            if pre_w is not None:
                kb.top = pre_w["top0"]
    kb.final_phase(XT, out_t)
    kb.p.fence("sync", kb.out_ops)
    kb.p.build()
    kb.es.close()
    return nc, kb


INPUT_NAMES = ["x", "ctx", "c", "c_ctx", "ada_w", "ada_b", "norm_mix", "norm_ffn", "norm_final", "ffn_w1", "ffn_w3", "ffn_w2",
               "s5_lam_re", "s5_lam_im", "s5_log_step", "s5_b_re", "s5_b_im", "s5_c_re", "s5_c_im", "s5_d", "s5_w_glu", "s5_b_glu",
               "ssd_w_in", "ssd_conv_w", "ssd_conv_b", "ssd_dt_bias", "ssd_a_log", "ssd_d", "ssd_norm", "ssd_w_out",
               "na_w_qkv", "na_w_o", "na_rpb"]


def kernel(**inputs):
    cfg = {}
    nc, kb = _build(cfg)
    n = 8
    in_maps = []
    for b in range(n):
        m = {}
        for k in INPUT_NAMES:
            v = np.ascontiguousarray(inputs[k], dtype=np.float32)
            if k in ("x", "ctx", "c"):
                v = np.ascontiguousarray(v[b])
            m[k] = v
        in_maps.append(m)
    res = run_bass_kernel_spmd(nc, in_maps, core_ids=list(range(n)))
    return np.stack([np.asarray(r["out"], dtype=np.float32) for r in res.results], axis=0)
```

```python
import numpy as np
from contextlib import ExitStack
import concourse.bass as bass
import concourse.mybir as mybir
from concourse.bass_utils import run_bass_kernel_spmd

F32 = mybir.dt.float32
BF16 = mybir.dt.bfloat16
I32 = mybir.dt.int32
AF = mybir.ActivationFunctionType
ALU = mybir.AluOpType
AX = mybir.AxisListType

ENGS = ("sync", "gpsimd", "scalar", "vector", "tensor")
NDMA_SEM = 24
SEM_EPOCH = 12000

D = 1024
LC = 256
LL = 4096
T = LC + LL
NFT = 8
FH = 2816
NHT = 22
DEPTH = 4
EPS = 1e-6
ARENA_F32 = 46 * 1024


class Buf:
    __slots__ = ("name", "w", "r", "psum")

    def __init__(self, name, psum=False):
        self.name = name
        self.w = None
        self.r = []
        self.psum = psum


class Op:
    __slots__ = ("eng", "fn", "idx", "deps", "need_inc", "val", "is_dma", "semi", "is_barrier", "bg")

    def __init__(self, eng, fn, is_dma):
        self.eng = eng
        self.fn = fn
        self.is_dma = is_dma
        self.deps = []
        self.need_inc = False
        self.val = 0
        self.semi = -1
        self.idx = -1
        self.is_barrier = False
        self.bg = False


class Prog:
    def __init__(self, nc):
        self.nc = nc
        self.ops = {e: [] for e in ENGS}
        self.nreal = {e: 0 for e in ENGS}
        self.es = ExitStack()
        self.engsem = {e: self.es.enter_context(nc.semaphore("s_" + e)) for e in ENGS}
        self.dmasem = [self.es.enter_context(nc.semaphore("d%d" % i)) for i in range(NDMA_SEM)]
        self.dma_last = [None] * NDMA_SEM
        self.dma_cnt = [0] * NDMA_SEM
        self.dma_rr = 0

    def emit(self, eng, fn, reads=(), writes=(), dma=False):
        op = Op(eng, fn, dma)
        op.idx = self.nreal[eng]
        self.nreal[eng] += 1
        deps = []
        for b in reads:
            if b.w is not None:
                deps.append(b.w)
            if b.psum:
                deps.extend(x for x in b.r if x.eng != eng)
        for b in writes:
            if b.w is not None:
                deps.append(b.w)
            deps.extend(b.r)
        if dma:
            k = self.dma_rr
            self.dma_rr = (k + 1) % NDMA_SEM
            if self.dma_last[k] is not None:
                deps.append(self.dma_last[k])
            self.dma_last[k] = op
            self.dma_cnt[k] += 16
            op.semi = k
            op.val = self.dma_cnt[k]
        seen = set()
        for d in deps:
            if d is op or id(d) in seen:
                continue
            seen.add(id(d))
            op.deps.append(d)
        for b in reads:
            if b.psum:
                b.r = [x for x in b.r if x.eng != eng]
                b.r.append(op)
            else:
                b.r.append(op)
        for b in writes:
            b.w = op
            b.r = []
        self.ops[eng].append(op)
        return op

    def fence(self, eng, deps):
        op = Op(eng, None, False)
        op.idx = self.nreal[eng]
        op.deps = list(deps)
        self.ops[eng].append(op)
        return op

    def barrier(self):
        lasts = []
        for e in ENGS:
            for o in reversed(self.ops[e]):
                if o.fn is not None and not o.bg:
                    lasts.append(o)
                    break
        for o in self.dma_last:
            if o is not None and not o.bg:
                lasts.append(o)
        for e in ENGS:
            self.fence(e, lasts).is_barrier = True

    def _needs_wait(self, op, d):
        if d.is_dma:
            return True
        if d.eng != op.eng:
            return True
        if op.eng == "tensor":
            return False
        return (op.idx - d.idx) <= 2

    def build(self):
        nc = self.nc
        for e in ENGS:
            for op in self.ops[e]:
                for d in op.deps:
                    if not d.is_dma and self._needs_wait(op, d):
                        d.need_inc = True
        self.epoch_sems = {e: [self.engsem[e]] for e in ENGS}
        for e in ENGS:
            c = 0
            ep = 0
            for op in self.ops[e]:
                if op.fn is None:
                    if op.is_barrier and c > SEM_EPOCH:
                        ep += 1
                        c = 0
                        self.epoch_sems[e].append(self.es.enter_context(nc.semaphore("s_%s_%d" % (e, ep))))
                    continue
                if op.is_dma:
                    continue
                op.semi = ep
                if op.need_inc:
                    c += 1
                    op.val = c
        self.counts = {e: 0 for e in ENGS}

        def mk_body(e):
            def body(eng):
                waited = {}
                for op in self.ops[e]:
                    for d in op.deps:
                        if not self._needs_wait(op, d):
                            continue
                        if d.is_dma:
                            key = ("d", d.semi)
                            sem = self.dmasem[d.semi]
                        else:
                            key = ("e", d.eng, d.semi)
                            sem = self.epoch_sems[d.eng][d.semi]
                        if waited.get(key, 0) >= d.val:
                            continue
                        waited[key] = d.val
                        eng.wait_ge(sem, d.val)
                    if op.fn is None:
                        continue
                    inst = op.fn(eng)
                    self.counts[e] += 1
                    if op.is_dma:
                        inst.then_inc(self.dmasem[op.semi], 16)
                    elif op.need_inc:
                        inst.then_inc(self.epoch_sems[e][op.semi], 1)
            return body

        with nc.Block() as block:
            for e in ENGS:
                if self.ops[e]:
                    getattr(block, e)(mk_body(e))
        self.es.close()


class Tl:
    __slots__ = ("ap", "buf")

    def __init__(self, ap, buf):
        self.ap = ap
        self.buf = buf


class KB:
    def __init__(self, nc, cfg):
        self.nc = nc
        self.cfg = cfg
        self.p = Prog(nc)
        self.es = ExitStack()
        self.arena = self.es.enter_context(nc.sbuf_tensor("arena", [128, ARENA_F32], F32))
        self.arena_bf = self.arena.bitcast(BF16)
        self.psum = [self.es.enter_context(nc.psum_tensor("ps%d" % i, [128, 512], F32)) for i in range(8)]
        self.PS = [Tl(self.psum[i][:], Buf("ps%d" % i, psum=True)) for i in range(8)]
        self.PSB = [self.psum[i].bitcast(BF16) for i in range(8)]
        self.top = 0
        self.ptr = 0
        self.nb = 0
        self.dram = {}
        self.out_ops = []

    def _al(self, n_f32, persistent):
        n_f32 = (n_f32 + 7) // 8 * 8
        if persistent:
            assert self.ptr == self.top, "persistent alloc only between phases"
            off = self.top
            self.top += n_f32
            self.ptr = self.top
        else:
            off = self.ptr
            self.ptr += n_f32
        assert self.ptr <= ARENA_F32, "arena overflow %d" % self.ptr
        return off

    def f32(self, n, name=None, persistent=False, shape=None):
        off = self._al(n, persistent)
        ap = self.arena[:, off:off + n]
        if shape is not None:
            ap = self._reshape(ap, shape)
        self.nb += 1
        return Tl(ap, Buf(name or "t%d" % self.nb))

    def bf16(self, n, name=None, persistent=False, shape=None):
        off = self._al((n + 1) // 2, persistent)
        ap = self.arena_bf[:, 2 * off:2 * off + n]
        if shape is not None:
            ap = self._reshape(ap, shape)
        self.nb += 1
        return Tl(ap, Buf(name or "t%d" % self.nb))

    def i32(self, n, name=None):
        off = self._al(n, False)
        ap = self.arena.bitcast(I32)[:, off:off + n]
        self.nb += 1
        return Tl(ap, Buf(name or "t%d" % self.nb))

    @staticmethod
    def _reshape(ap, shape):
        if len(shape) == 2:
            return ap.rearrange("p (a b) -> p a b", b=shape[1])
        if len(shape) == 3:
            return ap.rearrange("p (a b c) -> p a b c", b=shape[1], c=shape[2])
        raise ValueError

    def new_phase(self):
        self.p.barrier()
        self.ptr = self.top

    def dram_t(self, name, shape, dt, kind="Internal"):
        t = self.nc.dram_tensor(name, shape, dt, kind=kind)
        tl = Tl(t.ap(), Buf(name))
        self.dram[name] = tl
        return tl

    def E(self, eng, fn, r=(), w=()):
        return self.p.emit(eng, fn, [t.buf for t in r], [t.buf for t in w])

    def DMA(self, eng, out_ap, in_ap, r=(), w=(), slow=False):
        if slow:
            fn = lambda e: e.dma_start(out=out_ap, in_=in_ap, allow_slow_non_contiguous=True)
        else:
            fn = lambda e: e.dma_start(out=out_ap, in_=in_ap)
        return self.p.emit(eng, fn, [t.buf for t in r], [t.buf for t in w], dma=True)

    def mm(self, ps_ap, lhsT, rhs, start, stop, r=(), w=()):
        return self.E("tensor", lambda e: e.matmul(ps_ap, lhsT=lhsT, rhs=rhs, start=start, stop=stop), r, w)

    def tr(self, ps_ap, in_ap, ident_ap, r=(), w=()):
        return self.E("tensor", lambda e: e.transpose(out=ps_ap, in_=in_ap, identity=ident_ap), r, w)

    def act(self, out, in_, func, r=(), w=(), scale=None, bias=None, accum=None):
        kw = {}
        if scale is not None:
            kw["scale"] = scale
        if bias is not None:
            kw["bias"] = bias
        if accum is not None:
            kw["accum_out"] = accum
        return self.E("scalar", lambda e: e.activation(out=out, in_=in_, func=func, **kw), r, w)

    def tt(self, eng, out, in0, in1, op, r=(), w=()):
        return self.E(eng, lambda e: e.tensor_tensor(out=out, in0=in0, in1=in1, op=op), r, w)

    def ts(self, eng, out, in0, s1, op0, s2=None, op1=None, r=(), w=(), accum=None):
        if op1 is None:
            return self.E(eng, lambda e: e.tensor_scalar(out=out, in0=in0, scalar1=s1, scalar2=None, op0=op0), r, w)
        if accum is not None:
            return self.E(eng, lambda e: e.tensor_scalar(out=out, in0=in0, scalar1=s1, scalar2=s2, op0=op0, op1=op1, accum_out=accum), r, w)
        return self.E(eng, lambda e: e.tensor_scalar(out=out, in0=in0, scalar1=s1, scalar2=s2, op0=op0, op1=op1), r, w)

    def stt(self, out, in0, scalar, in1, op0, op1, r=(), w=()):
        return self.E("vector", lambda e: e.scalar_tensor_tensor(out=out, in0=in0, scalar=scalar, in1=in1, op0=op0, op1=op1), r, w)

    def cp(self, eng, out, in_, r=(), w=()):
        if eng == "scalar":
            return self.act(out, in_, AF.Copy, r, w)
        return self.E(eng, lambda e: e.tensor_copy(out=out, in_=in_), r, w)

    def memset(self, eng, ap, val, w=()):
        return self.E(eng, lambda e: e.memset(ap, val), (), w)

    def consts(self):
        self.ident = self.f32(128, "ident", True)
        self.identb = self.bf16(128, "identb", True)
        self.ones = self.f32(128, "ones", True)
        self.epsc = self.f32(8, "epsc", True)
        self.memset("gpsimd", self.ident.ap, 0.0, [self.ident])
        idap = self.ident.ap
        self.E("gpsimd", lambda e: e.affine_select(out=idap, in_=idap, pattern=[[-1, 128]], compare_op=ALU.not_equal,
                                                   fill=1.0, base=0, channel_multiplier=1), [self.ident], [self.ident])
        self.cp("vector", self.identb.ap, self.ident.ap, [self.ident], [self.identb])
        self.memset("vector", self.ones.ap, 1.0, [self.ones])
        self.memset("vector", self.epsc.ap[:, 0:1], EPS, [self.epsc])
        self.memset("vector", self.epsc.ap[:, 1:2], 0.0, [self.epsc])
        self.memset("vector", self.epsc.ap[:, 2:3], 1.0, [self.epsc])

    @staticmethod
    def chunks(w_lat=512):
        ch = [(0, LC, True)]
        for s in range(LC, T, w_lat):
            ch.append((s, w_lat, False))
        return ch

    def prologue_transpose(self, x_in, ctx_in, XT):
        self.new_phase()
        xin = [self.f32(4 * D, "xin%d" % i, shape=(4, D)) for i in range(2)]
        stage = [self.f32(NFT * 512, "stg%d" % i, shape=(NFT, 512)) for i in range(2)]
        for ci, (s, w, isctx) in enumerate(self.chunks()):
            xi = xin[ci % 2]
            st = stage[ci % 2]
            ntt = w // 128
            if isctx:
                src = ctx_in.ap.rearrange("(tt p) f -> p tt f", p=128)
            else:
                src = x_in.ap[s - LC:s - LC + w, :].rearrange("(tt p) f -> p tt f", p=128)
            self.DMA("sync", xi.ap[:, 0:ntt, :], src, r=[x_in], w=[xi])
            for ft in range(NFT):
                ps = self.PS[ft]
                for tt in range(ntt):
                    self.tr(ps.ap[:, tt * 128:(tt + 1) * 128], xi.ap[:, tt, ft * 128:(ft + 1) * 128], self.ident.ap,
                            r=[xi, self.ident], w=[ps])
                self.cp("scalar" if ft % 2 else "vector", st.ap[:, ft, 0:w], ps.ap[:, 0:w], r=[ps], w=[st])
            dst = XT.ap[:, s:s + w].rearrange("(ft p) t -> p ft t", p=128)
            self.DMA("sync", dst, st.ap[:, :, 0:w], r=[st], w=[XT])

    def adaln(self, c_in, cctx_in, ada_w, ada_b, norm_mix, norm_ffn, norm_final, layers):
        self.mod = {}
        for l in layers:
            self.mod[l] = self.f32(96, "mod%d" % l, True, shape=(6, 8, 2))
        self.nw = self.f32(9 * 8, "nw", True, shape=(9, 8))
        self.AB = {}
        for l in layers:
            self.AB[l] = self.f32(4 * 16, "AB%d" % l, True, shape=(4, 8, 2))
        self.new_phase()
        sT = self.f32(16, "sT", shape=(8, 2))
        craw = self.f32(16, "craw", shape=(8, 2))
        self.DMA("sync", craw.ap[:, :, 0], c_in.ap.rearrange("(kt p) -> p kt", p=128), r=[c_in], w=[craw], slow=True)
        self.DMA("sync", craw.ap[:, :, 1], cctx_in.ap.rearrange("(kt p) -> p kt", p=128), r=[cctx_in], w=[craw], slow=True)
        self.act(sT.ap, craw.ap, AF.Silu, r=[craw], w=[sT])
        for k, nwt in enumerate([norm_mix, norm_ffn]):
            self.DMA("sync", self.nw.ap[:, 4 * k:4 * k + 4, :], nwt.ap.rearrange("l (ft p) -> p l ft", p=128), r=[nwt], w=[self.nw], slow=True)
        self.DMA("sync", self.nw.ap[:, 8, :], norm_final.ap.rearrange("(ft p) -> p ft", p=128), r=[norm_final], w=[self.nw], slow=True)
        wbuf = [self.f32(8 * 512, "adaw%d" % i, shape=(8, 512)) for i in range(3)]
        bbuf = [self.f32(512, "adab%d" % i) for i in range(3)]
        onesrow = self.ones.ap[0:1, 0:2]
        it = 0
        for l in layers:
            for cj in range(12):
                wb = wbuf[it % 3]
                bb = bbuf[it % 3]
                ps = self.PS[it % 4]
                it += 1
                self.DMA("sync", wb.ap, ada_w.ap[l, :, cj * 512:(cj + 1) * 512].rearrange("(kt p) n -> p kt n", p=128), r=[ada_w], w=[wb])
                self.DMA("sync", bb.ap[0:1, :], ada_b.ap[l:l + 1, cj * 512:(cj + 1) * 512], r=[ada_b], w=[bb])
                for jj in range(4):
                    j = cj * 4 + jj
                    o = ps.ap[:, jj * 2:jj * 2 + 2]
                    for kt in range(8):
                        self.mm(o, wb.ap[:, kt, jj * 128:(jj + 1) * 128], sT.ap[:, kt, :], kt == 0, False, r=[wb, sT], w=[ps])
                    self.mm(o, bb.ap[0:1, jj * 128:(jj + 1) * 128], onesrow, False, True, r=[bb, self.ones], w=[ps])
                m = cj * 4 // 8
                ft0 = (cj * 4) % 8
                self.cp("vector", self.mod[l].ap[:, m, ft0:ft0 + 4, :], ps.ap[:, 0:8].rearrange("p (a b) -> p a b", b=2), r=[ps], w=[self.mod[l]])
        for l in layers:
            for k, (mi, nwi) in enumerate([(1, l), (4, 4 + l)]):
                nwb = self.nw.ap[:, nwi, :].unsqueeze(2).to_broadcast([128, 8, 2])
                self.stt(self.AB[l].ap[:, k, :, :], self.mod[l].ap[:, mi, :, :], 1.0, nwb, ALU.add, ALU.mult, r=[self.mod[l], self.nw], w=[self.AB[l]])

    def norm_phase(self, XT, HT, A_sel, B_sel, deps_r):
        self.new_phase()
        xin = [self.f32(NFT * 512, "nx%d" % i, shape=(NFT, 512)) for i in range(2)]
        sq = [self.f32(NFT * 512, "nsq%d" % i, shape=(NFT, 512)) for i in range(2)]
        hb = [self.bf16(NFT * 512, "nh%d" % i, shape=(NFT, 512)) for i in range(2)]
        rt = [self.f32(512, "nrt%d" % i) for i in range(2)]
        chs = self.chunks()

        def load(ci):
            s, w, isctx = chs[ci]
            xi = xin[ci % 2]
            self.DMA("sync", xi.ap[:, :, 0:w], XT.ap[:, s:s + w].rearrange("(ft p) t -> p ft t", p=128), r=[XT], w=[xi])

        load(0)
        for ci, (s, w, isctx) in enumerate(chs):
            if ci + 1 < len(chs):
                load(ci + 1)
            xi, sqi, hbi, rti = xin[ci % 2], sq[ci % 2], hb[ci % 2], rt[ci % 2]
            ps = self.PS[ci % 2]
            self.act(sqi.ap[:, :, 0:w], xi.ap[:, :, 0:w], AF.Square, r=[xi], w=[sqi])
            for ft in range(NFT):
                self.mm(ps.ap[:, 0:w], self.ones.ap, sqi.ap[:, ft, 0:w], ft == 0, ft == NFT - 1, r=[self.ones, sqi], w=[ps])
            self.act(rti.ap[:, 0:w], ps.ap[:, 0:w], AF.Sqrt, r=[ps, self.epsc], w=[rti], scale=1.0 / D, bias=self.epsc.ap[:, 0:1])
            self.E("vector", lambda e, o=rti.ap[:, 0:w]: e.reciprocal(out=o, in_=o), r=[rti], w=[rti])
            for ft in range(NFT):
                a = A_sel(ft, isctx)
                b = B_sel(ft, isctx)
                self.stt(sqi.ap[:, ft, 0:w], xi.ap[:, ft, 0:w], a, rti.ap[:, 0:w], ALU.mult, ALU.mult, r=[xi, rti] + deps_r, w=[sqi])
                self.act(hbi.ap[:, ft, 0:w], sqi.ap[:, ft, 0:w], AF.Identity, r=[sqi] + deps_r, w=[hbi], bias=b)
            self.DMA("sync", HT.ap[:, s:s + w].rearrange("(ft p) t -> p ft t", p=128), hbi.ap[:, :, 0:w], r=[hbi], w=[HT])

    def norm_layer(self, XT, HT, l, which):
        AB = self.AB[l]
        mod = self.mod[l]
        k = 0 if which == 0 else 1
        smi = 0 if which == 0 else 3
        A_sel = lambda ft, isctx: AB.ap[:, k, ft, (1 if isctx else 0):(1 if isctx else 0) + 1]
        B_sel = lambda ft, isctx: mod.ap[:, smi, ft, (1 if isctx else 0):(1 if isctx else 0) + 1]
        self.norm_phase(XT, HT, A_sel, B_sel, [AB, mod])

    def final_phase(self, XT, out_t):
        self.new_phase()
        xin = [self.f32(NFT * 512, "fx%d" % i, shape=(NFT, 512)) for i in range(2)]
        sq = [self.f32(NFT * 512, "fsq%d" % i, shape=(NFT, 512)) for i in range(2)]
        rt = [self.f32(512, "frt%d" % i) for i in range(2)]
        ob = [self.f32(4 * D, "fo%d" % i, shape=(4, D)) for i in range(2)]
        ci = 0
        lat = [c for c in self.chunks() if not c[2]]

        def loadf(i):
            s, w, _ = lat[i]
            self.DMA("sync", xin[i % 2].ap, XT.ap[:, s:s + w].rearrange("(ft p) t -> p ft t", p=128), r=[XT], w=[xin[i % 2]])

        loadf(0)
        for (s, w, isctx) in lat:
            xi, sqi, rti, obi = xin[ci % 2], sq[ci % 2], rt[ci % 2], ob[ci % 2]
            ps = self.PS[ci % 2]
            ci += 1
            if ci < len(lat):
                loadf(ci)
            self.act(sqi.ap, xi.ap, AF.Square, r=[xi], w=[sqi])
            for ft in range(NFT):
                self.mm(ps.ap, self.ones.ap, sqi.ap[:, ft, :], ft == 0, ft == NFT - 1, r=[self.ones, sqi], w=[ps])
            self.act(rti.ap, ps.ap, AF.Sqrt, r=[ps, self.epsc], w=[rti], scale=1.0 / D, bias=self.epsc.ap[:, 0:1])
            self.E("vector", lambda e, o=rti.ap: e.reciprocal(out=o, in_=o), r=[rti], w=[rti])
            for ft in range(NFT):
                self.stt(sqi.ap[:, ft, :], xi.ap[:, ft, :], self.nw.ap[:, 8, ft:ft + 1], rti.ap, ALU.mult, ALU.mult, r=[xi, rti, self.nw], w=[sqi])
            for tt in range(4):
                for half in range(2):
                    pso = self.PS[2 + (tt * 2 + half) % 6]
                    for q in range(4):
                        ft = half * 4 + q
                        self.tr(pso.ap[:, q * 128:(q + 1) * 128], sqi.ap[:, ft, tt * 128:(tt + 1) * 128], self.ident.ap, r=[sqi, self.ident], w=[pso])
                    self.cp("scalar" if half else "vector", obi.ap[:, tt, half * 512:(half + 1) * 512], pso.ap, r=[pso], w=[obi])
            dst = out_t.ap[s - LC:s - LC + w, :].rearrange("(tt p) f -> p tt f", p=128)
            self.out_ops.append(self.DMA("sync", dst, obi.ap, r=[obi], w=[out_t]))

    def load_w_bf16(self, dst, src_ap, src_tl, nkt, col0=None, col1=None):
        for kt in range(nkt):
            s = src_ap[kt * 128:(kt + 1) * 128, :] if col0 is None else src_ap[kt * 128:(kt + 1) * 128, col0:col1]
            self.DMA("gpsimd", dst.ap[:, kt, :], s, r=[src_tl], w=[dst])

    def ffn_weights(self, w1, w3, w2, l, persistent, bg):
        top0 = self.top
        w1s = self.bf16(8 * FH, "w1s", persistent=persistent, shape=(8, FH))
        w3s = self.bf16(8 * FH, "w3s", persistent=persistent, shape=(8, FH))
        w2s = self.bf16(NHT * D, "w2s", persistent=persistent, shape=(NHT, D))
        hk = {"w1": [], "w3": [], "w2": []}
        for (nm, dst, src, nkt) in (("w1", w1s, w1, 8), ("w3", w3s, w3, 8), ("w2", w2s, w2, NHT)):
            for kt in range(nkt):
                tl = Tl(dst.ap, Buf("%s_%d" % (nm, kt)))
                op = self.DMA("gpsimd", dst.ap[:, kt, :], src.ap[l][kt * 128:(kt + 1) * 128, :], r=[src], w=[tl])
                op.bg = bg
                hk[nm].append(tl)
        return dict(w1s=w1s, w3s=w3s, w2s=w2s, hk=hk, top0=top0)

    def ffn_phase(self, XT, HT, w1, w3, w2, l, wts=None):
        self.new_phase()
        W = 256
        if wts is None:
            wts = self.ffn_weights(w1, w3, w2, l, False, False)
        w1s, w3s, w2s, hk = wts["w1s"], wts["w3s"], wts["w2s"], wts["hk"]
        hb = [self.bf16(NFT * W, "fh%d" % i, shape=(NFT, W)) for i in range(2)]
        xb = [self.f32(NFT * W, "fxx%d" % i, shape=(NFT, W)) for i in range(2)]
        sqb = self.f32(NFT * W, "fsq", shape=(NFT, W))
        rtb = [self.f32(W, "frt%d" % i) for i in range(2)]
        gb = [self.bf16(NHT * W, "fg%d" % i, shape=(NHT, W)) for i in range(1)]
        sl = [self.bf16(W, "fs%d" % i) for i in range(2)]
        mod = self.mod[l]
        AB = self.AB[l]
        ntile = T // W

        def load(ti):
            s = ti * W
            self.DMA("sync", xb[ti % 2].ap, XT.ap[:, s:s + W].rearrange("(ft p) t -> p ft t", p=128), r=[XT], w=[xb[ti % 2]])

        def norm(ti):
            s = ti * W
            cs = 1 if s < LC else 0
            hbi, xbi, rti = hb[ti % 2], xb[ti % 2], rtb[ti % 2]
            ps = self.PS[7]
            self.act(sqb.ap, xbi.ap, AF.Square, r=[xbi], w=[sqb])
            for ft in range(NFT):
                self.mm(ps.ap[:, 0:W], self.ones.ap, sqb.ap[:, ft, :], ft == 0, ft == NFT - 1, r=[self.ones, sqb], w=[ps])
            self.act(rti.ap, ps.ap[:, 0:W], AF.Sqrt, r=[ps, self.epsc], w=[rti], scale=1.0 / D, bias=self.epsc.ap[:, 0:1])
            self.E("vector", lambda e, o=rti.ap: e.reciprocal(out=o, in_=o), r=[rti], w=[rti])
            for ft in range(NFT):
                self.stt(sqb.ap[:, ft, :], xbi.ap[:, ft, :], AB.ap[:, 1, ft, cs:cs + 1], rti.ap, ALU.mult, ALU.mult, r=[xbi, rti, AB], w=[sqb])
                self.act(hbi.ap[:, ft, :], sqb.ap[:, ft, :], AF.Identity, r=[sqb, mod], w=[hbi], bias=mod.ap[:, 3, ft, cs:cs + 1])

        load(0)
        norm(0)
        for ti in range(ntile):
            s = ti * W
            cs = 1 if s < LC else 0
            hbi, xbi, gbi = hb[ti % 2], xb[ti % 2], gb[0]
            if ti + 1 < ntile:
                load(ti + 1)
            for j in range(NHT):
                pa = self.PS[(j % 2) * 2]
                pb = self.PS[(j % 2) * 2 + 1]
                for kt in range(8):
                    self.mm(pa.ap[:, 0:W], w1s.ap[:, kt, j * 128:(j + 1) * 128], hbi.ap[:, kt, :], kt == 0, kt == 7, r=[hk["w1"][kt], hbi], w=[pa])
                for kt in range(8):
                    self.mm(pb.ap[:, 0:W], w3s.ap[:, kt, j * 128:(j + 1) * 128], hbi.ap[:, kt, :], kt == 0, kt == 7, r=[hk["w3"][kt], hbi], w=[pb])
                sli = sl[j % 2]
                self.act(sli.ap, pa.ap[:, 0:W], AF.Silu, r=[pa], w=[sli])
                self.tt("vector", gbi.ap[:, j, :], pb.ap[:, 0:W], sli.ap, ALU.mult, r=[pb, sli], w=[gbi])
                if j == 10 and ti + 1 < ntile:
                    norm(ti + 1)
            for fo in range(NFT):
                po = self.PS[4 + fo % 3]
                for j in range(NHT):
                    self.mm(po.ap[:, 0:W], w2s.ap[:, j, fo * 128:(fo + 1) * 128], gbi.ap[:, j, :], j == 0, j == NHT - 1, r=[hk["w2"][j], gbi], w=[po])
                self.stt(xbi.ap[:, fo, :], po.ap[:, 0:W], mod.ap[:, 5, fo, cs:cs + 1], xbi.ap[:, fo, :], ALU.mult, ALU.add, r=[po, mod, xbi], w=[xbi])
            self.DMA("sync", XT.ap[:, s:s + W].rearrange("(ft p) t -> p ft t", p=128), xbi.ap, r=[xbi], w=[XT])

    def rev_ap(self, ap2d, start, n):
        pstride = ap2d.ap[0][0]
        return bass.AP(ap2d.tensor, ap2d.offset + start + n - 1, [[pstride, 128], [-1, n]])

    def sincos_turns(self, eng, turns, n, osin, ocos, tmp, cast_eng="vector"):
        ti, tf, fr, s2, s4 = tmp["ti"], tmp["tf"], tmp["fr"], tmp["s2"], tmp["s4"]
        sl = lambda t: t.ap[:, 0:n]
        self.cp(cast_eng, sl(ti), sl(turns), r=[turns], w=[ti])
        self.cp(cast_eng, sl(tf), sl(ti), r=[ti], w=[tf])
        self.tt(eng, sl(fr), sl(turns), sl(tf), ALU.subtract, r=[turns, tf], w=[fr])
        self.act(sl(s2), sl(fr), AF.Sin, r=[fr], w=[s2], scale=float(np.pi))
        self.act(sl(s4), sl(fr), AF.Sin, r=[fr], w=[s4], scale=float(np.pi / 2))
        self.tt(eng, sl(s4), sl(s4), sl(s4), ALU.mult, r=[s4], w=[s4])
        self.ts(eng, sl(s4), sl(s4), -4.0, ALU.mult, 2.0, ALU.add, r=[s4], w=[s4])
        self.tt(eng, sl(osin), sl(s2), sl(s4), ALU.mult, r=[s2, s4], w=[osin])
        self.tt(eng, sl(s2), sl(s2), sl(s2), ALU.mult, r=[s2], w=[s2])
        self.ts(eng, sl(ocos), sl(s2), -2.0, ALU.mult, 1.0, ALU.add, r=[s2], w=[ocos])

    def build_mask8(self):
        self.mask8 = self.f32(8, "mask8", True)
        m = self.mask8.ap
        self.memset("gpsimd", m, 1.0, [self.mask8])
        self.E("gpsimd", lambda e: e.affine_select(out=m, in_=m, pattern=[[-16, 8]], compare_op=ALU.is_ge, fill=0.0, base=0, channel_multiplier=1), [self.mask8], [self.mask8])
        self.E("gpsimd", lambda e: e.affine_select(out=m, in_=m, pattern=[[16, 8]], compare_op=ALU.is_ge, fill=0.0, base=15, channel_multiplier=-1), [self.mask8], [self.mask8])

    def s5_phase(self, HT, GT, P, j):
        self.new_phase()
        V, G = "vector", "gpsimd"
        def sc_tile(nm):
            return self.f32(64, nm)
        lr, li, ls = sc_tile("lr"), sc_tile("li"), sc_tile("ls")
        for d in range(2):
            self.DMA("sync", lr.ap[:, d * 32:(d + 1) * 32], P["lam_re"].ap[j, d].rearrange("(p two) n -> (two n) p", two=2), r=[P["lam_re"]], w=[lr], slow=True)
            self.DMA("sync", li.ap[:, d * 32:(d + 1) * 32], P["lam_im"].ap[j, d].rearrange("(p two) n -> (two n) p", two=2), r=[P["lam_im"]], w=[li], slow=True)
        lsrow = self.f32(128, "lsrow")
        self.DMA("sync", lsrow.ap[0:1, :], P["log_step"].ap[j:j + 1].rearrange("o d g -> o (d g)"), r=[P["log_step"]], w=[lsrow])
        psb = self.PS[0]
        self.mm(psb.ap[:, 0:128], self.ones.ap[0:1, :], lsrow.ap[0:1, :], True, True, r=[self.ones, lsrow], w=[psb])
        for d in range(2):
            src = psb.ap[:, d * 64:(d + 1) * 64].rearrange("q (p two) -> q p two", two=2)
            self.cp(V, ls.ap[0:64, d * 32:(d + 1) * 32], src[0:64, :, 0], r=[psb], w=[ls])
            self.cp(V, ls.ap[64:128, d * 32:(d + 1) * 32], src[64:128, :, 1], r=[psb], w=[ls])
        step, zr, zi, rr, tq = sc_tile("step"), sc_tile("zr"), sc_tile("zi"), sc_tile("rr"), sc_tile("tq")
        tmp = {"ti": self.i32(512, "ti"), "tf": self.f32(512, "tf"), "fr": self.f32(512, "fr"), "s2": self.f32(512, "s2"), "s4": self.f32(512, "s4")}
        tmp2 = {"ti": self.i32(512, "ti2"), "tf": self.f32(512, "tf2"), "fr": self.f32(512, "fr2"), "s2": self.f32(512, "s22"), "s4": self.f32(512, "s42")}
        sphi, cphi, frac = sc_tile("sphi"), sc_tile("cphi"), sc_tile("frac")
        self.act(step.ap, ls.ap, AF.Exp, r=[ls], w=[step])
        self.tt(V, zr.ap, lr.ap, step.ap, ALU.mult, r=[lr, step], w=[zr])
        self.tt(V, zi.ap, li.ap, step.ap, ALU.mult, r=[li, step], w=[zi])
        self.act(rr.ap, zr.ap, AF.Exp, r=[zr], w=[rr])
        self.ts(V, tq.ap, zi.ap, float(1.0 / (2 * np.pi)), ALU.mult, r=[zi], w=[tq])
        self.sincos_turns(V, tq, 64, sphi, cphi, tmp)
        self.cp(V, frac.ap, tmp["fr"].ap[:, 0:64], r=[tmp["fr"]], w=[frac])
        carry = {}
        for Q in (256, 512):
            tQ, sQ, cQ = sc_tile("tQ%d" % Q), sc_tile("sQ%d" % Q), sc_tile("cQ%d" % Q)
            self.ts(V, tQ.ap, frac.ap, float(Q), ALU.mult, r=[frac], w=[tQ])
            self.sincos_turns(V, tQ, 64, sQ, cQ, tmp)
            carry[Q] = (sQ, cQ)
        ar, ai, den, u, cr, ci, t1s, t2s = [sc_tile(n) for n in ("ar", "ai", "den", "u", "cr", "ci", "t1s", "t2s")]
        self.tt(V, ar.ap, rr.ap, cphi.ap, ALU.mult, r=[rr, cphi], w=[ar])
        self.tt(V, ai.ap, rr.ap, sphi.ap, ALU.mult, r=[rr, sphi], w=[ai])
        self.tt(V, t1s.ap, lr.ap, lr.ap, ALU.mult, r=[lr], w=[t1s])
        self.tt(V, t2s.ap, li.ap, li.ap, ALU.mult, r=[li], w=[t2s])
        self.tt(V, den.ap, t1s.ap, t2s.ap, ALU.add, r=[t1s, t2s], w=[den])
        self.E(V, lambda e: e.reciprocal(out=den.ap, in_=den.ap), r=[den], w=[den])
        self.ts(V, u.ap, ar.ap, -1.0, ALU.add, r=[ar], w=[u])
        self.tt(V, t1s.ap, u.ap, lr.ap, ALU.mult, r=[u, lr], w=[t1s])
        self.tt(V, t2s.ap, ai.ap, li.ap, ALU.mult, r=[ai, li], w=[t2s])
        self.tt(V, t1s.ap, t1s.ap, t2s.ap, ALU.add, r=[t1s, t2s], w=[t1s])
        self.tt(V, cr.ap, t1s.ap, den.ap, ALU.mult, r=[t1s, den], w=[cr])
        self.tt(V, t1s.ap, ai.ap, lr.ap, ALU.mult, r=[ai, lr], w=[t1s])
        self.tt(V, t2s.ap, u.ap, li.ap, ALU.mult, r=[u, li], w=[t2s])
        self.tt(V, t1s.ap, t1s.ap, t2s.ap, ALU.subtract, r=[t1s, t2s], w=[t1s])
        self.tt(V, ci.ap, t1s.ap, den.ap, ALU.mult, r=[t1s, den], w=[ci])
        braw = {}
        for nm in ("b_re", "b_im"):
            braw[nm] = self.f32(2 * 32 * 16, "braw_" + nm, shape=(2, 32, 16))
            for d in range(2):
                self.DMA("sync", braw[nm].ap[:, d, :, :], P[nm].ap[j, d].rearrange("(p two) n h -> (two n) p h", two=2), r=[P[nm]], w=[braw[nm]], slow=True)
        dsk = self.f32(8, "dsk")
        self.DMA("sync", dsk.ap, P["d"].ap[j].rearrange("(ft p) -> p ft", p=128), r=[P["d"]], w=[dsk], slow=True)
        M1 = {}
        for k in range(4):
            for nm in ("b_re", "b_im"):
                M1[(k, nm)] = self.f32(128, "M1_%d%s" % (k, nm))
                self.memset(G, M1[(k, nm)].ap, 0.0, [M1[(k, nm)]])
        Jrow = self.f32(512, "Jrow")
        self.E(G, lambda e: e.iota(Jrow.ap, pattern=[[1, 512]], base=0, channel_multiplier=0, allow_small_or_imprecise_dtypes=True), (), [Jrow])
        WTS = [self.bf16(8 * 6 * 128, "wts%d" % i, shape=(8, 6, 128)) for i in range(2)]
        craw = [[self.f32(64, "craw%d_%d" % (i, q)) for q in range(2)] for i in range(2)]
        Spair = [self.f32(128, "Spair%d" % i) for i in range(2)]
        U = [self.bf16(T, "U%d" % i) for i in range(2)]
        Yacc = self.f32(T, "Yacc")
        gt = self.bf16(T, "gt")
        cmr, cpr, ncpr = sc_tile("cmr"), sc_tile("cpr"), sc_tile("ncpr")
        self.tt(V, cmr.ap, ci.ap, cr.ap, ALU.subtract, r=[ci, cr], w=[cmr])
        self.tt(V, cpr.ap, ci.ap, cr.ap, ALU.add, r=[ci, cr], w=[cpr])
        self.ts(V, ncpr.ap, cpr.ap, -1.0, ALU.mult, r=[cpr], w=[ncpr])
        lastc = [self.f32(2, "lastc%d" % i) for i in range(2)]
        nsQ = {}
        for Q in (256, 512):
            nsQ[Q] = sc_tile("nsQ%d" % Q)
            self.ts(V, nsQ[Q].ap, carry[Q][0].ap, -1.0, ALU.mult, r=[carry[Q][0]], w=[nsQ[Q]])
        TAB = [{n: self.f32(512, "%s%d" % (n, i)) for n in ("COS", "SIN", "wr", "bma", "apb", "ta", "tb")} for i in range(2)]
        WK = [{n: self.f32(512, "%s%d" % (n, i)) for n in ("k1", "k2", "k3", "pss", "bre", "bim")} for i in range(2)]
        MK = [{n: self.bf16(512, "%s%d" % (n, i)) for n in ("m1", "m2", "m3", "m4")} for i in range(2)]
        init = [self.f32(4, "init%d" % i) for i in range(2)]
        fwd_chunks = [(0, LC)] + [(s, 512) for s in range(LC, T, 512)]
        bwd_chunks = [(0, LC)] + [(T - 512 * (i + 1), 512) for i in range(8)]
        tgc = [0]

        def emit_prep(ft):
            Ui = U[ft % 2]
            self.DMA("sync", Ui.ap, HT.ap[ft * 128:(ft + 1) * 128, :], r=[HT], w=[Ui])
            W = WTS[ft % 2]
            for d in range(2):
                cr_ = craw[d]
                self.DMA("sync", cr_[0].ap, P["c_re"].ap[j, d, ft * 8:(ft + 1) * 8].rearrange("g h n -> (g h) n"), r=[P["c_re"]], w=[cr_[0]])
                self.DMA("sync", cr_[1].ap, P["c_im"].ap[j, d, ft * 8:(ft + 1) * 8].rearrange("g h n -> (g h) n"), r=[P["c_im"]], w=[cr_[1]])
                for k in range(4):
                    p_ = ft * 4 + k
                    wi_ = d * 4 + k
                    c1, c2 = 32 * k, 32 * k + 16
                    for bi, nm in enumerate(("b_re", "b_im")):
                        m1t = M1[(k, nm)]
                        self.cp(G, m1t.ap[0:64, c1:c1 + 16], braw[nm].ap[0:64, d, p_, :], r=[braw[nm]], w=[m1t])
                        self.cp(G, m1t.ap[64:128, c2:c2 + 16], braw[nm].ap[64:128, d, p_, :], r=[braw[nm]], w=[m1t])
                        ps = self.PS[5]
                        self.tr(ps.ap[:, bi * 128:(bi + 1) * 128], m1t.ap, self.ident.ap, r=[m1t, self.ident], w=[ps])
                        self.cp("scalar", W.ap[:, wi_, bi, :], ps.ap[:, bi * 128:(bi + 1) * 128], r=[ps], w=[W])
                    self.tt(G, W.ap[:, wi_, 5, :], W.ap[:, wi_, 0, :], W.ap[:, wi_, 1, :], ALU.add, r=[W], w=[W])
                    for q in range(2):
                        sp = Spair[q]
                        self.ts(G, sp.ap[:, 0:64], cr_[q].ap, self.mask8.ap[:, 2 * k:2 * k + 1], ALU.mult, 0.0, ALU.add, r=[cr_[q], self.mask8], w=[sp])
                        self.ts(G, sp.ap[:, 64:128], cr_[q].ap, self.mask8.ap[:, 2 * k + 1:2 * k + 2], ALU.mult, 0.0, ALU.add, r=[cr_[q], self.mask8], w=[sp])
                        ps = self.PS[5]
                        self.tr(ps.ap[:, 256 + q * 128:256 + (q + 1) * 128], sp.ap, self.ident.ap, r=[sp, self.ident], w=[ps])
                        src = ps.ap[:, 256 + q * 128:256 + (q + 1) * 128]
                        if q == 0:
                            self.cp("scalar", W.ap[:, wi_, 2, :], src, r=[ps], w=[W])
                            self.act(W.ap[:, wi_, 3, :], src, AF.Copy, r=[ps], w=[W], scale=-1.0)
                        else:
                            self.act(W.ap[:, wi_, 4, :], src, AF.Copy, r=[ps], w=[W], scale=-1.0)

        def table_thunks(tab, col):
            th = []
            add = th.append
            MAGIC = 12582912.0
            ti, tf, fr, s2, s4 = tmp2["ti"], tmp2["tf"], tmp2["fr"], tmp2["s2"], tmp2["s4"]
            sc1 = lambda t: t.ap[:, col:col + 1]
            add(lambda: self.act(tab["ta"].ap, Jrow.ap, AF.Identity, r=[Jrow, frac], w=[tab["ta"]], scale=sc1(frac)))
            add(lambda: self.act(tf.ap, tab["ta"].ap, AF.Identity, r=[tab["ta"]], w=[tf], bias=MAGIC))
            add(lambda: self.act(tf.ap, tf.ap, AF.Identity, r=[tf], w=[tf], bias=-MAGIC))
            add(lambda: self.tt(G, fr.ap, tab["ta"].ap, tf.ap, ALU.subtract, r=[tab["ta"], tf], w=[fr]))
            add(lambda: self.act(s2.ap, fr.ap, AF.Sin, r=[fr], w=[s2], scale=float(np.pi)))
            add(lambda: self.act(s4.ap, fr.ap, AF.Sin, r=[fr], w=[s4], scale=float(np.pi / 2)))
            add(lambda: self.act(s4.ap, s4.ap, AF.Square, r=[s4], w=[s4]))
            add(lambda: self.act(s4.ap, s4.ap, AF.Identity, r=[s4], w=[s4], scale=-4.0, bias=2.0))
            add(lambda: self.tt(G, tab["SIN"].ap, s2.ap, s4.ap, ALU.mult, r=[s2, s4], w=[tab["SIN"]]))
            add(lambda: self.act(s2.ap, s2.ap, AF.Square, r=[s2], w=[s2]))
            add(lambda: self.act(tab["COS"].ap, s2.ap, AF.Identity, r=[s2], w=[tab["COS"]], scale=-2.0, bias=1.0))
            for (c1, c2, dst) in ((cr, ci, "wr"), (cmr, ncpr, "bma"), (cpr, cmr, "apb")):
                add(lambda c1=c1: self.act(tab["ta"].ap, tab["COS"].ap, AF.Identity, r=[tab["COS"], c1], w=[tab["ta"]], scale=sc1(c1)))
                add(lambda c2=c2: self.act(tab["tb"].ap, tab["SIN"].ap, AF.Identity, r=[tab["SIN"], c2], w=[tab["tb"]], scale=sc1(c2)))
                add(lambda dst=dst: self.tt(G, tab[dst].ap, tab["ta"].ap, tab["tb"].ap, ALU.add, r=[tab["ta"], tab["tb"]], w=[tab[dst]]))
            return th

        items = []
        pd = 0
        for ft in range(NFT):
            for d in range(2):
                chunks = fwd_chunks if d == 0 else bwd_chunks
                for k in range(4):
                    for cidx, (s, n) in enumerate(chunks):
                        items.append(dict(ft=ft, d=d, k=k, cidx=cidx, s=s, n=n, pd=pd, last=(cidx == len(chunks) - 1),
                                          first_pd=(cidx == 0), first_y=(d == 0 and k == 0),
                                          ft_last=(d == 1 and k == 3 and cidx == len(chunks) - 1)))
                    pd += 1
        pending = []

        def stage_a(i, it_):
            ft, d, k, s, n = it_["ft"], it_["d"], it_["k"], it_["s"], it_["n"]
            if i == 0:
                emit_prep(0)
                for f in table_thunks(TAB[0], 0):
                    f()
            if d == 1 and k == 0 and it_["cidx"] == 0 and ft + 1 < NFT:
                emit_prep(ft + 1)
            if it_["cidx"] == 1 and it_["pd"] + 1 < 64:
                npd = it_["pd"] + 1
                nft, nd, nk = npd // 8, (npd // 4) % 2, npd % 4
                pending.extend(table_thunks(TAB[npd % 2], nd * 32 + nft * 4 + nk))
            if it_["cidx"] >= 1:
                ntake = len(pending) if it_["last"] else min(3, len(pending))
                for _ in range(ntake):
                    pending.pop(0)()
            tab = TAB[it_["pd"] % 2]
            Ui, W, wi_ = U[ft % 2], WTS[ft % 2], d * 4 + k
            wk = WK[i % 2]
            pre, pim, psu = self.PS[0], self.PS[1], self.PS[2]
            urhs = Ui.ap[:, s:s + n] if d == 0 else self.rev_ap(Ui.ap, s, n)
            self.mm(pre.ap[:, 0:n], W.ap[:, wi_, 0, :], urhs, True, True, r=[W, Ui], w=[pre])
            self.mm(pim.ap[:, 0:n], W.ap[:, wi_, 1, :], urhs, True, True, r=[W, Ui], w=[pim])
            self.mm(psu.ap[:, 0:n], W.ap[:, wi_, 5, :], urhs, True, True, r=[W, Ui], w=[psu])
            c_ = lambda t: t.ap[:, 0:n]
            self.cp("scalar", c_(wk["pss"]), psu.ap[:, 0:n], r=[psu], w=[wk["pss"]])
            self.tt(V, c_(wk["k2"]), pre.ap[:, 0:n], c_(tab["bma"]), ALU.mult, r=[pre, tab["bma"]], w=[wk["k2"]])
            self.tt(V, c_(wk["k3"]), pim.ap[:, 0:n], c_(tab["apb"]), ALU.mult, r=[pim, tab["apb"]], w=[wk["k3"]])
            self.tt(G, c_(wk["k1"]), c_(wk["pss"]), c_(tab["wr"]), ALU.mult, r=[wk["pss"], tab["wr"]], w=[wk["k1"]])
            self.tt(G, c_(wk["bre"]), c_(wk["k1"]), c_(wk["k3"]), ALU.subtract, r=[wk["k1"], wk["k3"]], w=[wk["bre"]])
            self.tt(G, c_(wk["bim"]), c_(wk["k1"]), c_(wk["k2"]), ALU.add, r=[wk["k1"], wk["k2"]], w=[wk["bim"]])

        def stage_b(i, it_):
            ft, d, k, s, n, cidx = it_["ft"], it_["d"], it_["k"], it_["s"], it_["n"], it_["cidx"]
            tab = TAB[it_["pd"] % 2]
            col = d * 32 + ft * 4 + k
            Ui, W, wi_ = U[ft % 2], WTS[ft % 2], d * 4 + k
            wk, mk, ini = WK[i % 2], MK[i % 2], init[i % 2]
            py = self.PS[3 + i % 2]
            c_ = lambda t: t.ap[:, 0:n]
            rcol = rr.ap[:, col:col + 1]
            rb = rcol.to_broadcast([128, n])
            if cidx == 0:
                i_re, i_im, ir = 0.0, 0.0, []
            else:
                pini = init[(i - 1) % 2]
                i_re, i_im, ir = pini.ap[:, 0:1], pini.ap[:, 1:2], [pini]
            gre_t, gim_t = self.PS[6], self.PS[7]
            self.E(V, lambda e, o=gre_t.ap[:, 0:n], d1=c_(wk["bre"]), i0=i_re, rb=rb: e.tensor_tensor_scan(out=o, data0=rb, data1=d1, initial=i0, op0=ALU.mult, op1=ALU.add),
                   r=[wk["bre"], rr] + ir, w=[gre_t])
            self.E(V, lambda e, o=gim_t.ap[:, 0:n], d1=c_(wk["bim"]), i0=i_im, rb=rb: e.tensor_tensor_scan(out=o, data0=rb, data1=d1, initial=i0, op0=ALU.mult, op1=ALU.add),
                   r=[wk["bim"], rr] + ir, w=[gim_t])
            if not it_["last"]:
                sQ, cQ = carry[n]
                sq_, cq_, nsq_ = sQ.ap[:, col:col + 1], cQ.ap[:, col:col + 1], nsQ[n].ap[:, col:col + 1]
                lc = lastc[i % 2]
                self.cp(V, lc.ap[:, 0:1], gre_t.ap[:, n - 1:n], r=[gre_t], w=[lc])
                self.cp(V, lc.ap[:, 1:2], gim_t.ap[:, n - 1:n], r=[gim_t], w=[lc])
                gre_l, gim_l = lc.ap[:, 0:1], lc.ap[:, 1:2]
                self.act(ini.ap[:, 2:3], gim_l, AF.Identity, r=[lc, nsQ[n]], w=[ini], scale=nsq_)
                self.act(ini.ap[:, 3:4], gim_l, AF.Identity, r=[lc, cQ], w=[ini], scale=cq_)
                self.act(ini.ap[:, 0:1], gre_l, AF.Identity, r=[lc, cQ, ini], w=[ini], scale=cq_, bias=ini.ap[:, 2:3])
                self.act(ini.ap[:, 1:2], gre_l, AF.Identity, r=[lc, sQ, ini], w=[ini], scale=sq_, bias=ini.ap[:, 3:4])
            self.tt(V, c_(mk["m1"]), gre_t.ap[:, 0:n], c_(tab["COS"]), ALU.mult, r=[gre_t, tab["COS"]], w=[mk["m1"]])
            self.tt(V, c_(mk["m3"]), gre_t.ap[:, 0:n], c_(tab["SIN"]), ALU.mult, r=[gre_t, tab["SIN"]], w=[mk["m3"]])
            self.tt(V, c_(mk["m2"]), gim_t.ap[:, 0:n], c_(tab["SIN"]), ALU.mult, r=[gim_t, tab["SIN"]], w=[mk["m2"]])
            self.tt(V, c_(mk["m4"]), gim_t.ap[:, 0:n], c_(tab["COS"]), ALU.mult, r=[gim_t, tab["COS"]], w=[mk["m4"]])
            for mi, (mn, wsel) in enumerate((("m1", 2), ("m2", 3), ("m3", 4), ("m4", 4))):
                mr = mk[mn].ap[:, 0:n] if d == 0 else self.rev_ap(mk[mn].ap, 0, n)
                self.mm(py.ap[:, 0:n], W.ap[:, wi_, wsel, :], mr, mi == 0, mi == 3, r=[W, mk[mn]], w=[py])
            if it_["first_y"]:
                self.cp("scalar", Yacc.ap[:, s:s + n], py.ap[:, 0:n], r=[py], w=[Yacc])
            else:
                self.tt(V, Yacc.ap[:, s:s + n], py.ap[:, 0:n], Yacc.ap[:, s:s + n], ALU.add, r=[py, Yacc], w=[Yacc])
            if it_["ft_last"]:
                self.stt(Yacc.ap, Ui.ap, dsk.ap[:, ft:ft + 1], Yacc.ap, ALU.mult, ALU.add, r=[Ui, dsk, Yacc], w=[Yacc])
                self.act(gt.ap, Yacc.ap, AF.Gelu_apprx_tanh, r=[Yacc], w=[gt])
                self.DMA("sync", GT.ap[ft * 128:(ft + 1) * 128, :], gt.ap, r=[gt], w=[GT])

        stage_a(0, items[0])
        for i in range(len(items)):
            if i + 1 < len(items):
                stage_a(i + 1, items[i + 1])
            stage_b(i, items[i])

    def proj_phase(self, XT, IN, Wd, w_ap, nkt, l, glu_bias=None, Wt=512):
        self.new_phase()
        ws = self.bf16(nkt * D, "pw", shape=(nkt, D))
        self.load_w_bf16(ws, w_ap, Wd, nkt)
        bg = None
        if glu_bias is not None:
            bg = self.f32(8, "bglu")
            self.DMA("sync", bg.ap, glu_bias[1].rearrange("(ft p) -> p ft", p=128), r=[glu_bias[0]], w=[bg], slow=True)
        ib = [self.bf16(nkt * Wt, "pi%d" % i, shape=(nkt, Wt)) for i in range(2)]
        xb = [self.f32(NFT * Wt, "px%d" % i, shape=(NFT, Wt)) for i in range(2)]
        sg = [self.f32(Wt, "psg%d" % i) for i in range(2)]
        mod = self.mod[l]
        chs = self.chunks(Wt)

        def load(ci):
            s, w, isctx = chs[ci]
            self.DMA("sync", ib[ci % 2].ap[:, :, 0:w], IN.ap[:, s:s + w].rearrange("(kt p) t -> p kt t", p=128), r=[IN], w=[ib[ci % 2]])
            self.DMA("sync", xb[ci % 2].ap[:, :, 0:w], XT.ap[:, s:s + w].rearrange("(ft p) t -> p ft t", p=128), r=[XT], w=[xb[ci % 2]])

        load(0)
        for ci, (s, w, isctx) in enumerate(chs):
            cs = 1 if isctx else 0
            ibi, xbi = ib[ci % 2], xb[ci % 2]
            if ci + 1 < len(chs):
                load(ci + 1)
            for fo in range(NFT):
                po = self.PS[fo % 4]
                for kt in range(nkt):
                    self.mm(po.ap[:, 0:w], ws.ap[:, kt, fo * 128:(fo + 1) * 128], ibi.ap[:, kt, 0:w], kt == 0, kt == nkt - 1, r=[ws, ibi], w=[po])
                gate = mod.ap[:, 2, fo, cs:cs + 1]
                if glu_bias is not None:
                    sgi = sg[fo % 2]
                    self.act(sgi.ap[:, 0:w], po.ap[:, 0:w], AF.Sigmoid, r=[po, bg], w=[sgi], bias=bg.ap[:, fo:fo + 1])
                    self.tt("gpsimd", sgi.ap[:, 0:w], sgi.ap[:, 0:w], ibi.ap[:, fo, 0:w], ALU.mult, r=[sgi, ibi], w=[sgi])
                    self.stt(xbi.ap[:, fo, 0:w], sgi.ap[:, 0:w], gate, xbi.ap[:, fo, 0:w], ALU.mult, ALU.add, r=[sgi, mod, xbi], w=[xbi])
                else:
                    self.stt(xbi.ap[:, fo, 0:w], po.ap[:, 0:w], gate, xbi.ap[:, fo, 0:w], ALU.mult, ALU.add, r=[po, mod, xbi], w=[xbi])
            self.DMA("sync", XT.ap[:, s:s + w].rearrange("(ft p) t -> p ft t", p=128), xbi.ap[:, :, 0:w], r=[xbi], w=[XT])


    def row_bcast(self, dst, row_ap, src_tl, n, rowtmp, func=None, scale=None):
        self.DMA("sync", rowtmp.ap[0:1, 0:n], row_ap, r=[src_tl], w=[rowtmp])
        for c0 in range(0, n, 512):
            w = min(512, n - c0)
            ps = self.PS[7]
            self.mm(ps.ap[:, 0:w], self.ones.ap[0:1, :], rowtmp.ap[0:1, c0:c0 + w], True, True, r=[self.ones, rowtmp], w=[ps])
            if func is None:
                self.cp("vector", dst.ap[:, c0:c0 + w], ps.ap[:, 0:w], r=[ps], w=[dst])
            else:
                self.act(dst.ap[:, c0:c0 + w], ps.ap[:, 0:w], func, r=[ps], w=[dst], scale=scale)

    def tri_mask(self, nm, base, cm, step, op):
        t = self.f32(128, nm)
        self.memset("gpsimd", t.ap, 1.0, [t])
        self.E("gpsimd", lambda e: e.affine_select(out=t.ap, in_=t.ap, pattern=[[step, 128]], compare_op=op, fill=0.0, base=base, channel_multiplier=cm), [t], [t])
        return t

    def ssd_phase_a(self, HT, P, SZ, XS, BC, DTR):
        self.new_phase()
        w_in = P["w_in"]
        Hres = self.bf16(8 * T, "Hres", shape=(8, T))
        for kt in range(8):
            self.DMA("sync", Hres.ap[:, kt, :], HT.ap[kt * 128:(kt + 1) * 128, :], r=[HT], w=[Hres])
        wz = self.bf16(8 * 2048, "wz", shape=(8, 2048))
        self.load_w_bf16(wz, w_in.ap[0], w_in, 8, 0, 2048)
        szb = [self.bf16(2048, "szb%d" % i) for i in range(2)]
        for tt in range(T // 128):
            sb = szb[tt % 2]
            for zc in range(4):
                ps = self.PS[zc]
                for kt in range(8):
                    self.mm(ps.ap, Hres.ap[:, kt, tt * 128:(tt + 1) * 128], wz.ap[:, kt, zc * 512:(zc + 1) * 512], kt == 0, kt == 7, r=[Hres, wz], w=[ps])
                self.act(sb.ap[:, zc * 512:(zc + 1) * 512], ps.ap, AF.Silu, r=[ps], w=[sb])
            self.DMA("sync", SZ.ap[tt * 128:(tt + 1) * 128, :], sb.ap, r=[sb], w=[SZ])
        self.new_phase()
        Hres = self.bf16(8 * T, "Hres2", shape=(8, T))
        for kt in range(8):
            self.DMA("sync", Hres.ap[:, kt, :], HT.ap[kt * 128:(kt + 1) * 128, :], r=[HT], w=[Hres])
        cw = self.f32(5 * 48, "cw", shape=(5, 48))
        cb = self.f32(48, "cb")
        for k in range(5):
            self.DMA("sync", cw.ap[:, k, :], P["conv_w"].ap[0, k].rearrange("(ot p) -> p ot", p=128), r=[P["conv_w"]], w=[cw], slow=True)
        self.DMA("sync", cb.ap, P["conv_b"].ap[0].rearrange("(ot p) -> p ot", p=128), r=[P["conv_b"]], w=[cb], slow=True)
        wch = [self.bf16(8 * 512, "wch%d" % i, shape=(8, 512)) for i in range(2)]
        xp = [self.bf16(T, "xp%d" % i) for i in range(2)]
        ob = [self.bf16(T, "ob%d" % i) for i in range(2)]
        dg = [self.bf16(5 * 128, "dg%d" % i, shape=(5, 128)) for i in range(2)]
        chunks = self.chunks()
        pi = 0
        for wc in range(12):
            wb = wch[wc % 2]
            self.load_w_bf16(wb, w_in.ap[0], w_in, 8, 2048 + wc * 512, 2048 + (wc + 1) * 512)
            for q in range(4):
                ot = wc * 4 + q
                xpi, obi, dgi = xp[ot % 2], ob[ot % 2], dg[ot % 2]
                for k in range(5):
                    self.ts("gpsimd", dgi.ap[:, k, :], self.identb.ap, cw.ap[:, k, ot:ot + 1], ALU.mult, 0.0, ALU.add, r=[self.identb, cw], w=[dgi])
                for ci, (s, w, isctx) in enumerate(chunks):
                    ps = self.PS[pi % 4]
                    pi += 1
                    for kt in range(8):
                        self.mm(ps.ap[:, 0:w], wb.ap[:, kt, q * 128:(q + 1) * 128], Hres.ap[:, kt, s:s + w], kt == 0, kt == 7, r=[wb, Hres], w=[ps])
                    self.cp("vector" if ci % 2 else "scalar", xpi.ap[:, s:s + w], ps.ap[:, 0:w], r=[ps], w=[xpi])
                for ci, (s, w, isctx) in enumerate(chunks):
                    q0, q1 = (0, LC) if isctx else (LC, T)
                    ps = self.PS[4 + ci % 4]
                    for ki, k in enumerate((2, 0, 1, 3, 4)):
                        o = k - 2
                        i0 = max(0, q0 - s - o)
                        i1 = min(w, q1 - s - o)
                        self.mm(ps.ap[:, i0:i1], dgi.ap[:, k, :], xpi.ap[:, s + i0 + o:s + i1 + o], ki == 0, ki == 4, r=[dgi, xpi], w=[ps])
                    self.act(obi.ap[:, s:s + w], ps.ap[:, 0:w], AF.Silu, r=[ps, cb], w=[obi], bias=cb.ap[:, ot:ot + 1])
                dst = XS.ap[ot * 128:(ot + 1) * 128, :] if ot < 16 else BC.ap[(ot - 16) * 128:(ot - 15) * 128, :]
                self.DMA("sync", dst, obi.ap, r=[obi], w=[XS if ot < 16 else BC])
        wdt = self.bf16(8 * 64, "wdt", shape=(8, 64))
        self.load_w_bf16(wdt, w_in.ap[0], w_in, 8, 8192, 8256)
        dtb = self.f32(T, "dtb")
        for ci, (s, w, isctx) in enumerate(chunks):
            ps = self.PS[ci % 4]
            for kt in range(8):
                self.mm(ps.ap[0:64, 0:w], wdt.ap[:, kt, :], Hres.ap[:, kt, s:s + w], kt == 0, kt == 7, r=[wdt, Hres], w=[ps])
            self.cp("vector", dtb.ap[0:64, s:s + w], ps.ap[0:64, 0:w], r=[ps], w=[dtb])
        self.DMA("sync", DTR.ap, dtb.ap[0:64, :], r=[dtb], w=[DTR])

    def ssd_phase_b(self, P, XS, BC, DTR, YF, YB):
        self.new_phase()
        V, G = "vector", "gpsimd"
        GT_ = self.tri_mask("mGT", 0, 1, -1, ALU.is_gt)
        LE_ = self.tri_mask("mLE", 0, -1, 1, ALU.is_ge)
        LT_ = self.tri_mask("mLT", 0, -1, 1, ALU.is_gt)
        GE_ = self.tri_mask("mGE", 0, 1, -1, ALU.is_ge)
        rowtmp = self.f32(64, "rowtmp")
        Arow = [self.f32(32, "Arow%d" % d) for d in range(2)]
        Brow = [self.f32(32, "Brow%d" % d) for d in range(2)]
        Drow = self.f32(32, "Drow")
        for d in range(2):
            self.row_bcast(Arow[d], P["a_log"].ap[0, d:d + 1, :], P["a_log"], 32, rowtmp, func=AF.Exp)
            self.ts(V, Arow[d].ap, Arow[d].ap, -1.0, ALU.mult, r=[Arow[d]], w=[Arow[d]])
            self.row_bcast(Brow[d], P["dt_bias"].ap[0, d:d + 1, :], P["dt_bias"], 32, rowtmp)
        self.row_bcast(Drow, P["d"].ap[0:1, :], P["d"], 32, rowtmp)
        H = self.f32(2048, "Hst", shape=(32, 64))
        Hb = self.bf16(2048, "Hb")
        Hg = [Tl(H.ap, Buf("Hg%d" % g)) for g in range(8)]
        Hbg = [Tl(Hb.ap, Buf("Hbg%d" % g)) for g in range(8)]
        NB = 2
        xsT = [self.bf16(16 * 128, "xsT%d" % i, shape=(16, 128)) for i in range(NB)]
        BTt = [self.bf16(8 * 128, "BT%d" % i, shape=(8, 128)) for i in range(NB)]
        CTt = [self.bf16(8 * 128, "CT%d" % i, shape=(8, 128)) for i in range(NB)]
        dtr = [self.f32(128, "dtr%d" % i) for i in range(NB)]
        ybuf = [self.bf16(2048, "ybuf%d" % i) for i in range(NB)]
        xdt = [self.bf16(2048, "xdt%d" % i, shape=(32, 64)) for i in range(NB)]
        xdte = [self.bf16(2048, "xdte%d" % i, shape=(32, 64)) for i in range(NB)]
        dskt = [self.f32(2048, "dskt%d" % i, shape=(32, 64)) for i in range(NB)]
        Btok = [self.bf16(1024, "Btok%d" % i, shape=(8, 128)) for i in range(NB)]
        dtt = [self.f32(32, "dtt%d" % i) for i in range(NB)]
        adt = [self.f32(32, "adt%d" % i) for i in range(NB)]
        dex = [self.f32(96, "dex%d" % i) for i in range(NB)]
        dtd = [self.f32(32, "dtd%d" % i) for i in range(NB)]
        Xh = [self.f32(512, "Xh%d" % i, shape=(4, 128)) for i in range(2)]
        Ld = [self.bf16(512, "Ld%d" % i, shape=(4, 128)) for i in range(2)]
        Mt = [self.bf16(512, "Mt%d" % i, shape=(4, 128)) for i in range(2)]
        CBm = [self.bf16(128, "CBm%d" % i) for i in range(2)]
        ytmp = [self.f32(256, "ytmp%d" % i, shape=(4, 64)) for i in range(2)]
        htmp = [self.f32(256, "htmp%d" % i, shape=(4, 64)) for i in range(2)]
        nchunk = T // 128
        work = []
        for d in range(2):
            order = list(range(nchunk)) if d == 0 else [1, 0] + list(range(nchunk - 1, 1, -1))
            for oi, c in enumerate(order):
                work.append((d, c, oi == 0))
        cfgd = {0: dict(mX=GT_, mE=LE_, mTE=GT_, mSeg=LE_, mCB=LE_), 1: dict(mX=LT_, mE=GE_, mTE=LT_, mSeg=GE_, mCB=GE_)}

        def prologue(wi):
            d, c, first = work[wi]
            cf = cfgd[d]
            b = wi % NB
            s0 = c * 128
            xi, bi, cti, dri = xsT[b], BTt[b], CTt[b], dtr[b]
            self.DMA("sync", xi.ap, XS.ap[:, s0:s0 + 128].rearrange("(t p) c -> p t c", p=128), r=[XS], w=[xi])
            self.DMA("sync", bi.ap, BC.ap[(d * 2) * 1024:(d * 2 + 1) * 1024, s0:s0 + 128].rearrange("(t p) c -> p t c", p=128), r=[BC], w=[bi])
            self.DMA("sync", cti.ap, BC.ap[(d * 2 + 1) * 1024:(d * 2 + 2) * 1024, s0:s0 + 128].rearrange("(t p) c -> p t c", p=128), r=[BC], w=[cti])
            self.DMA("sync", dri.ap[0:32, :], DTR.ap[d * 32:(d + 1) * 32, s0:s0 + 128], r=[DTR], w=[dri])
            psm = self.PS[6]
            dtt_, adt_, dex_, dtd_ = dtt[b], adt[b], dex[b], dtd[b]
            self.tr(psm.ap[:, 0:32], dri.ap[0:32, :], self.ident.ap[0:32, 0:32], r=[dri, self.ident], w=[psm])
            self.tt(V, dtt_.ap, psm.ap[:, 0:32], Brow[d].ap, ALU.add, r=[psm, Brow[d]], w=[dtt_])
            self.act(dtt_.ap, dtt_.ap, AF.Exp, r=[dtt_], w=[dtt_])
            self.act(dtt_.ap, dtt_.ap, AF.Ln, r=[dtt_, self.epsc], w=[dtt_], bias=self.epsc.ap[:, 2:3])
            self.tt(V, adt_.ap, dtt_.ap, Arow[d].ap, ALU.mult, r=[dtt_, Arow[d]], w=[adt_])
            self.mm(psm.ap[:, 32:64], self.ones.ap, adt_.ap, True, True, r=[self.ones, adt_], w=[psm])
            self.mm(psm.ap[:, 64:96], cf["mTE"].ap, adt_.ap, True, True, r=[cf["mTE"], adt_], w=[psm])
            self.mm(psm.ap[:, 96:128], cf["mE"].ap, adt_.ap, True, True, r=[cf["mE"], adt_], w=[psm])
            self.act(dex_.ap, psm.ap[:, 32:128], AF.Exp, r=[psm], w=[dex_])
            self.tt(V, dtd_.ap, dtt_.ap, dex_.ap[:, 32:64], ALU.mult, r=[dtt_, dex_], w=[dtd_])
            pst = self.PS[7]
            psb16 = self.PSB[7]
            for half in range(2):
                for q in range(8):
                    t_ = half * 8 + q
                    self.tr(psb16[:, q * 128:(q + 1) * 128], xi.ap[:, t_, :], self.identb.ap, r=[xi, self.identb], w=[pst])
                src = psb16[:, 0:1024].rearrange("p (h c) -> p h c", c=64)
                hs = slice(half * 16, (half + 1) * 16)
                bc = lambda col: col.unsqueeze(2).to_broadcast([128, 16, 64])
                self.tt(V, xdt[b].ap[:, hs, :], src, bc(dtt_.ap[:, hs]), ALU.mult, r=[pst, dtt_], w=[xdt[b]])
                self.tt(V, xdte[b].ap[:, hs, :], src, bc(dtd_.ap[:, hs]), ALU.mult, r=[pst, dtd_], w=[xdte[b]])
                if d == 0:
                    self.tt(V, dskt[b].ap[:, hs, :], src, bc(Drow.ap[:, hs]), ALU.mult, r=[pst, Drow], w=[dskt[b]])
            for g in range(8):
                self.tr(psb16[:, g * 128:(g + 1) * 128], bi.ap[:, g, :], self.identb.ap, r=[bi, self.identb], w=[pst])
            self.cp("scalar", Btok[b].ap, psb16[:, 0:1024].rearrange("p (g c) -> p g c", c=128), r=[pst], w=[Btok[b]])

        gi = [0]

        def s1(wi, g):
            d, c, first = work[wi]
            cf = cfgd[d]
            b = wi % NB
            k = (wi * 8 + g) % 2
            xh = Xh[k]
            for hh in range(4):
                h = g * 4 + hh
                self.ts(G, xh.ap[:, hh, :], cf["mX"].ap, adt[b].ap[:, h:h + 1], ALU.mult, 0.0, ALU.add, r=[cf["mX"], adt[b]], w=[xh])
            pcb = self.PS[k]
            self.mm(pcb.ap[:, 0:128], BTt[b].ap[:, g, :], CTt[b].ap[:, g, :], True, True, r=[BTt[b], CTt[b]], w=[pcb])
            psg = self.PS[4 + k]
            for hh in range(4):
                self.mm(psg.ap[:, hh * 128:(hh + 1) * 128], xh.ap[:, hh, :], cf["mSeg"].ap, True, True, r=[xh, cf["mSeg"]], w=[psg])

        def s2(wi, g):
            d, c, first = work[wi]
            cf = cfgd[d]
            k = (wi * 8 + g) % 2
            pcb, psg = self.PS[k], self.PS[4 + k]
            self.act(Ld[k].ap, psg.ap.rearrange("p (h c) -> p h c", c=128), AF.Exp, r=[psg], w=[Ld[k]])
            self.tt(V, CBm[k].ap, pcb.ap[:, 0:128], cf["mCB"].ap, ALU.mult, r=[pcb, cf["mCB"]], w=[CBm[k]])
            self.tt(V, Mt[k].ap, Ld[k].ap, CBm[k].ap.unsqueeze(1).to_broadcast([128, 4, 128]), ALU.mult, r=[Ld[k], CBm[k]], w=[Mt[k]])

        def s3(wi, g):
            d, c, first = work[wi]
            b = wi % NB
            k = (wi * 8 + g) % 2
            py = self.PS[2 + k]
            cti = CTt[b]
            for hh in range(4):
                h = g * 4 + hh
                self.mm(py.ap[:, hh * 64:(hh + 1) * 64], Mt[k].ap[:, hh, :], xdt[b].ap[:, h, :], True, True, r=[Mt[k], xdt[b]], w=[py])
                self.mm(py.ap[:, 256 + hh * 64:256 + (hh + 1) * 64], cti.ap[:, g, :], Hb.ap[:, h * 64:(h + 1) * 64], True, True, r=[cti, Hbg[g]], w=[py])
            gs = slice(g * 4, (g + 1) * 4)
            yt = ytmp[k]
            yb = ybuf[b]
            Eb = dex[b].ap[:, 64 + g * 4:64 + (g + 1) * 4].unsqueeze(2).to_broadcast([128, 4, 64])
            self.tt(V, yt.ap, py.ap[:, 256:512].rearrange("p (h c) -> p h c", c=64), Eb, ALU.mult, r=[py, dex[b]], w=[yt])
            if d == 0:
                self.tt(G, yt.ap, yt.ap, dskt[b].ap[:, gs, :], ALU.add, r=[yt, dskt[b]], w=[yt])
            self.tt(V, yb.ap[:, g * 256:(g + 1) * 256].rearrange("p (h c) -> p h c", c=64), py.ap[:, 0:256].rearrange("p (h c) -> p h c", c=64), yt.ap, ALU.add, r=[py, yt], w=[yb])
            pst2 = self.PS[6]
            self.mm(pst2.ap[:, 256:512], Btok[b].ap[:, g, :], xdte[b].ap[:, gs, :].rearrange("p h c -> p (h c)"), True, True, r=[Btok[b], xdte[b]], w=[pst2])
            ht = htmp[k]
            Db = dex[b].ap[:, g * 4:(g + 1) * 4].unsqueeze(2).to_broadcast([128, 4, 64])
            self.tt(G, ht.ap, H.ap[:, gs, :], Db, ALU.mult, r=[Hg[g], dex[b]], w=[ht])
            self.tt(V, H.ap[:, gs, :], pst2.ap[:, 256:512].rearrange("p (h c) -> p h c", c=64), ht.ap, ALU.add, r=[pst2, ht], w=[Hg[g]])
            self.cp("scalar", Hb.ap[:, g * 256:(g + 1) * 256], H.ap[:, gs, :].rearrange("p h c -> p (h c)"), r=[Hg[g]], w=[Hbg[g]])

        nw = len(work)
        prologue(0)
        s1(0, 0)
        for wi in range(nw):
            d, c, first = work[wi]
            if first:
                self.memset(V, H.ap, 0.0, Hg)
                self.memset(V, Hb.ap, 0.0, Hbg)
            for g in range(8):
                if g == 4 and wi + 1 < nw:
                    prologue(wi + 1)
                if g + 1 < 8:
                    s1(wi, g + 1)
                elif wi + 1 < nw:
                    s1(wi + 1, 0)
                s2(wi, g)
                s3(wi, g)
            YO = YF if d == 0 else YB
            self.DMA("sync", YO.ap[c * 128:c * 128 + 128, :], ybuf[wi % NB].ap, r=[ybuf[wi % NB]], w=[YO])

    def ssd_phase_c(self, XT, P, SZ, YF, YB, l):
        self.new_phase()
        V, G = "vector", "gpsimd"
        wo = self.bf16(16 * D, "wo", shape=(16, D))
        self.load_w_bf16(wo, P["w_out"].ap[0], P["w_out"], 16)
        nrow = self.f32(2048, "nrow")
        rowtmp = self.f32(2048, "rowtmp2")
        self.row_bcast(nrow, P["norm"].ap[0:1, :], P["norm"], 2048, rowtmp)
        yf = [self.bf16(2048, "cyf%d" % i) for i in range(2)]
        ybk = [self.bf16(2048, "cyb%d" % i) for i in range(2)]
        sz = [self.bf16(2048, "csz%d" % i) for i in range(2)]
        yy = [self.f32(2048, "cyy%d" % i) for i in range(2)]
        sqj = self.f32(2048, "csq")
        gnb = [self.bf16(2048, "cgn%d" % i) for i in range(2)]
        ss = [self.f32(8, "css%d" % i) for i in range(2)]
        gnT = [self.bf16(16 * 512, "gnT%d" % i, shape=(16, 512)) for i in range(2)]
        xb = [self.f32(NFT * 512, "cx%d" % i, shape=(NFT, 512)) for i in range(2)]
        mod = self.mod[l]
        ti = 0
        chs = self.chunks()

        def load_tile(tix):
            t0 = tix * 128
            self.DMA("sync", yf[tix % 2].ap, YF.ap[t0:t0 + 128, :], r=[YF], w=[yf[tix % 2]])
            self.DMA("sync", ybk[tix % 2].ap, YB.ap[t0:t0 + 128, :], r=[YB], w=[ybk[tix % 2]])
            self.DMA("sync", sz[tix % 2].ap, SZ.ap[t0:t0 + 128, :], r=[SZ], w=[sz[tix % 2]])

        def load_x(ci):
            s, w, isctx = chs[ci]
            self.DMA("sync", xb[ci % 2].ap[:, :, 0:w], XT.ap[:, s:s + w].rearrange("(ft p) t -> p ft t", p=128), r=[XT], w=[xb[ci % 2]])

        load_tile(0)
        load_x(0)
        for ci, (s, w, isctx) in enumerate(chs):
            cs = 1 if isctx else 0
            gT, xbi = gnT[ci % 2], xb[ci % 2]
            if ci + 1 < len(chs):
                load_x(ci + 1)
            for tt in range(w // 128):
                t0 = s + tt * 128
                a, b, z_, y_, g_, s_ = yf[ti % 2], ybk[ti % 2], sz[ti % 2], yy[ti % 2], gnb[ti % 2], ss[ti % 2]
                ti += 1
                if ti < T // 128:
                    load_tile(ti)
                self.tt(G, y_.ap, a.ap, b.ap, ALU.add, r=[a, b], w=[y_])
                self.tt(V, y_.ap, y_.ap, z_.ap, ALU.mult, r=[y_, z_], w=[y_])
                self.act(sqj.ap, y_.ap, AF.Square, r=[y_], w=[sqj, s_], accum=s_.ap[:, 0:1])
                self.act(s_.ap[:, 1:2], s_.ap[:, 0:1], AF.Sqrt, r=[s_, self.epsc], w=[s_], scale=1.0 / 2048, bias=self.epsc.ap[:, 0:1])
                self.E(V, lambda e, o=s_.ap[:, 2:3], i=s_.ap[:, 1:2]: e.reciprocal(out=o, in_=i), r=[s_], w=[s_])
                self.stt(g_.ap, y_.ap, s_.ap[:, 2:3], nrow.ap, ALU.mult, ALU.mult, r=[y_, s_, nrow], w=[g_])
                for half in range(2):
                    pst = self.PS[4 + (ti * 2 + half) % 4]
                    psb16 = self.PSB[4 + (ti * 2 + half) % 4]
                    for q in range(8):
                        kt = half * 8 + q
                        self.tr(psb16[:, q * 128:(q + 1) * 128], g_.ap[:, kt * 128:(kt + 1) * 128], self.identb.ap, r=[g_, self.identb], w=[pst])
                    self.cp("scalar" if half else "vector", gT.ap[:, half * 8:(half + 1) * 8, tt * 128:(tt + 1) * 128],
                            psb16[:, 0:1024].rearrange("p (k c) -> p k c", c=128), r=[pst], w=[gT])
            for fo in range(NFT):
                po = self.PS[fo % 4]
                for kt in range(16):
                    self.mm(po.ap[:, 0:w], wo.ap[:, kt, fo * 128:(fo + 1) * 128], gT.ap[:, kt, 0:w], kt == 0, kt == 15, r=[wo, gT], w=[po])
                self.stt(xbi.ap[:, fo, 0:w], po.ap[:, 0:w], mod.ap[:, 2, fo, cs:cs + 1], xbi.ap[:, fo, 0:w], ALU.mult, ALU.add, r=[po, mod, xbi], w=[xbi])
            self.DMA("sync", XT.ap[:, s:s + w].rearrange("(ft p) t -> p ft t", p=128), xbi.ap[:, :, 0:w], r=[xbi], w=[XT])


    def na_bias_table(self, rpb, BTD):
        self.new_phase()
        neg = self.f32(7680, "negt")
        self.memset("vector", neg.ap, -30000.0, [neg])
        self.DMA("sync", BTD.ap.rearrange("h w r c -> (h w r c)").rearrange("(p n) -> p n", p=128), neg.ap, r=[neg], w=[BTD])
        HS = 64 * 15 * 64
        for h in range(16):
            so = h * 465
            do = h * HS
            dst = bass.AP(BTD.ap.tensor, do + 8 * 960, [[64, 15], [961, 49], [1, 16]])
            src = bass.AP(rpb.ap.tensor, so + 7, [[31, 15], [0, 49], [1, 16]])
            self.DMA("sync", dst, src, r=[rpb], w=[BTD], slow=True)
            dst = bass.AP(BTD.ap.tensor, do, [[64, 15], [960, 8], [1, 16]])
            src = bass.AP(rpb.ap.tensor, so + 15, [[31, 15], [-1, 8], [1, 16]])
            self.DMA("sync", dst, src, r=[rpb], w=[BTD], slow=True)
            dst = bass.AP(BTD.ap.tensor, do + 57 * 960 + 48, [[64, 15], [960, 7], [1, 16]])
            src = bass.AP(rpb.ap.tensor, so + 6, [[31, 15], [-1, 7], [1, 16]])
            self.DMA("sync", dst, src, r=[rpb], w=[BTD], slow=True)

    def na_phase_a(self, HT, wqkv, QT, KT, VT):
        self.new_phase()
        Hres = self.bf16(8 * T, "Hres", shape=(8, T))
        for kt in range(8):
            self.DMA("sync", Hres.ap[:, kt, :], HT.ap[kt * 128:(kt + 1) * 128, :], r=[HT], w=[Hres])
        wv = self.bf16(8 * 1024, "wv", shape=(8, 1024))
        self.load_w_bf16(wv, wqkv.ap[0], wqkv, 8, 2048, 3072)
        vb = [self.bf16(1024, "vb%d" % i) for i in range(2)]
        for tt in range(T // 128):
            v_ = vb[tt % 2]
            for vc in range(2):
                ps = self.PS[vc + 2 * (tt % 2)]
                for kt in range(8):
                    self.mm(ps.ap, Hres.ap[:, kt, tt * 128:(tt + 1) * 128], wv.ap[:, kt, vc * 512:(vc + 1) * 512], kt == 0, kt == 7, r=[Hres, wv], w=[ps])
                self.cp("scalar" if vc else "vector", v_.ap[:, vc * 512:(vc + 1) * 512], ps.ap, r=[ps], w=[v_])
            self.DMA("sync", VT.ap[tt * 128:(tt + 1) * 128, :], v_.ap, r=[v_], w=[VT])
        wch = [self.bf16(8 * 512, "wq%d" % i, shape=(8, 512)) for i in range(2)]
        ob = [self.bf16(T, "qo%d" % i) for i in range(2)]
        chunks = self.chunks()
        pi = 0
        for wc in range(4):
            wb = wch[wc % 2]
            self.load_w_bf16(wb, wqkv.ap[0], wqkv, 8, wc * 512, (wc + 1) * 512)
            for q in range(4):
                ot = wc * 4 + q
                obi = ob[ot % 2]
                for ci, (s, w, isctx) in enumerate(chunks):
                    ps = self.PS[4 + pi % 4]
                    pi += 1
                    for kt in range(8):
                        self.mm(ps.ap[:, 0:w], wb.ap[:, kt, q * 128:(q + 1) * 128], Hres.ap[:, kt, s:s + w], kt == 0, kt == 7, r=[wb, Hres], w=[ps])
                    if ot < 8:
                        self.act(obi.ap[:, s:s + w], ps.ap[:, 0:w], AF.Copy, r=[ps], w=[obi], scale=0.125)
                    else:
                        self.cp("vector", obi.ap[:, s:s + w], ps.ap[:, 0:w], r=[ps], w=[obi])
                dst = QT.ap[ot * 128:(ot + 1) * 128, :] if ot < 8 else KT.ap[(ot - 8) * 128:(ot - 7) * 128, :]
                self.DMA("sync", dst, obi.ap, r=[obi], w=[QT if ot < 8 else KT])

    def na_phase_b(self, QT, KT, VT, BTD, YT):
        self.new_phase()
        V, G = "vector", "gpsimd"
        Qp = [self.bf16(T, "Qp%d" % i) for i in range(2)]
        Kp = [self.bf16(T, "Kp%d" % i) for i in range(2)]
        Ve = [self.bf16(34 * 128, "Ve%d" % i, shape=(34, 128)) for i in range(2)]
        Vo = [self.bf16(33 * 128, "Vo%d" % i, shape=(33, 128)) for i in range(2)]
        BTt = [self.f32(960, "BTt%d" % i) for i in range(2)]
        YTp = [self.bf16(T, "YTp%d" % i) for i in range(2)]
        NBLK = 68
        Qbd = [self.bf16(NBLK * 128, "Qbd%d" % i, shape=(NBLK, 128)) for i in range(2)]
        for q_ in Qbd:
            self.memset(G, q_.ap, 0.0, [q_])
        NP = 4
        sc = [self.f32(768, "sc%d" % i) for i in range(NP)]
        pe = [self.bf16(768, "pe%d" % i) for i in range(NP)]
        pn = [self.bf16(768, "pn%d" % i) for i in range(NP)]
        pT = [self.bf16(768, "pT%d" % i, shape=(6, 128)) for i in range(NP)]
        st = [self.f32(8, "st%d" % i) for i in range(NP)]
        blocks = []
        for hp in range(8):
            bl = [("c", i) for i in range(4)] + [("l", r) for r in range(64)]
            for bi, (kind, idx) in enumerate(bl):
                blocks.append((hp, kind, idx, bi == 0, bi == len(bl) - 1))

        def geom(kind, idx):
            if kind == "c":
                return idx * 64, 256, 0, 0
            r = idx
            start = min(max(r - 4, 0), 56)
            return LC + r * 64, 768, start - r + 7, LC + start * 64

        def load_pair(hp):
            h2 = hp % 2
            self.DMA("sync", Qp[h2].ap, QT.ap[hp * 128:(hp + 1) * 128, :], r=[QT], w=[Qp[h2]])
            self.DMA("sync", Kp[h2].ap, KT.ap[hp * 128:(hp + 1) * 128, :], r=[KT], w=[Kp[h2]])
            self.DMA("sync", Ve[h2].ap, VT.ap[:, hp * 128:(hp + 1) * 128].rearrange("(tt p) c -> p tt c", p=128), r=[VT], w=[Ve[h2]])
            self.DMA("sync", Vo[h2].ap, VT.ap[64:64 + 33 * 128, hp * 128:(hp + 1) * 128].rearrange("(tt p) c -> p tt c", p=128), r=[VT], w=[Vo[h2]])
            self.DMA("sync", BTt[h2].ap, BTD.ap[2 * hp:2 * hp + 2].rearrange("two w r c -> (two w) (r c)"), r=[BTD], w=[BTt[h2]])
            qv = Qp[h2].ap.rearrange("p (b c) -> p b c", c=64)
            self.cp(G, Qbd[h2].ap[0:64, :, 0:64], qv[0:64, :, :], r=[Qp[h2]], w=[Qbd[h2]])
            self.cp(G, Qbd[h2].ap[64:128, :, 64:128], qv[64:128, :, :], r=[Qp[h2]], w=[Qbd[h2]])

        def s1(i, late=None):
            hp, kind, idx, pfirst, plast = blocks[i]
            h2 = hp % 2
            if pfirst and late is not True:
                load_pair(hp)
            qpos, nk, ro0, kpos = geom(kind, idx)
            qbt, sci, pei, sti = Qbd[h2], sc[i % NP], pe[i % NP], st[i % NP]
            ps_l, ps_c = self.PS[(i % 2) * 2], self.PS[(i % 2) * 2 + 1]
            qb_ap = qbt.ap[:, qpos // 64, :]
            if late is None or late is False:
                if kind == "l":
                    self.mm(ps_l.ap, qb_ap, Kp[h2].ap[:, kpos:kpos + 512], True, True, r=[qbt, Kp[h2]], w=[ps_l])
                    self.mm(ps_c.ap[:, 0:256], qb_ap, Kp[h2].ap[:, 0:256], True, True, r=[qbt, Kp[h2]], w=[ps_c])
                    self.tt(V, sci.ap[:, 0:512], ps_l.ap, BTt[h2].ap[:, ro0 * 64:ro0 * 64 + 512], ALU.add, r=[ps_l, BTt[h2]], w=[sci])
                    self.cp("scalar", sci.ap[:, 512:768], ps_c.ap[:, 0:256], r=[ps_c], w=[sci])
                else:
                    self.mm(ps_c.ap[:, 0:256], qb_ap, Kp[h2].ap[:, 0:256], True, True, r=[qbt, Kp[h2]], w=[ps_c])
                    self.cp("scalar", sci.ap[:, 0:256], ps_c.ap[:, 0:256], r=[ps_c], w=[sci])
                if late is False:
                    return
            self.E(V, lambda e, o=sti.ap[:, 0:1], i_=sci.ap[:, 0:nk]: e.reduce_max(out=o, in_=i_, axis=AX.X), r=[sci], w=[sti])
            self.ts(V, sti.ap[:, 1:2], sti.ap[:, 0:1], -1.0, ALU.mult, r=[sti], w=[sti])
            self.act(pei.ap[:, 0:nk], sci.ap[:, 0:nk], AF.Exp, r=[sci, sti], w=[pei, sti], bias=sti.ap[:, 1:2], accum=sti.ap[:, 2:3])
            self.E(V, lambda e, o=sti.ap[:, 3:4], i_=sti.ap[:, 2:3]: e.reciprocal(out=o, in_=i_), r=[sti], w=[sti])

        def s2(i, part=None):
            hp, kind, idx, pfirst, plast = blocks[i]
            h2 = hp % 2
            qpos, nk, ro0, kpos = geom(kind, idx)
            pei, pni, pTi, sti = pe[i % NP], pn[i % NP], pT[i % NP], st[i % NP]
            pst, pstb = self.PS[4 + i % 2], self.PSB[4 + i % 2]
            pso = self.PS[6 + i % 2]
            nkt = nk // 128
            if part in (None, 0):
                self.ts(G, pni.ap[:, 0:nk], pei.ap[:, 0:nk], sti.ap[:, 3:4], ALU.mult, 0.0, ALU.add, r=[pei, sti], w=[pni])
                for kt in range(nkt):
                    self.tr(pstb[:, kt * 128:(kt + 1) * 128], pni.ap[:, kt * 128:(kt + 1) * 128], self.identb.ap, r=[pni, self.identb], w=[pst])
                self.cp("scalar", pTi.ap[:, 0:nkt, :], pstb[:, 0:nk].rearrange("p (k c) -> p k c", c=128), r=[pst], w=[pTi])
                if part == 0:
                    return
            for kt in range(nkt):
                if kind == "c":
                    vt = Ve[h2].ap[:, kt, :]
                elif kt >= 4:
                    vt = Ve[h2].ap[:, kt - 4, :]
                else:
                    tok0 = kpos + kt * 128
                    vt = Ve[h2].ap[:, tok0 // 128, :] if tok0 % 128 == 0 else Vo[h2].ap[:, (tok0 - 64) // 128, :]
                self.mm(pso.ap[:, 0:128], vt, pTi.ap[:, kt, :], kt == 0, kt == nkt - 1, r=[Ve[h2], Vo[h2], pTi], w=[pso])
            self.cp(V, YTp[h2].ap[0:64, qpos:qpos + 64], pso.ap[0:64, 0:64], r=[pso], w=[YTp[h2]])
            self.cp("scalar", YTp[h2].ap[64:128, qpos:qpos + 64], pso.ap[64:128, 64:128], r=[pso], w=[YTp[h2]])
            if plast:
                self.DMA("sync", YT.ap[hp * 128:(hp + 1) * 128, :], YTp[h2].ap, r=[YTp[h2]], w=[YT])

        nb_ = len(blocks)
        for step in range(nb_ + 3):
            a = step
            if a < nb_:
                s1(a, late=False)
            b_ = step - 1
            if 0 <= b_ < nb_:
                s1(b_, late=True)
            c_ = step - 2
            if 0 <= c_ < nb_:
                s2(c_, part=0)
            d_ = step - 3
            if 0 <= d_ < nb_:
                s2(d_, part=1)


def _build(cfg):
    nc = bass.Bass("TRN2", target_bir_lowering=False)
    kb = KB(nc, cfg)
    IN = lambda n, s: kb.dram_t(n, s, F32, kind="ExternalInput")
    x_in = IN("x", [LL, D])
    ctx_in = IN("ctx", [LC, D])
    c_in = IN("c", [D])
    cctx_in = IN("c_ctx", [D])
    ada_w = IN("ada_w", [DEPTH, D, 6 * D])
    ada_b = IN("ada_b", [DEPTH, 6 * D])
    norm_mix = IN("norm_mix", [DEPTH, D])
    norm_ffn = IN("norm_ffn", [DEPTH, D])
    norm_final = IN("norm_final", [D])
    w1 = IN("ffn_w1", [DEPTH, D, FH])
    w3 = IN("ffn_w3", [DEPTH, D, FH])
    w2 = IN("ffn_w2", [DEPTH, FH, D])
    S5 = {
        "lam_re": IN("s5_lam_re", [2, 2, 64, 64]), "lam_im": IN("s5_lam_im", [2, 2, 64, 64]),
        "log_step": IN("s5_log_step", [2, 2, 64]),
        "b_re": IN("s5_b_re", [2, 2, 64, 64, 16]), "b_im": IN("s5_b_im", [2, 2, 64, 64, 16]),
        "c_re": IN("s5_c_re", [2, 2, 64, 16, 64]), "c_im": IN("s5_c_im", [2, 2, 64, 16, 64]),
        "d": IN("s5_d", [2, D]), "w_glu": IN("s5_w_glu", [2, D, D]), "b_glu": IN("s5_b_glu", [2, D]),
    }
    SSD = {
        "w_in": IN("ssd_w_in", [1, D, 8256]), "conv_w": IN("ssd_conv_w", [1, 5, 6144]), "conv_b": IN("ssd_conv_b", [1, 6144]),
        "dt_bias": IN("ssd_dt_bias", [1, 2, 32]), "a_log": IN("ssd_a_log", [1, 2, 32]), "d": IN("ssd_d", [1, 32]),
        "norm": IN("ssd_norm", [1, 2048]), "w_out": IN("ssd_w_out", [1, 2048, D]),
    }
    NA = {"w_qkv": IN("na_w_qkv", [1, D, 3 * D]), "w_o": IN("na_w_o", [1, D, D]), "rpb": IN("na_rpb", [1, 16, 15, 31])}
    out_t = kb.dram_t("out", [LL, D], F32, kind="ExternalOutput")
    QT = kb.dram_t("QT", [D, T], BF16)
    KT = kb.dram_t("KT", [D, T], BF16)
    VT = kb.dram_t("VT", [T, D], BF16)
    YT = kb.dram_t("YT", [D, T], BF16)
    BTD = kb.dram_t("BTD", [16, 64, 15, 64], F32)
    SZ = kb.dram_t("SZ", [T, 2048], BF16)
    XS = kb.dram_t("XS", [2048, T], BF16)
    BC = kb.dram_t("BC", [4096, T], BF16)
    DTR = kb.dram_t("DTR", [64, T], F32)
    YF = kb.dram_t("YF", [T, 2048], BF16)
    YB = kb.dram_t("YB", [T, 2048], BF16)
    XT = kb.dram_t("XT", [D, T], F32)
    HT = kb.dram_t("HT", [D, T], BF16)
    GT = kb.dram_t("GT", [D, T], BF16)
    layers = cfg.get("layers", list(range(DEPTH)))
    kb.consts()
    kb.build_mask8()
    kb.adaln(c_in, cctx_in, ada_w, ada_b, norm_mix, norm_ffn, norm_final, layers)
    kb.prologue_transpose(x_in, ctx_in, XT)
    for l in layers:
        kind, j = l % 3, l // 3
        pre_w = None
        if cfg.get("mixer", True):
            kb.norm_layer(XT, HT, l, 0)
            if kind == 0:
                kb.s5_phase(HT, GT, S5, j)
                if cfg.get("ffn", True) and cfg.get("prefetch_ffn", False):
                    kb.new_phase()
                    pre_w = kb.ffn_weights(w1, w3, w2, l, True, True)
                kb.proj_phase(XT, GT, S5["w_glu"], S5["w_glu"].ap[j], 8, l, glu_bias=(S5["b_glu"], S5["b_glu"].ap[j]), Wt=256 if pre_w else 512)
            elif kind == 1:
                kb.ssd_phase_a(HT, SSD, SZ, XS, BC, DTR)
                kb.ssd_phase_b(SSD, XS, BC, DTR, YF, YB)
                kb.ssd_phase_c(XT, SSD, SZ, YF, YB, l)
            else:
                kb.na_bias_table(NA["rpb"], BTD)
                kb.na_phase_a(HT, NA["w_qkv"], QT, KT, VT)
                kb.na_phase_b(QT, KT, VT, BTD, YT)
                if cfg.get("ffn", True) and cfg.get("prefetch_ffn", False):
                    kb.new_phase()
                    pre_w = kb.ffn_weights(w1, w3, w2, l, True, True)
                kb.proj_phase(XT, YT, NA["w_o"], NA["w_o"].ap[0], 8, l, Wt=256 if pre_w else 512)
        if cfg.get("ffn", True):
            kb.ffn_phase(XT, HT, w1, w3, w2, l, wts=pre_w)
            if pre_w is not None:
                kb.top = pre_w["top0"]
    kb.final_phase(XT, out_t)
    kb.p.fence("sync", kb.out_ops)
    kb.p.build()
    kb.es.close()
    return nc, kb


INPUT_NAMES = ["x", "ctx", "c", "c_ctx", "ada_w", "ada_b", "norm_mix", "norm_ffn", "norm_final", "ffn_w1", "ffn_w3", "ffn_w2",
               "s5_lam_re", "s5_lam_im", "s5_log_step", "s5_b_re", "s5_b_im", "s5_c_re", "s5_c_im", "s5_d", "s5_w_glu", "s5_b_glu",
               "ssd_w_in", "ssd_conv_w", "ssd_conv_b", "ssd_dt_bias", "ssd_a_log", "ssd_d", "ssd_norm", "ssd_w_out",
               "na_w_qkv", "na_w_o", "na_rpb"]


def kernel(**inputs):
    cfg = {}
    nc, kb = _build(cfg)
    n = 8
    in_maps = []
    for b in range(n):
        m = {}
        for k in INPUT_NAMES:
            v = np.ascontiguousarray(inputs[k], dtype=np.float32)
            if k in ("x", "ctx", "c"):
                v = np.ascontiguousarray(v[b])
            m[k] = v
        in_maps.append(m)
    res = run_bass_kernel_spmd(nc, in_maps, core_ids=list(range(n)))
    return np.stack([np.asarray(r["out"], dtype=np.float32) for r in res.results], axis=0)
```
